# Optimizing a Trainium2 kernel written in Bass

```python
import math
import jax, jax.numpy as jnp
from jax import lax
import numpy as np

D_MODEL = 1024
BATCH = 8
SEQ = 4096
DEPTH = 4

CHUNK = 64
Q_BLOCK = 128
N_MIXERS = 4
HEAD_DIM = 64
GW = D_MODEL // N_MIXERS
HG = GW // HEAD_DIM
CG = GW // HG
DQ = HEAD_DIM // 2
RW_DECAY_RANK = 32
RW_A_RANK = 32
RW_GATE_RANK = 64
RW_GN_EPS = 64e-5
SG_CHUNK = 128
D_FF = 2816
FFN_CONV = 3
ROPE_THETA = 10000.0
EPS = 1e-6
NEG_INF = -1e30

A_SIZES = [GW, GW, GW, RW_DECAY_RANK, RW_A_RANK, RW_GATE_RANK]
B_SIZES = [GW, GW, GW]
C_SIZES = [GW, GW]
D_SIZES = [GW, GW, GW, HG]
GROUP_COLS = [sum(A_SIZES), sum(B_SIZES), sum(C_SIZES), sum(D_SIZES)]
N_IN = sum(GROUP_COLS)

kernel_name = 'hybrid_parallel_heads_streaming_encoder'


def _split(t, sizes):
    idx = [int(i) for i in np.cumsum(sizes)[:-1]]
    return jnp.split(t, idx, axis=-1)


def _rms_norm(x, g):
    xf = x.astype(jnp.float32)
    y = xf * lax.rsqrt(jnp.mean(xf * xf, axis=-1, keepdims=True) + EPS)
    return (y * g).astype(x.dtype)


def _layer_norm(x, g, b):
    xf = x.astype(jnp.float32)
    mu = jnp.mean(xf, axis=-1, keepdims=True)
    xc = xf - mu
    var = jnp.mean(xc * xc, axis=-1, keepdims=True)
    return (xc * lax.rsqrt(var + EPS) * g + b).astype(x.dtype)


def _rope_tables(seq, dim):
    inv = 1.0 / (ROPE_THETA ** (jnp.arange(0, dim, 2, dtype=jnp.float32) / dim))
    ang = jnp.arange(seq, dtype=jnp.float32)[:, None] * inv[None, :]
    return jnp.cos(ang), jnp.sin(ang)


def _apply_rope(x, cos, sin):
    x1, x2 = jnp.split(x, 2, axis=-1)
    c = cos.astype(x.dtype)
    s = sin.astype(x.dtype)
    return jnp.concatenate([x1 * c - x2 * s, x2 * c + x1 * s], axis=-1)


def _causal_dwconv(x, w, b):
    K, C = w.shape
    y = lax.conv_general_dilated(x, w[:, None, :].astype(x.dtype), window_strides=(1,),
                                 padding=[(K - 1, 0)],
                                 dimension_numbers=('NWC', 'WIO', 'NWC'),
                                 feature_group_count=C)
    return y + b


def _rwkv7_scan(r, w, k, v, kk, b):
    Bn, S, H, N = r.shape

    def step(state, inp):
        r_t, w_t, k_t, v_t, kk_t, b_t = inp
        sa = jnp.einsum('bhvk,bhk->bhv', state, -kk_t)
        state = (state * w_t[:, :, None, :]
                 + sa[..., None] * b_t[:, :, None, :]
                 + v_t[..., None] * k_t[:, :, None, :])
        return state, jnp.einsum('bhvk,bhk->bhv', state, r_t)

    xs = tuple(jnp.swapaxes(t, 0, 1) for t in (r, w, k, v, kk, b))
    state0 = jnp.zeros((Bn, H, N, N), jnp.float32)
    _, ys = lax.scan(step, state0, xs)
    return jnp.swapaxes(ys, 0, 1)


def _rwkv7_mixer(p, mu, w0, w_up, a0, a_up, g_up, k_k, k_a, r_k, ln_g, ln_b):
    Bn, S, _ = p.shape
    f32 = jnp.float32
    p_prev = jnp.pad(p, ((0, 0), (1, 0), (0, 0)))[:, :-1]
    p = p + (p_prev - p) * mu
    r, k, v, wd, ad, gd = _split(p, A_SIZES)
    w_log = -jax.nn.softplus(-(w0 + jnp.tanh(wd) @ w_up).astype(f32)) - 0.5
    decay = jnp.exp(-jnp.exp(w_log))
    a = jax.nn.sigmoid((a0 + ad @ a_up).astype(f32))
    g = jax.nn.sigmoid(gd) @ g_up
    heads = lambda t: t.reshape(Bn, S, HG, HEAD_DIM)
    k = k.astype(f32)
    kk = heads(k * k_k)
    kk = kk / jnp.maximum(jnp.sqrt(jnp.sum(kk * kk, axis=-1, keepdims=True)), 1e-12)
    k = heads(k * (1.0 + (a - 1.0) * k_a))
    r = heads(r.astype(f32))
    v = heads(v.astype(f32))
    ah = heads(a)
    y = _rwkv7_scan(r, heads(decay), k, v, kk, kk * ah)
    mean = jnp.mean(y, axis=-1, keepdims=True)
    yc = y - mean
    y = yc * lax.rsqrt(jnp.mean(yc * yc, axis=-1, keepdims=True) + RW_GN_EPS)
    y = y.reshape(Bn, S, GW) * ln_g + ln_b
    bonus = jnp.sum(r * k * r_k, axis=-1, keepdims=True) * v
    return (y + bonus.reshape(Bn, S, GW)) * g


def _diff_attention(q1, q2, k1, k2, v, lam):
    Bn, H, S, dq = q1.shape
    scale = dq ** -0.5
    key_chunk = jnp.arange(S) // CHUNK

    def block(i):
        start = i * Q_BLOCK
        q1b = lax.dynamic_slice_in_dim(q1, start, Q_BLOCK, axis=2)
        q2b = lax.dynamic_slice_in_dim(q2, start, Q_BLOCK, axis=2)
        q_chunk = (start + jnp.arange(Q_BLOCK)) // CHUNK
        mask = key_chunk[None, :] <= q_chunk[:, None]
        s1 = jnp.einsum('bhqd,bhkd->bhqk', q1b, k1).astype(jnp.float32) * scale
        s2 = jnp.einsum('bhqd,bhkd->bhqk', q2b, k2).astype(jnp.float32) * scale
        p1 = jax.nn.softmax(jnp.where(mask, s1, NEG_INF), axis=-1)
        p2 = jax.nn.softmax(jnp.where(mask, s2, NEG_INF), axis=-1)
        return jnp.einsum('bhqk,bhkd->bhqd', (p1 - lam * p2).astype(v.dtype), v)

    out = lax.map(block, jnp.arange(S // Q_BLOCK))
    return jnp.moveaxis(out, 0, 2).reshape(Bn, H, S, v.shape[-1])


def _diff_attn_mixer(p, lam_q1, lam_k1, lam_q2, lam_k2, q_g, k_g, sub_g, cos, sin, lambda_init):
    Bn, S, _ = p.shape
    q, k, v = _split(p, B_SIZES)
    q = _rms_norm(q.reshape(Bn, S, HG, 2, DQ), q_g).transpose(0, 3, 2, 1, 4)
    k = _rms_norm(k.reshape(Bn, S, HG, 2, DQ), k_g).transpose(0, 3, 2, 1, 4)
    q = _apply_rope(q, cos, sin)
    k = _apply_rope(k, cos, sin)
    v = v.reshape(Bn, S, HG, HEAD_DIM).transpose(0, 2, 1, 3)
    f32 = jnp.float32
    lam = (jnp.exp(jnp.sum(lam_q1.astype(f32) * lam_k1.astype(f32)))
           - jnp.exp(jnp.sum(lam_q2.astype(f32) * lam_k2.astype(f32))) + lambda_init)
    o = _diff_attention(q[:, 0], q[:, 1], k[:, 0], k[:, 1], v, lam)
    o = _rms_norm(o, sub_g) * (1.0 - lambda_init)
    return o.transpose(0, 2, 1, 3).reshape(Bn, S, GW)


def _spatial_gating_mixer(p, sg_w, sg_b, ln_g, ln_b):
    Bn, S, _ = p.shape
    u, v = _split(jax.nn.gelu(p, approximate=False), C_SIZES)
    v = _layer_norm(v, ln_g, ln_b)
    v = v.reshape(Bn, S // SG_CHUNK, SG_CHUNK, HG, CG)
    w = sg_w * jnp.tril(jnp.ones((SG_CHUNK, SG_CHUNK), sg_w.dtype))
    sv = jnp.einsum('gij,bnjgc->bnigc', w, v) + sg_b.T[None, None, :, :, None]
    return u * sv.reshape(Bn, S, GW)


def _forgetting_attention(q, k, v, log_f):
    Bn, H, S, d = q.shape
    scale = d ** -0.5
    F = lax.cumsum(log_f, axis=2)
    key_pos = jnp.arange(S)

    def block(i):
        start = i * Q_BLOCK
        qb = lax.dynamic_slice_in_dim(q, start, Q_BLOCK, axis=2)
        Fq = lax.dynamic_slice_in_dim(F, start, Q_BLOCK, axis=2)
        q_pos = start + jnp.arange(Q_BLOCK)
        mask = key_pos[None, :] <= q_pos[:, None]
        s = (jnp.einsum('bhqd,bhkd->bhqk', qb, k).astype(jnp.float32) * scale
             + (Fq[..., :, None] - F[..., None, :]))
        pr = jax.nn.softmax(jnp.where(mask, s, NEG_INF), axis=-1)
        return jnp.einsum('bhqk,bhkd->bhqd', pr.astype(v.dtype), v)

    out = lax.map(block, jnp.arange(S // Q_BLOCK))
    return jnp.moveaxis(out, 0, 2).reshape(Bn, H, S, d)


def _forgetting_attn_mixer(p, q_g, k_g, f_b):
    Bn, S, _ = p.shape
    q, k, v, fl = _split(p, D_SIZES)
    q = _rms_norm(q.reshape(Bn, S, HG, HEAD_DIM), q_g).transpose(0, 2, 1, 3)
    k = _rms_norm(k.reshape(Bn, S, HG, HEAD_DIM), k_g).transpose(0, 2, 1, 3)
    v = v.reshape(Bn, S, HG, HEAD_DIM).transpose(0, 2, 1, 3)
    log_f = jax.nn.log_sigmoid((fl + f_b).astype(jnp.float32)).transpose(0, 2, 1)
    o = _forgetting_attention(q, k, v, log_f)
    return o.transpose(0, 2, 1, 3).reshape(Bn, S, GW)


def _conv_glu_ffn(h, w_up, w_conv, b_conv, w_down):
    u = _causal_dwconv(h @ w_up, w_conv, b_conv)
    a, g = jnp.split(u, 2, axis=-1)
    return (a * jax.nn.silu(g)) @ w_down


def setup_inputs(seed: int = 0) -> dict:
    key = jax.random.key(seed)
    ks = iter(list(jax.random.split(key, 48)))
    L, D = DEPTH, D_MODEL
    f32 = jnp.float32

    def nrm(shape, scale):
        return jax.random.normal(next(ks), shape, f32) * scale

    def gain(shape):
        return 1.0 + 0.05 * jax.random.normal(next(ks), shape, f32)

    return {
        'x': nrm((BATCH, SEQ, D), 1.0),
        'c': nrm((BATCH, D), 1.0),
        'ada_w': nrm((L, D, 6 * D), 0.5 * D ** -0.5),
        'ada_b': nrm((L, 6 * D), 0.02),
        'norm1_g': gain((L, D)),
        'norm2_g': gain((L, D)),
        'w_in': nrm((L, D, N_IN), D ** -0.5),
        'w_out': nrm((L, D, D), D ** -0.5),
        'rw_mu': jax.random.uniform(next(ks), (L, GROUP_COLS[0]), f32),
        'rw_w0': nrm((L, GW), 0.5),
        'rw_w_up': nrm((L, RW_DECAY_RANK, GW), RW_DECAY_RANK ** -0.5),
        'rw_a0': nrm((L, GW), 0.1),
        'rw_a_up': nrm((L, RW_A_RANK, GW), RW_A_RANK ** -0.5),
        'rw_g_up': nrm((L, RW_GATE_RANK, GW), RW_GATE_RANK ** -0.5),
        'rw_k_k': 0.85 + 0.05 * jax.random.normal(next(ks), (L, GW), f32),
        'rw_k_a': gain((L, GW)),
        'rw_r_k': nrm((L, HG, HEAD_DIM), 0.1),
        'rw_ln_g': gain((L, GW)),
        'rw_ln_b': nrm((L, GW), 0.02),
        'df_lam_q1': nrm((L, DQ), 0.1),
        'df_lam_k1': nrm((L, DQ), 0.1),
        'df_lam_q2': nrm((L, DQ), 0.1),
        'df_lam_k2': nrm((L, DQ), 0.1),
        'df_q_g': gain((L, DQ)),
        'df_k_g': gain((L, DQ)),
        'df_sub_g': gain((L, HEAD_DIM)),
        'sg_w': nrm((L, HG, SG_CHUNK, SG_CHUNK), SG_CHUNK ** -0.5),
        'sg_b': gain((L, HG, SG_CHUNK)),
        'sg_ln_g': gain((L, GW)),
        'sg_ln_b': nrm((L, GW), 0.02),
        'fx_q_g': gain((L, HEAD_DIM)),
        'fx_k_g': gain((L, HEAD_DIM)),
        'fx_f_b': 2.0 + 0.5 * jax.random.normal(next(ks), (L, HG), f32),
        'ffn_up': nrm((L, D, 2 * D_FF), D ** -0.5),
        'ffn_conv': nrm((L, FFN_CONV, 2 * D_FF), FFN_CONV ** -0.5),
        'ffn_conv_b': nrm((L, 2 * D_FF), 0.02),
        'ffn_down': nrm((L, D_FF, D), D_FF ** -0.5),
    }


def reference(x, c, ada_w, ada_b, norm1_g, norm2_g, w_in, w_out,
              rw_mu, rw_w0, rw_w_up, rw_a0, rw_a_up, rw_g_up, rw_k_k, rw_k_a, rw_r_k,
              rw_ln_g, rw_ln_b,
              df_lam_q1, df_lam_k1, df_lam_q2, df_lam_k2, df_q_g, df_k_g, df_sub_g,
              sg_w, sg_b, sg_ln_g, sg_ln_b,
              fx_q_g, fx_k_g, fx_f_b,
              ffn_up, ffn_conv, ffn_conv_b, ffn_down):
    Bn, S, D = x.shape
    dt = x.dtype
    cos, sin = _rope_tables(S, DQ)
    cond = jax.nn.silu(c)
    for l in range(DEPTH):
        lambda_init = 0.8 - 0.6 * math.exp(-0.3 * l)
        mod = cond @ ada_w[l] + ada_b[l]
        sh1, sc1, g1, sh2, sc2, g2 = [m[:, None, :] for m in jnp.split(mod, 6, axis=-1)]
        h = _rms_norm(x, norm1_g[l]) * (1.0 + sc1) + sh1
        pa, pb, pc, pd = _split(h @ w_in[l], GROUP_COLS)
        ya = _rwkv7_mixer(pa, rw_mu[l], rw_w0[l], rw_w_up[l], rw_a0[l], rw_a_up[l], rw_g_up[l],
                          rw_k_k[l], rw_k_a[l], rw_r_k[l], rw_ln_g[l], rw_ln_b[l])
        yb = _diff_attn_mixer(pb, df_lam_q1[l], df_lam_k1[l], df_lam_q2[l], df_lam_k2[l],
                              df_q_g[l], df_k_g[l], df_sub_g[l], cos, sin, lambda_init)
        yc = _spatial_gating_mixer(pc, sg_w[l], sg_b[l], sg_ln_g[l], sg_ln_b[l])
        yd = _forgetting_attn_mixer(pd, fx_q_g[l], fx_k_g[l], fx_f_b[l])
        y = jnp.concatenate([ya.astype(dt), yb.astype(dt), yc.astype(dt), yd.astype(dt)], axis=-1)
        x = x + g1 * (y @ w_out[l])
        h = _rms_norm(x, norm2_g[l]) * (1.0 + sc2) + sh2
        x = x + g2 * _conv_glu_ffn(h, ffn_up[l], ffn_conv[l], ffn_conv_b[l], ffn_down[l])
    return x
```

```python
import math
import os
import numpy as np
import concourse.bass as bass
import concourse.mybir as mybir
from concourse.bass_utils import run_bass_kernel_spmd
from contextlib import ExitStack

F32 = mybir.dt.float32
BF16 = mybir.dt.bfloat16
AF = mybir.ActivationFunctionType
ALU = mybir.AluOpType
AX = mybir.AxisListType

S = 4096
D = 1024
L = 4
NIN = 2948
DFF = 2816
NG = 8
TG = 512
EPS = 1e-6
ALPHA = math.exp(-0.5)

ENGS = ("pe", "act", "dve", "pool", "sp")
EPOCH = 30000


class Prog:
    def __init__(self, nc, es):
        self.nc = nc
        self.es = es
        self.q = {e: [] for e in ENGS}
        self.cnt = {e: 0 for e in ENGS}
        self.epoch = {e: 0 for e in ENGS}
        self.sems = {}
        self.seen = {e: {} for e in ENGS}
        self.res_w = {}
        self.res_r = {}
        self.dma_val = {}
        self.n_inst = 0
        self.rr = 0

    def _sem(self, key):
        if key not in self.sems:
            self.sems[key] = self.es.enter_context(self.nc.semaphore("s_" + "_".join(str(k) for k in key)))
        return self.sems[key]

    def _deps(self, eng, reads, writes, extra=()):
        need = {}

        def add(ev):
            if ev is None:
                return
            k, v = ev
            if eng == "pe" and k[0] == "pe":
                return
            if need.get(k, 0) < v:
                need[k] = v
        for r in reads:
            add(self.res_w.get(r))
        for w in writes:
            add(self.res_w.get(w))
            for ev in self.res_r.get(w, ()):
                add(ev)
        for ev in extra:
            add(ev)
        waits = []
        for k, v in need.items():
            if self.seen[eng].get(k, 0) >= v:
                continue
            self.seen[eng][k] = v
            waits.append((k, v))
        return waits

    def _commit(self, ev, reads, writes):
        for r in reads:
            lst = self.res_r.setdefault(r, [])
            lst.append(ev)
            if len(lst) > 64:
                mx = {}
                for k, v in lst:
                    if mx.get(k, 0) < v:
                        mx[k] = v
                self.res_r[r] = list(mx.items())
        for w in writes:
            self.res_w[w] = ev
            self.res_r[w] = []

    @staticmethod
    def _is_psum(r):
        return (isinstance(r, str) and r.startswith("bank")) or (isinstance(r, tuple) and r[0] == "hb")

    def op(self, eng, fn, reads=(), writes=()):
        pr = [r for r in reads if self._is_psum(r)]
        if pr:
            writes = list(writes) + pr
        waits = self._deps(eng, reads, writes)
        if self.cnt[eng] >= EPOCH:
            self.epoch[eng] += 1
            self.cnt[eng] = 0
        self.cnt[eng] += 1
        key = (eng, self.epoch[eng])
        ev = (key, self.cnt[eng])
        self.q[eng].append((waits, fn, key, 1))
        self._commit(ev, reads, writes)
        self.n_inst += 1
        return ev

    def dma(self, queue, pairs, reads=(), writes=(), sem=None):
        if sem is None:
            sem = ("dma", "rr%d" % (self.rr % 20))
            self.rr += 1
        key = sem
        prev = self.dma_val.get(key, 0)
        extra = [(key, prev)] if prev > 0 else []
        waits = self._deps(queue, reads, writes, extra)
        val = prev
        for i, pr in enumerate(pairs):
            out_ap, in_ap = pr[0], pr[1]
            kw = pr[2] if len(pr) > 2 else {}
            val += 16

            def fn(e, out_ap=out_ap, in_ap=in_ap, kw=kw):
                return e.dma_start(out=out_ap, in_=in_ap, **kw)
            self.q[queue].append((waits if i == 0 else [], fn, key, 16))
            self.n_inst += 1
        self.dma_val[key] = val
        ev = (key, val)
        self._commit(ev, reads, writes)
        return ev

    def barrier(self):
        evs = []
        for e in ENGS:
            for ep in range(self.epoch[e] + 1):
                k = (e, ep)
                v = self.cnt[e] if ep == self.epoch[e] else EPOCH
                if v > 0:
                    evs.append((k, v))
        for k, v in self.dma_val.items():
            evs.append((k, v))
        for e in ENGS:
            waits = []
            for k, v in evs:
                if self.seen[e].get(k, 0) < v:
                    self.seen[e][k] = v
                    waits.append((k, v))
            if waits:
                self.q[e].append((waits, None, None, 0))
        self.res_w = {}
        self.res_r = {}

    def emit(self):
        nc = self.nc
        for e in ENGS:
            for (waits, fn, key, inc) in self.q[e]:
                for k, v in waits:
                    self._sem(k)
                if key is not None:
                    self._sem(key)
        block = self.es.enter_context(nc.Block())
        engmap = {"pe": block.tensor, "act": block.scalar, "dve": block.vector, "pool": block.gpsimd,
                  "sp": block.sync}
        for e in ENGS:
            items = self.q[e]

            def body(eng, items=items):
                for (waits, fn, key, inc) in items:
                    for k, v in waits:
                        eng.wait_ge(self.sems[k], v)
                    if fn is not None:
                        ins = fn(eng)
                        ins.then_inc(self.sems[key], inc)
            engmap[e](body)


class K:
    def __init__(self, P):
        self.P = P
        self._rot = {}

    def rot(self, name, n):
        i = self._rot.get(name, 0)
        self._rot[name] = i + 1
        return i % n

    def mm(self, out, lhsT, rhs, start=True, stop=True, r=(), w=(), sgc=False):
        if sgc:
            return self.P.op("pe", lambda e: e.matmul(out, lhsT=lhsT, rhs=rhs, start=start, stop=stop, skip_group_check=True), r, w)
        return self.P.op("pe", lambda e: e.matmul(out, lhsT=lhsT, rhs=rhs, start=start, stop=stop), r, w)

    def tr(self, out, in_, ident, r=(), w=()):
        return self.P.op("pe", lambda e: e.transpose(out=out, in_=in_, identity=ident), r, w)

    def act(self, out, in_, func, bias=None, scale=None, accum_out=None, r=(), w=(), eng="act"):
        kw = {}
        if bias is not None:
            kw["bias"] = bias
        if scale is not None:
            kw["scale"] = scale
        if accum_out is not None:
            kw["accum_out"] = accum_out
        return self.P.op("act", lambda e: e.activation(out=out, in_=in_, func=func, **kw), r, w)

    def copy(self, eng, out, in_, r=(), w=()):
        if eng == "act":
            return self.P.op("act", lambda e: e.copy(out=out, in_=in_), r, w)
        return self.P.op(eng, lambda e: e.tensor_copy(out=out, in_=in_), r, w)

    def tt(self, eng, out, in0, in1, op, r=(), w=()):
        return self.P.op(eng, lambda e: e.tensor_tensor(out=out, in0=in0, in1=in1, op=op), r, w)

    def ts(self, eng, out, in0, s1, op0, s2=None, op1=None, r=(), w=()):
        if op1 is None:
            return self.P.op(eng, lambda e: e.tensor_scalar(out=out, in0=in0, scalar1=s1, scalar2=None, op0=op0), r, w)
        return self.P.op(eng, lambda e: e.tensor_scalar(out=out, in0=in0, scalar1=s1, scalar2=s2, op0=op0, op1=op1), r, w)

    def stt(self, eng, out, in0, scalar, in1, op0, op1, r=(), w=()):
        return self.P.op(eng, lambda e: e.scalar_tensor_tensor(out=out, in0=in0, scalar=scalar, in1=in1, op0=op0, op1=op1), r, w)

    def red(self, eng, out, in_, op=ALU.add, r=(), w=()):
        return self.P.op(eng, lambda e: e.tensor_reduce(out=out, in_=in_, axis=AX.X, op=op), r, w)

    def recip(self, out, in_, r=(), w=()):
        return self.P.op("dve", lambda e: e.reciprocal(out=out, in_=in_), r, w)

    def memset(self, eng, ap, val, w=()):
        return self.P.op(eng, lambda e: e.memset(ap, val), (), w)

    def scan(self, out, d0, d1, initial, op0, op1, r=(), w=()):
        return self.P.op("dve", lambda e: e.tensor_tensor_scan(out=out, data0=d0, data1=d1, initial=initial, op0=op0, op1=op1), r, w)

    def dma(self, out, in_, r=(), w=(), q="sp", slow=False, sem=None):
        kw = {"allow_slow_non_contiguous": True} if slow else {}
        return self.P.dma(q, [(out, in_, kw)], r, w, sem=sem)


def make_consts():
    c = {}
    c["ident"] = np.eye(128, dtype=np.float32)
    bo64 = np.zeros((128, 128), np.float32)
    bo64[:64, :64] = 1
    bo64[64:, 64:] = 1
    c["bo64"] = bo64
    bo32 = np.zeros((128, 128), np.float32)
    for i in range(4):
        bo32[i * 32:(i + 1) * 32, i * 32:(i + 1) * 32] = 1
    c["bo32"] = bo32
    prot = np.zeros((128, 128), np.float32)
    for b in range(4):
        for d in range(16):
            prot[b * 32 + d + 16, b * 32 + d] = -1.0
            prot[b * 32 + d, b * 32 + d + 16] = 1.0
    c["prot"] = prot
    inv = 1.0 / (10000.0 ** (np.arange(0, 32, 2, dtype=np.float32) / 32.0))
    ang = np.arange(S, dtype=np.float32)[:, None] * inv[None, :]
    cos = np.cos(ang).astype(np.float32).T
    sin = np.sin(ang).astype(np.float32).T
    c["cosT"] = np.ascontiguousarray(np.tile(cos, (8, 1)))
    c["sinT"] = np.ascontiguousarray(np.tile(sin, (8, 1)))
    k = np.arange(128)[:, None]
    q = np.arange(128)[None, :]
    c["negmask"] = np.where(k > q, -1e30, 0.0).astype(np.float32)
    c["cmask"] = ((k // 64) <= (q // 64)).astype(np.float32)
    c["triu"] = (k <= q).astype(np.float32)
    k6 = np.arange(64)[:, None]
    q6 = np.arange(64)[None, :]
    m64 = np.zeros((64, 3, 64), np.float32)
    m64[:, 0, :] = (k6 < q6)
    m64[:, 1, :] = (k6 <= q6)
    m64[:, 2, :] = (k6 > q6)
    c["m64"] = m64
    sel = np.zeros((4, 4, 128), np.float32)
    for h in range(4):
        sel[h, h, :] = 1.0
    c["sel"] = sel.transpose(1, 0, 2).copy()
    return c


CONST_SHAPES = {"ident": [128, 128], "bo64": [128, 128], "bo32": [128, 128], "prot": [128, 128],
                "cosT": [128, S], "sinT": [128, S], "negmask": [128, 128], "cmask": [128, 128],
                "triu": [128, 128], "m64": [64, 3, 64], "sel": [4, 4, 128]}

WEIGHT_SHAPES = {
    'ada_w': [L, D, 6 * D], 'ada_b': [L, 6 * D], 'norm1_g': [L, D], 'norm2_g': [L, D],
    'w_in': [L, D, NIN], 'w_out': [L, D, D],
    'rw_mu': [L, 896], 'rw_w0': [L, 256], 'rw_w_up': [L, 32, 256], 'rw_a0': [L, 256], 'rw_a_up': [L, 32, 256],
    'rw_g_up': [L, 64, 256], 'rw_k_k': [L, 256], 'rw_k_a': [L, 256], 'rw_r_k': [L, 4, 64],
    'rw_ln_g': [L, 256], 'rw_ln_b': [L, 256],
    'df_lam_q1': [L, 32], 'df_lam_k1': [L, 32], 'df_lam_q2': [L, 32], 'df_lam_k2': [L, 32],
    'df_q_g': [L, 32], 'df_k_g': [L, 32], 'df_sub_g': [L, 64],
    'sg_w': [L, 4, 128, 128], 'sg_b': [L, 4, 128], 'sg_ln_g': [L, 256], 'sg_ln_b': [L, 256],
    'fx_q_g': [L, 64], 'fx_k_g': [L, 64], 'fx_f_b': [L, 4],
    'ffn_up': [L, D, 2 * DFF], 'ffn_conv': [L, 3, 2 * DFF], 'ffn_conv_b': [L, 2 * DFF], 'ffn_down': [L, DFF, D],
}


class Ctx:
    pass


def build(layers=(0, 1, 2, 3), phases=("P1", "A", "B", "D", "P3"), debug=False, y_in=False):
    nc = bass.Bass("TRN2", target_bir_lowering=False)
    dkind = "ExternalOutput" if debug else "Internal"
    I = {}

    def din(name, shape):
        I[name] = nc.dram_tensor(name, list(shape), F32, kind="ExternalInput").ap()
    din("x", [S, D])
    din("c", [1, D])
    for k, shp in WEIGHT_SHAPES.items():
        din(k, shp)
    for k, shp in CONST_SHAPES.items():
        din(k, shp)
    out = nc.dram_tensor("out", [S, D], F32, kind="ExternalOutput").ap()

    def dscr(name, shape, dt, kind=None):
        return nc.dram_tensor(name, list(shape), dt, kind=kind or dkind).ap()
    R = Ctx()
    R.xres = dscr("xres", [S, D], F32)
    R.pmA = dscr("pmA", [896, S], F32)
    R.qTB = dscr("qTB", [256, S], BF16)
    R.kTB = dscr("kTB", [256, S], BF16)
    R.vB = dscr("vB", [S, 260], BF16)
    R.qTD = dscr("qTD", [256, S], BF16)
    R.kTD = dscr("kTD", [256, S], BF16)
    R.vD = dscr("vD", [S, 260], BF16)
    R.Ffm = dscr("Ffm", [4, S], F32)
    if y_in:
        R.yscr = nc.dram_tensor("yscr_in", [S, 768], F32, kind="ExternalInput").ap()
        R.yTA = nc.dram_tensor("yTA_in", [256, S], F32, kind="ExternalInput").ap()
    else:
        R.yscr = dscr("yscr", [S, 768], BF16)
        R.yTA = dscr("yTA", [256, S], BF16)
    R.upbf = dscr("upbf", [L, 11, 128, 8 * 512], BF16, kind="Internal")
    R.dnbf = dscr("dnbf", [L, 128, 22 * 1024], BF16, kind="Internal")

    with ExitStack() as es:
        P = Prog(nc, es)
        k = K(P)
        G = Ctx()
        G.nc, G.P, G.k, G.I, G.R, G.out = nc, P, k, I, R, out
        G.y_in = y_in
        G.dbg_x1 = nc.dram_tensor("dbg_x1", [S, D], F32, kind="ExternalOutput").ap() if debug else None

        uid = [0]

        def sb(stack, name, shape, dt=F32):
            uid[0] += 1
            return stack.enter_context(nc.sbuf_tensor("%s_u%d" % (name, uid[0]), list(shape), dt))
        G.sb = sb
        G.banks = [es.enter_context(nc.psum_tensor("bank%d" % i, [128, 512], F32)) for i in range(8)]
        G.ident = sb(es, "ident", [128, 128])
        G.identb = sb(es, "identb", [128, 128], BF16)
        G.ones_row = sb(es, "ones_row", [1, 128])
        G.condB = sb(es, "condB", [128, 8, 128])
        G.modB = sb(es, "modB", [128, 6 * D])
        k.dma(G.ident[:], I["ident"], w=["ident"])
        k.copy("dve", G.identb[:], G.ident[:], r=["ident"], w=["identb"])
        k.memset("dve", G.ones_row[:], 1.0, w=["ones_row"])
        with ExitStack() as s0:
            cT = sb(s0, "cT", [128, 8])
            cS = sb(s0, "cS", [128, 8])
            k.dma(cT[:], I["c"].rearrange("o (kc p) -> p (o kc)", p=128), w=["cT"], slow=True)
            k.act(cS[:], cT[:], AF.Silu, r=["cT"], w=["cS"])
            k.copy("dve", G.condB[:], cS[:].unsqueeze(2).to_broadcast([128, 8, 128]), r=["cS"], w=["condB"])
            P.barrier()

        for li, l in enumerate(layers):
            xsrc = I["x"] if li == 0 else R.xres
            xdst = out if li == len(layers) - 1 else R.xres
            if "P3" in phases:
                prep_ffn(G, l)
            layer_setup(G, l)
            if "P1" in phases:
                phase_p1(G, l, xsrc)
            if "A" in phases:
                phase_A(G, l)
            if "B" in phases:
                phase_B(G, l)
            if "D" in phases:
                phase_D(G, l)
            if "P3" in phases:
                phase_p3(G, l, xsrc, xdst)
        P.barrier()
        P.emit()
    return nc, P


def prep_ffn(G, l):
    nc, P, k, I, R = G.nc, G.P, G.k, G.I, G.R
    with ExitStack() as s:
        st = [G.sb(s, "pf_st%d" % i, [128, 8, 512]) for i in range(2)]
        sbf = [G.sb(s, "pf_bf%d" % i, [128, 8, 512], BF16) for i in range(2)]
        up = I["ffn_up"][l].rearrange("(kc p) n -> p kc n", p=128)
        dn = I["ffn_down"][l].rearrange("(fc p) d -> p fc d", p=128)
        engs = ["pool", "dve", "act"]
        jobs = []
        for u in range(11):
            jobs.append((up[:, :, u * 512:(u + 1) * 512], R.upbf[l, u].rearrange("p (kc n) -> p kc n", kc=8), 8, 512))
        for u in range(11):
            jobs.append((dn[:, 2 * u:2 * u + 2, :], R.dnbf[l][:, 2 * u * 1024:(2 * u + 2) * 1024].rearrange("p (a d) -> p a d", a=2), 2, 1024))

        def load(i):
            src, dst, a, b = jobs[i]
            k.dma(st[i % 2][:].rearrange("p a b -> p (a b)")[:, 0:a * b].rearrange("p (a b) -> p a b", a=a), src,
                  w=["pf_st%d" % (i % 2)])
        load(0)
        load(1)
        for i in range(len(jobs)):
            src, dst, a, b = jobs[i]
            sv = st[i % 2][:].rearrange("p a b -> p (a b)")[:, 0:a * b]
            bv = sbf[i % 2][:].rearrange("p a b -> p (a b)")[:, 0:a * b]
            k.copy(engs[i % 3], bv, sv, r=["pf_st%d" % (i % 2)], w=["pf_bf%d" % (i % 2)])
            k.dma(dst, bv.rearrange("p (a b) -> p a b", a=a), r=["pf_bf%d" % (i % 2)], w=[("ffnw", l)])
            if i + 2 < len(jobs):
                load(i + 2)
        P.barrier()


def layer_setup(G, l):
    nc, P, k, I, R = G.nc, G.P, G.k, G.I, G.R
    banks = G.banks
    with ExitStack() as s:
        aw = [G.sb(s, "ls_aw%d" % i, [128, 8, 512]) for i in range(2)]
        rows = G.sb(s, "ls_rows", [1, 8 * D])
        k.dma(rows[:, 0:6 * D], I["ada_b"][l:l + 1, :], w=["ls_rows_b"])
        k.dma(rows[:, 6 * D:7 * D], I["norm1_g"][l:l + 1, :], w=["ls_rows_g"])
        k.dma(rows[:, 7 * D:8 * D], I["norm2_g"][l:l + 1, :], w=["ls_rows_g"])
        awv = I["ada_w"][l].rearrange("(kc p) n -> p kc n", p=128)
        for cc in range(12):
            b = cc % 2
            k.dma(aw[b][:], awv[:, :, cc * 512:(cc + 1) * 512], w=["ls_aw%d" % b])
            bk = banks[b]
            for kc in range(8):
                k.mm(bk[:], G.condB[:, kc, :], aw[b][:, kc, :], start=(kc == 0), stop=False,
                     r=["condB", "ls_aw%d" % b], w=["bank%d" % b])
            k.mm(bk[:], G.ones_row[0:1, :], rows[0:1, cc * 512:(cc + 1) * 512], start=False, stop=True,
                 r=["ones_row", "ls_rows_b"], w=["bank%d" % b])
            k.copy("act" if cc % 2 else "dve", G.modB[:, cc * 512:(cc + 1) * 512], bk[:], r=["bank%d" % b], w=["modB"])
        for gi, (goff, slot) in enumerate(((6 * D, 1 * D), (7 * D, 4 * D))):
            for hf in range(2):
                b = 2 + hf
                k.mm(banks[b][:], G.ones_row[0:1, :], rows[0:1, goff + hf * 512: goff + (hf + 1) * 512],
                     r=["ones_row", "ls_rows_g"], w=["bank%d" % b])
                sl = G.modB[:, slot + hf * 512: slot + (hf + 1) * 512]
                k.stt("dve", sl, sl, 1.0, banks[b][:], ALU.add, ALU.mult, r=["modB", "bank%d" % b], w=["modB"])
        P.barrier()


def norm_mod_T(G, xt, goff, shoff, T):
    k = G.k
    banks = G.banks
    for j in range(4):
        k.act(T.tmp[:], xt[:, j, :], AF.Square, r=["xt"], w=["tmp"])
        k.red("dve", T.ss[:, j:j + 1], T.tmp[:], r=["tmp"], w=["ss"])
    k.act(T.rstd[:], T.ss[:], AF.Sqrt, bias=EPS, scale=1.0 / D, r=["ss"], w=["rstd"])
    k.recip(T.rstd[:], T.rstd[:], r=["rstd"], w=["rstd"])
    for j in range(4):
        k.stt("dve", T.tmp[:], xt[:, j, :], T.rstd[:, j:j + 1], G.modB[:, goff:goff + D], ALU.mult, ALU.mult,
              r=["xt", "rstd", "modB"], w=["tmp"])
        k.tt("pool", T.hb[:, j, :], T.tmp[:], G.modB[:, shoff:shoff + D], ALU.add, r=["tmp", "modB"], w=["hb"])
    for j in range(4):
        b = 5 + (j % 2)
        pv = banks[b][:].bitcast(BF16).rearrange("p (a t) -> p a t", a=8)
        for kc in range(8):
            k.tr(pv[:, kc, :], T.hb[:, j, kc * 128:(kc + 1) * 128], G.identb[:], r=["hb", "identb"], w=["bank%d" % b])
        k.copy("act" if j % 2 else "dve", T.hT[:, :, j * 128:(j + 1) * 128], pv, r=["bank%d" % b], w=["hT"])


def bcast_load(G, tile_ap, row_ap, n, w):
    G.k.dma(tile_ap, row_ap.to_broadcast([128, n]), w=w, slow=True)


def phase_p1(G, l, xsrc):
    nc, P, k, I, R = G.nc, G.P, G.k, G.I, G.R
    banks = G.banks
    with ExitStack() as s:
        sb = lambda name, shape, dt=F32: G.sb(s, name, shape, dt)
        T = Ctx()
        w_in = sb("w_in", [128, 8, NIN], BF16)
        stg = [sb("p1_stg%d" % i, [128, 1474]) for i in range(2)]
        xt = sb("xt", [128, 4, D])
        T.sqj = sb("sqj", [128, D], BF16)
        T.ss = sb("ss", [128, 4])
        T.rstd = sb("rstd", [128, 4])
        T.tmp = sb("tmp", [128, D])
        T.hb = sb("hb", [128, 4, D], BF16)
        T.hT = sb("hT", [128, 8, 512], BF16)
        fA = [sb("fA%d" % i, [128, 512]) for i in range(3)]
        fB = [sb("fB%d" % i, [128, 512]) for i in range(3)]
        fC = [sb("fC%d" % i, [128, 512]) for i in range(3)]
        obf = [sb("obf%d" % i, [128, 512], BF16) for i in range(3)]
        xnb = [sb("xnb%d" % i, [128, 512], BF16) for i in range(2)]
        paA = sb("paA", [128, 7, 513])
        pmo = [sb("pmo%d" % i, [128, 512]) for i in range(2)]
        cs = [sb("cos%d" % i, [128, 512]) for i in range(2)]
        sn = [sb("sin%d" % i, [128, 512]) for i in range(2)]
        bo64 = sb("bo64", [128, 128])
        bo32 = sb("bo32", [128, 128])
        protf = sb("protf", [128, 128])
        protb = sb("protb", [128, 128], BF16)
        gcol = sb("gcol", [128, 8])
        mu = sb("mu", [128, 7])
        negfb = sb("negfb", [4, 1])
        ones4 = sb("ones4", [4, 512])
        Fg = [sb("Fg%d" % i, [4, 512]) for i in range(2)]
        f4 = sb("f4", [4, 512])
        vt = [sb("vt%d" % i, [128, 4, 65], BF16) for i in range(4)]
        sgw = sb("sgw", [128, 4, 128])
        WgT = sb("WgT", [128, 4, 128], BF16)
        triu = sb("triu", [128, 128])
        sgbT = sb("sgbT", [128, 4])
        lnCg = sb("lnCg", [128, 256])
        lnCb = sb("lnCb", [128, 256])
        glC = [sb("glC%d" % i, [128, 512]) for i in range(2)]
        stC = [sb("stC%d" % i, [128, 8]) for i in range(2)]
        tmpc = [sb("tmpc%d" % i, [128, 256]) for i in range(2)]
        vnb = [sb("vnb%d" % i, [128, 256], BF16) for i in range(2)]
        ycb = [sb("ycb%d" % i, [128, 256], BF16) for i in range(2)]

        for kc in range(8):
            for hf in range(2):
                i = (kc * 2 + hf) % 2
                k.dma(stg[i][:], I["w_in"][l, kc * 128:(kc + 1) * 128, hf * 1474:(hf + 1) * 1474], w=["p1_stg%d" % i])
                k.copy(("dve", "pool", "act")[(kc * 2 + hf) % 3], w_in[:, kc, hf * 1474:(hf + 1) * 1474], stg[i][:],
                       r=["p1_stg%d" % i], w=["w_in"])
        k.dma(bo64[:], I["bo64"], w=["bo64"])
        k.dma(bo32[:], I["bo32"], w=["bo32"])
        k.dma(protf[:], I["prot"], w=["protf"])
        k.copy("dve", protb[:], protf[:], r=["protf"], w=["protb"])
        k.dma(triu[:], I["triu"], w=["triu"])
        for rep in range(2):
            k.dma(gcol[rep * 64:(rep + 1) * 64, 0:1], I["fx_q_g"][l].rearrange("(d o) -> d o", o=1), w=["gcol"], slow=True)
            k.dma(gcol[rep * 64:(rep + 1) * 64, 1:2], I["fx_k_g"][l].rearrange("(d o) -> d o", o=1), w=["gcol"], slow=True)
        for rep in range(4):
            k.dma(gcol[rep * 32:(rep + 1) * 32, 2:3], I["df_q_g"][l].rearrange("(d o) -> d o", o=1), w=["gcol"], slow=True)
            k.dma(gcol[rep * 32:(rep + 1) * 32, 3:4], I["df_k_g"][l].rearrange("(d o) -> d o", o=1), w=["gcol"], slow=True)
        k.ts("dve", gcol[:, 0:1], gcol[:, 0:1], 0.125, ALU.mult, r=["gcol"], w=["gcol"])
        k.ts("dve", gcol[:, 2:3], gcol[:, 2:3], 32.0 ** -0.5, ALU.mult, r=["gcol"], w=["gcol"])
        k.dma(mu[:], I["rw_mu"][l].rearrange("(c p) -> p c", p=128), w=["mu"], slow=True)
        k.dma(negfb[:], I["fx_f_b"][l].rearrange("(h o) -> h o", o=1), w=["negfb"], slow=True)
        k.ts("dve", negfb[:], negfb[:], -1.0, ALU.mult, r=["negfb"], w=["negfb"])
        k.memset("dve", ones4[:], 1.0, w=["ones4"])
        k.memset("pool", paA[:], 0.0, w=["paA%d" % i for i in range(7)])
        for i in range(4):
            k.memset("pool", vt[i][:], 1.0, w=["vt%d" % i])
        k.dma(sgw[:], I["sg_w"][l].rearrange("g i j -> i g j"), w=["sgw"])
        for g in range(4):
            k.tr(banks[0][:, g * 128:(g + 1) * 128], sgw[:, g, :], G.ident[:], r=["sgw", "ident"], w=["bank0"])
        k.tt("dve", WgT[:], banks[0][:].rearrange("p (g i) -> p g i", g=4),
             triu[:].unsqueeze(1).to_broadcast([128, 4, 128]), ALU.mult, r=["bank0", "triu"], w=["WgT"])
        k.dma(sgbT[:], I["sg_b"][l].rearrange("g i -> i g"), w=["sgbT"], slow=True)
        bcast_load(G, lnCg[:], I["sg_ln_g"][l:l + 1, :], 256, ["lnCg"])
        bcast_load(G, lnCb[:], I["sg_ln_b"][l:l + 1, :], 256, ["lnCb"])

        def fm_mm(col0, ncols, bk):
            for kc in range(8):
                k.mm(banks[bk][0:ncols, :], w_in[:, kc, col0:col0 + ncols], T.hT[:, kc, :], start=(kc == 0), stop=(kc == 7),
                     r=["w_in", "hT"], w=["bank%d" % bk])

        for g in range(NG):
            tsl = slice(g * TG, (g + 1) * TG)
            k.dma(xt[:], xsrc[tsl, :].rearrange("(j p) d -> p j d", p=128), r=[("x", g)], w=["xt"])
            k.dma(cs[g % 2][:], I["cosT"][:, tsl], w=["cos%d" % (g % 2)])
            k.dma(sn[g % 2][:], I["sinT"][:, tsl], w=["sin%d" % (g % 2)])
            norm_mod_T(G, xt, 1 * D, 0, T)
            for ci in range(7):
                bk = k.rot("p1bank", 3)
                fm_mm(ci * 128, 128, bk)
                k.copy("pool", paA[:, ci, 0:1], paA[:, ci, 512:513], r=["paA%d" % ci], w=["paA%d" % ci])
                k.copy("act", paA[:, ci, 1:513], banks[bk][:], r=["bank%d" % bk], w=["paA%d" % ci])
                i = k.rot("fA", 3)
                k.tt("pool", fA[i][:], paA[:, ci, 0:512], paA[:, ci, 1:513], ALU.subtract, r=["paA%d" % ci], w=["fA%d" % i])
                o = k.rot("pmo", 2)
                k.stt("dve", pmo[o][:], fA[i][:], mu[:, ci:ci + 1], paA[:, ci, 1:513], ALU.mult, ALU.add,
                      r=["fA%d" % i, "mu", "paA%d" % ci], w=["pmo%d" % o])
                k.dma(R.pmA[ci * 128:(ci + 1) * 128, tsl], pmo[o][:], r=["pmo%d" % o], w=[("pmA", g)])
            for (mix, col0, gi, dst, rope) in (("B", 896, 2, R.qTB, True), ("B", 1152, 3, R.kTB, True),
                                               ("D", 2176, 0, R.qTD, False), ("D", 2432, 1, R.kTD, False)):
                for ci in range(2):
                    bk = k.rot("p1bank", 3)
                    fm_mm(col0 + ci * 128, 128, bk)
                    a = k.rot("fA", 3)
                    k.act(fA[a][:], banks[bk][:], AF.Square, r=["bank%d" % bk], w=["fA%d" % a])
                    sbk = 3 + k.rot("p1sbank", 2)
                    k.mm(banks[sbk][:], bo32[:] if mix == "B" else bo64[:], fA[a][:], r=["bo32", "bo64", "fA%d" % a],
                         w=["bank%d" % sbk])
                    b = k.rot("fB", 3)
                    nd = 32.0 if mix == "B" else 64.0
                    k.act(fB[b][:], banks[sbk][:], AF.Sqrt, bias=EPS, scale=1.0 / nd, r=["bank%d" % sbk], w=["fB%d" % b])
                    k.recip(fB[b][:], fB[b][:], r=["fB%d" % b], w=["fB%d" % b])
                    o = k.rot("obf", 3)
                    if not rope:
                        k.stt("dve", obf[o][:], banks[bk][:], gcol[:, gi:gi + 1], fB[b][:], ALU.mult, ALU.mult,
                              r=["bank%d" % bk, "gcol", "fB%d" % b], w=["obf%d" % o])
                    else:
                        c_ = k.rot("fC", 3)
                        k.stt("dve", fC[c_][:], banks[bk][:], gcol[:, gi:gi + 1], fB[b][:], ALU.mult, ALU.mult,
                              r=["bank%d" % bk, "gcol", "fB%d" % b], w=["fC%d" % c_])
                        xb = k.rot("xnb", 2)
                        k.copy("act", xnb[xb][:], fC[c_][:], r=["fC%d" % c_], w=["xnb%d" % xb])
                        rbk = 3 + k.rot("p1sbank", 2)
                        k.mm(banks[rbk][:], protb[:], xnb[xb][:], r=["protb", "xnb%d" % xb], w=["bank%d" % rbk])
                        a2 = k.rot("fA", 3)
                        k.tt("pool", fA[a2][:], fC[c_][:], cs[g % 2][:], ALU.mult, r=["fC%d" % c_, "cos%d" % (g % 2)], w=["fA%d" % a2])
                        b2 = k.rot("fB", 3)
                        k.tt("dve", fB[b2][:], banks[rbk][:], sn[g % 2][:], ALU.mult, r=["bank%d" % rbk, "sin%d" % (g % 2)],
                             w=["fB%d" % b2])
                        k.tt("pool", obf[o][:], fA[a2][:], fB[b2][:], ALU.add, r=["fA%d" % a2, "fB%d" % b2], w=["obf%d" % o])
                    k.dma(dst[ci * 128:(ci + 1) * 128, tsl], obf[o][:], r=["obf%d" % o], w=[(mix + "qk", g)])
            bk = k.rot("p1bank", 3)
            fm_mm(2944, 4, bk)
            k.act(f4[:], banks[bk][0:4, :], AF.Exp, bias=negfb[:, 0:1], scale=-1.0, r=["bank%d" % bk, "negfb"], w=["f4"])
            k.act(f4[:], f4[:], AF.Ln, bias=1.0, scale=1.0, r=["f4"], w=["f4"])
            if g == 0:
                k.scan(Fg[0][:], ones4[:], f4[:], 0.0, ALU.mult, ALU.subtract, r=["ones4", "f4"], w=["Fg0"])
            else:
                k.scan(Fg[g % 2][:], ones4[:], f4[:], Fg[(g - 1) % 2][:, 511:512], ALU.mult, ALU.subtract,
                       r=["ones4", "f4", "Fg%d" % ((g - 1) % 2)], w=["Fg%d" % (g % 2)])
            k.dma(R.Ffm[:, tsl], Fg[g % 2][:], r=["Fg%d" % (g % 2)], w=[("Ffm", g)])
            for j in range(4):
                rows = slice(g * TG + j * 128, g * TG + (j + 1) * 128)
                for (mix, col0, dst) in (("B", 1408, R.vB), ("D", 2688, R.vD)):
                    bk = k.rot("p1bank", 3)
                    for kc in range(8):
                        k.mm(banks[bk][:, 0:256], T.hT[:, kc, j * 128:(j + 1) * 128], w_in[:, kc, col0:col0 + 256],
                             start=(kc == 0), stop=(kc == 7), r=["w_in", "hT"], w=["bank%d" % bk])
                    vi = k.rot("vt", 4)
                    k.copy("act" if mix == "B" else "dve", vt[vi][:, :, 0:64], banks[bk][:, 0:256].rearrange("p (h e) -> p h e", h=4),
                           r=["bank%d" % bk], w=["vt%d" % vi])
                    k.dma(dst[rows, :], vt[vi][:].rearrange("p h e -> p (h e)"), r=["vt%d" % vi], w=[(mix + "v", g)])
                bk = k.rot("p1bank", 3)
                for kc in range(8):
                    k.mm(banks[bk][:], T.hT[:, kc, j * 128:(j + 1) * 128], w_in[:, kc, 1664:2176],
                         start=(kc == 0), stop=(kc == 7), r=["w_in", "hT"], w=["bank%d" % bk])
                ci = k.rot("glC", 2)
                gl, st, tc, vb, yb = glC[ci], stC[ci], tmpc[ci], vnb[ci], ycb[ci]
                kk = "C%d" % ci
                k.act(gl[:], banks[bk][:], AF.Gelu, r=["bank%d" % bk], w=[kk + "gl"])
                k.red("dve", st[:, 0:1], gl[:, 256:512], r=[kk + "gl"], w=[kk + "st"])
                k.act(tc[:], gl[:, 256:512], AF.Square, r=[kk + "gl"], w=[kk + "tc"])
                k.red("dve", st[:, 1:2], tc[:], r=[kk + "tc"], w=[kk + "st"])
                k.ts("dve", st[:, 2:3], st[:, 0:1], 1.0 / 256, ALU.mult, r=[kk + "st"], w=[kk + "st"])
                k.tt("dve", st[:, 3:4], st[:, 2:3], st[:, 2:3], ALU.mult, r=[kk + "st"], w=[kk + "st"])
                k.stt("dve", st[:, 4:5], st[:, 1:2], 1.0 / 256, st[:, 3:4], ALU.mult, ALU.subtract, r=[kk + "st"], w=[kk + "st"])
                k.act(st[:, 5:6], st[:, 4:5], AF.Sqrt, bias=EPS, scale=1.0, r=[kk + "st"], w=[kk + "st"])
                k.recip(st[:, 5:6], st[:, 5:6], r=[kk + "st"], w=[kk + "st"])
                k.ts("dve", tc[:], gl[:, 256:512], st[:, 2:3], ALU.subtract, st[:, 5:6], ALU.mult, r=[kk + "gl", kk + "st"], w=[kk + "tc"])
                k.tt("pool", tc[:], tc[:], lnCg[:], ALU.mult, r=[kk + "tc", "lnCg"], w=[kk + "tc"])
                k.tt("pool", vb[:], tc[:], lnCb[:], ALU.add, r=[kk + "tc", "lnCb"], w=[kk + "vb"])
                sbk = 3 + k.rot("p1sbank", 2)
                for hg in range(4):
                    k.mm(banks[sbk][:, hg * 64:(hg + 1) * 64], WgT[:, hg, :], vb[:, hg * 64:(hg + 1) * 64],
                         r=["WgT", kk + "vb"], w=["bank%d" % sbk])
                k.tt("dve", tc[:].rearrange("p (h e) -> p h e", h=4), banks[sbk][:, 0:256].rearrange("p (h e) -> p h e", h=4),
                     sgbT[:].unsqueeze(2).to_broadcast([128, 4, 64]), ALU.add, r=["bank%d" % sbk, "sgbT", kk + "vb"], w=[kk + "tc"])
                k.tt("pool", yb[:], tc[:], gl[:, 0:256], ALU.mult, r=[kk + "tc", kk + "gl"], w=[kk + "yb"])
                if not G.y_in:
                    k.dma(R.yscr[rows, 256:512], yb[:], r=[kk + "yb"], w=[("yC", g)])
        P.barrier()


def phase_p3(G, l, xsrc, xdst):
    nc, P, k, I, R = G.nc, G.P, G.k, G.I, G.R
    banks = G.banks
    ydt = F32 if G.y_in else BF16
    with ExitStack() as s:
        sb = lambda name, shape, dt=F32: G.sb(s, name, shape, dt)
        T = Ctx()
        w_out = sb("w_out", [128, 8, D], BF16)
        xt = sb("xt3", [128, 4, D])
        T.sqj = sb("sqj3", [128, D], BF16)
        T.ss = sb("ss3", [128, 4])
        T.rstd = sb("rstd3", [128, 4])
        T.tmp = sb("tmp3", [128, D])
        T.hb = sb("hb3", [128, 4, D], BF16)
        T.hT = sb("hT3", [128, 8, 512], BF16)
        yT = T.hT
        actT = sb("actT", [128, 22, 512], BF16)
        upw = [sb("upw%d" % i, [128, 8, 512], BF16) for i in range(2)]
        dnw = sb("dnw", [128, 22, D], BF16)
        ub = [sb("ub%d" % i, [128, 514]) for i in range(2)]
        c1 = [sb("c1_%d" % i, [128, 512]) for i in range(2)]
        c2 = [sb("c2_%d" % i, [128, 512]) for i in range(2)]
        c3 = [sb("c3_%d" % i, [128, 512]) for i in range(2)]
        sgt = [sb("sgt%d" % i, [128, 512]) for i in range(2)]
        convw = sb("convw", [128, 44, 3])
        convb = sb("convb", [128, 44])
        carryF = sb("carryF", [128, 44, 2])

        for kc in range(8):
            k.dma(T.tmp[:], I["w_out"][l, kc * 128:(kc + 1) * 128, :], w=["tmp"])
            k.copy(("dve", "pool", "act")[kc % 3], w_out[:, kc, :], T.tmp[:], r=["tmp"], w=["w_out"])
        for t in range(3):
            k.dma(convw[:, :, t], I["ffn_conv"][l, t].rearrange("(cc p) -> p cc", p=128), w=["convw"], slow=True)
        k.dma(convb[:], I["ffn_conv_b"][l].rearrange("(cc p) -> p cc", p=128), w=["convb"], slow=True)
        k.memset("pool", carryF[:], 0.0, w=["carryF"])

        for g in range(NG):
            tsl = slice(g * TG, (g + 1) * TG)
            k.dma(xt[:], xsrc[tsl, :].rearrange("(j p) d -> p j d", p=128), r=[("x", g)], w=["xt"])
            k.dma(dnw[:].rearrange("p a d -> p (a d)"), R.dnbf[l], r=[("ffnw", l)], w=["dnw"])
            if G.y_in:
                for j in range(4):
                    k.dma(T.tmp[:, 0:768], R.yscr[g * TG + j * 128: g * TG + (j + 1) * 128, :], w=["tmp"])
                    k.copy("pool", T.hb[:, j, 0:768], T.tmp[:, 0:768], r=["tmp"], w=["hb"])
                for kc in range(2):
                    k.dma(c1[kc][:], R.yTA[kc * 128:(kc + 1) * 128, tsl], w=["c1_%d" % kc])
                    k.copy("act", yT[:, kc, :], c1[kc][:], r=["c1_%d" % kc], w=["hT"])
            else:
                k.dma(T.hb[:, :, 0:768], R.yscr[tsl, :].rearrange("(j p) c -> p j c", p=128),
                      r=[("yC", g), ("yB", g), ("yD", g)], w=["hb"])
                k.dma(yT[:, 0:2, :], R.yTA[:, tsl].rearrange("(kc p) t -> p kc t", p=128), r=[("yA", g)], w=["hT"])
            for j in range(4):
                b = 5 + (j % 2)
                pv = banks[b][:].bitcast(BF16).rearrange("p (a t) -> p a t", a=8)
                for kc in range(6):
                    k.tr(pv[:, kc, :], T.hb[:, j, kc * 128:(kc + 1) * 128], G.identb[:], r=["hb", "identb"], w=["bank%d" % b])
                k.copy("act" if j % 2 else "dve", yT[:, 2:8, j * 128:(j + 1) * 128], pv[:, 0:6, :], r=["bank%d" % b], w=["hT"])
            for j in range(4):
                for hf in range(2):
                    bk = k.rot("p3bank", 2)
                    for kc in range(8):
                        k.mm(banks[bk][:], yT[:, kc, j * 128:(j + 1) * 128], w_out[:, kc, hf * 512:(hf + 1) * 512],
                             start=(kc == 0), stop=(kc == 7), r=["hT", "w_out"], w=["bank%d" % bk])
                    ci = k.rot("c1", 2)
                    k.tt("dve", c1[ci][:], banks[bk][:], G.modB[:, 2 * D + hf * 512: 2 * D + (hf + 1) * 512], ALU.mult,
                         r=["bank%d" % bk, "modB"], w=["c1_%d" % ci])
                    k.tt("pool", xt[:, j, hf * 512:(hf + 1) * 512], xt[:, j, hf * 512:(hf + 1) * 512], c1[ci][:], ALU.add,
                         r=["xt", "c1_%d" % ci], w=["xt"])
            if G.dbg_x1 is not None:
                k.dma(G.dbg_x1[tsl, :].rearrange("(j p) d -> p j d", p=128), xt[:], r=["xt"], w=[("dbgx1", g)])
            norm_mod_T(G, xt, 4 * D, 3 * D, T)
            for u in range(11):
                wi = u % 2
                k.dma(upw[wi][:].rearrange("p a n -> p (a n)"), R.upbf[l, u], r=[("ffnw", l)], w=["upw%d" % wi])
                for sc in range(4):
                    cc = 4 * u + sc
                    bk = 2 + k.rot("p3ubank", 3)
                    for kc in range(8):
                        k.mm(banks[bk][:], upw[wi][:, kc, sc * 128:(sc + 1) * 128], T.hT[:, kc, :], start=(kc == 0), stop=(kc == 7),
                             r=["upw%d" % wi, "hT"], w=["bank%d" % bk])
                    ui = k.rot("ub", 2)
                    k.copy("pool", ub[ui][:, 0:2], carryF[:, cc, :], r=["carryF"], w=["ub%d" % ui])
                    k.copy("act", ub[ui][:, 2:514], banks[bk][:], r=["bank%d" % bk], w=["ub%d" % ui])
                    k.copy("pool", carryF[:, cc, :], ub[ui][:, 512:514], r=["ub%d" % ui], w=["carryF"])
                    k.ts("dve", c1[ui][:], ub[ui][:, 2:514], convw[:, cc, 2:3], ALU.mult, convb[:, cc:cc + 1], ALU.add,
                         r=["ub%d" % ui, "convw", "convb"], w=["c1_%d" % ui])
                    k.ts("pool", c2[ui][:], ub[ui][:, 1:513], convw[:, cc, 1:2], ALU.mult, r=["ub%d" % ui, "convw"], w=["c2_%d" % ui])
                    k.tt("pool", c2[ui][:], c2[ui][:], c1[ui][:], ALU.add, r=["c2_%d" % ui, "c1_%d" % ui], w=["c2_%d" % ui])
                    if cc < 22:
                        k.stt("dve", actT[:, cc, :], ub[ui][:, 0:512], convw[:, cc, 0:1], c2[ui][:], ALU.mult, ALU.add,
                              r=["ub%d" % ui, "convw", "c2_%d" % ui], w=["actT%d" % cc])
                    else:
                        cu = cc - 22
                        k.stt("dve", c3[ui][:], ub[ui][:, 0:512], convw[:, cc, 0:1], c2[ui][:], ALU.mult, ALU.add,
                              r=["ub%d" % ui, "convw", "c2_%d" % ui], w=["c3_%d" % ui])
                        k.act(sgt[ui][:], c3[ui][:], AF.Silu, r=["c3_%d" % ui], w=["sgt%d" % ui])
                        k.tt("pool", actT[:, cu, :], actT[:, cu, :], sgt[ui][:], ALU.mult, r=["actT%d" % cu, "sgt%d" % ui],
                             w=["actT%d" % cu])
            aks = ["actT%d" % i for i in range(22)]
            for j in range(4):
                for hf in range(2):
                    bk = k.rot("p3bank", 2)
                    for cu in range(22):
                        k.mm(banks[bk][:], actT[:, cu, j * 128:(j + 1) * 128], dnw[:, cu, hf * 512:(hf + 1) * 512],
                             start=(cu == 0), stop=(cu == 21), r=["actT%d" % cu, "dnw"], w=["bank%d" % bk])
                    ci = k.rot("c1", 2)
                    k.tt("dve", c1[ci][:], banks[bk][:], G.modB[:, 5 * D + hf * 512: 5 * D + (hf + 1) * 512], ALU.mult,
                         r=["bank%d" % bk, "modB"], w=["c1_%d" % ci])
                    k.tt("pool", xt[:, j, hf * 512:(hf + 1) * 512], xt[:, j, hf * 512:(hf + 1) * 512], c1[ci][:], ALU.add,
                         r=["xt", "c1_%d" % ci], w=["xt"])
            k.dma(xdst[tsl, :].rearrange("(j p) d -> p j d", p=128), xt[:], r=["xt"], w=[("x", g)])
        P.barrier()


def phase_A(G, l):
    nc, P, k, I, R = G.nc, G.P, G.k, G.I, G.R
    banks = G.banks
    with ExitStack() as s:
        sb = lambda name, shape, dt=F32: G.sb(s, name, shape, dt)
        prm = sb("prm", [64, 7, 4])
        wup = sb("wup", [32, 256])
        aup = sb("aup", [32, 256])
        gup = sb("gup", [64, 256])
        ones64 = sb("ones64", [64, 64])
        ones512 = sb("ones512", [64, 512])
        m64 = sb("m64", [64, 3, 64])
        Tst = sb("Tst", [64, 4, 64])
        big = lambda nm: sb(nm, [64, 4, 512])
        r_, k_, v_ = big("r_"), big("k_"), big("v_")
        lwt, at, gt, kkn, k2, b_, bonus, Yblk, t1, t2 = (big("lwt"), big("at"), big("gt"), big("kkn"), big("k2"), big("b_"),
                                                         big("bonus"), big("Yblk"), big("t1"), big("t2"))
        Gblk = sb("Gblk", [64, 4, 513])
        wd = sb("wd", [32, 512])
        ad = sb("ad", [32, 512])
        gd = sb("gd", [64, 512])
        yo = sb("yo", [64, 4, 512], BF16)
        sm = lambda nm: sb(nm, [64, 4, 64])
        Pc, Pp, eG, eGn, eGp = sm("Pc"), sm("Pp"), sm("eG"), sm("eGn"), sm("eGp")
        At, Bt, Kt, Rt = sm("At"), sm("Bt"), sm("Kt"), sm("Rt")
        Btm, Ktm, Vtm = sm("Btm"), sm("Ktm"), sm("Vtm")
        PT = [sm("PT0"), sm("PT1")]
        Pj = [sm("Pj0"), sm("Pj1")]
        LakT, MrbT, MrkT, U, tS = sm("LakT"), sm("MrbT"), sm("MrkT"), sm("U"), sm("tS")

        for n, nm in enumerate(("rw_w0", "rw_a0", "rw_k_k", "rw_k_a")):
            k.dma(prm[:, n, :], I[nm][l].rearrange("(h k) -> k h", k=64), w=["prm"], slow=True)
        k.dma(prm[:, 4, :], I["rw_r_k"][l].rearrange("h k -> k h"), w=["prm"], slow=True)
        k.dma(prm[:, 5, :], I["rw_ln_g"][l].rearrange("(h k) -> k h", k=64), w=["prm"], slow=True)
        k.dma(prm[:, 6, :], I["rw_ln_b"][l].rearrange("(h k) -> k h", k=64), w=["prm"], slow=True)
        k.dma(wup[:], I["rw_w_up"][l], w=["wup"])
        k.dma(aup[:], I["rw_a_up"][l], w=["aup"])
        k.dma(gup[:], I["rw_g_up"][l], w=["gup"])
        k.dma(m64[:], I["m64"], w=["m64"])
        k.memset("dve", ones64[:], 1.0, w=["ones64"])
        k.memset("dve", ones512[:], 1.0, w=["ones512"])
        k.memset("pool", Tst[:], 0.0, w=["Tst"])
        k.memset("pool", Gblk[:], 0.0, w=["Gblk"])

        def bc(col):
            return prm[:, col, :].unsqueeze(2).to_broadcast([64, 4, 512])

        def HB(b, half):
            return banks[b][0:64, half * 256:(half + 1) * 256].rearrange("p (h t) -> p h t", h=4)

        def hk(b, half):
            return ("hb", b)

        def fb(b):
            return [("hb", b)]
        allpm = [("pmA", g) for g in range(NG)]

        for g in range(NG):
            tsl = slice(g * TG, (g + 1) * TG)
            k.dma(r_[:], R.pmA[0:256, tsl].rearrange("(h k) t -> k h t", k=64), r=allpm, w=["r_"])
            k.dma(k_[:], R.pmA[256:512, tsl].rearrange("(h k) t -> k h t", k=64), r=allpm, w=["k_"])
            k.dma(v_[:], R.pmA[512:768, tsl].rearrange("(h k) t -> k h t", k=64), r=allpm, w=["v_"])
            k.dma(wd[:], R.pmA[768:800, tsl], r=allpm, w=["wd"])
            k.dma(ad[:], R.pmA[800:832, tsl], r=allpm, w=["ad"])
            k.dma(gd[:], R.pmA[832:896, tsl], r=allpm, w=["gd"])
            k.act(wd[:], wd[:], AF.Tanh, r=["wd"], w=["wd"])
            k.act(gd[:], gd[:], AF.Sigmoid, r=["gd"], w=["gd"])
            for h in range(4):
                k.mm(banks[h][0:64, :], wup[:, h * 64:(h + 1) * 64], wd[:], r=["wup", "wd"], w=fb(h))
                k.act(lwt[:, h, :], banks[h][0:64, :], AF.Sigmoid, bias=prm[:, 0, h:h + 1], scale=1.0, r=fb(h) + ["prm"], w=["lwt"])
            for h in range(4):
                k.mm(banks[4 + h][0:64, :], aup[:, h * 64:(h + 1) * 64], ad[:], r=["aup", "ad"], w=fb(4 + h))
                k.act(at[:, h, :], banks[4 + h][0:64, :], AF.Sigmoid, bias=prm[:, 1, h:h + 1], scale=1.0, r=fb(4 + h) + ["prm"], w=["at"])
            for h in range(4):
                k.mm(banks[h][0:64, :], gup[:, h * 64:(h + 1) * 64], gd[:], r=["gup", "gd"], w=fb(h))
                k.copy("dve" if h % 2 else "act", gt[:, h, :], banks[h][0:64, :], r=fb(h), w=["gt"])
            k.tt("dve", kkn[:], k_[:], bc(2), ALU.mult, r=["k_", "prm"], w=["kkn"])
            k.tt("pool", t1[:], kkn[:], kkn[:], ALU.mult, r=["kkn"], w=["t1"])
            for h in range(4):
                k.mm(banks[4 + h][0:64, :], ones64[:], t1[:, h, :], r=["ones64", "t1"], w=fb(4 + h))
            for h in range(4):
                k.act(t2[:, h, :], banks[4 + h][0:64, :], AF.Sqrt, r=fb(4 + h), w=["t2"])
            k.ts("dve", t2[:], t2[:], 1e-12, ALU.max, r=["t2"], w=["t2"])
            k.recip(t2[:], t2[:], r=["t2"], w=["t2"])
            k.tt("pool", kkn[:], kkn[:], t2[:], ALU.mult, r=["kkn", "t2"], w=["kkn"])
            k.stt("dve", t1[:], at[:], -1.0, bc(3), ALU.add, ALU.mult, r=["at", "prm", "t1"], w=["t1"])
            k.tt("pool", t1[:], t1[:], k_[:], ALU.mult, r=["t1", "k_"], w=["t1"])
            k.tt("pool", k2[:], t1[:], k_[:], ALU.add, r=["t1", "k_"], w=["k2"])
            k.tt("pool", b_[:], kkn[:], at[:], ALU.mult, r=["kkn", "at"], w=["b_"])
            k.tt("pool", t1[:], r_[:], k2[:], ALU.mult, r=["r_", "k2", "t1"], w=["t1"])
            k.tt("dve", t1[:], t1[:], bc(4), ALU.mult, r=["t1", "prm"], w=["t1"])
            for h in range(4):
                k.mm(banks[h][0:64, :], ones64[:], t1[:, h, :], r=["ones64", "t1"], w=fb(h))
                k.tt("dve", bonus[:, h, :], banks[h][0:64, :], v_[:, h, :], ALU.mult, r=fb(h) + ["v_"], w=["bonus"])
            for h in range(4):
                k.scan(Gblk[:, h, 1:513], ones512[:], lwt[:, h, :], 0.0, ALU.mult, ALU.add, r=["ones512", "lwt"], w=["Gblk"])

            if int(os.environ.get("A_STOP", "9")) <= 1:
                continue
            for ci in range(8 if int(os.environ.get("A_STOP", "9")) > 2 else 0):
                c0 = ci * 64
                ts_ = slice(c0, c0 + 64)
                k.tt("dve", Pc[:], Gblk[:, :, 1 + c0:1 + c0 + 64], Gblk[:, :, c0:c0 + 1].to_broadcast([64, 4, 64]), ALU.subtract,
                     r=["Gblk"], w=["Pc"])
                k.tt("pool", Pp[:], Pc[:], lwt[:, :, ts_], ALU.subtract, r=["Pc", "lwt"], w=["Pp"])
                k.act(eG[:], Pc[:], AF.Exp, scale=-ALPHA, r=["Pc"], w=["eG"])
                k.act(eGn[:], Pc[:], AF.Exp, scale=ALPHA, r=["Pc"], w=["eGn"])
                k.act(eGp[:], Pp[:], AF.Exp, scale=-ALPHA, r=["Pp"], w=["eGp"])
                k.stt("dve", At[:], kkn[:, :, ts_], -1.0, eGp[:], ALU.mult, ALU.mult, r=["kkn", "eGp"], w=["At"])
                k.tt("pool", Bt[:], b_[:, :, ts_], eGn[:], ALU.mult, r=["b_", "eGn"], w=["Bt"])
                k.tt("pool", Kt[:], k2[:, :, ts_], eGn[:], ALU.mult, r=["k2", "eGn"], w=["Kt"])
                k.tt("dve", Rt[:], r_[:, :, ts_], eG[:], ALU.mult, r=["r_", "eG"], w=["Rt"])
                CH = int(os.environ.get("A_CH", "99"))
                if CH < 2:
                    continue
                id64 = G.ident[0:64, 0:64]
                trs = ((Bt, "Bt", Btm, "Btm", 2, 1, "act"), (Kt, "Kt", Ktm, "Ktm", 3, 0, "dve"), (None, "v_", Vtm, "Vtm", 3, 1, "act"))
                if "A_TR" in os.environ:
                    trs = tuple(trs[int(c)] for c in os.environ["A_TR"])
                for (X, xk, Xtm, xtk, bq, hf, ce) in trs:
                    for h in range(4):
                        src = v_[:, h, ts_] if X is None else X[:, h, :]
                        k.mm(HB(bq, hf)[:, h, :], src, id64, r=[xk, "ident"], w=[hk(bq, hf)])
                for (X, xk, Xtm, xtk, bq, hf, ce) in trs:
                    k.copy(ce, Xtm[:], HB(bq, hf), r=[hk(bq, hf)], w=[xtk])
                if CH < 3:
                    continue
                specs = ((Bt, "Bt", At, "At", 0, 0, PT[0], "PT0", 0), (At, "At", Bt, "Bt", 0, 1, Pj[0], "Pj0", 2),
                         (Kt, "Kt", At, "At", 1, 0, LakT, "LakT", 0), (Bt, "Bt", Rt, "Rt", 1, 1, MrbT, "MrbT", 1),
                         (Kt, "Kt", Rt, "Rt", 2, 0, MrkT, "MrkT", 1))
                for (La, lk, Ra, rk_, bq, hf, dst, dk, mi) in specs:
                    for h in range(4):
                        k.mm(HB(bq, hf)[:, h, :], La[:, h, :], Ra[:, h, :], r=[lk, rk_], w=[hk(bq, hf)])
                for (La, lk, Ra, rk_, bq, hf, dst, dk, mi) in specs:
                    k.tt("dve", dst[:], HB(bq, hf), m64[:, mi, :].unsqueeze(1).to_broadcast([64, 4, 64]), ALU.mult,
                         r=[hk(bq, hf), "m64"], w=[dk])
                if CH < 4:
                    continue
                for h in range(4):
                    k.mm(HB(4, 0)[:, h, :], At[:, h, :], Tst[:, h, :], start=True, stop=False, r=["At", "Tst"], w=[hk(4, 0)])
                    k.mm(HB(4, 0)[:, h, :], LakT[:, h, :], Vtm[:, h, :], start=False, stop=True, r=["LakT", "Vtm"], w=[hk(4, 0)])
                k.copy("act", U[:], HB(4, 0), r=[hk(4, 0)], w=["U"])
                if CH < 5:
                    continue
                for j in range(6):
                    cur, nxt = j % 2, (j + 1) % 2
                    for h in range(4):
                        k.mm(HB(4, 1)[:, h, :], PT[cur][:, h, :], U[:, h, :], r=["PT%d" % cur, "U"], w=[hk(4, 1)])
                    k.tt("dve", U[:], U[:], HB(4, 1), ALU.add, r=["U", hk(4, 1)], w=["U"])
                    if j < 5:
                        for h in range(4):
                            k.mm(HB(5, 0)[:, h, :], PT[cur][:, h, :], Pj[cur][:, h, :], r=["PT%d" % cur, "Pj%d" % cur], w=[hk(5, 0)])
                        for h in range(4):
                            k.mm(HB(5, 1)[:, h, :], Pj[cur][:, h, :], PT[cur][:, h, :], r=["PT%d" % cur, "Pj%d" % cur], w=[hk(5, 1)])
                        k.copy("act", Pj[nxt][:], HB(5, 0), r=[hk(5, 0)], w=["Pj%d" % nxt])
                        k.copy("act", PT[nxt][:], HB(5, 1), r=[hk(5, 1)], w=["PT%d" % nxt])
                if CH < 6:
                    continue
                for h in range(4):
                    k.mm(HB(6, 0)[:, h, :], Tst[:, h, :], Rt[:, h, :], start=True, stop=False, r=["Tst", "Rt"], w=[hk(6, 0)])
                    k.mm(HB(6, 0)[:, h, :], U[:, h, :], MrbT[:, h, :], start=False, stop=False, r=["U", "MrbT"], w=[hk(6, 0)])
                    k.mm(HB(6, 0)[:, h, :], Vtm[:, h, :], MrkT[:, h, :], start=False, stop=True, r=["Vtm", "MrkT"], w=[hk(6, 0)])
                k.copy("act", Yblk[:, :, ts_], HB(6, 0), r=[hk(6, 0)], w=["Yblk"])
                if CH < 7:
                    continue
                for h in range(4):
                    k.mm(HB(7, 0)[:, h, :], Btm[:, h, :], U[:, h, :], start=True, stop=False, r=["Btm", "U"], w=[hk(7, 0)])
                    k.mm(HB(7, 0)[:, h, :], Ktm[:, h, :], Vtm[:, h, :], start=False, stop=True, r=["Ktm", "Vtm"], w=[hk(7, 0)])
                k.tt("dve", tS[:], HB(7, 0), Tst[:], ALU.add, r=[hk(7, 0), "Tst"], w=["tS"])
                k.tt("pool", Tst[:], tS[:], eG[:, :, 63:64].to_broadcast([64, 4, 64]), ALU.mult, r=["tS", "eG"], w=["Tst"])

            if int(os.environ.get("A_STOP", "9")) <= 3:
                continue
            for h in range(4):
                k.mm(banks[h][0:64, :], ones64[:], Yblk[:, h, :], r=["ones64", "Yblk"], w=fb(h))
                k.stt("dve", t1[:, h, :], banks[h][0:64, :], -1.0 / 64, Yblk[:, h, :], ALU.mult, ALU.add, r=fb(h) + ["Yblk"], w=["t1"])
            k.tt("pool", t2[:], t1[:], t1[:], ALU.mult, r=["t1"], w=["t2"])
            for h in range(4):
                k.mm(banks[4 + h][0:64, :], ones64[:], t2[:, h, :], r=["ones64", "t2"], w=fb(4 + h))
            for h in range(4):
                k.act(t2[:, h, :], banks[4 + h][0:64, :], AF.Sqrt, bias=64e-5, scale=1.0 / 64, r=fb(4 + h), w=["t2"])
            k.recip(t2[:], t2[:], r=["t2"], w=["t2"])
            k.tt("pool", t1[:], t1[:], t2[:], ALU.mult, r=["t1", "t2"], w=["t1"])
            k.tt("dve", t1[:], t1[:], bc(5), ALU.mult, r=["t1", "prm"], w=["t1"])
            k.tt("pool", t1[:], t1[:], bc(6), ALU.add, r=["t1", "prm"], w=["t1"])
            k.tt("pool", t1[:], t1[:], bonus[:], ALU.add, r=["t1", "bonus"], w=["t1"])
            k.tt("dve", yo[:], t1[:], gt[:], ALU.mult, r=["t1", "gt"], w=["yo"])
            if not G.y_in:
                k.dma(R.yTA[:, tsl].rearrange("(h v) t -> v h t", v=64), yo[:], r=["yo"], w=[("yA", g)])
        P.barrier()


def attn_common(G, s, Vsrc, mix):
    k, R, I = G.k, G.R, G.I
    sb = lambda name, shape, dt=F32: G.sb(s, name, shape, dt)
    A = Ctx()
    A.kT = [sb("kT%d" % i, [64, S], BF16) for i in range(2)]
    A.V = sb("V", [128, 32, 260], BF16)
    k.dma(A.V[:], Vsrc.rearrange("(kt p) c -> p kt c", p=128), r=[(mix + "v", g) for g in range(NG)], w=["V"])
    A.Pm = [sb("Pm%d" % i, [128, 512], BF16) for i in range(3)]
    A.rec = [sb("rec%d" % i, [128, 8]) for i in range(2)]
    A.ob = [sb("ob%d" % i, [128, 4, 64], BF16) for i in range(2)]
    return A


def phase_D(G, l):
    nc, P, k, I, R = G.nc, G.P, G.k, G.I, G.R
    banks = G.banks
    with ExitStack() as s:
        sb = lambda name, shape, dt=F32: G.sb(s, name, shape, dt)
        A = attn_common(G, s, R.vD, "D")
        Fk = sb("Fk", [128, 4, 32])
        Frow = sb("Frow", [4, S])
        sel = sb("sel", [4, 4, 128])
        negm = sb("negm", [128, 128])
        qT = [sb("qT%d" % i, [64, 512], BF16) for i in range(2)]
        FqB = [sb("FqB%d" % i, [128, 512]) for i in range(2)]
        FqD = [sb("FqD%d" % i, [128, 512]) for i in range(2)]
        tb = [sb("tb%d" % i, [128, 512]) for i in range(3)]
        allF = [("Ffm", g) for g in range(NG)]
        for h in range(4):
            k.dma(Fk[:, h, :], R.Ffm[h].rearrange("(kt p) -> p kt", p=128), r=allF, w=["Fk"], slow=True)
        k.dma(Frow[:], R.Ffm, r=allF, w=["Frow"])
        k.dma(sel[:], I["sel"], w=["sel"])
        k.dma(negm[:], I["negmask"], w=["negm"])
        allqk = [("Dqk", g) for g in range(NG)]
        for h in range(4):
            kt_ = A.kT[h % 2]
            kk = "kT%d" % (h % 2)
            k.dma(kt_[:], R.kTD[h * 64:(h + 1) * 64, :], r=allqk, w=[kk])
            for g in range(NG):
                i = k.rot("Dq", 2)
                k.dma(qT[i][:], R.qTD[h * 64:(h + 1) * 64, g * TG:(g + 1) * TG], r=allqk, w=["qT%d" % i])
                k.mm(banks[5][:], sel[:, h, :], Frow[:, g * TG:(g + 1) * TG], r=["sel", "Frow"], w=["bank5"])
                k.copy("act", FqB[i][:], banks[5][:], r=["bank5"], w=["FqB%d" % i])
                k.tt("pool", FqD[i][:].rearrange("p (a b) -> p a b", a=4), FqB[i][:].rearrange("p (a b) -> p a b", a=4),
                     negm[:].unsqueeze(1).to_broadcast([128, 4, 128]), ALU.add, r=["FqB%d" % i, "negm"], w=["FqD%d" % i])
                ob_ = 3 + k.rot("DO", 2)
                O = banks[ob_][:, 0:260].rearrange("p (a e) -> p a e", a=4)
                for kt in range(4 * g + 4):
                    m = kt - 4 * g
                    c0 = max(m, 0) * 128
                    N = 512 - c0
                    sbk = k.rot("Ds", 3)
                    k.mm(banks[sbk][:, 0:N], kt_[:, kt * 128:(kt + 1) * 128], qT[i][:, c0:512], r=[kk, "qT%d" % i], w=["bank%d" % sbk])
                    ti = k.rot("Dt", 3)
                    if m < 0:
                        k.stt("dve", tb[ti][:], banks[sbk][:], Fk[:, h, kt:kt + 1], FqB[i][:], ALU.subtract, ALU.add,
                              r=["bank%d" % sbk, "Fk", "FqB%d" % i], w=["tb%d" % ti])
                    else:
                        k.stt("dve", tb[ti][:, 0:128], banks[sbk][:, 0:128], Fk[:, h, kt:kt + 1], FqD[i][:, c0:c0 + 128],
                              ALU.subtract, ALU.add, r=["bank%d" % sbk, "Fk", "FqD%d" % i], w=["tb%d" % ti])
                        if N > 128:
                            k.stt("dve", tb[ti][:, 128:N], banks[sbk][:, 128:N], Fk[:, h, kt:kt + 1], FqB[i][:, c0 + 128:512],
                                  ALU.subtract, ALU.add, r=["bank%d" % sbk, "Fk", "FqB%d" % i], w=["tb%d" % ti])
                    k.act(A.Pm[ti][:, 0:N], tb[ti][:, 0:N], AF.Exp, r=["tb%d" % ti], w=["Pm%d" % ti])
                    for jq in range(max(m, 0), 4):
                        k.mm(O[:, jq, :], A.Pm[ti][:, jq * 128 - c0: jq * 128 - c0 + 128], A.V[:, kt, h * 65:(h + 1) * 65],
                             start=(kt == 0 and jq == 0), stop=(kt == 4 * g + jq), r=["Pm%d" % ti, "V"], w=["bank%d" % ob_], sgc=True)
                ri = k.rot("Drec", 2)
                k.recip(A.rec[ri][:, 0:4], O[:, :, 64], r=["bank%d" % ob_], w=["rec%d" % ri])
                k.tt("dve", A.ob[ri][:], O[:, :, 0:64], A.rec[ri][:, 0:4].unsqueeze(2).to_broadcast([128, 4, 64]), ALU.mult,
                     r=["bank%d" % ob_, "rec%d" % ri], w=["ob%d" % ri])
                k.dma(R.yscr[g * TG:(g + 1) * TG, 512 + h * 64: 512 + (h + 1) * 64].rearrange("(a p) e -> p a e", p=128),
                      A.ob[ri][:], r=["ob%d" % ri], w=[("yD", g)], slow=True)
        P.barrier()


def phase_B(G, l):
    nc, P, k, I, R = G.nc, G.P, G.k, G.I, G.R
    banks = G.banks
    lambda_init = 0.8 - 0.6 * math.exp(-0.3 * l)
    with ExitStack() as s:
        sb = lambda name, shape, dt=F32: G.sb(s, name, shape, dt)
        A = attn_common(G, s, R.vB, "B")
        cmf = sb("cmf", [128, 128])
        cm = sb("cm", [128, 128], BF16)
        qp = [[sb("qp%d_%d" % (i, mp), [64, 512], BF16) for mp in range(2)] for i in range(2)]
        lamv = sb("lamv", [128, 4, 32])
        lamw = sb("lamw", [128, 2, 32])
        lams = sb("lams", [128, 4])
        subg = sb("subg", [128, 64])
        o1 = [sb("o1_%d" % i, [128, 4, 64]) for i in range(2)]
        o2 = [sb("o2_%d" % i, [128, 4, 64]) for i in range(2)]
        k.dma(cmf[:], I["cmask"], w=["cmf"])
        k.copy("dve", cm[:], cmf[:], r=["cmf"], w=["cm"])
        for i in range(2):
            for mp in range(2):
                k.memset("pool", qp[i][mp][:], 0.0, w=["qp%d" % i])
        for n, nm in enumerate(("df_lam_q1", "df_lam_k1", "df_lam_q2", "df_lam_k2")):
            bcast_load(G, lamv[:, n, :], I[nm][l:l + 1, :], 32, ["lamv"])
        bcast_load(G, subg[:], I["df_sub_g"][l:l + 1, :], 64, ["subg"])
        k.ts("dve", subg[:], subg[:], 1.0 - lambda_init, ALU.mult, r=["subg"], w=["subg"])
        k.tt("dve", lamw[:, 0, :], lamv[:, 0, :], lamv[:, 1, :], ALU.mult, r=["lamv"], w=["lamw"])
        k.tt("dve", lamw[:, 1, :], lamv[:, 2, :], lamv[:, 3, :], ALU.mult, r=["lamv"], w=["lamw"])
        k.red("dve", lams[:, 0:2], lamw[:], r=["lamw"], w=["lams"])
        k.act(lams[:, 0:2], lams[:, 0:2], AF.Exp, r=["lams"], w=["lams"])
        k.ts("dve", lams[:, 2:3], lams[:, 1:2], -lambda_init, ALU.add, r=["lams"], w=["lams"])
        k.tt("dve", lams[:, 3:4], lams[:, 2:3], lams[:, 0:1], ALU.subtract, r=["lams"], w=["lams"])
        allqk = [("Bqk", g) for g in range(NG)]
        for h in range(4):
            kt_ = A.kT[h % 2]
            kk = "kT%d" % (h % 2)
            k.dma(kt_[:], R.kTB[h * 64:(h + 1) * 64, :], r=allqk, w=[kk])
            for g in range(NG):
                i = k.rot("Bq", 2)
                k.dma(qp[i][0][0:32, :], R.qTB[h * 64:h * 64 + 32, g * TG:(g + 1) * TG], r=allqk, w=["qp%d" % i])
                k.dma(qp[i][1][32:64, :], R.qTB[h * 64 + 32:h * 64 + 64, g * TG:(g + 1) * TG], r=allqk, w=["qp%d" % i])
                oi = k.rot("BO", 2)
                Os = [banks[3 + oi][:, 0:260].rearrange("p (a e) -> p a e", a=4),
                      banks[5 + oi][:, 0:260].rearrange("p (a e) -> p a e", a=4)]
                obk = ["bank%d" % (3 + oi), "bank%d" % (5 + oi)]
                for kt in range(4 * g + 4):
                    m = kt - 4 * g
                    c0 = max(m, 0) * 128
                    N = 512 - c0
                    for mp in range(2):
                        sbk = k.rot("Bs", 3)
                        k.mm(banks[sbk][:, 0:N], kt_[:, kt * 128:(kt + 1) * 128], qp[i][mp][:, c0:512], r=[kk, "qp%d" % i],
                             w=["bank%d" % sbk])
                        ti = k.rot("Bt", 3)
                        k.act(A.Pm[ti][:, 0:N], banks[sbk][:, 0:N], AF.Exp, r=["bank%d" % sbk], w=["Pm%d" % ti])
                        if m >= 0:
                            k.tt("pool", A.Pm[ti][:, 0:128], A.Pm[ti][:, 0:128], cm[:], ALU.mult, r=["Pm%d" % ti, "cm"], w=["Pm%d" % ti])
                        for jq in range(max(m, 0), 4):
                            k.mm(Os[mp][:, jq, :], A.Pm[ti][:, jq * 128 - c0: jq * 128 - c0 + 128], A.V[:, kt, h * 65:(h + 1) * 65],
                                 start=(kt == 0 and jq == 0), stop=(kt == 4 * g + jq), r=["Pm%d" % ti, "V"], w=[obk[mp]], sgc=True)
                ri = k.rot("Brec", 2)
                rc = A.rec[ri]
                rk = "rec%d" % ri
                k.recip(rc[:, 0:4], Os[0][:, :, 64], r=[obk[0]], w=[rk])
                k.recip(rc[:, 4:8], Os[1][:, :, 64], r=[obk[1]], w=[rk])
                k.ts("dve", rc[:, 4:8], rc[:, 4:8], lams[:, 3:4], ALU.mult, r=[rk, "lams"], w=[rk])
                k.tt("dve", o1[ri][:], Os[0][:, :, 0:64], rc[:, 0:4].unsqueeze(2).to_broadcast([128, 4, 64]), ALU.mult,
                     r=[obk[0], rk], w=["o1_%d" % ri])
                k.tt("dve", o2[ri][:], Os[1][:, :, 0:64], rc[:, 4:8].unsqueeze(2).to_broadcast([128, 4, 64]), ALU.mult,
                     r=[obk[1], rk], w=["o2_%d" % ri])
                k.tt("pool", o1[ri][:], o1[ri][:], o2[ri][:], ALU.add, r=["o1_%d" % ri, "o2_%d" % ri], w=["o1_%d" % ri])
                k.tt("pool", o2[ri][:], o1[ri][:], o1[ri][:], ALU.mult, r=["o1_%d" % ri], w=["o2_%d" % ri])
                k.red("dve", rc[:, 0:4], o2[ri][:], r=["o2_%d" % ri], w=[rk])
                k.act(rc[:, 0:4], rc[:, 0:4], AF.Sqrt, bias=EPS, scale=1.0 / 64, r=[rk], w=[rk])
                k.recip(rc[:, 0:4], rc[:, 0:4], r=[rk], w=[rk])
                k.tt("dve", o1[ri][:], o1[ri][:], rc[:, 0:4].unsqueeze(2).to_broadcast([128, 4, 64]), ALU.mult,
                     r=["o1_%d" % ri, rk], w=["o1_%d" % ri])
                k.tt("pool", A.ob[ri][:], o1[ri][:], subg[:].unsqueeze(1).to_broadcast([128, 4, 64]), ALU.mult,
                     r=["o1_%d" % ri, "subg"], w=["ob%d" % ri])
                k.dma(R.yscr[g * TG:(g + 1) * TG, h * 64:(h + 1) * 64].rearrange("(a p) e -> p a e", p=128),
                      A.ob[ri][:], r=["ob%d" % ri], w=[("yB", g)], slow=True)
        P.barrier()


_CACHE = {}


def kernel(**inputs):
    if "prog" not in _CACHE:
        _CACHE["prog"] = build()
    nc, P = _CACHE["prog"]
    consts = make_consts()
    weights = {k: np.ascontiguousarray(np.asarray(inputs[k], dtype=np.float32)) for k in WEIGHT_SHAPES}
    x = np.asarray(inputs["x"], dtype=np.float32)
    c = np.asarray(inputs["c"], dtype=np.float32)
    in_maps = []
    for b in range(8):
        m = {"x": np.ascontiguousarray(x[b]), "c": np.ascontiguousarray(c[b:b + 1])}
        m.update(weights)
        m.update(consts)
        in_maps.append(m)
    res = run_bass_kernel_spmd(nc, in_maps, core_ids=list(range(8)))
    return np.stack([np.asarray(r["out"], dtype=np.float32) for r in res.results], axis=0)
```

```python
import math
import os
import numpy as np
import concourse.bass as bass
import concourse.mybir as mybir
from concourse.bass_utils import run_bass_kernel_spmd
from contextlib import ExitStack

F32 = mybir.dt.float32
BF16 = mybir.dt.bfloat16
AF = mybir.ActivationFunctionType
ALU = mybir.AluOpType
AX = mybir.AxisListType

S = 4096
D = 1024
L = 4
NIN = 2948
DFF = 2816
NG = 8
TG = 512
EPS = 1e-6
ALPHA = math.exp(-0.5)

ENGS = ("pe", "act", "dve", "pool", "sp")
EPOCH = 30000


class Prog:
    def __init__(self, nc, es):
        self.nc = nc
        self.es = es
        self.q = {e: [] for e in ENGS}
        self.cnt = {e: 0 for e in ENGS}
        self.epoch = {e: 0 for e in ENGS}
        self.sems = {}
        self.seen = {e: {} for e in ENGS}
        self.res_w = {}
        self.res_r = {}
        self.dma_val = {}
        self.n_inst = 0
        self.rr = 0

    def _sem(self, key):
        if key not in self.sems:
            self.sems[key] = self.es.enter_context(self.nc.semaphore("s_" + "_".join(str(k) for k in key)))
        return self.sems[key]

    def _deps(self, eng, reads, writes, extra=()):
        need = {}

        def add(ev):
            if ev is None:
                return
            k, v = ev
            if eng == "pe" and k[0] == "pe":
                return
            if need.get(k, 0) < v:
                need[k] = v
        for r in reads:
            add(self.res_w.get(r))
        for w in writes:
            add(self.res_w.get(w))
            for ev in self.res_r.get(w, ()):
                add(ev)
        for ev in extra:
            add(ev)
        waits = []
        for k, v in need.items():
            if self.seen[eng].get(k, 0) >= v:
                continue
            self.seen[eng][k] = v
            waits.append((k, v))
        return waits

    def _commit(self, ev, reads, writes):
        for r in reads:
            lst = self.res_r.setdefault(r, [])
            lst.append(ev)
            if len(lst) > 64:
                mx = {}
                for k, v in lst:
                    if mx.get(k, 0) < v:
                        mx[k] = v
                self.res_r[r] = list(mx.items())
        for w in writes:
            self.res_w[w] = ev
            self.res_r[w] = []

    @staticmethod
    def _is_psum(r):
        return (isinstance(r, str) and r.startswith("bank")) or (isinstance(r, tuple) and r[0] == "hb")

    def op(self, eng, fn, reads=(), writes=()):
        pr = [r for r in reads if self._is_psum(r)]
        if pr:
            writes = list(writes) + pr
        waits = self._deps(eng, reads, writes)
        if self.cnt[eng] >= EPOCH:
            self.epoch[eng] += 1
            self.cnt[eng] = 0
        self.cnt[eng] += 1
        key = (eng, self.epoch[eng])
        ev = (key, self.cnt[eng])
        self.q[eng].append((waits, fn, key, 1))
        self._commit(ev, reads, writes)
        self.n_inst += 1
        return ev

    def dma(self, queue, pairs, reads=(), writes=(), sem=None):
        if sem is None:
            sem = ("dma", "rr%d" % (self.rr % 20))
            self.rr += 1
        key = sem
        prev = self.dma_val.get(key, 0)
        extra = [(key, prev)] if prev > 0 else []
        waits = self._deps(queue, reads, writes, extra)
        val = prev
        for i, pr in enumerate(pairs):
            out_ap, in_ap = pr[0], pr[1]
            kw = pr[2] if len(pr) > 2 else {}
            val += 16

            def fn(e, out_ap=out_ap, in_ap=in_ap, kw=kw):
                return e.dma_start(out=out_ap, in_=in_ap, **kw)
            self.q[queue].append((waits if i == 0 else [], fn, key, 16))
            self.n_inst += 1
        self.dma_val[key] = val
        ev = (key, val)
        self._commit(ev, reads, writes)
        return ev

    def barrier(self):
        evs = []
        for e in ENGS:
            for ep in range(self.epoch[e] + 1):
                k = (e, ep)
                v = self.cnt[e] if ep == self.epoch[e] else EPOCH
                if v > 0:
                    evs.append((k, v))
        for k, v in self.dma_val.items():
            evs.append((k, v))
        for e in ENGS:
            waits = []
            for k, v in evs:
                if self.seen[e].get(k, 0) < v:
                    self.seen[e][k] = v
                    waits.append((k, v))
            if waits:
                self.q[e].append((waits, None, None, 0))
        self.res_w = {}
        self.res_r = {}

    def emit(self):
        nc = self.nc
        for e in ENGS:
            for (waits, fn, key, inc) in self.q[e]:
                for k, v in waits:
                    self._sem(k)
                if key is not None:
                    self._sem(key)
        block = self.es.enter_context(nc.Block())
        engmap = {"pe": block.tensor, "act": block.scalar, "dve": block.vector, "pool": block.gpsimd,
                  "sp": block.sync}
        for e in ENGS:
            items = self.q[e]

            def body(eng, items=items):
                for (waits, fn, key, inc) in items:
                    for k, v in waits:
                        eng.wait_ge(self.sems[k], v)
                    if fn is not None:
                        ins = fn(eng)
                        ins.then_inc(self.sems[key], inc)
            engmap[e](body)


class K:
    def __init__(self, P):
        self.P = P
        self._rot = {}

    def rot(self, name, n):
        i = self._rot.get(name, 0)
        self._rot[name] = i + 1
        return i % n

    def mm(self, out, lhsT, rhs, start=True, stop=True, r=(), w=(), sgc=False):
        if sgc:
            return self.P.op("pe", lambda e: e.matmul(out, lhsT=lhsT, rhs=rhs, start=start, stop=stop, skip_group_check=True), r, w)
        return self.P.op("pe", lambda e: e.matmul(out, lhsT=lhsT, rhs=rhs, start=start, stop=stop), r, w)

    def tr(self, out, in_, ident, r=(), w=()):
        return self.P.op("pe", lambda e: e.transpose(out=out, in_=in_, identity=ident), r, w)

    def act(self, out, in_, func, bias=None, scale=None, accum_out=None, r=(), w=(), eng="act"):
        kw = {}
        if bias is not None:
            kw["bias"] = bias
        if scale is not None:
            kw["scale"] = scale
        if accum_out is not None:
            kw["accum_out"] = accum_out
        return self.P.op("act", lambda e: e.activation(out=out, in_=in_, func=func, **kw), r, w)

    def copy(self, eng, out, in_, r=(), w=()):
        if eng == "act":
            return self.P.op("act", lambda e: e.copy(out=out, in_=in_), r, w)
        return self.P.op(eng, lambda e: e.tensor_copy(out=out, in_=in_), r, w)

    def tt(self, eng, out, in0, in1, op, r=(), w=()):
        return self.P.op(eng, lambda e: e.tensor_tensor(out=out, in0=in0, in1=in1, op=op), r, w)

    def ts(self, eng, out, in0, s1, op0, s2=None, op1=None, r=(), w=()):
        if op1 is None:
            return self.P.op(eng, lambda e: e.tensor_scalar(out=out, in0=in0, scalar1=s1, scalar2=None, op0=op0), r, w)
        return self.P.op(eng, lambda e: e.tensor_scalar(out=out, in0=in0, scalar1=s1, scalar2=s2, op0=op0, op1=op1), r, w)

    def stt(self, eng, out, in0, scalar, in1, op0, op1, r=(), w=()):
        return self.P.op(eng, lambda e: e.scalar_tensor_tensor(out=out, in0=in0, scalar=scalar, in1=in1, op0=op0, op1=op1), r, w)

    def red(self, eng, out, in_, op=ALU.add, r=(), w=()):
        return self.P.op(eng, lambda e: e.tensor_reduce(out=out, in_=in_, axis=AX.X, op=op), r, w)

    def recip(self, out, in_, r=(), w=()):
        return self.P.op("dve", lambda e: e.reciprocal(out=out, in_=in_), r, w)

    def memset(self, eng, ap, val, w=()):
        return self.P.op(eng, lambda e: e.memset(ap, val), (), w)

    def scan(self, out, d0, d1, initial, op0, op1, r=(), w=()):
        return self.P.op("dve", lambda e: e.tensor_tensor_scan(out=out, data0=d0, data1=d1, initial=initial, op0=op0, op1=op1), r, w)

    def dma(self, out, in_, r=(), w=(), q="sp", slow=False, sem=None):
        kw = {"allow_slow_non_contiguous": True} if slow else {}
        return self.P.dma(q, [(out, in_, kw)], r, w, sem=sem)


def make_consts():
    c = {}
    c["ident"] = np.eye(128, dtype=np.float32)
    bo64 = np.zeros((128, 128), np.float32)
    bo64[:64, :64] = 1
    bo64[64:, 64:] = 1
    c["bo64"] = bo64
    bo32 = np.zeros((128, 128), np.float32)
    for i in range(4):
        bo32[i * 32:(i + 1) * 32, i * 32:(i + 1) * 32] = 1
    c["bo32"] = bo32
    prot = np.zeros((128, 128), np.float32)
    for b in range(4):
        for d in range(16):
            prot[b * 32 + d + 16, b * 32 + d] = -1.0
            prot[b * 32 + d, b * 32 + d + 16] = 1.0
    c["prot"] = prot
    inv = 1.0 / (10000.0 ** (np.arange(0, 32, 2, dtype=np.float32) / 32.0))
    ang = np.arange(S, dtype=np.float32)[:, None] * inv[None, :]
    cos = np.cos(ang).astype(np.float32).T
    sin = np.sin(ang).astype(np.float32).T
    c["cosT"] = np.ascontiguousarray(np.tile(cos, (8, 1)))
    c["sinT"] = np.ascontiguousarray(np.tile(sin, (8, 1)))
    k = np.arange(128)[:, None]
    q = np.arange(128)[None, :]
    c["negmask"] = np.where(k > q, -1e30, 0.0).astype(np.float32)
    c["cmask"] = ((k // 64) <= (q // 64)).astype(np.float32)
    c["triu"] = (k <= q).astype(np.float32)
    k6 = np.arange(64)[:, None]
    q6 = np.arange(64)[None, :]
    m64 = np.zeros((64, 3, 64), np.float32)
    m64[:, 0, :] = (k6 < q6)
    m64[:, 1, :] = (k6 <= q6)
    m64[:, 2, :] = (k6 > q6)
    c["m64"] = m64
    sel = np.zeros((4, 4, 128), np.float32)
    for h in range(4):
        sel[h, h, :] = 1.0
    c["sel"] = sel.transpose(1, 0, 2).copy()
    return c


CONST_SHAPES = {"ident": [128, 128], "bo64": [128, 128], "bo32": [128, 128], "prot": [128, 128],
                "cosT": [128, S], "sinT": [128, S], "negmask": [128, 128], "cmask": [128, 128],
                "triu": [128, 128], "m64": [64, 3, 64], "sel": [4, 4, 128]}

WEIGHT_SHAPES = {
    'ada_w': [L, D, 6 * D], 'ada_b': [L, 6 * D], 'norm1_g': [L, D], 'norm2_g': [L, D],
    'w_in': [L, D, NIN], 'w_out': [L, D, D],
    'rw_mu': [L, 896], 'rw_w0': [L, 256], 'rw_w_up': [L, 32, 256], 'rw_a0': [L, 256], 'rw_a_up': [L, 32, 256],
    'rw_g_up': [L, 64, 256], 'rw_k_k': [L, 256], 'rw_k_a': [L, 256], 'rw_r_k': [L, 4, 64],
    'rw_ln_g': [L, 256], 'rw_ln_b': [L, 256],
    'df_lam_q1': [L, 32], 'df_lam_k1': [L, 32], 'df_lam_q2': [L, 32], 'df_lam_k2': [L, 32],
    'df_q_g': [L, 32], 'df_k_g': [L, 32], 'df_sub_g': [L, 64],
    'sg_w': [L, 4, 128, 128], 'sg_b': [L, 4, 128], 'sg_ln_g': [L, 256], 'sg_ln_b': [L, 256],
    'fx_q_g': [L, 64], 'fx_k_g': [L, 64], 'fx_f_b': [L, 4],
    'ffn_up': [L, D, 2 * DFF], 'ffn_conv': [L, 3, 2 * DFF], 'ffn_conv_b': [L, 2 * DFF], 'ffn_down': [L, DFF, D],
}


class Ctx:
    pass


def build(layers=(0, 1, 2, 3), phases=("P1", "A", "B", "D", "P3"), debug=False, y_in=False):
    nc = bass.Bass("TRN2", target_bir_lowering=False)
    dkind = "ExternalOutput" if debug else "Internal"
    I = {}

    def din(name, shape):
        I[name] = nc.dram_tensor(name, list(shape), F32, kind="ExternalInput").ap()
    din("x", [S, D])
    din("c", [1, D])
    for k, shp in WEIGHT_SHAPES.items():
        din(k, shp)
    for k, shp in CONST_SHAPES.items():
        din(k, shp)
    out = nc.dram_tensor("out", [S, D], F32, kind="ExternalOutput").ap()

    def dscr(name, shape, dt, kind=None):
        return nc.dram_tensor(name, list(shape), dt, kind=kind or dkind).ap()
    R = Ctx()
    R.xres = dscr("xres", [S, D], F32)
    R.pmA = dscr("pmA", [896, S], F32)
    R.qTB = dscr("qTB", [256, S], BF16)
    R.kTB = dscr("kTB", [256, S], BF16)
    R.vB = dscr("vB", [S, 260], BF16)
    R.qTD = dscr("qTD", [256, S], BF16)
    R.kTD = dscr("kTD", [256, S], BF16)
    R.vD = dscr("vD", [S, 260], BF16)
    R.Ffm = dscr("Ffm", [4, S], F32)
    if y_in:
        R.yscr = nc.dram_tensor("yscr_in", [S, 768], F32, kind="ExternalInput").ap()
        R.yTA = nc.dram_tensor("yTA_in", [256, S], F32, kind="ExternalInput").ap()
    else:
        R.yscr = dscr("yscr", [S, 768], BF16)
        R.yTA = dscr("yTA", [256, S], BF16)
    R.upbf = dscr("upbf", [L, 11, 128, 8 * 512], BF16, kind="Internal")
    R.dnbf = dscr("dnbf", [L, 128, 22 * 1024], BF16, kind="Internal")

    with ExitStack() as es:
        P = Prog(nc, es)
        k = K(P)
        G = Ctx()
        G.nc, G.P, G.k, G.I, G.R, G.out = nc, P, k, I, R, out
        G.y_in = y_in
        G.dbg_x1 = nc.dram_tensor("dbg_x1", [S, D], F32, kind="ExternalOutput").ap() if debug else None

        uid = [0]

        def sb(stack, name, shape, dt=F32):
            uid[0] += 1
            return stack.enter_context(nc.sbuf_tensor("%s_u%d" % (name, uid[0]), list(shape), dt))
        G.sb = sb
        G.banks = [es.enter_context(nc.psum_tensor("bank%d" % i, [128, 512], F32)) for i in range(8)]
        G.ident = sb(es, "ident", [128, 128])
        G.identb = sb(es, "identb", [128, 128], BF16)
        G.ones_row = sb(es, "ones_row", [1, 128])
        G.condB = sb(es, "condB", [128, 8, 128])
        G.modB = sb(es, "modB", [128, 6 * D])
        k.dma(G.ident[:], I["ident"], w=["ident"])
        k.copy("dve", G.identb[:], G.ident[:], r=["ident"], w=["identb"])
        k.memset("dve", G.ones_row[:], 1.0, w=["ones_row"])
        with ExitStack() as s0:
            cT = sb(s0, "cT", [128, 8])
            cS = sb(s0, "cS", [128, 8])
            k.dma(cT[:], I["c"].rearrange("o (kc p) -> p (o kc)", p=128), w=["cT"], slow=True)
            k.act(cS[:], cT[:], AF.Silu, r=["cT"], w=["cS"])
            k.copy("dve", G.condB[:], cS[:].unsqueeze(2).to_broadcast([128, 8, 128]), r=["cS"], w=["condB"])
            P.barrier()

        for li, l in enumerate(layers):
            xsrc = I["x"] if li == 0 else R.xres
            xdst = out if li == len(layers) - 1 else R.xres
            if "P3" in phases:
                prep_ffn(G, l)
            layer_setup(G, l)
            if "P1" in phases:
                phase_p1(G, l, xsrc)
            if "A" in phases:
                phase_A(G, l)
            if "B" in phases:
                phase_B(G, l)
            if "D" in phases:
                phase_D(G, l)
            if "P3" in phases:
                phase_p3(G, l, xsrc, xdst)
        P.barrier()
        P.emit()
    return nc, P


def prep_ffn(G, l):
    nc, P, k, I, R = G.nc, G.P, G.k, G.I, G.R
    with ExitStack() as s:
        st = [G.sb(s, "pf_st%d" % i, [128, 8, 512]) for i in range(2)]
        sbf = [G.sb(s, "pf_bf%d" % i, [128, 8, 512], BF16) for i in range(2)]
        up = I["ffn_up"][l].rearrange("(kc p) n -> p kc n", p=128)
        dn = I["ffn_down"][l].rearrange("(fc p) d -> p fc d", p=128)
        engs = ["dve", "act", "dve", "act", "pool"]
        jobs = []
        for u in range(11):
            jobs.append((up[:, :, u * 512:(u + 1) * 512], R.upbf[l, u].rearrange("p (kc n) -> p kc n", kc=8), 8, 512))
        for u in range(11):
            jobs.append((dn[:, 2 * u:2 * u + 2, :], R.dnbf[l][:, 2 * u * 1024:(2 * u + 2) * 1024].rearrange("p (a d) -> p a d", a=2), 2, 1024))

        def load(i):
            src, dst, a, b = jobs[i]
            k.dma(st[i % 2][:].rearrange("p a b -> p (a b)")[:, 0:a * b].rearrange("p (a b) -> p a b", a=a), src,
                  w=["pf_st%d" % (i % 2)])
        load(0)
        load(1)
        for i in range(len(jobs)):
            src, dst, a, b = jobs[i]
            sv = st[i % 2][:].rearrange("p a b -> p (a b)")[:, 0:a * b]
            bv = sbf[i % 2][:].rearrange("p a b -> p (a b)")[:, 0:a * b]
            k.copy(engs[i % 5], bv, sv, r=["pf_st%d" % (i % 2)], w=["pf_bf%d" % (i % 2)])
            k.dma(dst, bv.rearrange("p (a b) -> p a b", a=a), r=["pf_bf%d" % (i % 2)], w=[("ffnw", l)])
            if i + 2 < len(jobs):
                load(i + 2)
        P.barrier()


def layer_setup(G, l):
    nc, P, k, I, R = G.nc, G.P, G.k, G.I, G.R
    banks = G.banks
    with ExitStack() as s:
        aw = [G.sb(s, "ls_aw%d" % i, [128, 8, 512]) for i in range(2)]
        rows = G.sb(s, "ls_rows", [1, 8 * D])
        k.dma(rows[:, 0:6 * D], I["ada_b"][l:l + 1, :], w=["ls_rows_b"])
        k.dma(rows[:, 6 * D:7 * D], I["norm1_g"][l:l + 1, :], w=["ls_rows_g"])
        k.dma(rows[:, 7 * D:8 * D], I["norm2_g"][l:l + 1, :], w=["ls_rows_g"])
        awv = I["ada_w"][l].rearrange("(kc p) n -> p kc n", p=128)
        for cc in range(12):
            b = cc % 2
            k.dma(aw[b][:], awv[:, :, cc * 512:(cc + 1) * 512], w=["ls_aw%d" % b])
            bk = banks[b]
            for kc in range(8):
                k.mm(bk[:], G.condB[:, kc, :], aw[b][:, kc, :], start=(kc == 0), stop=False,
                     r=["condB", "ls_aw%d" % b], w=["bank%d" % b])
            k.mm(bk[:], G.ones_row[0:1, :], rows[0:1, cc * 512:(cc + 1) * 512], start=False, stop=True,
                 r=["ones_row", "ls_rows_b"], w=["bank%d" % b])
            k.copy("act" if cc % 2 else "dve", G.modB[:, cc * 512:(cc + 1) * 512], bk[:], r=["bank%d" % b], w=["modB"])
        for gi, (goff, slot) in enumerate(((6 * D, 1 * D), (7 * D, 4 * D))):
            for hf in range(2):
                b = 2 + hf
                k.mm(banks[b][:], G.ones_row[0:1, :], rows[0:1, goff + hf * 512: goff + (hf + 1) * 512],
                     r=["ones_row", "ls_rows_g"], w=["bank%d" % b])
                sl = G.modB[:, slot + hf * 512: slot + (hf + 1) * 512]
                k.stt("dve", sl, sl, 1.0, banks[b][:], ALU.add, ALU.mult, r=["modB", "bank%d" % b], w=["modB"])
        P.barrier()


def norm_mod_T(G, xt, goff, shoff, T):
    k = G.k
    banks = G.banks
    for j in range(4):
        k.act(T.tmp[:], xt[:, j, :], AF.Square, r=["xt"], w=["tmp"])
        k.red("dve", T.ss[:, j:j + 1], T.tmp[:], r=["tmp"], w=["ss"])
    k.act(T.rstd[:], T.ss[:], AF.Sqrt, bias=EPS, scale=1.0 / D, r=["ss"], w=["rstd"])
    k.recip(T.rstd[:], T.rstd[:], r=["rstd"], w=["rstd"])
    for j in range(4):
        k.stt("dve", T.tmp[:], xt[:, j, :], T.rstd[:, j:j + 1], G.modB[:, goff:goff + D], ALU.mult, ALU.mult,
              r=["xt", "rstd", "modB"], w=["tmp"])
        k.tt("dve", T.hb[:, j, :], T.tmp[:], G.modB[:, shoff:shoff + D], ALU.add, r=["tmp", "modB"], w=["hb"])
    for j in range(4):
        b = 5 + (j % 2)
        pv = banks[b][:].bitcast(BF16).rearrange("p (a t) -> p a t", a=8)
        for kc in range(8):
            k.tr(pv[:, kc, :], T.hb[:, j, kc * 128:(kc + 1) * 128], G.identb[:], r=["hb", "identb"], w=["bank%d" % b])
        k.copy("act" if j % 2 else "dve", T.hT[:, :, j * 128:(j + 1) * 128], pv, r=["bank%d" % b], w=["hT"])


def bcast_load(G, tile_ap, row_ap, n, w):
    G.k.dma(tile_ap, row_ap.to_broadcast([128, n]), w=w, slow=True)


def phase_p1(G, l, xsrc):
    nc, P, k, I, R = G.nc, G.P, G.k, G.I, G.R
    banks = G.banks
    with ExitStack() as s:
        sb = lambda name, shape, dt=F32: G.sb(s, name, shape, dt)
        T = Ctx()
        w_in = sb("w_in", [128, 8, NIN], BF16)
        stg = [sb("p1_stg%d" % i, [128, 1474]) for i in range(2)]
        xt = sb("xt", [128, 4, D])
        T.sqj = sb("sqj", [128, D], BF16)
        T.ss = sb("ss", [128, 4])
        T.rstd = sb("rstd", [128, 4])
        T.tmp = sb("tmp", [128, D])
        T.hb = sb("hb", [128, 4, D], BF16)
        T.hT = sb("hT", [128, 8, 512], BF16)
        fA = [sb("fA%d" % i, [128, 512]) for i in range(3)]
        fB = [sb("fB%d" % i, [128, 512]) for i in range(3)]
        fC = [sb("fC%d" % i, [128, 512]) for i in range(3)]
        obf = [sb("obf%d" % i, [128, 512], BF16) for i in range(3)]
        xnb = [sb("xnb%d" % i, [128, 512], BF16) for i in range(2)]
        paA = sb("paA", [128, 7, 513])
        pmo = [sb("pmo%d" % i, [128, 512]) for i in range(2)]
        cs = [sb("cos%d" % i, [128, 512]) for i in range(2)]
        sn = [sb("sin%d" % i, [128, 512]) for i in range(2)]
        bo64 = sb("bo64", [128, 128])
        bo32 = sb("bo32", [128, 128])
        protf = sb("protf", [128, 128])
        protb = sb("protb", [128, 128], BF16)
        gcol = sb("gcol", [128, 8])
        mu = sb("mu", [128, 7])
        negfb = sb("negfb", [4, 1])
        ones4 = sb("ones4", [4, 512])
        Fg = [sb("Fg%d" % i, [4, 512]) for i in range(2)]
        f4 = sb("f4", [4, 512])
        vt = [sb("vt%d" % i, [128, 4, 65], BF16) for i in range(4)]
        sgw = sb("sgw", [128, 4, 128])
        WgT = sb("WgT", [128, 4, 128], BF16)
        triu = sb("triu", [128, 128])
        sgbT = sb("sgbT", [128, 4])
        lnCg = sb("lnCg", [128, 256])
        lnCb = sb("lnCb", [128, 256])
        glC = [sb("glC%d" % i, [128, 512]) for i in range(2)]
        stC = [sb("stC%d" % i, [128, 8]) for i in range(2)]
        tmpc = [sb("tmpc%d" % i, [128, 256]) for i in range(2)]
        vnb = [sb("vnb%d" % i, [128, 256], BF16) for i in range(2)]
        ycb = [sb("ycb%d" % i, [128, 256], BF16) for i in range(2)]

        for kc in range(8):
            for hf in range(2):
                i = (kc * 2 + hf) % 2
                k.dma(stg[i][:], I["w_in"][l, kc * 128:(kc + 1) * 128, hf * 1474:(hf + 1) * 1474], w=["p1_stg%d" % i])
                k.copy(("dve", "pool", "act")[(kc * 2 + hf) % 3], w_in[:, kc, hf * 1474:(hf + 1) * 1474], stg[i][:],
                       r=["p1_stg%d" % i], w=["w_in"])
        k.dma(bo64[:], I["bo64"], w=["bo64"])
        k.dma(bo32[:], I["bo32"], w=["bo32"])
        k.dma(protf[:], I["prot"], w=["protf"])
        k.copy("dve", protb[:], protf[:], r=["protf"], w=["protb"])
        k.dma(triu[:], I["triu"], w=["triu"])
        for rep in range(2):
            k.dma(gcol[rep * 64:(rep + 1) * 64, 0:1], I["fx_q_g"][l].rearrange("(d o) -> d o", o=1), w=["gcol"], slow=True)
            k.dma(gcol[rep * 64:(rep + 1) * 64, 1:2], I["fx_k_g"][l].rearrange("(d o) -> d o", o=1), w=["gcol"], slow=True)
        for rep in range(4):
            k.dma(gcol[rep * 32:(rep + 1) * 32, 2:3], I["df_q_g"][l].rearrange("(d o) -> d o", o=1), w=["gcol"], slow=True)
            k.dma(gcol[rep * 32:(rep + 1) * 32, 3:4], I["df_k_g"][l].rearrange("(d o) -> d o", o=1), w=["gcol"], slow=True)
        k.ts("dve", gcol[:, 0:1], gcol[:, 0:1], 0.125, ALU.mult, r=["gcol"], w=["gcol"])
        k.ts("dve", gcol[:, 2:3], gcol[:, 2:3], 32.0 ** -0.5, ALU.mult, r=["gcol"], w=["gcol"])
        k.dma(mu[:], I["rw_mu"][l].rearrange("(c p) -> p c", p=128), w=["mu"], slow=True)
        k.dma(negfb[:], I["fx_f_b"][l].rearrange("(h o) -> h o", o=1), w=["negfb"], slow=True)
        k.ts("dve", negfb[:], negfb[:], -1.0, ALU.mult, r=["negfb"], w=["negfb"])
        k.memset("dve", ones4[:], 1.0, w=["ones4"])
        k.memset("pool", paA[:], 0.0, w=["paA%d" % i for i in range(7)])
        for i in range(4):
            k.memset("pool", vt[i][:], 1.0, w=["vt%d" % i])
        k.dma(sgw[:], I["sg_w"][l].rearrange("g i j -> i g j"), w=["sgw"])
        for g in range(4):
            k.tr(banks[0][:, g * 128:(g + 1) * 128], sgw[:, g, :], G.ident[:], r=["sgw", "ident"], w=["bank0"])
        k.tt("dve", WgT[:], banks[0][:].rearrange("p (g i) -> p g i", g=4),
             triu[:].unsqueeze(1).to_broadcast([128, 4, 128]), ALU.mult, r=["bank0", "triu"], w=["WgT"])
        k.dma(sgbT[:], I["sg_b"][l].rearrange("g i -> i g"), w=["sgbT"], slow=True)
        bcast_load(G, lnCg[:], I["sg_ln_g"][l:l + 1, :], 256, ["lnCg"])
        bcast_load(G, lnCb[:], I["sg_ln_b"][l:l + 1, :], 256, ["lnCb"])

        def fm_mm(col0, ncols, bk):
            for kc in range(8):
                k.mm(banks[bk][0:ncols, :], w_in[:, kc, col0:col0 + ncols], T.hT[:, kc, :], start=(kc == 0), stop=(kc == 7),
                     r=["w_in", "hT"], w=["bank%d" % bk])

        for g in range(NG):
            tsl = slice(g * TG, (g + 1) * TG)
            k.dma(xt[:], xsrc[tsl, :].rearrange("(j p) d -> p j d", p=128), r=[("x", g)], w=["xt"])
            k.dma(cs[g % 2][:], I["cosT"][:, tsl], w=["cos%d" % (g % 2)])
            k.dma(sn[g % 2][:], I["sinT"][:, tsl], w=["sin%d" % (g % 2)])
            norm_mod_T(G, xt, 1 * D, 0, T)
            for ci in range(7):
                bk = k.rot("p1bank", 3)
                fm_mm(ci * 128, 128, bk)
                k.copy("pool", paA[:, ci, 0:1], paA[:, ci, 512:513], r=["paA%d" % ci], w=["paA%d" % ci])
                k.copy("act", paA[:, ci, 1:513], banks[bk][:], r=["bank%d" % bk], w=["paA%d" % ci])
                i = k.rot("fA", 3)
                k.tt("pool", fA[i][:], paA[:, ci, 0:512], paA[:, ci, 1:513], ALU.subtract, r=["paA%d" % ci], w=["fA%d" % i])
                o = k.rot("pmo", 2)
                k.stt("dve", pmo[o][:], fA[i][:], mu[:, ci:ci + 1], paA[:, ci, 1:513], ALU.mult, ALU.add,
                      r=["fA%d" % i, "mu", "paA%d" % ci], w=["pmo%d" % o])
                k.dma(R.pmA[ci * 128:(ci + 1) * 128, tsl], pmo[o][:], r=["pmo%d" % o], w=[("pmA", g)])
            for (mix, col0, gi, dst, rope) in (("B", 896, 2, R.qTB, True), ("B", 1152, 3, R.kTB, True),
                                               ("D", 2176, 0, R.qTD, False), ("D", 2432, 1, R.kTD, False)):
                for ci in range(2):
                    bk = k.rot("p1bank", 3)
                    fm_mm(col0 + ci * 128, 128, bk)
                    a = k.rot("fA", 3)
                    k.act(fA[a][:], banks[bk][:], AF.Square, r=["bank%d" % bk], w=["fA%d" % a])
                    sbk = 3 + k.rot("p1sbank", 2)
                    k.mm(banks[sbk][:], bo32[:] if mix == "B" else bo64[:], fA[a][:], r=["bo32", "bo64", "fA%d" % a],
                         w=["bank%d" % sbk])
                    b = k.rot("fB", 3)
                    nd = 32.0 if mix == "B" else 64.0
                    k.act(fB[b][:], banks[sbk][:], AF.Sqrt, bias=EPS, scale=1.0 / nd, r=["bank%d" % sbk], w=["fB%d" % b])
                    k.recip(fB[b][:], fB[b][:], r=["fB%d" % b], w=["fB%d" % b])
                    o = k.rot("obf", 3)
                    if not rope:
                        k.stt("dve", obf[o][:], banks[bk][:], gcol[:, gi:gi + 1], fB[b][:], ALU.mult, ALU.mult,
                              r=["bank%d" % bk, "gcol", "fB%d" % b], w=["obf%d" % o])
                    else:
                        c_ = k.rot("fC", 3)
                        k.stt("dve", fC[c_][:], banks[bk][:], gcol[:, gi:gi + 1], fB[b][:], ALU.mult, ALU.mult,
                              r=["bank%d" % bk, "gcol", "fB%d" % b], w=["fC%d" % c_])
                        xb = k.rot("xnb", 2)
                        k.copy("act", xnb[xb][:], fC[c_][:], r=["fC%d" % c_], w=["xnb%d" % xb])
                        rbk = 3 + k.rot("p1sbank", 2)
                        k.mm(banks[rbk][:], protb[:], xnb[xb][:], r=["protb", "xnb%d" % xb], w=["bank%d" % rbk])
                        a2 = k.rot("fA", 3)
                        k.tt("pool", fA[a2][:], fC[c_][:], cs[g % 2][:], ALU.mult, r=["fC%d" % c_, "cos%d" % (g % 2)], w=["fA%d" % a2])
                        b2 = k.rot("fB", 3)
                        k.tt("dve", fB[b2][:], banks[rbk][:], sn[g % 2][:], ALU.mult, r=["bank%d" % rbk, "sin%d" % (g % 2)],
                             w=["fB%d" % b2])
                        k.tt("pool", obf[o][:], fA[a2][:], fB[b2][:], ALU.add, r=["fA%d" % a2, "fB%d" % b2], w=["obf%d" % o])
                    k.dma(dst[ci * 128:(ci + 1) * 128, tsl], obf[o][:], r=["obf%d" % o], w=[(mix + "qk", g)])
            bk = k.rot("p1bank", 3)
            fm_mm(2944, 4, bk)
            k.act(f4[:], banks[bk][0:4, :], AF.Exp, bias=negfb[:, 0:1], scale=-1.0, r=["bank%d" % bk, "negfb"], w=["f4"])
            k.act(f4[:], f4[:], AF.Ln, bias=1.0, scale=1.0, r=["f4"], w=["f4"])
            if g == 0:
                k.scan(Fg[0][:], ones4[:], f4[:], 0.0, ALU.mult, ALU.subtract, r=["ones4", "f4"], w=["Fg0"])
            else:
                k.scan(Fg[g % 2][:], ones4[:], f4[:], Fg[(g - 1) % 2][:, 511:512], ALU.mult, ALU.subtract,
                       r=["ones4", "f4", "Fg%d" % ((g - 1) % 2)], w=["Fg%d" % (g % 2)])
            k.dma(R.Ffm[:, tsl], Fg[g % 2][:], r=["Fg%d" % (g % 2)], w=[("Ffm", g)])
            for j in range(4):
                rows = slice(g * TG + j * 128, g * TG + (j + 1) * 128)
                for (mix, col0, dst) in (("B", 1408, R.vB), ("D", 2688, R.vD)):
                    bk = k.rot("p1bank", 3)
                    for kc in range(8):
                        k.mm(banks[bk][:, 0:256], T.hT[:, kc, j * 128:(j + 1) * 128], w_in[:, kc, col0:col0 + 256],
                             start=(kc == 0), stop=(kc == 7), r=["w_in", "hT"], w=["bank%d" % bk])
                    vi = k.rot("vt", 4)
                    k.copy("act" if mix == "B" else "dve", vt[vi][:, :, 0:64], banks[bk][:, 0:256].rearrange("p (h e) -> p h e", h=4),
                           r=["bank%d" % bk], w=["vt%d" % vi])
                    k.dma(dst[rows, :], vt[vi][:].rearrange("p h e -> p (h e)"), r=["vt%d" % vi], w=[(mix + "v", g)])
                bk = k.rot("p1bank", 3)
                for kc in range(8):
                    k.mm(banks[bk][:], T.hT[:, kc, j * 128:(j + 1) * 128], w_in[:, kc, 1664:2176],
                         start=(kc == 0), stop=(kc == 7), r=["w_in", "hT"], w=["bank%d" % bk])
                ci = k.rot("glC", 2)
                gl, st, tc, vb, yb = glC[ci], stC[ci], tmpc[ci], vnb[ci], ycb[ci]
                kk = "C%d" % ci
                k.act(gl[:], banks[bk][:], AF.Gelu, r=["bank%d" % bk], w=[kk + "gl"])
                k.red("dve", st[:, 0:1], gl[:, 256:512], r=[kk + "gl"], w=[kk + "st"])
                k.act(tc[:], gl[:, 256:512], AF.Square, r=[kk + "gl"], w=[kk + "tc"])
                k.red("dve", st[:, 1:2], tc[:], r=[kk + "tc"], w=[kk + "st"])
                k.ts("dve", st[:, 2:3], st[:, 0:1], 1.0 / 256, ALU.mult, r=[kk + "st"], w=[kk + "st"])
                k.tt("dve", st[:, 3:4], st[:, 2:3], st[:, 2:3], ALU.mult, r=[kk + "st"], w=[kk + "st"])
                k.stt("dve", st[:, 4:5], st[:, 1:2], 1.0 / 256, st[:, 3:4], ALU.mult, ALU.subtract, r=[kk + "st"], w=[kk + "st"])
                k.act(st[:, 5:6], st[:, 4:5], AF.Sqrt, bias=EPS, scale=1.0, r=[kk + "st"], w=[kk + "st"])
                k.recip(st[:, 5:6], st[:, 5:6], r=[kk + "st"], w=[kk + "st"])
                k.ts("dve", tc[:], gl[:, 256:512], st[:, 2:3], ALU.subtract, st[:, 5:6], ALU.mult, r=[kk + "gl", kk + "st"], w=[kk + "tc"])
                k.tt("pool", tc[:], tc[:], lnCg[:], ALU.mult, r=[kk + "tc", "lnCg"], w=[kk + "tc"])
                k.tt("pool", vb[:], tc[:], lnCb[:], ALU.add, r=[kk + "tc", "lnCb"], w=[kk + "vb"])
                sbk = 3 + k.rot("p1sbank", 2)
                for hg in range(4):
                    k.mm(banks[sbk][:, hg * 64:(hg + 1) * 64], WgT[:, hg, :], vb[:, hg * 64:(hg + 1) * 64],
                         r=["WgT", kk + "vb"], w=["bank%d" % sbk])
                k.tt("dve", tc[:].rearrange("p (h e) -> p h e", h=4), banks[sbk][:, 0:256].rearrange("p (h e) -> p h e", h=4),
                     sgbT[:].unsqueeze(2).to_broadcast([128, 4, 64]), ALU.add, r=["bank%d" % sbk, "sgbT", kk + "vb"], w=[kk + "tc"])
                k.tt("pool", yb[:], tc[:], gl[:, 0:256], ALU.mult, r=[kk + "tc", kk + "gl"], w=[kk + "yb"])
                if not G.y_in:
                    k.dma(R.yscr[rows, 256:512], yb[:], r=[kk + "yb"], w=[("yC", g)])
        P.barrier()


def phase_p3(G, l, xsrc, xdst):
    nc, P, k, I, R = G.nc, G.P, G.k, G.I, G.R
    banks = G.banks
    ydt = F32 if G.y_in else BF16
    with ExitStack() as s:
        sb = lambda name, shape, dt=F32: G.sb(s, name, shape, dt)
        T = Ctx()
        w_out = sb("w_out", [128, 8, D], BF16)
        xt = sb("xt3", [128, 4, D])
        T.sqj = sb("sqj3", [128, D], BF16)
        T.ss = sb("ss3", [128, 4])
        T.rstd = sb("rstd3", [128, 4])
        T.tmp = sb("tmp3", [128, D])
        T.hb = sb("hb3", [128, 4, D], BF16)
        T.hT = sb("hT3", [128, 8, 512], BF16)
        yT = T.hT
        actT = sb("actT", [128, 22, 512], BF16)
        upw = [sb("upw%d" % i, [128, 8, 512], BF16) for i in range(2)]
        dnw = sb("dnw", [128, 22, D], BF16)
        ub = [sb("ub%d" % i, [128, 514]) for i in range(2)]
        c1 = [sb("c1_%d" % i, [128, 512]) for i in range(2)]
        c2 = [sb("c2_%d" % i, [128, 512]) for i in range(2)]
        c3 = [sb("c3_%d" % i, [128, 512]) for i in range(2)]
        sgt = [sb("sgt%d" % i, [128, 512]) for i in range(2)]
        convw = sb("convw", [128, 44, 3])
        convb = sb("convb", [128, 44])
        carryF = sb("carryF", [128, 44, 2])

        for kc in range(8):
            k.dma(T.tmp[:], I["w_out"][l, kc * 128:(kc + 1) * 128, :], w=["tmp"])
            k.copy(("dve", "pool", "act")[kc % 3], w_out[:, kc, :], T.tmp[:], r=["tmp"], w=["w_out"])
        for t in range(3):
            k.dma(convw[:, :, t], I["ffn_conv"][l, t].rearrange("(cc p) -> p cc", p=128), w=["convw"], slow=True)
        k.dma(convb[:], I["ffn_conv_b"][l].rearrange("(cc p) -> p cc", p=128), w=["convb"], slow=True)
        k.memset("pool", carryF[:], 0.0, w=["carryF"])

        for g in range(NG):
            tsl = slice(g * TG, (g + 1) * TG)
            k.dma(xt[:], xsrc[tsl, :].rearrange("(j p) d -> p j d", p=128), r=[("x", g)], w=["xt"])
            k.dma(dnw[:].rearrange("p a d -> p (a d)"), R.dnbf[l], r=[("ffnw", l)], w=["dnw"])
            if G.y_in:
                for j in range(4):
                    k.dma(T.tmp[:, 0:768], R.yscr[g * TG + j * 128: g * TG + (j + 1) * 128, :], w=["tmp"])
                    k.copy("pool", T.hb[:, j, 0:768], T.tmp[:, 0:768], r=["tmp"], w=["hb"])
                for kc in range(2):
                    k.dma(c1[kc][:], R.yTA[kc * 128:(kc + 1) * 128, tsl], w=["c1_%d" % kc])
                    k.copy("act", yT[:, kc, :], c1[kc][:], r=["c1_%d" % kc], w=["hT"])
            else:
                k.dma(T.hb[:, :, 0:768], R.yscr[tsl, :].rearrange("(j p) c -> p j c", p=128),
                      r=[("yC", g), ("yB", g), ("yD", g)], w=["hb"])
                k.dma(yT[:, 0:2, :], R.yTA[:, tsl].rearrange("(kc p) t -> p kc t", p=128), r=[("yA", g)], w=["hT"])
            for j in range(4):
                b = 5 + (j % 2)
                pv = banks[b][:].bitcast(BF16).rearrange("p (a t) -> p a t", a=8)
                for kc in range(6):
                    k.tr(pv[:, kc, :], T.hb[:, j, kc * 128:(kc + 1) * 128], G.identb[:], r=["hb", "identb"], w=["bank%d" % b])
                k.copy("act" if j % 2 else "dve", yT[:, 2:8, j * 128:(j + 1) * 128], pv[:, 0:6, :], r=["bank%d" % b], w=["hT"])
            for j in range(4):
                for hf in range(2):
                    bk = k.rot("p3bank", 2)
                    for kc in range(8):
                        k.mm(banks[bk][:], yT[:, kc, j * 128:(j + 1) * 128], w_out[:, kc, hf * 512:(hf + 1) * 512],
                             start=(kc == 0), stop=(kc == 7), r=["hT", "w_out"], w=["bank%d" % bk])
                    ci = k.rot("c1", 2)
                    k.tt("dve", c1[ci][:], banks[bk][:], G.modB[:, 2 * D + hf * 512: 2 * D + (hf + 1) * 512], ALU.mult,
                         r=["bank%d" % bk, "modB"], w=["c1_%d" % ci])
                    k.tt("dve", xt[:, j, hf * 512:(hf + 1) * 512], xt[:, j, hf * 512:(hf + 1) * 512], c1[ci][:], ALU.add,
                         r=["xt", "c1_%d" % ci], w=["xt"])
            if G.dbg_x1 is not None:
                k.dma(G.dbg_x1[tsl, :].rearrange("(j p) d -> p j d", p=128), xt[:], r=["xt"], w=[("dbgx1", g)])
            norm_mod_T(G, xt, 4 * D, 3 * D, T)
            for u in range(11):
                wi = u % 2
                k.dma(upw[wi][:].rearrange("p a n -> p (a n)"), R.upbf[l, u], r=[("ffnw", l)], w=["upw%d" % wi])
                for sc in range(4):
                    cc = 4 * u + sc
                    bk = 2 + k.rot("p3ubank", 3)
                    for kc in range(8):
                        k.mm(banks[bk][:], upw[wi][:, kc, sc * 128:(sc + 1) * 128], T.hT[:, kc, :], start=(kc == 0), stop=(kc == 7),
                             r=["upw%d" % wi, "hT"], w=["bank%d" % bk])
                    ui = k.rot("ub", 2)
                    k.copy("pool", ub[ui][:, 0:2], carryF[:, cc, :], r=["carryF"], w=["ub%d" % ui])
                    k.copy("act", ub[ui][:, 2:514], banks[bk][:], r=["bank%d" % bk], w=["ub%d" % ui])
                    k.copy("pool", carryF[:, cc, :], ub[ui][:, 512:514], r=["ub%d" % ui], w=["carryF"])
                    k.act(c1[ui][:], ub[ui][:, 2:514], AF.Identity, bias=convb[:, cc:cc + 1], scale=convw[:, cc, 2:3],
                          r=["ub%d" % ui, "convw", "convb"], w=["c1_%d" % ui])
                    k.stt("dve", c2[ui][:], ub[ui][:, 1:513], convw[:, cc, 1:2], c1[ui][:], ALU.mult, ALU.add,
                          r=["ub%d" % ui, "convw", "c1_%d" % ui], w=["c2_%d" % ui])
                    if cc < 22:
                        k.stt("dve", actT[:, cc, :], ub[ui][:, 0:512], convw[:, cc, 0:1], c2[ui][:], ALU.mult, ALU.add,
                              r=["ub%d" % ui, "convw", "c2_%d" % ui], w=["actT%d" % cc])
                    else:
                        cu = cc - 22
                        k.stt("dve", c3[ui][:], ub[ui][:, 0:512], convw[:, cc, 0:1], c2[ui][:], ALU.mult, ALU.add,
                              r=["ub%d" % ui, "convw", "c2_%d" % ui], w=["c3_%d" % ui])
                        k.act(sgt[ui][:], c3[ui][:], AF.Silu, r=["c3_%d" % ui], w=["sgt%d" % ui])
                        k.tt("pool", actT[:, cu, :], actT[:, cu, :], sgt[ui][:], ALU.mult, r=["actT%d" % cu, "sgt%d" % ui],
                             w=["actT%d" % cu])
            aks = ["actT%d" % i for i in range(22)]
            for j in range(4):
                for hf in range(2):
                    bk = k.rot("p3bank", 2)
                    for cu in range(22):
                        k.mm(banks[bk][:], actT[:, cu, j * 128:(j + 1) * 128], dnw[:, cu, hf * 512:(hf + 1) * 512],
                             start=(cu == 0), stop=(cu == 21), r=["actT%d" % cu, "dnw"], w=["bank%d" % bk])
                    ci = k.rot("c1", 2)
                    k.tt("dve", c1[ci][:], banks[bk][:], G.modB[:, 5 * D + hf * 512: 5 * D + (hf + 1) * 512], ALU.mult,
                         r=["bank%d" % bk, "modB"], w=["c1_%d" % ci])
                    k.tt("dve", xt[:, j, hf * 512:(hf + 1) * 512], xt[:, j, hf * 512:(hf + 1) * 512], c1[ci][:], ALU.add,
                         r=["xt", "c1_%d" % ci], w=["xt"])
            k.dma(xdst[tsl, :].rearrange("(j p) d -> p j d", p=128), xt[:], r=["xt"], w=[("x", g)])
        P.barrier()


def phase_A(G, l):
    nc, P, k, I, R = G.nc, G.P, G.k, G.I, G.R
    banks = G.banks
    with ExitStack() as s:
        sb = lambda name, shape, dt=F32: G.sb(s, name, shape, dt)
        prm = sb("prm", [64, 7, 4])
        wup = sb("wup", [32, 256])
        aup = sb("aup", [32, 256])
        gup = sb("gup", [64, 256])
        ones64 = sb("ones64", [64, 64])
        ones512 = sb("ones512", [64, 512])
        m64 = sb("m64", [64, 3, 64])
        Tst = sb("Tst", [64, 4, 64])
        big = lambda nm: sb(nm, [64, 4, 512])
        r_, k_, v_ = big("r_"), big("k_"), big("v_")
        lwt, at, gt, kkn, k2, b_, bonus, Yblk, t1, t2 = (big("lwt"), big("at"), big("gt"), big("kkn"), big("k2"), big("b_"),
                                                         big("bonus"), big("Yblk"), big("t1"), big("t2"))
        Gblk = sb("Gblk", [64, 4, 513])
        wd = sb("wd", [32, 512])
        ad = sb("ad", [32, 512])
        gd = sb("gd", [64, 512])
        yo = sb("yo", [64, 4, 512], BF16)
        sm = lambda nm: sb(nm, [64, 4, 64])
        Pc, Pp, eG, eGn, eGp = sm("Pc"), sm("Pp"), sm("eG"), sm("eGn"), sm("eGp")
        At, Bt, Kt, Rt = sm("At"), sm("Bt"), sm("Kt"), sm("Rt")
        Btm, Ktm, Vtm = sm("Btm"), sm("Ktm"), sm("Vtm")
        PT = [sm("PT0"), sm("PT1")]
        Pj = [sm("Pj0"), sm("Pj1")]
        LakT, MrbT, MrkT, U, tS = sm("LakT"), sm("MrbT"), sm("MrkT"), sm("U"), sm("tS")

        for n, nm in enumerate(("rw_w0", "rw_a0", "rw_k_k", "rw_k_a")):
            k.dma(prm[:, n, :], I[nm][l].rearrange("(h k) -> k h", k=64), w=["prm"], slow=True)
        k.dma(prm[:, 4, :], I["rw_r_k"][l].rearrange("h k -> k h"), w=["prm"], slow=True)
        k.dma(prm[:, 5, :], I["rw_ln_g"][l].rearrange("(h k) -> k h", k=64), w=["prm"], slow=True)
        k.dma(prm[:, 6, :], I["rw_ln_b"][l].rearrange("(h k) -> k h", k=64), w=["prm"], slow=True)
        k.dma(wup[:], I["rw_w_up"][l], w=["wup"])
        k.dma(aup[:], I["rw_a_up"][l], w=["aup"])
        k.dma(gup[:], I["rw_g_up"][l], w=["gup"])
        k.dma(m64[:], I["m64"], w=["m64"])
        k.memset("dve", ones64[:], 1.0, w=["ones64"])
        k.memset("dve", ones512[:], 1.0, w=["ones512"])
        k.memset("pool", Tst[:], 0.0, w=["Tst"])
        k.memset("pool", Gblk[:], 0.0, w=["Gblk"])

        def bc(col):
            return prm[:, col, :].unsqueeze(2).to_broadcast([64, 4, 512])

        def HB(b, half):
            return banks[b][0:64, half * 256:(half + 1) * 256].rearrange("p (h t) -> p h t", h=4)

        def hk(b, half):
            return ("hb", b)

        def fb(b):
            return [("hb", b)]
        allpm = [("pmA", g) for g in range(NG)]

        for g in range(NG):
            tsl = slice(g * TG, (g + 1) * TG)
            k.dma(r_[:], R.pmA[0:256, tsl].rearrange("(h k) t -> k h t", k=64), r=allpm, w=["r_"])
            k.dma(k_[:], R.pmA[256:512, tsl].rearrange("(h k) t -> k h t", k=64), r=allpm, w=["k_"])
            k.dma(v_[:], R.pmA[512:768, tsl].rearrange("(h k) t -> k h t", k=64), r=allpm, w=["v_"])
            k.dma(wd[:], R.pmA[768:800, tsl], r=allpm, w=["wd"])
            k.dma(ad[:], R.pmA[800:832, tsl], r=allpm, w=["ad"])
            k.dma(gd[:], R.pmA[832:896, tsl], r=allpm, w=["gd"])
            k.act(wd[:], wd[:], AF.Tanh, r=["wd"], w=["wd"])
            k.act(gd[:], gd[:], AF.Sigmoid, r=["gd"], w=["gd"])
            for h in range(4):
                k.mm(banks[h][0:64, :], wup[:, h * 64:(h + 1) * 64], wd[:], r=["wup", "wd"], w=fb(h))
                k.act(lwt[:, h, :], banks[h][0:64, :], AF.Sigmoid, bias=prm[:, 0, h:h + 1], scale=1.0, r=fb(h) + ["prm"], w=["lwt"])
            for h in range(4):
                k.mm(banks[4 + h][0:64, :], aup[:, h * 64:(h + 1) * 64], ad[:], r=["aup", "ad"], w=fb(4 + h))
                k.act(at[:, h, :], banks[4 + h][0:64, :], AF.Sigmoid, bias=prm[:, 1, h:h + 1], scale=1.0, r=fb(4 + h) + ["prm"], w=["at"])
            for h in range(4):
                k.mm(banks[h][0:64, :], gup[:, h * 64:(h + 1) * 64], gd[:], r=["gup", "gd"], w=fb(h))
                k.copy("dve" if h % 2 else "act", gt[:, h, :], banks[h][0:64, :], r=fb(h), w=["gt"])
            k.tt("dve", kkn[:], k_[:], bc(2), ALU.mult, r=["k_", "prm"], w=["kkn"])
            k.tt("pool", t1[:], kkn[:], kkn[:], ALU.mult, r=["kkn"], w=["t1"])
            for h in range(4):
                k.mm(banks[4 + h][0:64, :], ones64[:], t1[:, h, :], r=["ones64", "t1"], w=fb(4 + h))
            for h in range(4):
                k.act(t2[:, h, :], banks[4 + h][0:64, :], AF.Sqrt, r=fb(4 + h), w=["t2"])
            k.ts("dve", t2[:], t2[:], 1e-12, ALU.max, r=["t2"], w=["t2"])
            k.recip(t2[:], t2[:], r=["t2"], w=["t2"])
            k.tt("pool", kkn[:], kkn[:], t2[:], ALU.mult, r=["kkn", "t2"], w=["kkn"])
            k.stt("dve", t1[:], at[:], -1.0, bc(3), ALU.add, ALU.mult, r=["at", "prm", "t1"], w=["t1"])
            k.tt("pool", t1[:], t1[:], k_[:], ALU.mult, r=["t1", "k_"], w=["t1"])
            k.tt("pool", k2[:], t1[:], k_[:], ALU.add, r=["t1", "k_"], w=["k2"])
            k.tt("pool", b_[:], kkn[:], at[:], ALU.mult, r=["kkn", "at"], w=["b_"])
            k.tt("pool", t1[:], r_[:], k2[:], ALU.mult, r=["r_", "k2", "t1"], w=["t1"])
            k.tt("dve", t1[:], t1[:], bc(4), ALU.mult, r=["t1", "prm"], w=["t1"])
            for h in range(4):
                k.mm(banks[h][0:64, :], ones64[:], t1[:, h, :], r=["ones64", "t1"], w=fb(h))
                k.tt("dve", bonus[:, h, :], banks[h][0:64, :], v_[:, h, :], ALU.mult, r=fb(h) + ["v_"], w=["bonus"])
            for h in range(4):
                k.scan(Gblk[:, h, 1:513], ones512[:], lwt[:, h, :], 0.0, ALU.mult, ALU.add, r=["ones512", "lwt"], w=["Gblk"])

            if int(os.environ.get("A_STOP", "9")) <= 1:
                continue
            for ci in range(8 if int(os.environ.get("A_STOP", "9")) > 2 else 0):
                c0 = ci * 64
                ts_ = slice(c0, c0 + 64)
                k.tt("dve", Pc[:], Gblk[:, :, 1 + c0:1 + c0 + 64], Gblk[:, :, c0:c0 + 1].to_broadcast([64, 4, 64]), ALU.subtract,
                     r=["Gblk"], w=["Pc"])
                k.tt("pool", Pp[:], Pc[:], lwt[:, :, ts_], ALU.subtract, r=["Pc", "lwt"], w=["Pp"])
                k.act(eG[:], Pc[:], AF.Exp, scale=-ALPHA, r=["Pc"], w=["eG"])
                k.act(eGn[:], Pc[:], AF.Exp, scale=ALPHA, r=["Pc"], w=["eGn"])
                k.act(eGp[:], Pp[:], AF.Exp, scale=-ALPHA, r=["Pp"], w=["eGp"])
                k.stt("dve", At[:], kkn[:, :, ts_], -1.0, eGp[:], ALU.mult, ALU.mult, r=["kkn", "eGp"], w=["At"])
                k.tt("pool", Bt[:], b_[:, :, ts_], eGn[:], ALU.mult, r=["b_", "eGn"], w=["Bt"])
                k.tt("pool", Kt[:], k2[:, :, ts_], eGn[:], ALU.mult, r=["k2", "eGn"], w=["Kt"])
                k.tt("dve", Rt[:], r_[:, :, ts_], eG[:], ALU.mult, r=["r_", "eG"], w=["Rt"])
                CH = int(os.environ.get("A_CH", "99"))
                if CH < 2:
                    continue
                id64 = G.ident[0:64, 0:64]
                trs = ((Bt, "Bt", Btm, "Btm", 2, 1, "act"), (Kt, "Kt", Ktm, "Ktm", 3, 0, "dve"), (None, "v_", Vtm, "Vtm", 3, 1, "act"))
                if "A_TR" in os.environ:
                    trs = tuple(trs[int(c)] for c in os.environ["A_TR"])
                for (X, xk, Xtm, xtk, bq, hf, ce) in trs:
                    for h in range(4):
                        src = v_[:, h, ts_] if X is None else X[:, h, :]
                        k.mm(HB(bq, hf)[:, h, :], src, id64, r=[xk, "ident"], w=[hk(bq, hf)])
                for (X, xk, Xtm, xtk, bq, hf, ce) in trs:
                    k.copy(ce, Xtm[:], HB(bq, hf), r=[hk(bq, hf)], w=[xtk])
                if CH < 3:
                    continue
                specs = ((Bt, "Bt", At, "At", 0, 0, PT[0], "PT0", 0), (At, "At", Bt, "Bt", 0, 1, Pj[0], "Pj0", 2),
                         (Kt, "Kt", At, "At", 1, 0, LakT, "LakT", 0), (Bt, "Bt", Rt, "Rt", 1, 1, MrbT, "MrbT", 1),
                         (Kt, "Kt", Rt, "Rt", 2, 0, MrkT, "MrkT", 1))
                for (La, lk, Ra, rk_, bq, hf, dst, dk, mi) in specs:
                    for h in range(4):
                        k.mm(HB(bq, hf)[:, h, :], La[:, h, :], Ra[:, h, :], r=[lk, rk_], w=[hk(bq, hf)])
                for (La, lk, Ra, rk_, bq, hf, dst, dk, mi) in specs:
                    k.tt("dve", dst[:], HB(bq, hf), m64[:, mi, :].unsqueeze(1).to_broadcast([64, 4, 64]), ALU.mult,
                         r=[hk(bq, hf), "m64"], w=[dk])
                if CH < 4:
                    continue
                for h in range(4):
                    k.mm(HB(4, 0)[:, h, :], At[:, h, :], Tst[:, h, :], start=True, stop=False, r=["At", "Tst"], w=[hk(4, 0)])
                    k.mm(HB(4, 0)[:, h, :], LakT[:, h, :], Vtm[:, h, :], start=False, stop=True, r=["LakT", "Vtm"], w=[hk(4, 0)])
                k.copy("act", U[:], HB(4, 0), r=[hk(4, 0)], w=["U"])
                if CH < 5:
                    continue
                for j in range(6):
                    cur, nxt = j % 2, (j + 1) % 2
                    for h in range(4):
                        k.mm(HB(4, 1)[:, h, :], PT[cur][:, h, :], U[:, h, :], r=["PT%d" % cur, "U"], w=[hk(4, 1)])
                    k.tt("dve", U[:], U[:], HB(4, 1), ALU.add, r=["U", hk(4, 1)], w=["U"])
                    if j < 5:
                        for h in range(4):
                            k.mm(HB(5, 0)[:, h, :], PT[cur][:, h, :], Pj[cur][:, h, :], r=["PT%d" % cur, "Pj%d" % cur], w=[hk(5, 0)])
                        for h in range(4):
                            k.mm(HB(5, 1)[:, h, :], Pj[cur][:, h, :], PT[cur][:, h, :], r=["PT%d" % cur, "Pj%d" % cur], w=[hk(5, 1)])
                        k.copy("act", Pj[nxt][:], HB(5, 0), r=[hk(5, 0)], w=["Pj%d" % nxt])
                        k.copy("act", PT[nxt][:], HB(5, 1), r=[hk(5, 1)], w=["PT%d" % nxt])
                if CH < 6:
                    continue
                for h in range(4):
                    k.mm(HB(6, 0)[:, h, :], Tst[:, h, :], Rt[:, h, :], start=True, stop=False, r=["Tst", "Rt"], w=[hk(6, 0)])
                    k.mm(HB(6, 0)[:, h, :], U[:, h, :], MrbT[:, h, :], start=False, stop=False, r=["U", "MrbT"], w=[hk(6, 0)])
                    k.mm(HB(6, 0)[:, h, :], Vtm[:, h, :], MrkT[:, h, :], start=False, stop=True, r=["Vtm", "MrkT"], w=[hk(6, 0)])
                k.copy("act", Yblk[:, :, ts_], HB(6, 0), r=[hk(6, 0)], w=["Yblk"])
                if CH < 7:
                    continue
                for h in range(4):
                    k.mm(HB(7, 0)[:, h, :], Btm[:, h, :], U[:, h, :], start=True, stop=False, r=["Btm", "U"], w=[hk(7, 0)])
                    k.mm(HB(7, 0)[:, h, :], Ktm[:, h, :], Vtm[:, h, :], start=False, stop=True, r=["Ktm", "Vtm"], w=[hk(7, 0)])
                k.tt("dve", tS[:], HB(7, 0), Tst[:], ALU.add, r=[hk(7, 0), "Tst"], w=["tS"])
                k.tt("pool", Tst[:], tS[:], eG[:, :, 63:64].to_broadcast([64, 4, 64]), ALU.mult, r=["tS", "eG"], w=["Tst"])

            if int(os.environ.get("A_STOP", "9")) <= 3:
                continue
            for h in range(4):
                k.mm(banks[h][0:64, :], ones64[:], Yblk[:, h, :], r=["ones64", "Yblk"], w=fb(h))
                k.stt("dve", t1[:, h, :], banks[h][0:64, :], -1.0 / 64, Yblk[:, h, :], ALU.mult, ALU.add, r=fb(h) + ["Yblk"], w=["t1"])
            k.tt("pool", t2[:], t1[:], t1[:], ALU.mult, r=["t1"], w=["t2"])
            for h in range(4):
                k.mm(banks[4 + h][0:64, :], ones64[:], t2[:, h, :], r=["ones64", "t2"], w=fb(4 + h))
            for h in range(4):
                k.act(t2[:, h, :], banks[4 + h][0:64, :], AF.Sqrt, bias=64e-5, scale=1.0 / 64, r=fb(4 + h), w=["t2"])
            k.recip(t2[:], t2[:], r=["t2"], w=["t2"])
            k.tt("pool", t1[:], t1[:], t2[:], ALU.mult, r=["t1", "t2"], w=["t1"])
            k.tt("dve", t1[:], t1[:], bc(5), ALU.mult, r=["t1", "prm"], w=["t1"])
            k.tt("pool", t1[:], t1[:], bc(6), ALU.add, r=["t1", "prm"], w=["t1"])
            k.tt("pool", t1[:], t1[:], bonus[:], ALU.add, r=["t1", "bonus"], w=["t1"])
            k.tt("dve", yo[:], t1[:], gt[:], ALU.mult, r=["t1", "gt"], w=["yo"])
            if not G.y_in:
                k.dma(R.yTA[:, tsl].rearrange("(h v) t -> v h t", v=64), yo[:], r=["yo"], w=[("yA", g)])
        P.barrier()


def attn_common(G, s, Vsrc, mix):
    k, R, I = G.k, G.R, G.I
    sb = lambda name, shape, dt=F32: G.sb(s, name, shape, dt)
    A = Ctx()
    A.kT = [sb("kT%d" % i, [64, S], BF16) for i in range(2)]
    A.V = sb("V", [128, 32, 260], BF16)
    k.dma(A.V[:], Vsrc.rearrange("(kt p) c -> p kt c", p=128), r=[(mix + "v", g) for g in range(NG)], w=["V"])
    A.Pm = [sb("Pm%d" % i, [128, 512], BF16) for i in range(3)]
    A.rec = [sb("rec%d" % i, [128, 8]) for i in range(2)]
    A.ob = [sb("ob%d" % i, [128, 4, 64], BF16) for i in range(2)]
    return A


def phase_D(G, l):
    nc, P, k, I, R = G.nc, G.P, G.k, G.I, G.R
    banks = G.banks
    with ExitStack() as s:
        sb = lambda name, shape, dt=F32: G.sb(s, name, shape, dt)
        A = attn_common(G, s, R.vD, "D")
        Fk = sb("Fk", [128, 4, 32])
        Frow = sb("Frow", [4, S])
        sel = sb("sel", [4, 4, 128])
        negm = sb("negm", [128, 128])
        qT = [sb("qT%d" % i, [64, 512], BF16) for i in range(2)]
        FqB = [sb("FqB%d" % i, [128, 512]) for i in range(2)]
        FqD = [sb("FqD%d" % i, [128, 512]) for i in range(2)]
        tb = [sb("tb%d" % i, [128, 512]) for i in range(3)]
        allF = [("Ffm", g) for g in range(NG)]
        for h in range(4):
            k.dma(Fk[:, h, :], R.Ffm[h].rearrange("(kt p) -> p kt", p=128), r=allF, w=["Fk"], slow=True)
        k.dma(Frow[:], R.Ffm, r=allF, w=["Frow"])
        k.dma(sel[:], I["sel"], w=["sel"])
        k.dma(negm[:], I["negmask"], w=["negm"])
        allqk = [("Dqk", g) for g in range(NG)]
        for h in range(4):
            kt_ = A.kT[h % 2]
            kk = "kT%d" % (h % 2)
            k.dma(kt_[:], R.kTD[h * 64:(h + 1) * 64, :], r=allqk, w=[kk])
            for g in range(NG):
                i = k.rot("Dq", 2)
                k.dma(qT[i][:], R.qTD[h * 64:(h + 1) * 64, g * TG:(g + 1) * TG], r=allqk, w=["qT%d" % i])
                k.mm(banks[5][:], sel[:, h, :], Frow[:, g * TG:(g + 1) * TG], r=["sel", "Frow"], w=["bank5"])
                k.copy("act", FqB[i][:], banks[5][:], r=["bank5"], w=["FqB%d" % i])
                k.tt("pool", FqD[i][:].rearrange("p (a b) -> p a b", a=4), FqB[i][:].rearrange("p (a b) -> p a b", a=4),
                     negm[:].unsqueeze(1).to_broadcast([128, 4, 128]), ALU.add, r=["FqB%d" % i, "negm"], w=["FqD%d" % i])
                ob_ = 3 + k.rot("DO", 2)
                O = banks[ob_][:, 0:260].rearrange("p (a e) -> p a e", a=4)
                def d_stage1(kt):
                    m = kt - 4 * g
                    c0 = max(m, 0) * 128
                    N = 512 - c0
                    sbk = k.rot("Ds", 3)
                    k.mm(banks[sbk][:, 0:N], kt_[:, kt * 128:(kt + 1) * 128], qT[i][:, c0:512], r=[kk, "qT%d" % i], w=["bank%d" % sbk])
                    ti = k.rot("Dt", 3)
                    if m < 0:
                        k.stt("dve", tb[ti][:], banks[sbk][:], Fk[:, h, kt:kt + 1], FqB[i][:], ALU.subtract, ALU.add,
                              r=["bank%d" % sbk, "Fk", "FqB%d" % i], w=["tb%d" % ti])
                    else:
                        k.stt("dve", tb[ti][:, 0:128], banks[sbk][:, 0:128], Fk[:, h, kt:kt + 1], FqD[i][:, c0:c0 + 128],
                              ALU.subtract, ALU.add, r=["bank%d" % sbk, "Fk", "FqD%d" % i], w=["tb%d" % ti])
                        if N > 128:
                            k.stt("dve", tb[ti][:, 128:N], banks[sbk][:, 128:N], Fk[:, h, kt:kt + 1], FqB[i][:, c0 + 128:512],
                                  ALU.subtract, ALU.add, r=["bank%d" % sbk, "Fk", "FqB%d" % i], w=["tb%d" % ti])
                    k.act(A.Pm[ti][:, 0:N], tb[ti][:, 0:N], AF.Exp, r=["tb%d" % ti], w=["Pm%d" % ti])
                    return (kt, m, c0, ti)

                def d_stage2(st):
                    kt, m, c0, ti = st
                    for jq in range(max(m, 0), 4):
                        k.mm(O[:, jq, :], A.Pm[ti][:, jq * 128 - c0: jq * 128 - c0 + 128], A.V[:, kt, h * 65:(h + 1) * 65],
                             start=(kt == 0 and jq == 0), stop=(kt == 4 * g + jq), r=["Pm%d" % ti, "V"], w=["bank%d" % ob_], sgc=True)
                pend = []
                for kt in range(4 * g + 4):
                    pend.append(d_stage1(kt))
                    if len(pend) > 2:
                        d_stage2(pend.pop(0))
                while pend:
                    d_stage2(pend.pop(0))
                ri = k.rot("Drec", 2)
                k.recip(A.rec[ri][:, 0:4], O[:, :, 64], r=["bank%d" % ob_], w=["rec%d" % ri])
                k.tt("dve", A.ob[ri][:], O[:, :, 0:64], A.rec[ri][:, 0:4].unsqueeze(2).to_broadcast([128, 4, 64]), ALU.mult,
                     r=["bank%d" % ob_, "rec%d" % ri], w=["ob%d" % ri])
                k.dma(R.yscr[g * TG:(g + 1) * TG, 512 + h * 64: 512 + (h + 1) * 64].rearrange("(a p) e -> p a e", p=128),
                      A.ob[ri][:], r=["ob%d" % ri], w=[("yD", g)], slow=True)
        P.barrier()


def phase_B(G, l):
    nc, P, k, I, R = G.nc, G.P, G.k, G.I, G.R
    banks = G.banks
    lambda_init = 0.8 - 0.6 * math.exp(-0.3 * l)
    with ExitStack() as s:
        sb = lambda name, shape, dt=F32: G.sb(s, name, shape, dt)
        A = attn_common(G, s, R.vB, "B")
        cmf = sb("cmf", [128, 128])
        cm = sb("cm", [128, 128], BF16)
        qp = [[sb("qp%d_%d" % (i, mp), [64, 512], BF16) for mp in range(2)] for i in range(2)]
        lamv = sb("lamv", [128, 4, 32])
        lamw = sb("lamw", [128, 2, 32])
        lams = sb("lams", [128, 4])
        subg = sb("subg", [128, 64])
        o1 = [sb("o1_%d" % i, [128, 4, 64]) for i in range(2)]
        o2 = [sb("o2_%d" % i, [128, 4, 64]) for i in range(2)]
        k.dma(cmf[:], I["cmask"], w=["cmf"])
        k.copy("dve", cm[:], cmf[:], r=["cmf"], w=["cm"])
        for i in range(2):
            for mp in range(2):
                k.memset("pool", qp[i][mp][:], 0.0, w=["qp%d" % i])
        for n, nm in enumerate(("df_lam_q1", "df_lam_k1", "df_lam_q2", "df_lam_k2")):
            bcast_load(G, lamv[:, n, :], I[nm][l:l + 1, :], 32, ["lamv"])
        bcast_load(G, subg[:], I["df_sub_g"][l:l + 1, :], 64, ["subg"])
        k.ts("dve", subg[:], subg[:], 1.0 - lambda_init, ALU.mult, r=["subg"], w=["subg"])
        k.tt("dve", lamw[:, 0, :], lamv[:, 0, :], lamv[:, 1, :], ALU.mult, r=["lamv"], w=["lamw"])
        k.tt("dve", lamw[:, 1, :], lamv[:, 2, :], lamv[:, 3, :], ALU.mult, r=["lamv"], w=["lamw"])
        k.red("dve", lams[:, 0:2], lamw[:], r=["lamw"], w=["lams"])
        k.act(lams[:, 0:2], lams[:, 0:2], AF.Exp, r=["lams"], w=["lams"])
        k.ts("dve", lams[:, 2:3], lams[:, 1:2], -lambda_init, ALU.add, r=["lams"], w=["lams"])
        k.tt("dve", lams[:, 3:4], lams[:, 2:3], lams[:, 0:1], ALU.subtract, r=["lams"], w=["lams"])
        allqk = [("Bqk", g) for g in range(NG)]
        for h in range(4):
            kt_ = A.kT[h % 2]
            kk = "kT%d" % (h % 2)
            k.dma(kt_[:], R.kTB[h * 64:(h + 1) * 64, :], r=allqk, w=[kk])
            for g in range(NG):
                i = k.rot("Bq", 2)
                k.dma(qp[i][0][0:32, :], R.qTB[h * 64:h * 64 + 32, g * TG:(g + 1) * TG], r=allqk, w=["qp%d" % i])
                k.dma(qp[i][1][32:64, :], R.qTB[h * 64 + 32:h * 64 + 64, g * TG:(g + 1) * TG], r=allqk, w=["qp%d" % i])
                oi = k.rot("BO", 2)
                Os = [banks[3 + oi][:, 0:260].rearrange("p (a e) -> p a e", a=4),
                      banks[5 + oi][:, 0:260].rearrange("p (a e) -> p a e", a=4)]
                obk = ["bank%d" % (3 + oi), "bank%d" % (5 + oi)]
                def b_stage1(kt, mp):
                    m = kt - 4 * g
                    c0 = max(m, 0) * 128
                    N = 512 - c0
                    sbk = k.rot("Bs", 3)
                    k.mm(banks[sbk][:, 0:N], kt_[:, kt * 128:(kt + 1) * 128], qp[i][mp][:, c0:512], r=[kk, "qp%d" % i],
                         w=["bank%d" % sbk])
                    ti = k.rot("Bt", 3)
                    k.act(A.Pm[ti][:, 0:N], banks[sbk][:, 0:N], AF.Exp, r=["bank%d" % sbk], w=["Pm%d" % ti])
                    if m >= 0:
                        k.tt("dve", A.Pm[ti][:, 0:128], A.Pm[ti][:, 0:128], cm[:], ALU.mult, r=["Pm%d" % ti, "cm"], w=["Pm%d" % ti])
                    return (kt, mp, m, c0, ti)

                def b_stage2(st):
                    kt, mp, m, c0, ti = st
                    for jq in range(max(m, 0), 4):
                        k.mm(Os[mp][:, jq, :], A.Pm[ti][:, jq * 128 - c0: jq * 128 - c0 + 128], A.V[:, kt, h * 65:(h + 1) * 65],
                             start=(kt == 0 and jq == 0), stop=(kt == 4 * g + jq), r=["Pm%d" % ti, "V"], w=[obk[mp]], sgc=True)
                pend = []
                for kt in range(4 * g + 4):
                    for mp in range(2):
                        pend.append(b_stage1(kt, mp))
                        if len(pend) > 2:
                            b_stage2(pend.pop(0))
                while pend:
                    b_stage2(pend.pop(0))
                ri = k.rot("Brec", 2)
                rc = A.rec[ri]
                rk = "rec%d" % ri
                k.recip(rc[:, 0:4], Os[0][:, :, 64], r=[obk[0]], w=[rk])
                k.recip(rc[:, 4:8], Os[1][:, :, 64], r=[obk[1]], w=[rk])
                k.ts("dve", rc[:, 4:8], rc[:, 4:8], lams[:, 3:4], ALU.mult, r=[rk, "lams"], w=[rk])
                k.tt("dve", o1[ri][:], Os[0][:, :, 0:64], rc[:, 0:4].unsqueeze(2).to_broadcast([128, 4, 64]), ALU.mult,
                     r=[obk[0], rk], w=["o1_%d" % ri])
                k.tt("dve", o2[ri][:], Os[1][:, :, 0:64], rc[:, 4:8].unsqueeze(2).to_broadcast([128, 4, 64]), ALU.mult,
                     r=[obk[1], rk], w=["o2_%d" % ri])
                k.tt("pool", o1[ri][:], o1[ri][:], o2[ri][:], ALU.add, r=["o1_%d" % ri, "o2_%d" % ri], w=["o1_%d" % ri])
                k.tt("pool", o2[ri][:], o1[ri][:], o1[ri][:], ALU.mult, r=["o1_%d" % ri], w=["o2_%d" % ri])
                k.red("dve", rc[:, 0:4], o2[ri][:], r=["o2_%d" % ri], w=[rk])
                k.act(rc[:, 0:4], rc[:, 0:4], AF.Sqrt, bias=EPS, scale=1.0 / 64, r=[rk], w=[rk])
                k.recip(rc[:, 0:4], rc[:, 0:4], r=[rk], w=[rk])
                k.tt("dve", o1[ri][:], o1[ri][:], rc[:, 0:4].unsqueeze(2).to_broadcast([128, 4, 64]), ALU.mult,
                     r=["o1_%d" % ri, rk], w=["o1_%d" % ri])
                k.tt("pool", A.ob[ri][:], o1[ri][:], subg[:].unsqueeze(1).to_broadcast([128, 4, 64]), ALU.mult,
                     r=["o1_%d" % ri, "subg"], w=["ob%d" % ri])
                k.dma(R.yscr[g * TG:(g + 1) * TG, h * 64:(h + 1) * 64].rearrange("(a p) e -> p a e", p=128),
                      A.ob[ri][:], r=["ob%d" % ri], w=[("yB", g)], slow=True)
        P.barrier()


_CACHE = {}


def kernel(**inputs):
    if "prog" not in _CACHE:
        _CACHE["prog"] = build()
    nc, P = _CACHE["prog"]
    consts = make_consts()
    weights = {k: np.ascontiguousarray(np.asarray(inputs[k], dtype=np.float32)) for k in WEIGHT_SHAPES}
    x = np.asarray(inputs["x"], dtype=np.float32)
    c = np.asarray(inputs["c"], dtype=np.float32)
    in_maps = []
    for b in range(8):
        m = {"x": np.ascontiguousarray(x[b]), "c": np.ascontiguousarray(c[b:b + 1])}
        m.update(weights)
        m.update(consts)
        in_maps.append(m)
    res = run_bass_kernel_spmd(nc, in_maps, core_ids=list(range(8)))
    return np.stack([np.asarray(r["out"], dtype=np.float32) for r in res.results], axis=0)
```

```python
import math
import os
import numpy as np
import concourse.bass as bass
import concourse.mybir as mybir
from concourse.bass_utils import run_bass_kernel_spmd
from contextlib import ExitStack

F32 = mybir.dt.float32
BF16 = mybir.dt.bfloat16
AF = mybir.ActivationFunctionType
ALU = mybir.AluOpType
AX = mybir.AxisListType

S = 4096
D = 1024
L = 4
NIN = 2948
DFF = 2816
NG = 8
TG = 512
EPS = 1e-6
ALPHA = math.exp(-0.5)

ENGS = ("pe", "act", "dve", "pool", "sp")
EPOCH = 30000


class Prog:
    def __init__(self, nc, es):
        self.nc = nc
        self.es = es
        self.q = {e: [] for e in ENGS}
        self.cnt = {e: 0 for e in ENGS}
        self.epoch = {e: 0 for e in ENGS}
        self.sems = {}
        self.seen = {e: {} for e in ENGS}
        self.res_w = {}
        self.res_r = {}
        self.dma_val = {}
        self.n_inst = 0
        self.rr = 0

    def _sem(self, key):
        if key not in self.sems:
            self.sems[key] = self.es.enter_context(self.nc.semaphore("s_" + "_".join(str(k) for k in key)))
        return self.sems[key]

    def _deps(self, eng, reads, writes, extra=()):
        need = {}

        def add(ev):
            if ev is None:
                return
            k, v = ev
            if eng == "pe" and k[0] == "pe":
                return
            if need.get(k, 0) < v:
                need[k] = v
        for r in reads:
            add(self.res_w.get(r))
        for w in writes:
            add(self.res_w.get(w))
            for ev in self.res_r.get(w, ()):
                add(ev)
        for ev in extra:
            add(ev)
        waits = []
        for k, v in need.items():
            if self.seen[eng].get(k, 0) >= v:
                continue
            self.seen[eng][k] = v
            waits.append((k, v))
        return waits

    def _commit(self, ev, reads, writes):
        for r in reads:
            lst = self.res_r.setdefault(r, [])
            lst.append(ev)
            if len(lst) > 64:
                mx = {}
                for k, v in lst:
                    if mx.get(k, 0) < v:
                        mx[k] = v
                self.res_r[r] = list(mx.items())
        for w in writes:
            self.res_w[w] = ev
            self.res_r[w] = []

    @staticmethod
    def _is_psum(r):
        return (isinstance(r, str) and r.startswith("bank")) or (isinstance(r, tuple) and r[0] == "hb")

    def op(self, eng, fn, reads=(), writes=()):
        pr = [r for r in reads if self._is_psum(r)]
        if pr:
            writes = list(writes) + pr
        waits = self._deps(eng, reads, writes)
        if self.cnt[eng] >= EPOCH:
            self.epoch[eng] += 1
            self.cnt[eng] = 0
        self.cnt[eng] += 1
        key = (eng, self.epoch[eng])
        ev = (key, self.cnt[eng])
        self.q[eng].append((waits, fn, key, 1))
        self._commit(ev, reads, writes)
        self.n_inst += 1
        return ev

    def dma(self, queue, pairs, reads=(), writes=(), sem=None):
        if sem is None:
            sem = ("dma", "rr%d" % (self.rr % 20))
            self.rr += 1
        key = sem
        prev = self.dma_val.get(key, 0)
        extra = [(key, prev)] if prev > 0 else []
        waits = self._deps(queue, reads, writes, extra)
        val = prev
        for i, pr in enumerate(pairs):
            out_ap, in_ap = pr[0], pr[1]
            kw = pr[2] if len(pr) > 2 else {}
            val += 16

            def fn(e, out_ap=out_ap, in_ap=in_ap, kw=kw):
                return e.dma_start(out=out_ap, in_=in_ap, **kw)
            self.q[queue].append((waits if i == 0 else [], fn, key, 16))
            self.n_inst += 1
        self.dma_val[key] = val
        ev = (key, val)
        self._commit(ev, reads, writes)
        return ev

    def barrier(self):
        evs = []
        for e in ENGS:
            for ep in range(self.epoch[e] + 1):
                k = (e, ep)
                v = self.cnt[e] if ep == self.epoch[e] else EPOCH
                if v > 0:
                    evs.append((k, v))
        for k, v in self.dma_val.items():
            evs.append((k, v))
        for e in ENGS:
            waits = []
            for k, v in evs:
                if self.seen[e].get(k, 0) < v:
                    self.seen[e][k] = v
                    waits.append((k, v))
            if waits:
                self.q[e].append((waits, None, None, 0))
        self.res_w = {}
        self.res_r = {}

    def emit(self):
        nc = self.nc
        for e in ENGS:
            for (waits, fn, key, inc) in self.q[e]:
                for k, v in waits:
                    self._sem(k)
                if key is not None:
                    self._sem(key)
        block = self.es.enter_context(nc.Block())
        engmap = {"pe": block.tensor, "act": block.scalar, "dve": block.vector, "pool": block.gpsimd,
                  "sp": block.sync}
        for e in ENGS:
            items = self.q[e]

            def body(eng, items=items):
                for (waits, fn, key, inc) in items:
                    for k, v in waits:
                        eng.wait_ge(self.sems[k], v)
                    if fn is not None:
                        ins = fn(eng)
                        ins.then_inc(self.sems[key], inc)
            engmap[e](body)


class K:
    def __init__(self, P):
        self.P = P
        self._rot = {}

    def rot(self, name, n):
        i = self._rot.get(name, 0)
        self._rot[name] = i + 1
        return i % n

    def mm(self, out, lhsT, rhs, start=True, stop=True, r=(), w=(), sgc=False):
        if sgc:
            return self.P.op("pe", lambda e: e.matmul(out, lhsT=lhsT, rhs=rhs, start=start, stop=stop, skip_group_check=True), r, w)
        return self.P.op("pe", lambda e: e.matmul(out, lhsT=lhsT, rhs=rhs, start=start, stop=stop), r, w)

    def tr(self, out, in_, ident, r=(), w=()):
        return self.P.op("pe", lambda e: e.transpose(out=out, in_=in_, identity=ident), r, w)

    def act(self, out, in_, func, bias=None, scale=None, accum_out=None, r=(), w=(), eng="act"):
        kw = {}
        if bias is not None:
            kw["bias"] = bias
        if scale is not None:
            kw["scale"] = scale
        if accum_out is not None:
            kw["accum_out"] = accum_out
        return self.P.op("act", lambda e: e.activation(out=out, in_=in_, func=func, **kw), r, w)

    def copy(self, eng, out, in_, r=(), w=()):
        if eng == "act":
            return self.P.op("act", lambda e: e.copy(out=out, in_=in_), r, w)
        return self.P.op(eng, lambda e: e.tensor_copy(out=out, in_=in_), r, w)

    def tt(self, eng, out, in0, in1, op, r=(), w=()):
        return self.P.op(eng, lambda e: e.tensor_tensor(out=out, in0=in0, in1=in1, op=op), r, w)

    def ts(self, eng, out, in0, s1, op0, s2=None, op1=None, r=(), w=()):
        if op1 is None:
            return self.P.op(eng, lambda e: e.tensor_scalar(out=out, in0=in0, scalar1=s1, scalar2=None, op0=op0), r, w)
        return self.P.op(eng, lambda e: e.tensor_scalar(out=out, in0=in0, scalar1=s1, scalar2=s2, op0=op0, op1=op1), r, w)

    def stt(self, eng, out, in0, scalar, in1, op0, op1, r=(), w=()):
        return self.P.op(eng, lambda e: e.scalar_tensor_tensor(out=out, in0=in0, scalar=scalar, in1=in1, op0=op0, op1=op1), r, w)

    def red(self, eng, out, in_, op=ALU.add, r=(), w=()):
        return self.P.op(eng, lambda e: e.tensor_reduce(out=out, in_=in_, axis=AX.X, op=op), r, w)

    def recip(self, out, in_, r=(), w=()):
        return self.P.op("dve", lambda e: e.reciprocal(out=out, in_=in_), r, w)

    def memset(self, eng, ap, val, w=()):
        return self.P.op(eng, lambda e: e.memset(ap, val), (), w)

    def scan(self, out, d0, d1, initial, op0, op1, r=(), w=()):
        return self.P.op("dve", lambda e: e.tensor_tensor_scan(out=out, data0=d0, data1=d1, initial=initial, op0=op0, op1=op1), r, w)

    def dma(self, out, in_, r=(), w=(), q="sp", slow=False, sem=None):
        kw = {"allow_slow_non_contiguous": True} if slow else {}
        return self.P.dma(q, [(out, in_, kw)], r, w, sem=sem)


def make_consts():
    c = {}
    c["ident"] = np.eye(128, dtype=np.float32)
    bo64 = np.zeros((128, 128), np.float32)
    bo64[:64, :64] = 1
    bo64[64:, 64:] = 1
    c["bo64"] = bo64
    bo32 = np.zeros((128, 128), np.float32)
    for i in range(4):
        bo32[i * 32:(i + 1) * 32, i * 32:(i + 1) * 32] = 1
    c["bo32"] = bo32
    prot = np.zeros((128, 128), np.float32)
    for b in range(4):
        for d in range(16):
            prot[b * 32 + d + 16, b * 32 + d] = -1.0
            prot[b * 32 + d, b * 32 + d + 16] = 1.0
    c["prot"] = prot
    inv = 1.0 / (10000.0 ** (np.arange(0, 32, 2, dtype=np.float32) / 32.0))
    ang = np.arange(S, dtype=np.float32)[:, None] * inv[None, :]
    cos = np.cos(ang).astype(np.float32).T
    sin = np.sin(ang).astype(np.float32).T
    c["cosT"] = np.ascontiguousarray(np.tile(cos, (8, 1)))
    c["sinT"] = np.ascontiguousarray(np.tile(sin, (8, 1)))
    k = np.arange(128)[:, None]
    q = np.arange(128)[None, :]
    c["negmask"] = np.where(k > q, -1e30, 0.0).astype(np.float32)
    c["cmask"] = ((k // 64) <= (q // 64)).astype(np.float32)
    c["triu"] = (k <= q).astype(np.float32)
    k6 = np.arange(64)[:, None]
    q6 = np.arange(64)[None, :]
    m64 = np.zeros((64, 3, 64), np.float32)
    m64[:, 0, :] = (k6 < q6)
    m64[:, 1, :] = (k6 <= q6)
    m64[:, 2, :] = (k6 > q6)
    c["m64"] = m64
    sel = np.zeros((4, 4, 128), np.float32)
    for h in range(4):
        sel[h, h, :] = 1.0
    c["sel"] = sel.transpose(1, 0, 2).copy()
    return c


CONST_SHAPES = {"ident": [128, 128], "bo64": [128, 128], "bo32": [128, 128], "prot": [128, 128],
                "cosT": [128, S], "sinT": [128, S], "negmask": [128, 128], "cmask": [128, 128],
                "triu": [128, 128], "m64": [64, 3, 64], "sel": [4, 4, 128]}

WEIGHT_SHAPES = {
    'ada_w': [L, D, 6 * D], 'ada_b': [L, 6 * D], 'norm1_g': [L, D], 'norm2_g': [L, D],
    'w_in': [L, D, NIN], 'w_out': [L, D, D],
    'rw_mu': [L, 896], 'rw_w0': [L, 256], 'rw_w_up': [L, 32, 256], 'rw_a0': [L, 256], 'rw_a_up': [L, 32, 256],
    'rw_g_up': [L, 64, 256], 'rw_k_k': [L, 256], 'rw_k_a': [L, 256], 'rw_r_k': [L, 4, 64],
    'rw_ln_g': [L, 256], 'rw_ln_b': [L, 256],
    'df_lam_q1': [L, 32], 'df_lam_k1': [L, 32], 'df_lam_q2': [L, 32], 'df_lam_k2': [L, 32],
    'df_q_g': [L, 32], 'df_k_g': [L, 32], 'df_sub_g': [L, 64],
    'sg_w': [L, 4, 128, 128], 'sg_b': [L, 4, 128], 'sg_ln_g': [L, 256], 'sg_ln_b': [L, 256],
    'fx_q_g': [L, 64], 'fx_k_g': [L, 64], 'fx_f_b': [L, 4],
    'ffn_up': [L, D, 2 * DFF], 'ffn_conv': [L, 3, 2 * DFF], 'ffn_conv_b': [L, 2 * DFF], 'ffn_down': [L, DFF, D],
}


class Ctx:
    pass


def build(layers=(0, 1, 2, 3), phases=("P1", "A", "B", "D", "P3"), debug=False, y_in=False):
    nc = bass.Bass("TRN2", target_bir_lowering=False)
    dkind = "ExternalOutput" if debug else "Internal"
    I = {}

    def din(name, shape):
        I[name] = nc.dram_tensor(name, list(shape), F32, kind="ExternalInput").ap()
    din("x", [S, D])
    din("c", [1, D])
    for k, shp in WEIGHT_SHAPES.items():
        din(k, shp)
    for k, shp in CONST_SHAPES.items():
        din(k, shp)
    out = nc.dram_tensor("out", [S, D], F32, kind="ExternalOutput").ap()

    def dscr(name, shape, dt, kind=None):
        return nc.dram_tensor(name, list(shape), dt, kind=kind or dkind).ap()
    R = Ctx()
    R.xres = dscr("xres", [S, D], F32)
    R.pmA = dscr("pmA", [896, S], F32)
    R.qTB = dscr("qTB", [256, S], BF16)
    R.kTB = dscr("kTB", [256, S], BF16)
    R.vB = dscr("vB", [S, 260], BF16)
    R.qTD = dscr("qTD", [256, S], BF16)
    R.kTD = dscr("kTD", [256, S], BF16)
    R.vD = dscr("vD", [S, 260], BF16)
    R.Ffm = dscr("Ffm", [4, S], F32)
    if y_in:
        R.yscr = nc.dram_tensor("yscr_in", [S, 768], F32, kind="ExternalInput").ap()
        R.yTA = nc.dram_tensor("yTA_in", [256, S], F32, kind="ExternalInput").ap()
    else:
        R.yscr = dscr("yscr", [S, 768], BF16)
        R.yTA = dscr("yTA", [256, S], BF16)
    R.upbf = dscr("upbf", [L, 11, 128, 8 * 512], BF16, kind="Internal")
    R.dnbf = dscr("dnbf", [L, 128, 22 * 1024], BF16, kind="Internal")

    with ExitStack() as es:
        P = Prog(nc, es)
        k = K(P)
        G = Ctx()
        G.nc, G.P, G.k, G.I, G.R, G.out = nc, P, k, I, R, out
        G.y_in = y_in
        G.dbg_x1 = nc.dram_tensor("dbg_x1", [S, D], F32, kind="ExternalOutput").ap() if debug else None

        uid = [0]

        def sb(stack, name, shape, dt=F32):
            uid[0] += 1
            return stack.enter_context(nc.sbuf_tensor("%s_u%d" % (name, uid[0]), list(shape), dt))
        G.sb = sb
        G.banks = [es.enter_context(nc.psum_tensor("bank%d" % i, [128, 512], F32)) for i in range(8)]
        G.ident = sb(es, "ident", [128, 128])
        G.identb = sb(es, "identb", [128, 128], BF16)
        G.ones_row = sb(es, "ones_row", [1, 128])
        G.condB = sb(es, "condB", [128, 8, 128])
        G.modB = sb(es, "modB", [128, 6 * D])
        k.dma(G.ident[:], I["ident"], w=["ident"])
        k.copy("dve", G.identb[:], G.ident[:], r=["ident"], w=["identb"])
        k.memset("dve", G.ones_row[:], 1.0, w=["ones_row"])
        with ExitStack() as s0:
            cT = sb(s0, "cT", [128, 8])
            cS = sb(s0, "cS", [128, 8])
            k.dma(cT[:], I["c"].rearrange("o (kc p) -> p (o kc)", p=128), w=["cT"], slow=True)
            k.act(cS[:], cT[:], AF.Silu, r=["cT"], w=["cS"])
            k.copy("dve", G.condB[:], cS[:].unsqueeze(2).to_broadcast([128, 8, 128]), r=["cS"], w=["condB"])
            P.barrier()

        for li, l in enumerate(layers):
            xsrc = I["x"] if li == 0 else R.xres
            xdst = out if li == len(layers) - 1 else R.xres
            if "P3" in phases:
                prep_ffn(G, l)
            layer_setup(G, l)
            if "P1" in phases:
                phase_p1(G, l, xsrc)
            if "A" in phases:
                phase_A(G, l)
            if "B" in phases:
                phase_B(G, l)
            if "D" in phases:
                phase_D(G, l)
            if "P3" in phases:
                phase_p3(G, l, xsrc, xdst)
        P.barrier()
        P.emit()
    return nc, P


def prep_ffn(G, l):
    nc, P, k, I, R = G.nc, G.P, G.k, G.I, G.R
    with ExitStack() as s:
        st = [G.sb(s, "pf_st%d" % i, [128, 8, 512]) for i in range(2)]
        sbf = [G.sb(s, "pf_bf%d" % i, [128, 8, 512], BF16) for i in range(2)]
        up = I["ffn_up"][l].rearrange("(kc p) n -> p kc n", p=128)
        dn = I["ffn_down"][l].rearrange("(fc p) d -> p fc d", p=128)
        engs = ["dve", "act", "dve", "act", "pool"]
        jobs = []
        for u in range(11):
            jobs.append((up[:, :, u * 512:(u + 1) * 512], R.upbf[l, u].rearrange("p (kc n) -> p kc n", kc=8), 8, 512))
        for u in range(11):
            jobs.append((dn[:, 2 * u:2 * u + 2, :], R.dnbf[l][:, 2 * u * 1024:(2 * u + 2) * 1024].rearrange("p (a d) -> p a d", a=2), 2, 1024))

        def load(i):
            src, dst, a, b = jobs[i]
            k.dma(st[i % 2][:].rearrange("p a b -> p (a b)")[:, 0:a * b].rearrange("p (a b) -> p a b", a=a), src,
                  w=["pf_st%d" % (i % 2)])
        load(0)
        load(1)
        for i in range(len(jobs)):
            src, dst, a, b = jobs[i]
            sv = st[i % 2][:].rearrange("p a b -> p (a b)")[:, 0:a * b]
            bv = sbf[i % 2][:].rearrange("p a b -> p (a b)")[:, 0:a * b]
            k.copy(engs[i % 5], bv, sv, r=["pf_st%d" % (i % 2)], w=["pf_bf%d" % (i % 2)])
            k.dma(dst, bv.rearrange("p (a b) -> p a b", a=a), r=["pf_bf%d" % (i % 2)], w=[("ffnw", l)])
            if i + 2 < len(jobs):
                load(i + 2)
        P.barrier()


def layer_setup(G, l):
    nc, P, k, I, R = G.nc, G.P, G.k, G.I, G.R
    banks = G.banks
    with ExitStack() as s:
        aw = [G.sb(s, "ls_aw%d" % i, [128, 8, 512]) for i in range(2)]
        rows = G.sb(s, "ls_rows", [1, 8 * D])
        k.dma(rows[:, 0:6 * D], I["ada_b"][l:l + 1, :], w=["ls_rows_b"])
        k.dma(rows[:, 6 * D:7 * D], I["norm1_g"][l:l + 1, :], w=["ls_rows_g"])
        k.dma(rows[:, 7 * D:8 * D], I["norm2_g"][l:l + 1, :], w=["ls_rows_g"])
        awv = I["ada_w"][l].rearrange("(kc p) n -> p kc n", p=128)
        for cc in range(12):
            b = cc % 2
            k.dma(aw[b][:], awv[:, :, cc * 512:(cc + 1) * 512], w=["ls_aw%d" % b])
            bk = banks[b]
            for kc in range(8):
                k.mm(bk[:], G.condB[:, kc, :], aw[b][:, kc, :], start=(kc == 0), stop=False,
                     r=["condB", "ls_aw%d" % b], w=["bank%d" % b])
            k.mm(bk[:], G.ones_row[0:1, :], rows[0:1, cc * 512:(cc + 1) * 512], start=False, stop=True,
                 r=["ones_row", "ls_rows_b"], w=["bank%d" % b])
            k.copy("act" if cc % 2 else "dve", G.modB[:, cc * 512:(cc + 1) * 512], bk[:], r=["bank%d" % b], w=["modB"])
        for gi, (goff, slot) in enumerate(((6 * D, 1 * D), (7 * D, 4 * D))):
            for hf in range(2):
                b = 2 + hf
                k.mm(banks[b][:], G.ones_row[0:1, :], rows[0:1, goff + hf * 512: goff + (hf + 1) * 512],
                     r=["ones_row", "ls_rows_g"], w=["bank%d" % b])
                sl = G.modB[:, slot + hf * 512: slot + (hf + 1) * 512]
                k.stt("dve", sl, sl, 1.0, banks[b][:], ALU.add, ALU.mult, r=["modB", "bank%d" % b], w=["modB"])
        P.barrier()


def norm_mod_T(G, xt, goff, shoff, T):
    k = G.k
    banks = G.banks
    for j in range(4):
        k.act(T.tmp[:], xt[:, j, :], AF.Square, r=["xt"], w=["tmp"])
        k.red("dve", T.ss[:, j:j + 1], T.tmp[:], r=["tmp"], w=["ss"])
    k.act(T.rstd[:], T.ss[:], AF.Sqrt, bias=EPS, scale=1.0 / D, r=["ss"], w=["rstd"])
    k.recip(T.rstd[:], T.rstd[:], r=["rstd"], w=["rstd"])
    for j in range(4):
        k.stt("dve", T.tmp[:], xt[:, j, :], T.rstd[:, j:j + 1], G.modB[:, goff:goff + D], ALU.mult, ALU.mult,
              r=["xt", "rstd", "modB"], w=["tmp"])
        k.tt("dve", T.hb[:, j, :], T.tmp[:], G.modB[:, shoff:shoff + D], ALU.add, r=["tmp", "modB"], w=["hb"])
    for j in range(4):
        b = 5 + (j % 2)
        pv = banks[b][:].bitcast(BF16).rearrange("p (a t) -> p a t", a=8)
        for kc in range(8):
            k.tr(pv[:, kc, :], T.hb[:, j, kc * 128:(kc + 1) * 128], G.identb[:], r=["hb", "identb"], w=["bank%d" % b])
        k.copy("act" if j % 2 else "dve", T.hT[:, :, j * 128:(j + 1) * 128], pv, r=["bank%d" % b], w=["hT"])


def bcast_load(G, tile_ap, row_ap, n, w):
    G.k.dma(tile_ap, row_ap.to_broadcast([128, n]), w=w, slow=True)


def phase_p1(G, l, xsrc):
    nc, P, k, I, R = G.nc, G.P, G.k, G.I, G.R
    banks = G.banks
    with ExitStack() as s:
        sb = lambda name, shape, dt=F32: G.sb(s, name, shape, dt)
        T = Ctx()
        w_in = sb("w_in", [128, 8, NIN], BF16)
        stg = [sb("p1_stg%d" % i, [128, 1474]) for i in range(2)]
        xt = sb("xt", [128, 4, D])
        T.sqj = sb("sqj", [128, D], BF16)
        T.ss = sb("ss", [128, 4])
        T.rstd = sb("rstd", [128, 4])
        T.tmp = sb("tmp", [128, D])
        T.hb = sb("hb", [128, 4, D], BF16)
        T.hT = sb("hT", [128, 8, 512], BF16)
        fA = [sb("fA%d" % i, [128, 512]) for i in range(3)]
        fB = [sb("fB%d" % i, [128, 512]) for i in range(3)]
        fC = [sb("fC%d" % i, [128, 512]) for i in range(3)]
        obf = [sb("obf%d" % i, [128, 512], BF16) for i in range(3)]
        xnb = [sb("xnb%d" % i, [128, 512], BF16) for i in range(2)]
        paA = sb("paA", [128, 7, 513])
        pmo = [sb("pmo%d" % i, [128, 512]) for i in range(2)]
        cs = [sb("cos%d" % i, [128, 512]) for i in range(2)]
        sn = [sb("sin%d" % i, [128, 512]) for i in range(2)]
        bo64 = sb("bo64", [128, 128])
        bo32 = sb("bo32", [128, 128])
        protf = sb("protf", [128, 128])
        protb = sb("protb", [128, 128], BF16)
        gcol = sb("gcol", [128, 8])
        mu = sb("mu", [128, 7])
        negfb = sb("negfb", [4, 1])
        ones4 = sb("ones4", [4, 512])
        Fg = [sb("Fg%d" % i, [4, 512]) for i in range(2)]
        f4 = sb("f4", [4, 512])
        vt = [sb("vt%d" % i, [128, 4, 65], BF16) for i in range(4)]
        sgw = sb("sgw", [128, 4, 128])
        WgT = sb("WgT", [128, 4, 128], BF16)
        triu = sb("triu", [128, 128])
        sgbT = sb("sgbT", [128, 4])
        lnCg = sb("lnCg", [128, 256])
        lnCb = sb("lnCb", [128, 256])
        glC = [sb("glC%d" % i, [128, 512]) for i in range(2)]
        stC = [sb("stC%d" % i, [128, 8]) for i in range(2)]
        tmpc = [sb("tmpc%d" % i, [128, 256]) for i in range(2)]
        vnb = [sb("vnb%d" % i, [128, 256], BF16) for i in range(2)]
        ycb = [sb("ycb%d" % i, [128, 256], BF16) for i in range(2)]

        for kc in range(8):
            for hf in range(2):
                i = (kc * 2 + hf) % 2
                k.dma(stg[i][:], I["w_in"][l, kc * 128:(kc + 1) * 128, hf * 1474:(hf + 1) * 1474], w=["p1_stg%d" % i])
                k.copy(("dve", "pool", "act")[(kc * 2 + hf) % 3], w_in[:, kc, hf * 1474:(hf + 1) * 1474], stg[i][:],
                       r=["p1_stg%d" % i], w=["w_in"])
        k.dma(bo64[:], I["bo64"], w=["bo64"])
        k.dma(bo32[:], I["bo32"], w=["bo32"])
        k.dma(protf[:], I["prot"], w=["protf"])
        k.copy("dve", protb[:], protf[:], r=["protf"], w=["protb"])
        k.dma(triu[:], I["triu"], w=["triu"])
        for rep in range(2):
            k.dma(gcol[rep * 64:(rep + 1) * 64, 0:1], I["fx_q_g"][l].rearrange("(d o) -> d o", o=1), w=["gcol"], slow=True)
            k.dma(gcol[rep * 64:(rep + 1) * 64, 1:2], I["fx_k_g"][l].rearrange("(d o) -> d o", o=1), w=["gcol"], slow=True)
        for rep in range(4):
            k.dma(gcol[rep * 32:(rep + 1) * 32, 2:3], I["df_q_g"][l].rearrange("(d o) -> d o", o=1), w=["gcol"], slow=True)
            k.dma(gcol[rep * 32:(rep + 1) * 32, 3:4], I["df_k_g"][l].rearrange("(d o) -> d o", o=1), w=["gcol"], slow=True)
        k.ts("dve", gcol[:, 0:1], gcol[:, 0:1], 0.125, ALU.mult, r=["gcol"], w=["gcol"])
        k.ts("dve", gcol[:, 2:3], gcol[:, 2:3], 32.0 ** -0.5, ALU.mult, r=["gcol"], w=["gcol"])
        k.dma(mu[:], I["rw_mu"][l].rearrange("(c p) -> p c", p=128), w=["mu"], slow=True)
        k.dma(negfb[:], I["fx_f_b"][l].rearrange("(h o) -> h o", o=1), w=["negfb"], slow=True)
        k.ts("dve", negfb[:], negfb[:], -1.0, ALU.mult, r=["negfb"], w=["negfb"])
        k.memset("dve", ones4[:], 1.0, w=["ones4"])
        k.memset("pool", paA[:], 0.0, w=["paA%d" % i for i in range(7)])
        for i in range(4):
            k.memset("pool", vt[i][:], 1.0, w=["vt%d" % i])
        k.dma(sgw[:], I["sg_w"][l].rearrange("g i j -> i g j"), w=["sgw"])
        for g in range(4):
            k.tr(banks[0][:, g * 128:(g + 1) * 128], sgw[:, g, :], G.ident[:], r=["sgw", "ident"], w=["bank0"])
        k.tt("dve", WgT[:], banks[0][:].rearrange("p (g i) -> p g i", g=4),
             triu[:].unsqueeze(1).to_broadcast([128, 4, 128]), ALU.mult, r=["bank0", "triu"], w=["WgT"])
        k.dma(sgbT[:], I["sg_b"][l].rearrange("g i -> i g"), w=["sgbT"], slow=True)
        bcast_load(G, lnCg[:], I["sg_ln_g"][l:l + 1, :], 256, ["lnCg"])
        bcast_load(G, lnCb[:], I["sg_ln_b"][l:l + 1, :], 256, ["lnCb"])

        def fm_mm(col0, ncols, bk):
            for kc in range(8):
                k.mm(banks[bk][0:ncols, :], w_in[:, kc, col0:col0 + ncols], T.hT[:, kc, :], start=(kc == 0), stop=(kc == 7),
                     r=["w_in", "hT"], w=["bank%d" % bk])

        for g in range(NG):
            tsl = slice(g * TG, (g + 1) * TG)
            k.dma(xt[:], xsrc[tsl, :].rearrange("(j p) d -> p j d", p=128), r=[("x", g)], w=["xt"])
            k.dma(cs[g % 2][:], I["cosT"][:, tsl], w=["cos%d" % (g % 2)])
            k.dma(sn[g % 2][:], I["sinT"][:, tsl], w=["sin%d" % (g % 2)])
            norm_mod_T(G, xt, 1 * D, 0, T)
            for ci in range(7):
                bk = k.rot("p1bank", 3)
                fm_mm(ci * 128, 128, bk)
                k.copy("pool", paA[:, ci, 0:1], paA[:, ci, 512:513], r=["paA%d" % ci], w=["paA%d" % ci])
                k.copy("act", paA[:, ci, 1:513], banks[bk][:], r=["bank%d" % bk], w=["paA%d" % ci])
                i = k.rot("fA", 3)
                k.tt("pool", fA[i][:], paA[:, ci, 0:512], paA[:, ci, 1:513], ALU.subtract, r=["paA%d" % ci], w=["fA%d" % i])
                o = k.rot("pmo", 2)
                k.stt("dve", pmo[o][:], fA[i][:], mu[:, ci:ci + 1], paA[:, ci, 1:513], ALU.mult, ALU.add,
                      r=["fA%d" % i, "mu", "paA%d" % ci], w=["pmo%d" % o])
                k.dma(R.pmA[ci * 128:(ci + 1) * 128, tsl], pmo[o][:], r=["pmo%d" % o], w=[("pmA", g)])
            for (mix, col0, gi, dst, rope) in (("B", 896, 2, R.qTB, True), ("B", 1152, 3, R.kTB, True),
                                               ("D", 2176, 0, R.qTD, False), ("D", 2432, 1, R.kTD, False)):
                for ci in range(2):
                    bk = k.rot("p1bank", 3)
                    fm_mm(col0 + ci * 128, 128, bk)
                    a = k.rot("fA", 3)
                    k.act(fA[a][:], banks[bk][:], AF.Square, r=["bank%d" % bk], w=["fA%d" % a])
                    sbk = 3 + k.rot("p1sbank", 2)
                    k.mm(banks[sbk][:], bo32[:] if mix == "B" else bo64[:], fA[a][:], r=["bo32", "bo64", "fA%d" % a],
                         w=["bank%d" % sbk])
                    b = k.rot("fB", 3)
                    nd = 32.0 if mix == "B" else 64.0
                    k.act(fB[b][:], banks[sbk][:], AF.Sqrt, bias=EPS, scale=1.0 / nd, r=["bank%d" % sbk], w=["fB%d" % b])
                    k.recip(fB[b][:], fB[b][:], r=["fB%d" % b], w=["fB%d" % b])
                    o = k.rot("obf", 3)
                    if not rope:
                        k.stt("dve", obf[o][:], banks[bk][:], gcol[:, gi:gi + 1], fB[b][:], ALU.mult, ALU.mult,
                              r=["bank%d" % bk, "gcol", "fB%d" % b], w=["obf%d" % o])
                    else:
                        c_ = k.rot("fC", 3)
                        k.stt("dve", fC[c_][:], banks[bk][:], gcol[:, gi:gi + 1], fB[b][:], ALU.mult, ALU.mult,
                              r=["bank%d" % bk, "gcol", "fB%d" % b], w=["fC%d" % c_])
                        xb = k.rot("xnb", 2)
                        k.copy("act", xnb[xb][:], fC[c_][:], r=["fC%d" % c_], w=["xnb%d" % xb])
                        rbk = 3 + k.rot("p1sbank", 2)
                        k.mm(banks[rbk][:], protb[:], xnb[xb][:], r=["protb", "xnb%d" % xb], w=["bank%d" % rbk])
                        a2 = k.rot("fA", 3)
                        k.tt("pool", fA[a2][:], fC[c_][:], cs[g % 2][:], ALU.mult, r=["fC%d" % c_, "cos%d" % (g % 2)], w=["fA%d" % a2])
                        b2 = k.rot("fB", 3)
                        k.tt("dve", fB[b2][:], banks[rbk][:], sn[g % 2][:], ALU.mult, r=["bank%d" % rbk, "sin%d" % (g % 2)],
                             w=["fB%d" % b2])
                        k.tt("pool", obf[o][:], fA[a2][:], fB[b2][:], ALU.add, r=["fA%d" % a2, "fB%d" % b2], w=["obf%d" % o])
                    k.dma(dst[ci * 128:(ci + 1) * 128, tsl], obf[o][:], r=["obf%d" % o], w=[(mix + "qk", g)])
            bk = k.rot("p1bank", 3)
            fm_mm(2944, 4, bk)
            k.act(f4[:], banks[bk][0:4, :], AF.Exp, bias=negfb[:, 0:1], scale=-1.0, r=["bank%d" % bk, "negfb"], w=["f4"])
            k.act(f4[:], f4[:], AF.Ln, bias=1.0, scale=1.0, r=["f4"], w=["f4"])
            if g == 0:
                k.scan(Fg[0][:], ones4[:], f4[:], 0.0, ALU.mult, ALU.subtract, r=["ones4", "f4"], w=["Fg0"])
            else:
                k.scan(Fg[g % 2][:], ones4[:], f4[:], Fg[(g - 1) % 2][:, 511:512], ALU.mult, ALU.subtract,
                       r=["ones4", "f4", "Fg%d" % ((g - 1) % 2)], w=["Fg%d" % (g % 2)])
            k.dma(R.Ffm[:, tsl], Fg[g % 2][:], r=["Fg%d" % (g % 2)], w=[("Ffm", g)])
            for j in range(4):
                rows = slice(g * TG + j * 128, g * TG + (j + 1) * 128)
                for (mix, col0, dst) in (("B", 1408, R.vB), ("D", 2688, R.vD)):
                    bk = k.rot("p1bank", 3)
                    for kc in range(8):
                        k.mm(banks[bk][:, 0:256], T.hT[:, kc, j * 128:(j + 1) * 128], w_in[:, kc, col0:col0 + 256],
                             start=(kc == 0), stop=(kc == 7), r=["w_in", "hT"], w=["bank%d" % bk])
                    vi = k.rot("vt", 4)
                    k.copy("act" if mix == "B" else "dve", vt[vi][:, :, 0:64], banks[bk][:, 0:256].rearrange("p (h e) -> p h e", h=4),
                           r=["bank%d" % bk], w=["vt%d" % vi])
                    k.dma(dst[rows, :], vt[vi][:].rearrange("p h e -> p (h e)"), r=["vt%d" % vi], w=[(mix + "v", g)])
                bk = k.rot("p1bank", 3)
                for kc in range(8):
                    k.mm(banks[bk][:], T.hT[:, kc, j * 128:(j + 1) * 128], w_in[:, kc, 1664:2176],
                         start=(kc == 0), stop=(kc == 7), r=["w_in", "hT"], w=["bank%d" % bk])
                ci = k.rot("glC", 2)
                gl, st, tc, vb, yb = glC[ci], stC[ci], tmpc[ci], vnb[ci], ycb[ci]
                kk = "C%d" % ci
                k.act(gl[:], banks[bk][:], AF.Gelu, r=["bank%d" % bk], w=[kk + "gl"])
                k.red("dve", st[:, 0:1], gl[:, 256:512], r=[kk + "gl"], w=[kk + "st"])
                k.act(tc[:], gl[:, 256:512], AF.Square, r=[kk + "gl"], w=[kk + "tc"])
                k.red("dve", st[:, 1:2], tc[:], r=[kk + "tc"], w=[kk + "st"])
                k.ts("dve", st[:, 2:3], st[:, 0:1], 1.0 / 256, ALU.mult, r=[kk + "st"], w=[kk + "st"])
                k.tt("dve", st[:, 3:4], st[:, 2:3], st[:, 2:3], ALU.mult, r=[kk + "st"], w=[kk + "st"])
                k.stt("dve", st[:, 4:5], st[:, 1:2], 1.0 / 256, st[:, 3:4], ALU.mult, ALU.subtract, r=[kk + "st"], w=[kk + "st"])
                k.act(st[:, 5:6], st[:, 4:5], AF.Sqrt, bias=EPS, scale=1.0, r=[kk + "st"], w=[kk + "st"])
                k.recip(st[:, 5:6], st[:, 5:6], r=[kk + "st"], w=[kk + "st"])
                k.ts("dve", tc[:], gl[:, 256:512], st[:, 2:3], ALU.subtract, st[:, 5:6], ALU.mult, r=[kk + "gl", kk + "st"], w=[kk + "tc"])
                k.tt("pool", tc[:], tc[:], lnCg[:], ALU.mult, r=[kk + "tc", "lnCg"], w=[kk + "tc"])
                k.tt("pool", vb[:], tc[:], lnCb[:], ALU.add, r=[kk + "tc", "lnCb"], w=[kk + "vb"])
                sbk = 3 + k.rot("p1sbank", 2)
                for hg in range(4):
                    k.mm(banks[sbk][:, hg * 64:(hg + 1) * 64], WgT[:, hg, :], vb[:, hg * 64:(hg + 1) * 64],
                         r=["WgT", kk + "vb"], w=["bank%d" % sbk])
                k.tt("dve", tc[:].rearrange("p (h e) -> p h e", h=4), banks[sbk][:, 0:256].rearrange("p (h e) -> p h e", h=4),
                     sgbT[:].unsqueeze(2).to_broadcast([128, 4, 64]), ALU.add, r=["bank%d" % sbk, "sgbT", kk + "vb"], w=[kk + "tc"])
                k.tt("pool", yb[:], tc[:], gl[:, 0:256], ALU.mult, r=[kk + "tc", kk + "gl"], w=[kk + "yb"])
                if not G.y_in:
                    k.dma(R.yscr[rows, 256:512], yb[:], r=[kk + "yb"], w=[("yC", g)])
        P.barrier()


def phase_p3(G, l, xsrc, xdst):
    nc, P, k, I, R = G.nc, G.P, G.k, G.I, G.R
    banks = G.banks
    ydt = F32 if G.y_in else BF16
    with ExitStack() as s:
        sb = lambda name, shape, dt=F32: G.sb(s, name, shape, dt)
        T = Ctx()
        w_out = sb("w_out", [128, 8, D], BF16)
        xt = sb("xt3", [128, 4, D])
        T.sqj = sb("sqj3", [128, D], BF16)
        T.ss = sb("ss3", [128, 4])
        T.rstd = sb("rstd3", [128, 4])
        T.tmp = sb("tmp3", [128, D])
        T.hb = sb("hb3", [128, 4, D], BF16)
        T.hT = sb("hT3", [128, 8, 512], BF16)
        yT = T.hT
        actT = sb("actT", [128, 22, 512], BF16)
        upw = [sb("upw%d" % i, [128, 8, 512], BF16) for i in range(2)]
        dnw = sb("dnw", [128, 22, D], BF16)
        ub = [sb("ub%d" % i, [128, 514]) for i in range(2)]
        c1 = [sb("c1_%d" % i, [128, 512]) for i in range(2)]
        c2 = [sb("c2_%d" % i, [128, 512]) for i in range(2)]
        c3 = [sb("c3_%d" % i, [128, 512]) for i in range(2)]
        sgt = [sb("sgt%d" % i, [128, 512]) for i in range(2)]
        convw = sb("convw", [128, 44, 3])
        convb = sb("convb", [128, 44])
        carryF = sb("carryF", [128, 44, 2])

        for kc in range(8):
            k.dma(T.tmp[:], I["w_out"][l, kc * 128:(kc + 1) * 128, :], w=["tmp"])
            k.copy(("dve", "pool", "act")[kc % 3], w_out[:, kc, :], T.tmp[:], r=["tmp"], w=["w_out"])
        for t in range(3):
            k.dma(convw[:, :, t], I["ffn_conv"][l, t].rearrange("(cc p) -> p cc", p=128), w=["convw"], slow=True)
        k.dma(convb[:], I["ffn_conv_b"][l].rearrange("(cc p) -> p cc", p=128), w=["convb"], slow=True)
        k.memset("pool", carryF[:], 0.0, w=["carryF"])

        for g in range(NG):
            tsl = slice(g * TG, (g + 1) * TG)
            k.dma(xt[:], xsrc[tsl, :].rearrange("(j p) d -> p j d", p=128), r=[("x", g)], w=["xt"])
            k.dma(dnw[:].rearrange("p a d -> p (a d)"), R.dnbf[l], r=[("ffnw", l)], w=["dnw"])
            if G.y_in:
                for j in range(4):
                    k.dma(T.tmp[:, 0:768], R.yscr[g * TG + j * 128: g * TG + (j + 1) * 128, :], w=["tmp"])
                    k.copy("pool", T.hb[:, j, 0:768], T.tmp[:, 0:768], r=["tmp"], w=["hb"])
                for kc in range(2):
                    k.dma(c1[kc][:], R.yTA[kc * 128:(kc + 1) * 128, tsl], w=["c1_%d" % kc])
                    k.copy("act", yT[:, kc, :], c1[kc][:], r=["c1_%d" % kc], w=["hT"])
            else:
                k.dma(T.hb[:, :, 0:768], R.yscr[tsl, :].rearrange("(j p) c -> p j c", p=128),
                      r=[("yC", g), ("yB", g), ("yD", g)], w=["hb"])
                k.dma(yT[:, 0:2, :], R.yTA[:, tsl].rearrange("(kc p) t -> p kc t", p=128), r=[("yA", g)], w=["hT"])
            for j in range(4):
                b = 5 + (j % 2)
                pv = banks[b][:].bitcast(BF16).rearrange("p (a t) -> p a t", a=8)
                for kc in range(6):
                    k.tr(pv[:, kc, :], T.hb[:, j, kc * 128:(kc + 1) * 128], G.identb[:], r=["hb", "identb"], w=["bank%d" % b])
                k.copy("act" if j % 2 else "dve", yT[:, 2:8, j * 128:(j + 1) * 128], pv[:, 0:6, :], r=["bank%d" % b], w=["hT"])
            for j in range(4):
                for hf in range(2):
                    bk = k.rot("p3bank", 2)
                    for kc in range(8):
                        k.mm(banks[bk][:], yT[:, kc, j * 128:(j + 1) * 128], w_out[:, kc, hf * 512:(hf + 1) * 512],
                             start=(kc == 0), stop=(kc == 7), r=["hT", "w_out"], w=["bank%d" % bk])
                    ci = k.rot("c1", 2)
                    k.tt("dve", c1[ci][:], banks[bk][:], G.modB[:, 2 * D + hf * 512: 2 * D + (hf + 1) * 512], ALU.mult,
                         r=["bank%d" % bk, "modB"], w=["c1_%d" % ci])
                    k.tt("dve", xt[:, j, hf * 512:(hf + 1) * 512], xt[:, j, hf * 512:(hf + 1) * 512], c1[ci][:], ALU.add,
                         r=["xt", "c1_%d" % ci], w=["xt"])
            if G.dbg_x1 is not None:
                k.dma(G.dbg_x1[tsl, :].rearrange("(j p) d -> p j d", p=128), xt[:], r=["xt"], w=[("dbgx1", g)])
            norm_mod_T(G, xt, 4 * D, 3 * D, T)
            for u in range(11):
                wi = u % 2
                k.dma(upw[wi][:].rearrange("p a n -> p (a n)"), R.upbf[l, u], r=[("ffnw", l)], w=["upw%d" % wi])
                for sc in range(4):
                    cc = 4 * u + sc
                    bk = 2 + k.rot("p3ubank", 3)
                    for kc in range(8):
                        k.mm(banks[bk][:], upw[wi][:, kc, sc * 128:(sc + 1) * 128], T.hT[:, kc, :], start=(kc == 0), stop=(kc == 7),
                             r=["upw%d" % wi, "hT"], w=["bank%d" % bk])
                    ui = k.rot("ub", 2)
                    k.copy("pool", ub[ui][:, 0:2], carryF[:, cc, :], r=["carryF"], w=["ub%d" % ui])
                    k.copy("act", ub[ui][:, 2:514], banks[bk][:], r=["bank%d" % bk], w=["ub%d" % ui])
                    k.copy("pool", carryF[:, cc, :], ub[ui][:, 512:514], r=["ub%d" % ui], w=["carryF"])
                    k.act(c1[ui][:], ub[ui][:, 2:514], AF.Identity, bias=convb[:, cc:cc + 1], scale=convw[:, cc, 2:3],
                          r=["ub%d" % ui, "convw", "convb"], w=["c1_%d" % ui])
                    k.stt("dve", c2[ui][:], ub[ui][:, 1:513], convw[:, cc, 1:2], c1[ui][:], ALU.mult, ALU.add,
                          r=["ub%d" % ui, "convw", "c1_%d" % ui], w=["c2_%d" % ui])
                    if cc < 22:
                        k.stt("dve", actT[:, cc, :], ub[ui][:, 0:512], convw[:, cc, 0:1], c2[ui][:], ALU.mult, ALU.add,
                              r=["ub%d" % ui, "convw", "c2_%d" % ui], w=["actT%d" % cc])
                    else:
                        cu = cc - 22
                        k.stt("dve", c3[ui][:], ub[ui][:, 0:512], convw[:, cc, 0:1], c2[ui][:], ALU.mult, ALU.add,
                              r=["ub%d" % ui, "convw", "c2_%d" % ui], w=["c3_%d" % ui])
                        k.act(sgt[ui][:], c3[ui][:], AF.Silu, r=["c3_%d" % ui], w=["sgt%d" % ui])
                        k.tt("pool", actT[:, cu, :], actT[:, cu, :], sgt[ui][:], ALU.mult, r=["actT%d" % cu, "sgt%d" % ui],
                             w=["actT%d" % cu])
            aks = ["actT%d" % i for i in range(22)]
            for j in range(4):
                for hf in range(2):
                    bk = k.rot("p3bank", 2)
                    for cu in range(22):
                        k.mm(banks[bk][:], actT[:, cu, j * 128:(j + 1) * 128], dnw[:, cu, hf * 512:(hf + 1) * 512],
                             start=(cu == 0), stop=(cu == 21), r=["actT%d" % cu, "dnw"], w=["bank%d" % bk])
                    ci = k.rot("c1", 2)
                    k.tt("dve", c1[ci][:], banks[bk][:], G.modB[:, 5 * D + hf * 512: 5 * D + (hf + 1) * 512], ALU.mult,
                         r=["bank%d" % bk, "modB"], w=["c1_%d" % ci])
                    k.tt("dve", xt[:, j, hf * 512:(hf + 1) * 512], xt[:, j, hf * 512:(hf + 1) * 512], c1[ci][:], ALU.add,
                         r=["xt", "c1_%d" % ci], w=["xt"])
            k.dma(xdst[tsl, :].rearrange("(j p) d -> p j d", p=128), xt[:], r=["xt"], w=[("x", g)])
        P.barrier()


def phase_A(G, l):
    nc, P, k, I, R = G.nc, G.P, G.k, G.I, G.R
    banks = G.banks
    with ExitStack() as s:
        sb = lambda name, shape, dt=F32: G.sb(s, name, shape, dt)
        prm = sb("prm", [64, 7, 4])
        wup = sb("wup", [32, 256])
        aup = sb("aup", [32, 256])
        gup = sb("gup", [64, 256])
        ones64 = sb("ones64", [64, 64])
        ones512 = sb("ones512", [64, 512])
        m64 = sb("m64", [64, 3, 64])
        Tst = sb("Tst", [64, 4, 64])
        big = lambda nm: sb(nm, [64, 4, 512])
        r_, k_, v_ = big("r_"), big("k_"), big("v_")
        lwt, at, gt, kkn, k2, b_, bonus, Yblk, t1, t2 = (big("lwt"), big("at"), big("gt"), big("kkn"), big("k2"), big("b_"),
                                                         big("bonus"), big("Yblk"), big("t1"), big("t2"))
        Gblk = sb("Gblk", [64, 4, 513])
        wd = sb("wd", [32, 512])
        ad = sb("ad", [32, 512])
        gd = sb("gd", [64, 512])
        yo = sb("yo", [64, 4, 512], BF16)
        sm = lambda nm: sb(nm, [64, 4, 64])
        Pc, Pp, eG, eGn, eGp = sm("Pc"), sm("Pp"), sm("eG"), sm("eGn"), sm("eGp")
        smb = lambda nm: sb(nm, [64, 4, 64], BF16)
        At, Bt, Kt, Rt = smb("At"), smb("Bt"), smb("Kt"), smb("Rt")
        Btm, Ktm, Vtm = smb("Btm"), smb("Ktm"), smb("Vtm")
        PT = [smb("PT0"), smb("PT1")]
        Pj = [smb("Pj0"), smb("Pj1")]
        LakT, MrbT, MrkT, U, tS = smb("LakT"), smb("MrbT"), smb("MrkT"), sm("U"), sm("tS")
        Ub, Tb = smb("Ub"), smb("Tb")
        vb = sb("vb", [64, 4, 512], BF16)

        for n, nm in enumerate(("rw_w0", "rw_a0", "rw_k_k", "rw_k_a")):
            k.dma(prm[:, n, :], I[nm][l].rearrange("(h k) -> k h", k=64), w=["prm"], slow=True)
        k.dma(prm[:, 4, :], I["rw_r_k"][l].rearrange("h k -> k h"), w=["prm"], slow=True)
        k.dma(prm[:, 5, :], I["rw_ln_g"][l].rearrange("(h k) -> k h", k=64), w=["prm"], slow=True)
        k.dma(prm[:, 6, :], I["rw_ln_b"][l].rearrange("(h k) -> k h", k=64), w=["prm"], slow=True)
        k.dma(wup[:], I["rw_w_up"][l], w=["wup"])
        k.dma(aup[:], I["rw_a_up"][l], w=["aup"])
        k.dma(gup[:], I["rw_g_up"][l], w=["gup"])
        k.dma(m64[:], I["m64"], w=["m64"])
        k.memset("dve", ones64[:], 1.0, w=["ones64"])
        k.memset("dve", ones512[:], 1.0, w=["ones512"])
        k.memset("pool", Tst[:], 0.0, w=["Tst"])
        k.memset("pool", Tb[:], 0.0, w=["Tb"])
        k.memset("pool", Gblk[:], 0.0, w=["Gblk"])

        def bc(col):
            return prm[:, col, :].unsqueeze(2).to_broadcast([64, 4, 512])

        def HB(b, half):
            return banks[b][0:64, half * 256:(half + 1) * 256].rearrange("p (h t) -> p h t", h=4)

        def hk(b, half):
            return ("hb", b)

        def fb(b):
            return [("hb", b)]
        allpm = [("pmA", g) for g in range(NG)]

        for g in range(NG):
            tsl = slice(g * TG, (g + 1) * TG)
            k.dma(r_[:], R.pmA[0:256, tsl].rearrange("(h k) t -> k h t", k=64), r=allpm, w=["r_"])
            k.dma(k_[:], R.pmA[256:512, tsl].rearrange("(h k) t -> k h t", k=64), r=allpm, w=["k_"])
            k.dma(v_[:], R.pmA[512:768, tsl].rearrange("(h k) t -> k h t", k=64), r=allpm, w=["v_"])
            k.copy("pool", vb[:], v_[:], r=["v_"], w=["vb"])
            k.dma(wd[:], R.pmA[768:800, tsl], r=allpm, w=["wd"])
            k.dma(ad[:], R.pmA[800:832, tsl], r=allpm, w=["ad"])
            k.dma(gd[:], R.pmA[832:896, tsl], r=allpm, w=["gd"])
            k.act(wd[:], wd[:], AF.Tanh, r=["wd"], w=["wd"])
            k.act(gd[:], gd[:], AF.Sigmoid, r=["gd"], w=["gd"])
            for h in range(4):
                k.mm(banks[h][0:64, :], wup[:, h * 64:(h + 1) * 64], wd[:], r=["wup", "wd"], w=fb(h))
                k.act(lwt[:, h, :], banks[h][0:64, :], AF.Sigmoid, bias=prm[:, 0, h:h + 1], scale=1.0, r=fb(h) + ["prm"], w=["lwt"])
            for h in range(4):
                k.mm(banks[4 + h][0:64, :], aup[:, h * 64:(h + 1) * 64], ad[:], r=["aup", "ad"], w=fb(4 + h))
                k.act(at[:, h, :], banks[4 + h][0:64, :], AF.Sigmoid, bias=prm[:, 1, h:h + 1], scale=1.0, r=fb(4 + h) + ["prm"], w=["at"])
            for h in range(4):
                k.mm(banks[h][0:64, :], gup[:, h * 64:(h + 1) * 64], gd[:], r=["gup", "gd"], w=fb(h))
                k.copy("dve" if h % 2 else "act", gt[:, h, :], banks[h][0:64, :], r=fb(h), w=["gt"])
            k.tt("dve", kkn[:], k_[:], bc(2), ALU.mult, r=["k_", "prm"], w=["kkn"])
            k.tt("pool", t1[:], kkn[:], kkn[:], ALU.mult, r=["kkn"], w=["t1"])
            for h in range(4):
                k.mm(banks[4 + h][0:64, :], ones64[:], t1[:, h, :], r=["ones64", "t1"], w=fb(4 + h))
            for h in range(4):
                k.act(t2[:, h, :], banks[4 + h][0:64, :], AF.Sqrt, r=fb(4 + h), w=["t2"])
            k.ts("dve", t2[:], t2[:], 1e-12, ALU.max, r=["t2"], w=["t2"])
            k.recip(t2[:], t2[:], r=["t2"], w=["t2"])
            k.tt("pool", kkn[:], kkn[:], t2[:], ALU.mult, r=["kkn", "t2"], w=["kkn"])
            k.stt("dve", t1[:], at[:], -1.0, bc(3), ALU.add, ALU.mult, r=["at", "prm", "t1"], w=["t1"])
            k.tt("pool", t1[:], t1[:], k_[:], ALU.mult, r=["t1", "k_"], w=["t1"])
            k.tt("pool", k2[:], t1[:], k_[:], ALU.add, r=["t1", "k_"], w=["k2"])
            k.tt("pool", b_[:], kkn[:], at[:], ALU.mult, r=["kkn", "at"], w=["b_"])
            k.tt("pool", t1[:], r_[:], k2[:], ALU.mult, r=["r_", "k2", "t1"], w=["t1"])
            k.tt("dve", t1[:], t1[:], bc(4), ALU.mult, r=["t1", "prm"], w=["t1"])
            for h in range(4):
                k.mm(banks[h][0:64, :], ones64[:], t1[:, h, :], r=["ones64", "t1"], w=fb(h))
                k.tt("dve", bonus[:, h, :], banks[h][0:64, :], v_[:, h, :], ALU.mult, r=fb(h) + ["v_"], w=["bonus"])
            for h in range(4):
                k.scan(Gblk[:, h, 1:513], ones512[:], lwt[:, h, :], 0.0, ALU.mult, ALU.add, r=["ones512", "lwt"], w=["Gblk"])

            if int(os.environ.get("A_STOP", "9")) <= 1:
                continue
            for ci in range(8 if int(os.environ.get("A_STOP", "9")) > 2 else 0):
                c0 = ci * 64
                ts_ = slice(c0, c0 + 64)
                k.tt("dve", Pc[:], Gblk[:, :, 1 + c0:1 + c0 + 64], Gblk[:, :, c0:c0 + 1].to_broadcast([64, 4, 64]), ALU.subtract,
                     r=["Gblk"], w=["Pc"])
                k.tt("pool", Pp[:], Pc[:], lwt[:, :, ts_], ALU.subtract, r=["Pc", "lwt"], w=["Pp"])
                k.act(eG[:], Pc[:], AF.Exp, scale=-ALPHA, r=["Pc"], w=["eG"])
                k.act(eGn[:], Pc[:], AF.Exp, scale=ALPHA, r=["Pc"], w=["eGn"])
                k.act(eGp[:], Pp[:], AF.Exp, scale=-ALPHA, r=["Pp"], w=["eGp"])
                k.stt("dve", At[:], kkn[:, :, ts_], -1.0, eGp[:], ALU.mult, ALU.mult, r=["kkn", "eGp"], w=["At"])
                k.tt("pool", Bt[:], b_[:, :, ts_], eGn[:], ALU.mult, r=["b_", "eGn"], w=["Bt"])
                k.tt("pool", Kt[:], k2[:, :, ts_], eGn[:], ALU.mult, r=["k2", "eGn"], w=["Kt"])
                k.tt("dve", Rt[:], r_[:, :, ts_], eG[:], ALU.mult, r=["r_", "eG"], w=["Rt"])
                CH = int(os.environ.get("A_CH", "99"))
                if CH < 2:
                    continue
                id64 = G.identb[0:64, 0:64]
                trs = ((Bt, "Bt", Btm, "Btm", 2, 1, "act"), (Kt, "Kt", Ktm, "Ktm", 3, 0, "dve"), (None, "vb", Vtm, "Vtm", 3, 1, "act"))
                if "A_TR" in os.environ:
                    trs = tuple(trs[int(c)] for c in os.environ["A_TR"])
                for (X, xk, Xtm, xtk, bq, hf, ce) in trs:
                    for h in range(4):
                        src = vb[:, h, ts_] if X is None else X[:, h, :]
                        k.mm(HB(bq, hf)[:, h, :], src, id64, r=[xk, "identb"], w=[hk(bq, hf)])
                for (X, xk, Xtm, xtk, bq, hf, ce) in trs:
                    k.copy(ce, Xtm[:], HB(bq, hf), r=[hk(bq, hf)], w=[xtk])
                if CH < 3:
                    continue
                specs = ((Bt, "Bt", At, "At", 0, 0, PT[0], "PT0", 0), (At, "At", Bt, "Bt", 0, 1, Pj[0], "Pj0", 2),
                         (Kt, "Kt", At, "At", 1, 0, LakT, "LakT", 0), (Bt, "Bt", Rt, "Rt", 1, 1, MrbT, "MrbT", 1),
                         (Kt, "Kt", Rt, "Rt", 2, 0, MrkT, "MrkT", 1))
                for (La, lk, Ra, rk_, bq, hf, dst, dk, mi) in specs:
                    for h in range(4):
                        k.mm(HB(bq, hf)[:, h, :], La[:, h, :], Ra[:, h, :], r=[lk, rk_], w=[hk(bq, hf)])
                for (La, lk, Ra, rk_, bq, hf, dst, dk, mi) in specs:
                    k.tt("dve", dst[:], HB(bq, hf), m64[:, mi, :].unsqueeze(1).to_broadcast([64, 4, 64]), ALU.mult,
                         r=[hk(bq, hf), "m64"], w=[dk])
                if CH < 4:
                    continue
                for h in range(4):
                    k.mm(HB(4, 0)[:, h, :], At[:, h, :], Tb[:, h, :], start=True, stop=False, r=["At", "Tb"], w=[hk(4, 0)])
                    k.mm(HB(4, 0)[:, h, :], LakT[:, h, :], Vtm[:, h, :], start=False, stop=True, r=["LakT", "Vtm"], w=[hk(4, 0)])
                k.copy("dve", Ub[:], HB(4, 0), r=[hk(4, 0)], w=["Ub"])
                k.copy("act", U[:], HB(4, 0), r=[hk(4, 0)], w=["U"])
                if CH < 5:
                    continue
                for j in range(6):
                    cur, nxt = j % 2, (j + 1) % 2
                    for h in range(4):
                        k.mm(HB(4, 1)[:, h, :], PT[cur][:, h, :], Ub[:, h, :], r=["PT%d" % cur, "Ub"], w=[hk(4, 1)])
                    k.tt("dve", Ub[:], U[:], HB(4, 1), ALU.add, r=["U", hk(4, 1)], w=["Ub"])
                    if j < 5:
                        k.tt("dve", U[:], U[:], HB(4, 1), ALU.add, r=["U", hk(4, 1)], w=["U"])
                    if j < 5:
                        for h in range(4):
                            k.mm(HB(5, 0)[:, h, :], PT[cur][:, h, :], Pj[cur][:, h, :], r=["PT%d" % cur, "Pj%d" % cur], w=[hk(5, 0)])
                        for h in range(4):
                            k.mm(HB(5, 1)[:, h, :], Pj[cur][:, h, :], PT[cur][:, h, :], r=["PT%d" % cur, "Pj%d" % cur], w=[hk(5, 1)])
                        k.copy("act", Pj[nxt][:], HB(5, 0), r=[hk(5, 0)], w=["Pj%d" % nxt])
                        k.copy("act", PT[nxt][:], HB(5, 1), r=[hk(5, 1)], w=["PT%d" % nxt])
                if CH < 6:
                    continue
                for h in range(4):
                    k.mm(HB(6, 0)[:, h, :], Tb[:, h, :], Rt[:, h, :], start=True, stop=False, r=["Tb", "Rt"], w=[hk(6, 0)])
                    k.mm(HB(6, 0)[:, h, :], Ub[:, h, :], MrbT[:, h, :], start=False, stop=False, r=["Ub", "MrbT"], w=[hk(6, 0)])
                    k.mm(HB(6, 0)[:, h, :], Vtm[:, h, :], MrkT[:, h, :], start=False, stop=True, r=["Vtm", "MrkT"], w=[hk(6, 0)])
                k.copy("act", Yblk[:, :, ts_], HB(6, 0), r=[hk(6, 0)], w=["Yblk"])
                if CH < 7:
                    continue
                for h in range(4):
                    k.mm(HB(7, 0)[:, h, :], Btm[:, h, :], Ub[:, h, :], start=True, stop=False, r=["Btm", "Ub"], w=[hk(7, 0)])
                    k.mm(HB(7, 0)[:, h, :], Ktm[:, h, :], Vtm[:, h, :], start=False, stop=True, r=["Ktm", "Vtm"], w=[hk(7, 0)])
                k.tt("dve", tS[:], HB(7, 0), Tst[:], ALU.add, r=[hk(7, 0), "Tst"], w=["tS"])
                k.tt("dve", Tb[:], tS[:], eG[:, :, 63:64].to_broadcast([64, 4, 64]), ALU.mult, r=["tS", "eG"], w=["Tb"])
                k.tt("pool", Tst[:], tS[:], eG[:, :, 63:64].to_broadcast([64, 4, 64]), ALU.mult, r=["tS", "eG"], w=["Tst"])

            if int(os.environ.get("A_STOP", "9")) <= 3:
                continue
            for h in range(4):
                k.mm(banks[h][0:64, :], ones64[:], Yblk[:, h, :], r=["ones64", "Yblk"], w=fb(h))
                k.stt("dve", t1[:, h, :], banks[h][0:64, :], -1.0 / 64, Yblk[:, h, :], ALU.mult, ALU.add, r=fb(h) + ["Yblk"], w=["t1"])
            k.tt("pool", t2[:], t1[:], t1[:], ALU.mult, r=["t1"], w=["t2"])
            for h in range(4):
                k.mm(banks[4 + h][0:64, :], ones64[:], t2[:, h, :], r=["ones64", "t2"], w=fb(4 + h))
            for h in range(4):
                k.act(t2[:, h, :], banks[4 + h][0:64, :], AF.Sqrt, bias=64e-5, scale=1.0 / 64, r=fb(4 + h), w=["t2"])
            k.recip(t2[:], t2[:], r=["t2"], w=["t2"])
            k.tt("pool", t1[:], t1[:], t2[:], ALU.mult, r=["t1", "t2"], w=["t1"])
            k.tt("dve", t1[:], t1[:], bc(5), ALU.mult, r=["t1", "prm"], w=["t1"])
            k.tt("pool", t1[:], t1[:], bc(6), ALU.add, r=["t1", "prm"], w=["t1"])
            k.tt("pool", t1[:], t1[:], bonus[:], ALU.add, r=["t1", "bonus"], w=["t1"])
            k.tt("dve", yo[:], t1[:], gt[:], ALU.mult, r=["t1", "gt"], w=["yo"])
            if not G.y_in:
                k.dma(R.yTA[:, tsl].rearrange("(h v) t -> v h t", v=64), yo[:], r=["yo"], w=[("yA", g)])
        P.barrier()


def attn_common(G, s, Vsrc, mix):
    k, R, I = G.k, G.R, G.I
    sb = lambda name, shape, dt=F32: G.sb(s, name, shape, dt)
    A = Ctx()
    A.kT = [sb("kT%d" % i, [64, S], BF16) for i in range(2)]
    A.V = sb("V", [128, 32, 260], BF16)
    k.dma(A.V[:], Vsrc.rearrange("(kt p) c -> p kt c", p=128), r=[(mix + "v", g) for g in range(NG)], w=["V"])
    A.Pm = [sb("Pm%d" % i, [128, 512], BF16) for i in range(3)]
    A.rec = [sb("rec%d" % i, [128, 8]) for i in range(2)]
    A.ob = [sb("ob%d" % i, [128, 4, 64], BF16) for i in range(2)]
    return A


def phase_D(G, l):
    nc, P, k, I, R = G.nc, G.P, G.k, G.I, G.R
    banks = G.banks
    with ExitStack() as s:
        sb = lambda name, shape, dt=F32: G.sb(s, name, shape, dt)
        A = attn_common(G, s, R.vD, "D")
        Fk = sb("Fk", [128, 4, 32])
        Frow = sb("Frow", [4, S])
        sel = sb("sel", [4, 4, 128])
        negm = sb("negm", [128, 128])
        qT = [sb("qT%d" % i, [64, 512], BF16) for i in range(2)]
        FqB = [sb("FqB%d" % i, [128, 512]) for i in range(2)]
        FqD = [sb("FqD%d" % i, [128, 512]) for i in range(2)]
        tb = [sb("tb%d" % i, [128, 512]) for i in range(3)]
        allF = [("Ffm", g) for g in range(NG)]
        for h in range(4):
            k.dma(Fk[:, h, :], R.Ffm[h].rearrange("(kt p) -> p kt", p=128), r=allF, w=["Fk"], slow=True)
        k.dma(Frow[:], R.Ffm, r=allF, w=["Frow"])
        k.dma(sel[:], I["sel"], w=["sel"])
        k.dma(negm[:], I["negmask"], w=["negm"])
        allqk = [("Dqk", g) for g in range(NG)]
        for h in range(4):
            kt_ = A.kT[h % 2]
            kk = "kT%d" % (h % 2)
            k.dma(kt_[:], R.kTD[h * 64:(h + 1) * 64, :], r=allqk, w=[kk])
            for g in range(NG):
                i = k.rot("Dq", 2)
                k.dma(qT[i][:], R.qTD[h * 64:(h + 1) * 64, g * TG:(g + 1) * TG], r=allqk, w=["qT%d" % i])
                k.mm(banks[5][:], sel[:, h, :], Frow[:, g * TG:(g + 1) * TG], r=["sel", "Frow"], w=["bank5"])
                k.copy("act", FqB[i][:], banks[5][:], r=["bank5"], w=["FqB%d" % i])
                k.tt("pool", FqD[i][:].rearrange("p (a b) -> p a b", a=4), FqB[i][:].rearrange("p (a b) -> p a b", a=4),
                     negm[:].unsqueeze(1).to_broadcast([128, 4, 128]), ALU.add, r=["FqB%d" % i, "negm"], w=["FqD%d" % i])
                ob_ = 3 + k.rot("DO", 2)
                O = banks[ob_][:, 0:260].rearrange("p (a e) -> p a e", a=4)
                def d_stage1(kt):
                    m = kt - 4 * g
                    c0 = max(m, 0) * 128
                    N = 512 - c0
                    sbk = k.rot("Ds", 3)
                    k.mm(banks[sbk][:, 0:N], kt_[:, kt * 128:(kt + 1) * 128], qT[i][:, c0:512], r=[kk, "qT%d" % i], w=["bank%d" % sbk])
                    ti = k.rot("Dt", 3)
                    if m < 0:
                        k.stt("dve", tb[ti][:], banks[sbk][:], Fk[:, h, kt:kt + 1], FqB[i][:], ALU.subtract, ALU.add,
                              r=["bank%d" % sbk, "Fk", "FqB%d" % i], w=["tb%d" % ti])
                    else:
                        k.stt("dve", tb[ti][:, 0:128], banks[sbk][:, 0:128], Fk[:, h, kt:kt + 1], FqD[i][:, c0:c0 + 128],
                              ALU.subtract, ALU.add, r=["bank%d" % sbk, "Fk", "FqD%d" % i], w=["tb%d" % ti])
                        if N > 128:
                            k.stt("dve", tb[ti][:, 128:N], banks[sbk][:, 128:N], Fk[:, h, kt:kt + 1], FqB[i][:, c0 + 128:512],
                                  ALU.subtract, ALU.add, r=["bank%d" % sbk, "Fk", "FqB%d" % i], w=["tb%d" % ti])
                    k.act(A.Pm[ti][:, 0:N], tb[ti][:, 0:N], AF.Exp, r=["tb%d" % ti], w=["Pm%d" % ti])
                    return (kt, m, c0, ti)

                def d_stage2(st):
                    kt, m, c0, ti = st
                    for jq in range(max(m, 0), 4):
                        k.mm(O[:, jq, :], A.Pm[ti][:, jq * 128 - c0: jq * 128 - c0 + 128], A.V[:, kt, h * 65:(h + 1) * 65],
                             start=(kt == 0 and jq == 0), stop=(kt == 4 * g + jq), r=["Pm%d" % ti, "V"], w=["bank%d" % ob_], sgc=True)
                pend = []
                for kt in range(4 * g + 4):
                    pend.append(d_stage1(kt))
                    if len(pend) > 2:
                        d_stage2(pend.pop(0))
                while pend:
                    d_stage2(pend.pop(0))
                ri = k.rot("Drec", 2)
                k.recip(A.rec[ri][:, 0:4], O[:, :, 64], r=["bank%d" % ob_], w=["rec%d" % ri])
                k.tt("dve", A.ob[ri][:], O[:, :, 0:64], A.rec[ri][:, 0:4].unsqueeze(2).to_broadcast([128, 4, 64]), ALU.mult,
                     r=["bank%d" % ob_, "rec%d" % ri], w=["ob%d" % ri])
                k.dma(R.yscr[g * TG:(g + 1) * TG, 512 + h * 64: 512 + (h + 1) * 64].rearrange("(a p) e -> p a e", p=128),
                      A.ob[ri][:], r=["ob%d" % ri], w=[("yD", g)], slow=True)
        P.barrier()


def phase_B(G, l):
    nc, P, k, I, R = G.nc, G.P, G.k, G.I, G.R
    banks = G.banks
    lambda_init = 0.8 - 0.6 * math.exp(-0.3 * l)
    with ExitStack() as s:
        sb = lambda name, shape, dt=F32: G.sb(s, name, shape, dt)
        A = attn_common(G, s, R.vB, "B")
        cmf = sb("cmf", [128, 128])
        cm = sb("cm", [128, 128], BF16)
        qp = [[sb("qp%d_%d" % (i, mp), [64, 512], BF16) for mp in range(2)] for i in range(2)]
        lamv = sb("lamv", [128, 4, 32])
        lamw = sb("lamw", [128, 2, 32])
        lams = sb("lams", [128, 4])
        subg = sb("subg", [128, 64])
        o1 = [sb("o1_%d" % i, [128, 4, 64]) for i in range(2)]
        o2 = [sb("o2_%d" % i, [128, 4, 64]) for i in range(2)]
        k.dma(cmf[:], I["cmask"], w=["cmf"])
        k.copy("dve", cm[:], cmf[:], r=["cmf"], w=["cm"])
        for i in range(2):
            for mp in range(2):
                k.memset("pool", qp[i][mp][:], 0.0, w=["qp%d" % i])
        for n, nm in enumerate(("df_lam_q1", "df_lam_k1", "df_lam_q2", "df_lam_k2")):
            bcast_load(G, lamv[:, n, :], I[nm][l:l + 1, :], 32, ["lamv"])
        bcast_load(G, subg[:], I["df_sub_g"][l:l + 1, :], 64, ["subg"])
        k.ts("dve", subg[:], subg[:], 1.0 - lambda_init, ALU.mult, r=["subg"], w=["subg"])
        k.tt("dve", lamw[:, 0, :], lamv[:, 0, :], lamv[:, 1, :], ALU.mult, r=["lamv"], w=["lamw"])
        k.tt("dve", lamw[:, 1, :], lamv[:, 2, :], lamv[:, 3, :], ALU.mult, r=["lamv"], w=["lamw"])
        k.red("dve", lams[:, 0:2], lamw[:], r=["lamw"], w=["lams"])
        k.act(lams[:, 0:2], lams[:, 0:2], AF.Exp, r=["lams"], w=["lams"])
        k.ts("dve", lams[:, 2:3], lams[:, 1:2], -lambda_init, ALU.add, r=["lams"], w=["lams"])
        k.tt("dve", lams[:, 3:4], lams[:, 2:3], lams[:, 0:1], ALU.subtract, r=["lams"], w=["lams"])
        allqk = [("Bqk", g) for g in range(NG)]
        for h in range(4):
            kt_ = A.kT[h % 2]
            kk = "kT%d" % (h % 2)
            k.dma(kt_[:], R.kTB[h * 64:(h + 1) * 64, :], r=allqk, w=[kk])
            for g in range(NG):
                i = k.rot("Bq", 2)
                k.dma(qp[i][0][0:32, :], R.qTB[h * 64:h * 64 + 32, g * TG:(g + 1) * TG], r=allqk, w=["qp%d" % i])
                k.dma(qp[i][1][32:64, :], R.qTB[h * 64 + 32:h * 64 + 64, g * TG:(g + 1) * TG], r=allqk, w=["qp%d" % i])
                oi = k.rot("BO", 2)
                Os = [banks[3 + oi][:, 0:260].rearrange("p (a e) -> p a e", a=4),
                      banks[5 + oi][:, 0:260].rearrange("p (a e) -> p a e", a=4)]
                obk = ["bank%d" % (3 + oi), "bank%d" % (5 + oi)]
                def b_stage1(kt, mp):
                    m = kt - 4 * g
                    c0 = max(m, 0) * 128
                    N = 512 - c0
                    sbk = k.rot("Bs", 3)
                    k.mm(banks[sbk][:, 0:N], kt_[:, kt * 128:(kt + 1) * 128], qp[i][mp][:, c0:512], r=[kk, "qp%d" % i],
                         w=["bank%d" % sbk])
                    ti = k.rot("Bt", 3)
                    k.act(A.Pm[ti][:, 0:N], banks[sbk][:, 0:N], AF.Exp, r=["bank%d" % sbk], w=["Pm%d" % ti])
                    if m >= 0:
                        k.tt("dve", A.Pm[ti][:, 0:128], A.Pm[ti][:, 0:128], cm[:], ALU.mult, r=["Pm%d" % ti, "cm"], w=["Pm%d" % ti])
                    return (kt, mp, m, c0, ti)

                def b_stage2(st):
                    kt, mp, m, c0, ti = st
                    for jq in range(max(m, 0), 4):
                        k.mm(Os[mp][:, jq, :], A.Pm[ti][:, jq * 128 - c0: jq * 128 - c0 + 128], A.V[:, kt, h * 65:(h + 1) * 65],
                             start=(kt == 0 and jq == 0), stop=(kt == 4 * g + jq), r=["Pm%d" % ti, "V"], w=[obk[mp]], sgc=True)
                pend = []
                for kt in range(4 * g + 4):
                    for mp in range(2):
                        pend.append(b_stage1(kt, mp))
                        if len(pend) > 2:
                            b_stage2(pend.pop(0))
                while pend:
                    b_stage2(pend.pop(0))
                ri = k.rot("Brec", 2)
                rc = A.rec[ri]
                rk = "rec%d" % ri
                k.recip(rc[:, 0:4], Os[0][:, :, 64], r=[obk[0]], w=[rk])
                k.recip(rc[:, 4:8], Os[1][:, :, 64], r=[obk[1]], w=[rk])
                k.ts("dve", rc[:, 4:8], rc[:, 4:8], lams[:, 3:4], ALU.mult, r=[rk, "lams"], w=[rk])
                k.tt("dve", o1[ri][:], Os[0][:, :, 0:64], rc[:, 0:4].unsqueeze(2).to_broadcast([128, 4, 64]), ALU.mult,
                     r=[obk[0], rk], w=["o1_%d" % ri])
                k.tt("dve", o2[ri][:], Os[1][:, :, 0:64], rc[:, 4:8].unsqueeze(2).to_broadcast([128, 4, 64]), ALU.mult,
                     r=[obk[1], rk], w=["o2_%d" % ri])
                k.tt("pool", o1[ri][:], o1[ri][:], o2[ri][:], ALU.add, r=["o1_%d" % ri, "o2_%d" % ri], w=["o1_%d" % ri])
                k.tt("pool", o2[ri][:], o1[ri][:], o1[ri][:], ALU.mult, r=["o1_%d" % ri], w=["o2_%d" % ri])
                k.red("dve", rc[:, 0:4], o2[ri][:], r=["o2_%d" % ri], w=[rk])
                k.act(rc[:, 0:4], rc[:, 0:4], AF.Sqrt, bias=EPS, scale=1.0 / 64, r=[rk], w=[rk])
                k.recip(rc[:, 0:4], rc[:, 0:4], r=[rk], w=[rk])
                k.tt("dve", o1[ri][:], o1[ri][:], rc[:, 0:4].unsqueeze(2).to_broadcast([128, 4, 64]), ALU.mult,
                     r=["o1_%d" % ri, rk], w=["o1_%d" % ri])
                k.tt("pool", A.ob[ri][:], o1[ri][:], subg[:].unsqueeze(1).to_broadcast([128, 4, 64]), ALU.mult,
                     r=["o1_%d" % ri, "subg"], w=["ob%d" % ri])
                k.dma(R.yscr[g * TG:(g + 1) * TG, h * 64:(h + 1) * 64].rearrange("(a p) e -> p a e", p=128),
                      A.ob[ri][:], r=["ob%d" % ri], w=[("yB", g)], slow=True)
        P.barrier()


_CACHE = {}


def kernel(**inputs):
    if "prog" not in _CACHE:
        _CACHE["prog"] = build()
    nc, P = _CACHE["prog"]
    consts = make_consts()
    weights = {k: np.ascontiguousarray(np.asarray(inputs[k], dtype=np.float32)) for k in WEIGHT_SHAPES}
    x = np.asarray(inputs["x"], dtype=np.float32)
    c = np.asarray(inputs["c"], dtype=np.float32)
    in_maps = []
    for b in range(8):
        m = {"x": np.ascontiguousarray(x[b]), "c": np.ascontiguousarray(c[b:b + 1])}
        m.update(weights)
        m.update(consts)
        in_maps.append(m)
    res = run_bass_kernel_spmd(nc, in_maps, core_ids=list(range(8)))
    return np.stack([np.asarray(r["out"], dtype=np.float32) for r in res.results], axis=0)
```

```python
import math
import os
import numpy as np
import concourse.bass as bass
import concourse.mybir as mybir
from concourse.bass_utils import run_bass_kernel_spmd
from contextlib import ExitStack

F32 = mybir.dt.float32
BF16 = mybir.dt.bfloat16
AF = mybir.ActivationFunctionType
ALU = mybir.AluOpType
AX = mybir.AxisListType

S = 4096
D = 1024
L = 4
NIN = 2948
DFF = 2816
NG = 8
TG = 512
EPS = 1e-6
ALPHA = math.exp(-0.5)

ENGS = ("pe", "act", "dve", "pool", "sp")
EPOCH = 30000


class Prog:
    def __init__(self, nc, es):
        self.nc = nc
        self.es = es
        self.q = {e: [] for e in ENGS}
        self.cnt = {e: 0 for e in ENGS}
        self.epoch = {e: 0 for e in ENGS}
        self.sems = {}
        self.seen = {e: {} for e in ENGS}
        self.res_w = {}
        self.res_r = {}
        self.dma_val = {}
        self.n_inst = 0
        self.rr = 0

    def _sem(self, key):
        if key not in self.sems:
            self.sems[key] = self.es.enter_context(self.nc.semaphore("s_" + "_".join(str(k) for k in key)))
        return self.sems[key]

    def _deps(self, eng, reads, writes, extra=()):
        need = {}

        def add(ev):
            if ev is None:
                return
            k, v = ev
            if eng == "pe" and k[0] == "pe":
                return
            if need.get(k, 0) < v:
                need[k] = v
        for r in reads:
            add(self.res_w.get(r))
        for w in writes:
            add(self.res_w.get(w))
            for ev in self.res_r.get(w, ()):
                add(ev)
        for ev in extra:
            add(ev)
        waits = []
        for k, v in need.items():
            if self.seen[eng].get(k, 0) >= v:
                continue
            self.seen[eng][k] = v
            waits.append((k, v))
        return waits

    def _commit(self, ev, reads, writes):
        for r in reads:
            lst = self.res_r.setdefault(r, [])
            lst.append(ev)
            if len(lst) > 64:
                mx = {}
                for k, v in lst:
                    if mx.get(k, 0) < v:
                        mx[k] = v
                self.res_r[r] = list(mx.items())
        for w in writes:
            self.res_w[w] = ev
            self.res_r[w] = []

    @staticmethod
    def _is_psum(r):
        return (isinstance(r, str) and r.startswith("bank")) or (isinstance(r, tuple) and r[0] == "hb")

    def op(self, eng, fn, reads=(), writes=()):
        pr = [r for r in reads if self._is_psum(r)]
        if pr:
            writes = list(writes) + pr
        waits = self._deps(eng, reads, writes)
        if self.cnt[eng] >= EPOCH:
            self.epoch[eng] += 1
            self.cnt[eng] = 0
        self.cnt[eng] += 1
        key = (eng, self.epoch[eng])
        ev = (key, self.cnt[eng])
        self.q[eng].append((waits, fn, key, 1))
        self._commit(ev, reads, writes)
        self.n_inst += 1
        return ev

    def dma(self, queue, pairs, reads=(), writes=(), sem=None):
        if sem is None:
            sem = ("dma", "rr%d" % (self.rr % 20))
            self.rr += 1
        key = sem
        prev = self.dma_val.get(key, 0)
        extra = [(key, prev)] if prev > 0 else []
        waits = self._deps(queue, reads, writes, extra)
        val = prev
        for i, pr in enumerate(pairs):
            out_ap, in_ap = pr[0], pr[1]
            kw = pr[2] if len(pr) > 2 else {}
            val += 16

            def fn(e, out_ap=out_ap, in_ap=in_ap, kw=kw):
                return e.dma_start(out=out_ap, in_=in_ap, **kw)
            self.q[queue].append((waits if i == 0 else [], fn, key, 16))
            self.n_inst += 1
        self.dma_val[key] = val
        ev = (key, val)
        self._commit(ev, reads, writes)
        return ev

    def barrier(self):
        evs = []
        for e in ENGS:
            for ep in range(self.epoch[e] + 1):
                k = (e, ep)
                v = self.cnt[e] if ep == self.epoch[e] else EPOCH
                if v > 0:
                    evs.append((k, v))
        for k, v in self.dma_val.items():
            evs.append((k, v))
        for e in ENGS:
            waits = []
            for k, v in evs:
                if self.seen[e].get(k, 0) < v:
                    self.seen[e][k] = v
                    waits.append((k, v))
            if waits:
                self.q[e].append((waits, None, None, 0))
        self.res_w = {}
        self.res_r = {}

    def emit(self):
        nc = self.nc
        for e in ENGS:
            for (waits, fn, key, inc) in self.q[e]:
                for k, v in waits:
                    self._sem(k)
                if key is not None:
                    self._sem(key)
        block = self.es.enter_context(nc.Block())
        engmap = {"pe": block.tensor, "act": block.scalar, "dve": block.vector, "pool": block.gpsimd,
                  "sp": block.sync}
        for e in ENGS:
            items = self.q[e]

            def body(eng, items=items):
                for (waits, fn, key, inc) in items:
                    for k, v in waits:
                        eng.wait_ge(self.sems[k], v)
                    if fn is not None:
                        ins = fn(eng)
                        ins.then_inc(self.sems[key], inc)
            engmap[e](body)


class K:
    def __init__(self, P):
        self.P = P
        self._rot = {}

    def rot(self, name, n):
        i = self._rot.get(name, 0)
        self._rot[name] = i + 1
        return i % n

    def mm(self, out, lhsT, rhs, start=True, stop=True, r=(), w=(), sgc=False):
        if sgc:
            return self.P.op("pe", lambda e: e.matmul(out, lhsT=lhsT, rhs=rhs, start=start, stop=stop, skip_group_check=True), r, w)
        return self.P.op("pe", lambda e: e.matmul(out, lhsT=lhsT, rhs=rhs, start=start, stop=stop), r, w)

    def tr(self, out, in_, ident, r=(), w=()):
        return self.P.op("pe", lambda e: e.transpose(out=out, in_=in_, identity=ident), r, w)

    def act(self, out, in_, func, bias=None, scale=None, accum_out=None, r=(), w=(), eng="act"):
        kw = {}
        if bias is not None:
            kw["bias"] = bias
        if scale is not None:
            kw["scale"] = scale
        if accum_out is not None:
            kw["accum_out"] = accum_out
        return self.P.op("act", lambda e: e.activation(out=out, in_=in_, func=func, **kw), r, w)

    def copy(self, eng, out, in_, r=(), w=()):
        if eng == "act":
            return self.P.op("act", lambda e: e.copy(out=out, in_=in_), r, w)
        return self.P.op(eng, lambda e: e.tensor_copy(out=out, in_=in_), r, w)

    def tt(self, eng, out, in0, in1, op, r=(), w=()):
        return self.P.op(eng, lambda e: e.tensor_tensor(out=out, in0=in0, in1=in1, op=op), r, w)

    def ts(self, eng, out, in0, s1, op0, s2=None, op1=None, r=(), w=()):
        if op1 is None:
            return self.P.op(eng, lambda e: e.tensor_scalar(out=out, in0=in0, scalar1=s1, scalar2=None, op0=op0), r, w)
        return self.P.op(eng, lambda e: e.tensor_scalar(out=out, in0=in0, scalar1=s1, scalar2=s2, op0=op0, op1=op1), r, w)

    def stt(self, eng, out, in0, scalar, in1, op0, op1, r=(), w=()):
        return self.P.op(eng, lambda e: e.scalar_tensor_tensor(out=out, in0=in0, scalar=scalar, in1=in1, op0=op0, op1=op1), r, w)

    def red(self, eng, out, in_, op=ALU.add, r=(), w=()):
        return self.P.op(eng, lambda e: e.tensor_reduce(out=out, in_=in_, axis=AX.X, op=op), r, w)

    def recip(self, out, in_, r=(), w=()):
        return self.P.op("dve", lambda e: e.reciprocal(out=out, in_=in_), r, w)

    def memset(self, eng, ap, val, w=()):
        return self.P.op(eng, lambda e: e.memset(ap, val), (), w)

    def scan(self, out, d0, d1, initial, op0, op1, r=(), w=()):
        return self.P.op("dve", lambda e: e.tensor_tensor_scan(out=out, data0=d0, data1=d1, initial=initial, op0=op0, op1=op1), r, w)

    def dma(self, out, in_, r=(), w=(), q="sp", slow=False, sem=None):
        kw = {"allow_slow_non_contiguous": True} if slow else {}
        return self.P.dma(q, [(out, in_, kw)], r, w, sem=sem)


def make_consts():
    c = {}
    c["ident"] = np.eye(128, dtype=np.float32)
    bo64 = np.zeros((128, 128), np.float32)
    bo64[:64, :64] = 1
    bo64[64:, 64:] = 1
    c["bo64"] = bo64
    bo32 = np.zeros((128, 128), np.float32)
    for i in range(4):
        bo32[i * 32:(i + 1) * 32, i * 32:(i + 1) * 32] = 1
    c["bo32"] = bo32
    prot = np.zeros((128, 128), np.float32)
    for b in range(4):
        for d in range(16):
            prot[b * 32 + d + 16, b * 32 + d] = -1.0
            prot[b * 32 + d, b * 32 + d + 16] = 1.0
    c["prot"] = prot
    inv = 1.0 / (10000.0 ** (np.arange(0, 32, 2, dtype=np.float32) / 32.0))
    ang = np.arange(S, dtype=np.float32)[:, None] * inv[None, :]
    cos = np.cos(ang).astype(np.float32).T
    sin = np.sin(ang).astype(np.float32).T
    c["cosT"] = np.ascontiguousarray(np.tile(cos, (8, 1)))
    c["sinT"] = np.ascontiguousarray(np.tile(sin, (8, 1)))
    k = np.arange(128)[:, None]
    q = np.arange(128)[None, :]
    c["negmask"] = np.where(k > q, -1e30, 0.0).astype(np.float32)
    c["cmask"] = ((k // 64) <= (q // 64)).astype(np.float32)
    c["triu"] = (k <= q).astype(np.float32)
    k6 = np.arange(64)[:, None]
    q6 = np.arange(64)[None, :]
    m64 = np.zeros((64, 3, 64), np.float32)
    m64[:, 0, :] = (k6 < q6)
    m64[:, 1, :] = (k6 <= q6)
    m64[:, 2, :] = (k6 > q6)
    c["m64"] = m64
    sel = np.zeros((4, 4, 128), np.float32)
    for h in range(4):
        sel[h, h, :] = 1.0
    c["sel"] = sel.transpose(1, 0, 2).copy()
    return c


CONST_SHAPES = {"ident": [128, 128], "bo64": [128, 128], "bo32": [128, 128], "prot": [128, 128],
                "cosT": [128, S], "sinT": [128, S], "negmask": [128, 128], "cmask": [128, 128],
                "triu": [128, 128], "m64": [64, 3, 64], "sel": [4, 4, 128]}

WEIGHT_SHAPES = {
    'ada_w': [L, D, 6 * D], 'ada_b': [L, 6 * D], 'norm1_g': [L, D], 'norm2_g': [L, D],
    'w_in': [L, D, NIN], 'w_out': [L, D, D],
    'rw_mu': [L, 896], 'rw_w0': [L, 256], 'rw_w_up': [L, 32, 256], 'rw_a0': [L, 256], 'rw_a_up': [L, 32, 256],
    'rw_g_up': [L, 64, 256], 'rw_k_k': [L, 256], 'rw_k_a': [L, 256], 'rw_r_k': [L, 4, 64],
    'rw_ln_g': [L, 256], 'rw_ln_b': [L, 256],
    'df_lam_q1': [L, 32], 'df_lam_k1': [L, 32], 'df_lam_q2': [L, 32], 'df_lam_k2': [L, 32],
    'df_q_g': [L, 32], 'df_k_g': [L, 32], 'df_sub_g': [L, 64],
    'sg_w': [L, 4, 128, 128], 'sg_b': [L, 4, 128], 'sg_ln_g': [L, 256], 'sg_ln_b': [L, 256],
    'fx_q_g': [L, 64], 'fx_k_g': [L, 64], 'fx_f_b': [L, 4],
    'ffn_up': [L, D, 2 * DFF], 'ffn_conv': [L, 3, 2 * DFF], 'ffn_conv_b': [L, 2 * DFF], 'ffn_down': [L, DFF, D],
}


class Ctx:
    pass


def build(layers=(0, 1, 2, 3), phases=("P1", "A", "B", "D", "P3"), debug=False, y_in=False):
    nc = bass.Bass("TRN2", target_bir_lowering=False)
    dkind = "ExternalOutput" if debug else "Internal"
    I = {}

    def din(name, shape):
        I[name] = nc.dram_tensor(name, list(shape), F32, kind="ExternalInput").ap()
    din("x", [S, D])
    din("c", [1, D])
    for k, shp in WEIGHT_SHAPES.items():
        din(k, shp)
    for k, shp in CONST_SHAPES.items():
        din(k, shp)
    out = nc.dram_tensor("out", [S, D], F32, kind="ExternalOutput").ap()

    def dscr(name, shape, dt, kind=None):
        return nc.dram_tensor(name, list(shape), dt, kind=kind or dkind).ap()
    R = Ctx()
    R.xres = dscr("xres", [S, D], F32)
    R.pmA = dscr("pmA", [896, S], F32)
    R.qTB = dscr("qTB", [256, S], BF16)
    R.kTB = dscr("kTB", [256, S], BF16)
    R.vB = dscr("vB", [S, 260], BF16)
    R.qTD = dscr("qTD", [256, S], BF16)
    R.kTD = dscr("kTD", [256, S], BF16)
    R.vD = dscr("vD", [S, 260], BF16)
    R.Ffm = dscr("Ffm", [4, S], F32)
    if y_in:
        R.yscr = nc.dram_tensor("yscr_in", [S, 768], F32, kind="ExternalInput").ap()
        R.yTA = nc.dram_tensor("yTA_in", [256, S], F32, kind="ExternalInput").ap()
    else:
        R.yscr = dscr("yscr", [S, 768], BF16)
        R.yTA = dscr("yTA", [256, S], BF16)
    R.upbf = dscr("upbf", [L, 11, 128, 8 * 512], BF16, kind="Internal")
    R.dnbf = dscr("dnbf", [L, 128, 22 * 1024], BF16, kind="Internal")

    with ExitStack() as es:
        P = Prog(nc, es)
        k = K(P)
        G = Ctx()
        G.nc, G.P, G.k, G.I, G.R, G.out = nc, P, k, I, R, out
        G.y_in = y_in
        G.dbg_x1 = nc.dram_tensor("dbg_x1", [S, D], F32, kind="ExternalOutput").ap() if debug else None

        uid = [0]

        def sb(stack, name, shape, dt=F32):
            uid[0] += 1
            return stack.enter_context(nc.sbuf_tensor("%s_u%d" % (name, uid[0]), list(shape), dt))
        G.sb = sb
        G.banks = [es.enter_context(nc.psum_tensor("bank%d" % i, [128, 512], F32)) for i in range(8)]
        G.ident = sb(es, "ident", [128, 128])
        G.identb = sb(es, "identb", [128, 128], BF16)
        G.ones_row = sb(es, "ones_row", [1, 128])
        G.condB = sb(es, "condB", [128, 8, 128])
        G.modB = sb(es, "modB", [128, 6 * D])
        k.dma(G.ident[:], I["ident"], w=["ident"])
        k.copy("dve", G.identb[:], G.ident[:], r=["ident"], w=["identb"])
        k.memset("dve", G.ones_row[:], 1.0, w=["ones_row"])
        with ExitStack() as s0:
            cT = sb(s0, "cT", [128, 8])
            cS = sb(s0, "cS", [128, 8])
            k.dma(cT[:], I["c"].rearrange("o (kc p) -> p (o kc)", p=128), w=["cT"], slow=True)
            k.act(cS[:], cT[:], AF.Silu, r=["cT"], w=["cS"])
            k.copy("dve", G.condB[:], cS[:].unsqueeze(2).to_broadcast([128, 8, 128]), r=["cS"], w=["condB"])
            P.barrier()

        for li, l in enumerate(layers):
            xsrc = I["x"] if li == 0 else R.xres
            xdst = out if li == len(layers) - 1 else R.xres
            if "P3" in phases:
                prep_ffn(G, l)
            layer_setup(G, l)
            if "P1" in phases:
                phase_p1(G, l, xsrc)
            if "A" in phases:
                phase_A(G, l)
            if "B" in phases:
                phase_B(G, l)
            if "D" in phases:
                phase_D(G, l)
            if "P3" in phases:
                phase_p3(G, l, xsrc, xdst)
        P.barrier()
        P.emit()
    return nc, P


def prep_ffn(G, l):
    nc, P, k, I, R = G.nc, G.P, G.k, G.I, G.R
    with ExitStack() as s:
        st = [G.sb(s, "pf_st%d" % i, [128, 8, 512]) for i in range(2)]
        sbf = [G.sb(s, "pf_bf%d" % i, [128, 8, 512], BF16) for i in range(2)]
        up = I["ffn_up"][l].rearrange("(kc p) n -> p kc n", p=128)
        dn = I["ffn_down"][l].rearrange("(fc p) d -> p fc d", p=128)
        engs = ["dve", "act", "dve", "act", "pool"]
        jobs = []
        for u in range(11):
            jobs.append((up[:, :, u * 512:(u + 1) * 512], R.upbf[l, u].rearrange("p (kc n) -> p kc n", kc=8), 8, 512))
        for u in range(11):
            jobs.append((dn[:, 2 * u:2 * u + 2, :], R.dnbf[l][:, 2 * u * 1024:(2 * u + 2) * 1024].rearrange("p (a d) -> p a d", a=2), 2, 1024))

        def load(i):
            src, dst, a, b = jobs[i]
            k.dma(st[i % 2][:].rearrange("p a b -> p (a b)")[:, 0:a * b].rearrange("p (a b) -> p a b", a=a), src,
                  w=["pf_st%d" % (i % 2)])
        load(0)
        load(1)
        for i in range(len(jobs)):
            src, dst, a, b = jobs[i]
            sv = st[i % 2][:].rearrange("p a b -> p (a b)")[:, 0:a * b]
            bv = sbf[i % 2][:].rearrange("p a b -> p (a b)")[:, 0:a * b]
            k.copy(engs[i % 5], bv, sv, r=["pf_st%d" % (i % 2)], w=["pf_bf%d" % (i % 2)])
            k.dma(dst, bv.rearrange("p (a b) -> p a b", a=a), r=["pf_bf%d" % (i % 2)], w=[("ffnw", l)])
            if i + 2 < len(jobs):
                load(i + 2)
        P.barrier()


def layer_setup(G, l):
    nc, P, k, I, R = G.nc, G.P, G.k, G.I, G.R
    banks = G.banks
    with ExitStack() as s:
        aw = [G.sb(s, "ls_aw%d" % i, [128, 8, 512]) for i in range(2)]
        rows = G.sb(s, "ls_rows", [1, 8 * D])
        k.dma(rows[:, 0:6 * D], I["ada_b"][l:l + 1, :], w=["ls_rows_b"])
        k.dma(rows[:, 6 * D:7 * D], I["norm1_g"][l:l + 1, :], w=["ls_rows_g"])
        k.dma(rows[:, 7 * D:8 * D], I["norm2_g"][l:l + 1, :], w=["ls_rows_g"])
        awv = I["ada_w"][l].rearrange("(kc p) n -> p kc n", p=128)
        for cc in range(12):
            b = cc % 2
            k.dma(aw[b][:], awv[:, :, cc * 512:(cc + 1) * 512], w=["ls_aw%d" % b])
            bk = banks[b]
            for kc in range(8):
                k.mm(bk[:], G.condB[:, kc, :], aw[b][:, kc, :], start=(kc == 0), stop=False,
                     r=["condB", "ls_aw%d" % b], w=["bank%d" % b])
            k.mm(bk[:], G.ones_row[0:1, :], rows[0:1, cc * 512:(cc + 1) * 512], start=False, stop=True,
                 r=["ones_row", "ls_rows_b"], w=["bank%d" % b])
            k.copy("act" if cc % 2 else "dve", G.modB[:, cc * 512:(cc + 1) * 512], bk[:], r=["bank%d" % b], w=["modB"])
        for gi, (goff, slot) in enumerate(((6 * D, 1 * D), (7 * D, 4 * D))):
            for hf in range(2):
                b = 2 + hf
                k.mm(banks[b][:], G.ones_row[0:1, :], rows[0:1, goff + hf * 512: goff + (hf + 1) * 512],
                     r=["ones_row", "ls_rows_g"], w=["bank%d" % b])
                sl = G.modB[:, slot + hf * 512: slot + (hf + 1) * 512]
                k.stt("dve", sl, sl, 1.0, banks[b][:], ALU.add, ALU.mult, r=["modB", "bank%d" % b], w=["modB"])
        P.barrier()


def norm_mod_T(G, xt, goff, shoff, T):
    k = G.k
    banks = G.banks
    for j in range(4):
        k.act(T.tmp[:], xt[:, j, :], AF.Square, r=["xt"], w=["tmp"])
        k.red("dve", T.ss[:, j:j + 1], T.tmp[:], r=["tmp"], w=["ss"])
    k.act(T.rstd[:], T.ss[:], AF.Sqrt, bias=EPS, scale=1.0 / D, r=["ss"], w=["rstd"])
    k.recip(T.rstd[:], T.rstd[:], r=["rstd"], w=["rstd"])
    for j in range(4):
        k.stt("dve", T.tmp[:], xt[:, j, :], T.rstd[:, j:j + 1], G.modB[:, goff:goff + D], ALU.mult, ALU.mult,
              r=["xt", "rstd", "modB"], w=["tmp"])
        k.tt("dve", T.hb[:, j, :], T.tmp[:], G.modB[:, shoff:shoff + D], ALU.add, r=["tmp", "modB"], w=["hb"])
    for j in range(4):
        b = 5 + (j % 2)
        pv = banks[b][:].bitcast(BF16).rearrange("p (a t) -> p a t", a=8)
        for kc in range(8):
            k.tr(pv[:, kc, :], T.hb[:, j, kc * 128:(kc + 1) * 128], G.identb[:], r=["hb", "identb"], w=["bank%d" % b])
        k.copy("act" if j % 2 else "dve", T.hT[:, :, j * 128:(j + 1) * 128], pv, r=["bank%d" % b], w=["hT"])


def bcast_load(G, tile_ap, row_ap, n, w):
    G.k.dma(tile_ap, row_ap.to_broadcast([128, n]), w=w, slow=True)


def phase_p1(G, l, xsrc):
    nc, P, k, I, R = G.nc, G.P, G.k, G.I, G.R
    banks = G.banks
    with ExitStack() as s:
        sb = lambda name, shape, dt=F32: G.sb(s, name, shape, dt)
        T = Ctx()
        w_in = sb("w_in", [128, 8, NIN], BF16)
        stg = [sb("p1_stg%d" % i, [128, 1474]) for i in range(2)]
        xt = sb("xt", [128, 4, D])
        T.sqj = sb("sqj", [128, D], BF16)
        T.ss = sb("ss", [128, 4])
        T.rstd = sb("rstd", [128, 4])
        T.tmp = sb("tmp", [128, D])
        T.hb = sb("hb", [128, 4, D], BF16)
        T.hT = sb("hT", [128, 8, 512], BF16)
        fA = [sb("fA%d" % i, [128, 512]) for i in range(3)]
        fB = [sb("fB%d" % i, [128, 512]) for i in range(3)]
        fC = [sb("fC%d" % i, [128, 512]) for i in range(3)]
        obf = [sb("obf%d" % i, [128, 512], BF16) for i in range(3)]
        xnb = [sb("xnb%d" % i, [128, 512], BF16) for i in range(2)]
        paA = sb("paA", [128, 7, 513])
        pmo = [sb("pmo%d" % i, [128, 512]) for i in range(2)]
        cs = [sb("cos%d" % i, [128, 512]) for i in range(2)]
        sn = [sb("sin%d" % i, [128, 512]) for i in range(2)]
        bo64 = sb("bo64", [128, 128])
        bo32 = sb("bo32", [128, 128])
        protf = sb("protf", [128, 128])
        protb = sb("protb", [128, 128], BF16)
        gcol = sb("gcol", [128, 8])
        mu = sb("mu", [128, 7])
        negfb = sb("negfb", [4, 1])
        ones4 = sb("ones4", [4, 512])
        Fg = [sb("Fg%d" % i, [4, 512]) for i in range(2)]
        f4 = sb("f4", [4, 512])
        vt = [sb("vt%d" % i, [128, 4, 65], BF16) for i in range(4)]
        sgw = sb("sgw", [128, 4, 128])
        WgT = sb("WgT", [128, 4, 128], BF16)
        triu = sb("triu", [128, 128])
        sgbT = sb("sgbT", [128, 4])
        lnCg = sb("lnCg", [128, 256])
        lnCb = sb("lnCb", [128, 256])
        glC = [sb("glC%d" % i, [128, 512]) for i in range(2)]
        stC = [sb("stC%d" % i, [128, 8]) for i in range(2)]
        tmpc = [sb("tmpc%d" % i, [128, 256]) for i in range(2)]
        vnb = [sb("vnb%d" % i, [128, 256], BF16) for i in range(2)]
        ycb = [sb("ycb%d" % i, [128, 256], BF16) for i in range(2)]

        for kc in range(8):
            for hf in range(2):
                i = (kc * 2 + hf) % 2
                k.dma(stg[i][:], I["w_in"][l, kc * 128:(kc + 1) * 128, hf * 1474:(hf + 1) * 1474], w=["p1_stg%d" % i])
                k.copy(("dve", "pool", "act")[(kc * 2 + hf) % 3], w_in[:, kc, hf * 1474:(hf + 1) * 1474], stg[i][:],
                       r=["p1_stg%d" % i], w=["w_in"])
        k.dma(bo64[:], I["bo64"], w=["bo64"])
        k.dma(bo32[:], I["bo32"], w=["bo32"])
        k.dma(protf[:], I["prot"], w=["protf"])
        k.copy("dve", protb[:], protf[:], r=["protf"], w=["protb"])
        k.dma(triu[:], I["triu"], w=["triu"])
        for rep in range(2):
            k.dma(gcol[rep * 64:(rep + 1) * 64, 0:1], I["fx_q_g"][l].rearrange("(d o) -> d o", o=1), w=["gcol"], slow=True)
            k.dma(gcol[rep * 64:(rep + 1) * 64, 1:2], I["fx_k_g"][l].rearrange("(d o) -> d o", o=1), w=["gcol"], slow=True)
        for rep in range(4):
            k.dma(gcol[rep * 32:(rep + 1) * 32, 2:3], I["df_q_g"][l].rearrange("(d o) -> d o", o=1), w=["gcol"], slow=True)
            k.dma(gcol[rep * 32:(rep + 1) * 32, 3:4], I["df_k_g"][l].rearrange("(d o) -> d o", o=1), w=["gcol"], slow=True)
        k.ts("dve", gcol[:, 0:1], gcol[:, 0:1], 0.125, ALU.mult, r=["gcol"], w=["gcol"])
        k.ts("dve", gcol[:, 2:3], gcol[:, 2:3], 32.0 ** -0.5, ALU.mult, r=["gcol"], w=["gcol"])
        k.dma(mu[:], I["rw_mu"][l].rearrange("(c p) -> p c", p=128), w=["mu"], slow=True)
        k.dma(negfb[:], I["fx_f_b"][l].rearrange("(h o) -> h o", o=1), w=["negfb"], slow=True)
        k.ts("dve", negfb[:], negfb[:], -1.0, ALU.mult, r=["negfb"], w=["negfb"])
        k.memset("dve", ones4[:], 1.0, w=["ones4"])
        k.memset("pool", paA[:], 0.0, w=["paA%d" % i for i in range(7)])
        for i in range(4):
            k.memset("pool", vt[i][:], 1.0, w=["vt%d" % i])
        k.dma(sgw[:], I["sg_w"][l].rearrange("g i j -> i g j"), w=["sgw"])
        for g in range(4):
            k.tr(banks[0][:, g * 128:(g + 1) * 128], sgw[:, g, :], G.ident[:], r=["sgw", "ident"], w=["bank0"])
        k.tt("dve", WgT[:], banks[0][:].rearrange("p (g i) -> p g i", g=4),
             triu[:].unsqueeze(1).to_broadcast([128, 4, 128]), ALU.mult, r=["bank0", "triu"], w=["WgT"])
        k.dma(sgbT[:], I["sg_b"][l].rearrange("g i -> i g"), w=["sgbT"], slow=True)
        bcast_load(G, lnCg[:], I["sg_ln_g"][l:l + 1, :], 256, ["lnCg"])
        bcast_load(G, lnCb[:], I["sg_ln_b"][l:l + 1, :], 256, ["lnCb"])

        def fm_mm(col0, ncols, bk):
            for kc in range(8):
                k.mm(banks[bk][0:ncols, :], w_in[:, kc, col0:col0 + ncols], T.hT[:, kc, :], start=(kc == 0), stop=(kc == 7),
                     r=["w_in", "hT"], w=["bank%d" % bk])

        for g in range(NG):
            tsl = slice(g * TG, (g + 1) * TG)
            k.dma(xt[:], xsrc[tsl, :].rearrange("(j p) d -> p j d", p=128), r=[("x", g)], w=["xt"])
            k.dma(cs[g % 2][:], I["cosT"][:, tsl], w=["cos%d" % (g % 2)])
            k.dma(sn[g % 2][:], I["sinT"][:, tsl], w=["sin%d" % (g % 2)])
            norm_mod_T(G, xt, 1 * D, 0, T)
            for ci in range(7):
                bk = k.rot("p1bank", 3)
                fm_mm(ci * 128, 128, bk)
                k.copy("pool", paA[:, ci, 0:1], paA[:, ci, 512:513], r=["paA%d" % ci], w=["paA%d" % ci])
                k.copy("act", paA[:, ci, 1:513], banks[bk][:], r=["bank%d" % bk], w=["paA%d" % ci])
                i = k.rot("fA", 3)
                k.tt("pool", fA[i][:], paA[:, ci, 0:512], paA[:, ci, 1:513], ALU.subtract, r=["paA%d" % ci], w=["fA%d" % i])
                o = k.rot("pmo", 2)
                k.stt("dve", pmo[o][:], fA[i][:], mu[:, ci:ci + 1], paA[:, ci, 1:513], ALU.mult, ALU.add,
                      r=["fA%d" % i, "mu", "paA%d" % ci], w=["pmo%d" % o])
                k.dma(R.pmA[ci * 128:(ci + 1) * 128, tsl], pmo[o][:], r=["pmo%d" % o], w=[("pmA", g)])
            for (mix, col0, gi, dst, rope) in (("B", 896, 2, R.qTB, True), ("B", 1152, 3, R.kTB, True),
                                               ("D", 2176, 0, R.qTD, False), ("D", 2432, 1, R.kTD, False)):
                for ci in range(2):
                    bk = k.rot("p1bank", 3)
                    fm_mm(col0 + ci * 128, 128, bk)
                    a = k.rot("fA", 3)
                    k.act(fA[a][:], banks[bk][:], AF.Square, r=["bank%d" % bk], w=["fA%d" % a])
                    sbk = 3 + k.rot("p1sbank", 2)
                    k.mm(banks[sbk][:], bo32[:] if mix == "B" else bo64[:], fA[a][:], r=["bo32", "bo64", "fA%d" % a],
                         w=["bank%d" % sbk])
                    b = k.rot("fB", 3)
                    nd = 32.0 if mix == "B" else 64.0
                    k.act(fB[b][:], banks[sbk][:], AF.Sqrt, bias=EPS, scale=1.0 / nd, r=["bank%d" % sbk], w=["fB%d" % b])
                    k.recip(fB[b][:], fB[b][:], r=["fB%d" % b], w=["fB%d" % b])
                    o = k.rot("obf", 3)
                    if not rope:
                        k.stt("dve", obf[o][:], banks[bk][:], gcol[:, gi:gi + 1], fB[b][:], ALU.mult, ALU.mult,
                              r=["bank%d" % bk, "gcol", "fB%d" % b], w=["obf%d" % o])
                    else:
                        c_ = k.rot("fC", 3)
                        k.stt("dve", fC[c_][:], banks[bk][:], gcol[:, gi:gi + 1], fB[b][:], ALU.mult, ALU.mult,
                              r=["bank%d" % bk, "gcol", "fB%d" % b], w=["fC%d" % c_])
                        xb = k.rot("xnb", 2)
                        k.copy("act", xnb[xb][:], fC[c_][:], r=["fC%d" % c_], w=["xnb%d" % xb])
                        rbk = 3 + k.rot("p1sbank", 2)
                        k.mm(banks[rbk][:], protb[:], xnb[xb][:], r=["protb", "xnb%d" % xb], w=["bank%d" % rbk])
                        a2 = k.rot("fA", 3)
                        k.tt("pool", fA[a2][:], fC[c_][:], cs[g % 2][:], ALU.mult, r=["fC%d" % c_, "cos%d" % (g % 2)], w=["fA%d" % a2])
                        b2 = k.rot("fB", 3)
                        k.tt("dve", fB[b2][:], banks[rbk][:], sn[g % 2][:], ALU.mult, r=["bank%d" % rbk, "sin%d" % (g % 2)],
                             w=["fB%d" % b2])
                        k.tt("pool", obf[o][:], fA[a2][:], fB[b2][:], ALU.add, r=["fA%d" % a2, "fB%d" % b2], w=["obf%d" % o])
                    k.dma(dst[ci * 128:(ci + 1) * 128, tsl], obf[o][:], r=["obf%d" % o], w=[(mix + "qk", g)])
            bk = k.rot("p1bank", 3)
            fm_mm(2944, 4, bk)
            k.act(f4[:], banks[bk][0:4, :], AF.Exp, bias=negfb[:, 0:1], scale=-1.0, r=["bank%d" % bk, "negfb"], w=["f4"])
            k.act(f4[:], f4[:], AF.Ln, bias=1.0, scale=1.0, r=["f4"], w=["f4"])
            if g == 0:
                k.scan(Fg[0][:], ones4[:], f4[:], 0.0, ALU.mult, ALU.subtract, r=["ones4", "f4"], w=["Fg0"])
            else:
                k.scan(Fg[g % 2][:], ones4[:], f4[:], Fg[(g - 1) % 2][:, 511:512], ALU.mult, ALU.subtract,
                       r=["ones4", "f4", "Fg%d" % ((g - 1) % 2)], w=["Fg%d" % (g % 2)])
            k.dma(R.Ffm[:, tsl], Fg[g % 2][:], r=["Fg%d" % (g % 2)], w=[("Ffm", g)])
            for j in range(4):
                rows = slice(g * TG + j * 128, g * TG + (j + 1) * 128)
                for (mix, col0, dst) in (("B", 1408, R.vB), ("D", 2688, R.vD)):
                    bk = k.rot("p1bank", 3)
                    for kc in range(8):
                        k.mm(banks[bk][:, 0:256], T.hT[:, kc, j * 128:(j + 1) * 128], w_in[:, kc, col0:col0 + 256],
                             start=(kc == 0), stop=(kc == 7), r=["w_in", "hT"], w=["bank%d" % bk])
                    vi = k.rot("vt", 4)
                    k.copy("act" if mix == "B" else "dve", vt[vi][:, :, 0:64], banks[bk][:, 0:256].rearrange("p (h e) -> p h e", h=4),
                           r=["bank%d" % bk], w=["vt%d" % vi])
                    k.dma(dst[rows, :], vt[vi][:].rearrange("p h e -> p (h e)"), r=["vt%d" % vi], w=[(mix + "v", g)])
                bk = k.rot("p1bank", 3)
                for kc in range(8):
                    k.mm(banks[bk][:], T.hT[:, kc, j * 128:(j + 1) * 128], w_in[:, kc, 1664:2176],
                         start=(kc == 0), stop=(kc == 7), r=["w_in", "hT"], w=["bank%d" % bk])
                ci = k.rot("glC", 2)
                gl, st, tc, vb, yb = glC[ci], stC[ci], tmpc[ci], vnb[ci], ycb[ci]
                kk = "C%d" % ci
                k.act(gl[:], banks[bk][:], AF.Gelu, r=["bank%d" % bk], w=[kk + "gl"])
                k.red("dve", st[:, 0:1], gl[:, 256:512], r=[kk + "gl"], w=[kk + "st"])
                k.act(tc[:], gl[:, 256:512], AF.Square, r=[kk + "gl"], w=[kk + "tc"])
                k.red("dve", st[:, 1:2], tc[:], r=[kk + "tc"], w=[kk + "st"])
                k.ts("dve", st[:, 2:3], st[:, 0:1], 1.0 / 256, ALU.mult, r=[kk + "st"], w=[kk + "st"])
                k.tt("dve", st[:, 3:4], st[:, 2:3], st[:, 2:3], ALU.mult, r=[kk + "st"], w=[kk + "st"])
                k.stt("dve", st[:, 4:5], st[:, 1:2], 1.0 / 256, st[:, 3:4], ALU.mult, ALU.subtract, r=[kk + "st"], w=[kk + "st"])
                k.act(st[:, 5:6], st[:, 4:5], AF.Sqrt, bias=EPS, scale=1.0, r=[kk + "st"], w=[kk + "st"])
                k.recip(st[:, 5:6], st[:, 5:6], r=[kk + "st"], w=[kk + "st"])
                k.ts("dve", tc[:], gl[:, 256:512], st[:, 2:3], ALU.subtract, st[:, 5:6], ALU.mult, r=[kk + "gl", kk + "st"], w=[kk + "tc"])
                k.tt("pool", tc[:], tc[:], lnCg[:], ALU.mult, r=[kk + "tc", "lnCg"], w=[kk + "tc"])
                k.tt("pool", vb[:], tc[:], lnCb[:], ALU.add, r=[kk + "tc", "lnCb"], w=[kk + "vb"])
                sbk = 3 + k.rot("p1sbank", 2)
                for hg in range(4):
                    k.mm(banks[sbk][:, hg * 64:(hg + 1) * 64], WgT[:, hg, :], vb[:, hg * 64:(hg + 1) * 64],
                         r=["WgT", kk + "vb"], w=["bank%d" % sbk])
                k.tt("dve", tc[:].rearrange("p (h e) -> p h e", h=4), banks[sbk][:, 0:256].rearrange("p (h e) -> p h e", h=4),
                     sgbT[:].unsqueeze(2).to_broadcast([128, 4, 64]), ALU.add, r=["bank%d" % sbk, "sgbT", kk + "vb"], w=[kk + "tc"])
                k.tt("pool", yb[:], tc[:], gl[:, 0:256], ALU.mult, r=[kk + "tc", kk + "gl"], w=[kk + "yb"])
                if not G.y_in:
                    k.dma(R.yscr[rows, 256:512], yb[:], r=[kk + "yb"], w=[("yC", g)])
        P.barrier()


def phase_p3(G, l, xsrc, xdst):
    nc, P, k, I, R = G.nc, G.P, G.k, G.I, G.R
    banks = G.banks
    ydt = F32 if G.y_in else BF16
    with ExitStack() as s:
        sb = lambda name, shape, dt=F32: G.sb(s, name, shape, dt)
        T = Ctx()
        w_out = sb("w_out", [128, 8, D], BF16)
        xt = sb("xt3", [128, 4, D])
        T.sqj = sb("sqj3", [128, D], BF16)
        T.ss = sb("ss3", [128, 4])
        T.rstd = sb("rstd3", [128, 4])
        T.tmp = sb("tmp3", [128, D])
        T.hb = sb("hb3", [128, 4, D], BF16)
        T.hT = sb("hT3", [128, 8, 512], BF16)
        yT = T.hT
        actT = sb("actT", [128, 22, 512], BF16)
        upw = [sb("upw%d" % i, [128, 8, 512], BF16) for i in range(2)]
        dnw = sb("dnw", [128, 22, D], BF16)
        ub = [sb("ub%d" % i, [128, 514]) for i in range(2)]
        c1 = [sb("c1_%d" % i, [128, 512]) for i in range(2)]
        c2 = [sb("c2_%d" % i, [128, 512]) for i in range(2)]
        c3 = [sb("c3_%d" % i, [128, 512]) for i in range(2)]
        sgt = [sb("sgt%d" % i, [128, 512]) for i in range(2)]
        convw = sb("convw", [128, 44, 3])
        convb = sb("convb", [128, 44])
        carryF = sb("carryF", [128, 44, 2])

        for kc in range(8):
            k.dma(T.tmp[:], I["w_out"][l, kc * 128:(kc + 1) * 128, :], w=["tmp"])
            k.copy(("dve", "pool", "act")[kc % 3], w_out[:, kc, :], T.tmp[:], r=["tmp"], w=["w_out"])
        cwn = T.tmp[0:44, 0:512].rearrange("p (t c) -> p t c", t=4)
        for t in range(3):
            k.dma(cwn[:, t, :], I["ffn_conv"][l, t].rearrange("(cc p) -> cc p", p=128), w=["tmp"])
        k.dma(cwn[:, 3, :], I["ffn_conv_b"][l].rearrange("(cc p) -> cc p", p=128), w=["tmp"])
        for t in range(4):
            k.mm(banks[0][:, t * 44:(t + 1) * 44], cwn[:, t, :], G.ident[0:44, 0:44], r=["tmp", "ident"], w=["bank0"])
        k.copy("dve", convw[:], banks[0][:, 0:132].rearrange("p (t c) -> p c t", t=3), r=["bank0"], w=["convw"])
        k.copy("dve", convb[:], banks[0][:, 132:176], r=["bank0"], w=["convb"])
        k.memset("pool", carryF[:], 0.0, w=["carryF"])

        for g in range(NG):
            tsl = slice(g * TG, (g + 1) * TG)
            k.dma(xt[:], xsrc[tsl, :].rearrange("(j p) d -> p j d", p=128), r=[("x", g)], w=["xt"])
            k.dma(dnw[:].rearrange("p a d -> p (a d)"), R.dnbf[l], r=[("ffnw", l)], w=["dnw"])
            if G.y_in:
                for j in range(4):
                    k.dma(T.tmp[:, 0:768], R.yscr[g * TG + j * 128: g * TG + (j + 1) * 128, :], w=["tmp"])
                    k.copy("pool", T.hb[:, j, 0:768], T.tmp[:, 0:768], r=["tmp"], w=["hb"])
                for kc in range(2):
                    k.dma(c1[kc][:], R.yTA[kc * 128:(kc + 1) * 128, tsl], w=["c1_%d" % kc])
                    k.copy("act", yT[:, kc, :], c1[kc][:], r=["c1_%d" % kc], w=["hT"])
            else:
                k.dma(T.hb[:, :, 0:768], R.yscr[tsl, :].rearrange("(j p) c -> p j c", p=128),
                      r=[("yC", g), ("yB", g), ("yD", g)], w=["hb"])
                k.dma(yT[:, 0:2, :], R.yTA[:, tsl].rearrange("(kc p) t -> p kc t", p=128), r=[("yA", g)], w=["hT"])
            for j in range(4):
                b = 5 + (j % 2)
                pv = banks[b][:].bitcast(BF16).rearrange("p (a t) -> p a t", a=8)
                for kc in range(6):
                    k.tr(pv[:, kc, :], T.hb[:, j, kc * 128:(kc + 1) * 128], G.identb[:], r=["hb", "identb"], w=["bank%d" % b])
                k.copy("act" if j % 2 else "dve", yT[:, 2:8, j * 128:(j + 1) * 128], pv[:, 0:6, :], r=["bank%d" % b], w=["hT"])
            for j in range(4):
                for hf in range(2):
                    bk = k.rot("p3bank", 2)
                    for kc in range(8):
                        k.mm(banks[bk][:], yT[:, kc, j * 128:(j + 1) * 128], w_out[:, kc, hf * 512:(hf + 1) * 512],
                             start=(kc == 0), stop=(kc == 7), r=["hT", "w_out"], w=["bank%d" % bk])
                    ci = k.rot("c1", 2)
                    k.tt("dve", c1[ci][:], banks[bk][:], G.modB[:, 2 * D + hf * 512: 2 * D + (hf + 1) * 512], ALU.mult,
                         r=["bank%d" % bk, "modB"], w=["c1_%d" % ci])
                    k.tt("dve", xt[:, j, hf * 512:(hf + 1) * 512], xt[:, j, hf * 512:(hf + 1) * 512], c1[ci][:], ALU.add,
                         r=["xt", "c1_%d" % ci], w=["xt"])
            if G.dbg_x1 is not None:
                k.dma(G.dbg_x1[tsl, :].rearrange("(j p) d -> p j d", p=128), xt[:], r=["xt"], w=[("dbgx1", g)])
            norm_mod_T(G, xt, 4 * D, 3 * D, T)
            for u in range(11):
                wi = u % 2
                k.dma(upw[wi][:].rearrange("p a n -> p (a n)"), R.upbf[l, u], r=[("ffnw", l)], w=["upw%d" % wi])
                for sc in range(4):
                    cc = 4 * u + sc
                    bk = 2 + k.rot("p3ubank", 3)
                    for kc in range(8):
                        k.mm(banks[bk][:], upw[wi][:, kc, sc * 128:(sc + 1) * 128], T.hT[:, kc, :], start=(kc == 0), stop=(kc == 7),
                             r=["upw%d" % wi, "hT"], w=["bank%d" % bk])
                    ui = k.rot("ub", 2)
                    k.copy("pool", ub[ui][:, 0:2], carryF[:, cc, :], r=["carryF"], w=["ub%d" % ui])
                    k.copy("act", ub[ui][:, 2:514], banks[bk][:], r=["bank%d" % bk], w=["ub%d" % ui])
                    k.copy("pool", carryF[:, cc, :], ub[ui][:, 512:514], r=["ub%d" % ui], w=["carryF"])
                    k.act(c1[ui][:], ub[ui][:, 2:514], AF.Identity, bias=convb[:, cc:cc + 1], scale=convw[:, cc, 2:3],
                          r=["ub%d" % ui, "convw", "convb"], w=["c1_%d" % ui])
                    k.stt("dve", c2[ui][:], ub[ui][:, 1:513], convw[:, cc, 1:2], c1[ui][:], ALU.mult, ALU.add,
                          r=["ub%d" % ui, "convw", "c1_%d" % ui], w=["c2_%d" % ui])
                    if cc < 22:
                        k.stt("dve", actT[:, cc, :], ub[ui][:, 0:512], convw[:, cc, 0:1], c2[ui][:], ALU.mult, ALU.add,
                              r=["ub%d" % ui, "convw", "c2_%d" % ui], w=["actT%d" % cc])
                    else:
                        cu = cc - 22
                        k.stt("dve", c3[ui][:], ub[ui][:, 0:512], convw[:, cc, 0:1], c2[ui][:], ALU.mult, ALU.add,
                              r=["ub%d" % ui, "convw", "c2_%d" % ui], w=["c3_%d" % ui])
                        k.act(sgt[ui][:], c3[ui][:], AF.Silu, r=["c3_%d" % ui], w=["sgt%d" % ui])
                        k.tt("pool", actT[:, cu, :], actT[:, cu, :], sgt[ui][:], ALU.mult, r=["actT%d" % cu, "sgt%d" % ui],
                             w=["actT%d" % cu])
            aks = ["actT%d" % i for i in range(22)]
            for j in range(4):
                for hf in range(2):
                    bk = k.rot("p3bank", 2)
                    for cu in range(22):
                        k.mm(banks[bk][:], actT[:, cu, j * 128:(j + 1) * 128], dnw[:, cu, hf * 512:(hf + 1) * 512],
                             start=(cu == 0), stop=(cu == 21), r=["actT%d" % cu, "dnw"], w=["bank%d" % bk])
                    ci = k.rot("c1", 2)
                    k.tt("dve", c1[ci][:], banks[bk][:], G.modB[:, 5 * D + hf * 512: 5 * D + (hf + 1) * 512], ALU.mult,
                         r=["bank%d" % bk, "modB"], w=["c1_%d" % ci])
                    k.tt("dve", xt[:, j, hf * 512:(hf + 1) * 512], xt[:, j, hf * 512:(hf + 1) * 512], c1[ci][:], ALU.add,
                         r=["xt", "c1_%d" % ci], w=["xt"])
            k.dma(xdst[tsl, :].rearrange("(j p) d -> p j d", p=128), xt[:], r=["xt"], w=[("x", g)])
        P.barrier()


def phase_A(G, l):
    nc, P, k, I, R = G.nc, G.P, G.k, G.I, G.R
    banks = G.banks
    with ExitStack() as s:
        sb = lambda name, shape, dt=F32: G.sb(s, name, shape, dt)
        prm = sb("prm", [64, 7, 4])
        wup = sb("wup", [32, 256])
        aup = sb("aup", [32, 256])
        gup = sb("gup", [64, 256])
        ones64 = sb("ones64", [64, 64])
        ones512 = sb("ones512", [64, 512])
        m64 = sb("m64", [64, 3, 64])
        Tst = sb("Tst", [64, 4, 64])
        big = lambda nm: sb(nm, [64, 4, 512])
        r_, k_, v_ = big("r_"), big("k_"), big("v_")
        lwt, at, gt, kkn, k2, b_, bonus, Yblk, t1, t2 = (big("lwt"), big("at"), big("gt"), big("kkn"), big("k2"), big("b_"),
                                                         big("bonus"), big("Yblk"), big("t1"), big("t2"))
        Gblk = sb("Gblk", [64, 4, 513])
        wd = sb("wd", [32, 512])
        ad = sb("ad", [32, 512])
        gd = sb("gd", [64, 512])
        yo = sb("yo", [64, 4, 512], BF16)
        sm = lambda nm: sb(nm, [64, 4, 64])
        Pc, Pp, eGn, eGp = sm("Pc"), sm("Pp"), sm("eGn"), sm("eGp")
        smb = lambda nm: sb(nm, [64, 4, 64], BF16)
        Bt, Kt = smb("Bt"), smb("Kt")
        Pj = [smb("Pj0"), smb("Pj1")]
        U, tS = sm("U"), sm("tS")
        Ub, Tb = smb("Ub"), smb("Tb")
        DB = []
        for q in range(2):
            d = {"eG": sm("eG%d" % q)}
            for nm_ in ("At", "Rt", "Btm", "Ktm", "Vtm", "LakT", "MrbT", "MrkT"):
                d[nm_] = smb("%s%d" % (nm_, q))
            d["PT"] = [smb("PT%d_%d" % (j, q)) for j in range(6)]
            DB.append(d)
        vb = sb("vb", [64, 4, 512], BF16)

        for n, nm in enumerate(("rw_w0", "rw_a0", "rw_k_k", "rw_k_a")):
            k.dma(prm[:, n, :], I[nm][l].rearrange("(h k) -> k h", k=64), w=["prm"], slow=True)
        k.dma(prm[:, 4, :], I["rw_r_k"][l].rearrange("h k -> k h"), w=["prm"], slow=True)
        k.dma(prm[:, 5, :], I["rw_ln_g"][l].rearrange("(h k) -> k h", k=64), w=["prm"], slow=True)
        k.dma(prm[:, 6, :], I["rw_ln_b"][l].rearrange("(h k) -> k h", k=64), w=["prm"], slow=True)
        k.dma(wup[:], I["rw_w_up"][l], w=["wup"])
        k.dma(aup[:], I["rw_a_up"][l], w=["aup"])
        k.dma(gup[:], I["rw_g_up"][l], w=["gup"])
        k.dma(m64[:], I["m64"], w=["m64"])
        k.memset("dve", ones64[:], 1.0, w=["ones64"])
        k.memset("dve", ones512[:], 1.0, w=["ones512"])
        k.memset("pool", Tst[:], 0.0, w=["Tst"])
        k.memset("pool", Tb[:], 0.0, w=["Tb"])
        k.memset("pool", Gblk[:], 0.0, w=["Gblk"])

        def bc(col):
            return prm[:, col, :].unsqueeze(2).to_broadcast([64, 4, 512])

        def HB(b, half):
            return banks[b][0:64, half * 256:(half + 1) * 256].rearrange("p (h t) -> p h t", h=4)

        def hk(b, half):
            return ("hb", b)

        def fb(b):
            return [("hb", b)]
        allpm = [("pmA", g) for g in range(NG)]

        for g in range(NG):
            tsl = slice(g * TG, (g + 1) * TG)
            k.dma(r_[:], R.pmA[0:256, tsl].rearrange("(h k) t -> k h t", k=64), r=allpm, w=["r_"])
            k.dma(k_[:], R.pmA[256:512, tsl].rearrange("(h k) t -> k h t", k=64), r=allpm, w=["k_"])
            k.dma(v_[:], R.pmA[512:768, tsl].rearrange("(h k) t -> k h t", k=64), r=allpm, w=["v_"])
            k.copy("pool", vb[:], v_[:], r=["v_"], w=["vb"])
            k.dma(wd[:], R.pmA[768:800, tsl], r=allpm, w=["wd"])
            k.dma(ad[:], R.pmA[800:832, tsl], r=allpm, w=["ad"])
            k.dma(gd[:], R.pmA[832:896, tsl], r=allpm, w=["gd"])
            k.act(wd[:], wd[:], AF.Tanh, r=["wd"], w=["wd"])
            k.act(gd[:], gd[:], AF.Sigmoid, r=["gd"], w=["gd"])
            for h in range(4):
                k.mm(banks[h][0:64, :], wup[:, h * 64:(h + 1) * 64], wd[:], r=["wup", "wd"], w=fb(h))
                k.act(lwt[:, h, :], banks[h][0:64, :], AF.Sigmoid, bias=prm[:, 0, h:h + 1], scale=1.0, r=fb(h) + ["prm"], w=["lwt"])
            for h in range(4):
                k.mm(banks[4 + h][0:64, :], aup[:, h * 64:(h + 1) * 64], ad[:], r=["aup", "ad"], w=fb(4 + h))
                k.act(at[:, h, :], banks[4 + h][0:64, :], AF.Sigmoid, bias=prm[:, 1, h:h + 1], scale=1.0, r=fb(4 + h) + ["prm"], w=["at"])
            for h in range(4):
                k.mm(banks[h][0:64, :], gup[:, h * 64:(h + 1) * 64], gd[:], r=["gup", "gd"], w=fb(h))
                k.copy("dve" if h % 2 else "act", gt[:, h, :], banks[h][0:64, :], r=fb(h), w=["gt"])
            k.tt("dve", kkn[:], k_[:], bc(2), ALU.mult, r=["k_", "prm"], w=["kkn"])
            k.tt("pool", t1[:], kkn[:], kkn[:], ALU.mult, r=["kkn"], w=["t1"])
            for h in range(4):
                k.mm(banks[4 + h][0:64, :], ones64[:], t1[:, h, :], r=["ones64", "t1"], w=fb(4 + h))
            for h in range(4):
                k.act(t2[:, h, :], banks[4 + h][0:64, :], AF.Sqrt, r=fb(4 + h), w=["t2"])
            k.ts("dve", t2[:], t2[:], 1e-12, ALU.max, r=["t2"], w=["t2"])
            k.recip(t2[:], t2[:], r=["t2"], w=["t2"])
            k.tt("pool", kkn[:], kkn[:], t2[:], ALU.mult, r=["kkn", "t2"], w=["kkn"])
            k.stt("dve", t1[:], at[:], -1.0, bc(3), ALU.add, ALU.mult, r=["at", "prm", "t1"], w=["t1"])
            k.tt("pool", t1[:], t1[:], k_[:], ALU.mult, r=["t1", "k_"], w=["t1"])
            k.tt("pool", k2[:], t1[:], k_[:], ALU.add, r=["t1", "k_"], w=["k2"])
            k.tt("pool", b_[:], kkn[:], at[:], ALU.mult, r=["kkn", "at"], w=["b_"])
            k.tt("pool", t1[:], r_[:], k2[:], ALU.mult, r=["r_", "k2", "t1"], w=["t1"])
            k.tt("dve", t1[:], t1[:], bc(4), ALU.mult, r=["t1", "prm"], w=["t1"])
            for h in range(4):
                k.mm(banks[h][0:64, :], ones64[:], t1[:, h, :], r=["ones64", "t1"], w=fb(h))
                k.tt("dve", bonus[:, h, :], banks[h][0:64, :], v_[:, h, :], ALU.mult, r=fb(h) + ["v_"], w=["bonus"])
            for h in range(4):
                k.scan(Gblk[:, h, 1:513], ones512[:], lwt[:, h, :], 0.0, ALU.mult, ALU.add, r=["ones512", "lwt"], w=["Gblk"])

            if int(os.environ.get("A_STOP", "9")) <= 1:
                continue
            def prep_steps(ci):
                q = ci % 2
                D_ = DB[q]
                c0 = ci * 64
                ts_ = slice(c0, c0 + 64)
                eGq, Atq, Rtq = D_["eG"], D_["At"], D_["Rt"]
                nm = lambda x: "%s_%d" % (x, q)
                id64 = G.identb[0:64, 0:64]
                steps = []

                def s1():
                    k.tt("dve", Pc[:], Gblk[:, :, 1 + c0:1 + c0 + 64], Gblk[:, :, c0:c0 + 1].to_broadcast([64, 4, 64]), ALU.subtract,
                         r=["Gblk"], w=["Pc"])
                    k.tt("pool", Pp[:], Pc[:], lwt[:, :, ts_], ALU.subtract, r=["Pc", "lwt"], w=["Pp"])
                    k.act(eGq[:], Pc[:], AF.Exp, scale=-ALPHA, r=["Pc"], w=[nm("eG")])
                    k.act(eGn[:], Pc[:], AF.Exp, scale=ALPHA, r=["Pc"], w=["eGn"])
                    k.act(eGp[:], Pp[:], AF.Exp, scale=-ALPHA, r=["Pp"], w=["eGp"])
                steps.append(s1)

                def s2():
                    k.stt("dve", Atq[:], kkn[:, :, ts_], -1.0, eGp[:], ALU.mult, ALU.mult, r=["kkn", "eGp"], w=[nm("At")])
                    k.tt("pool", Bt[:], b_[:, :, ts_], eGn[:], ALU.mult, r=["b_", "eGn"], w=["Bt"])
                    k.tt("pool", Kt[:], k2[:, :, ts_], eGn[:], ALU.mult, r=["k2", "eGn"], w=["Kt"])
                    k.tt("dve", Rtq[:], r_[:, :, ts_], eGq[:], ALU.mult, r=["r_", nm("eG")], w=[nm("Rt")])
                steps.append(s2)

                def s3():
                    trs = ((Bt, "Bt", D_["Btm"], nm("Btm"), 2, 1, "act"), (Kt, "Kt", D_["Ktm"], nm("Ktm"), 3, 0, "act"),
                           (None, "vb", D_["Vtm"], nm("Vtm"), 3, 1, "act"))
                    for (X, xk, Xtm, xtk, bq, hf, ce) in trs:
                        for h in range(4):
                            src = vb[:, h, ts_] if X is None else X[:, h, :]
                            k.mm(HB(bq, hf)[:, h, :], src, id64, r=[xk, "identb"], w=[hk(bq, hf)])
                    for (X, xk, Xtm, xtk, bq, hf, ce) in trs:
                        k.copy(ce, Xtm[:], HB(bq, hf), r=[hk(bq, hf)], w=[xtk])
                steps.append(s3)

                def s4():
                    specs = ((Bt, "Bt", Atq, nm("At"), 0, 0, D_["PT"][0], nm("PT0"), 0), (Atq, nm("At"), Bt, "Bt", 0, 1, Pj[0], "Pj0", 2),
                             (Kt, "Kt", Atq, nm("At"), 1, 0, D_["LakT"], nm("LakT"), 0), (Bt, "Bt", Rtq, nm("Rt"), 1, 1, D_["MrbT"], nm("MrbT"), 1),
                             (Kt, "Kt", Rtq, nm("Rt"), 2, 0, D_["MrkT"], nm("MrkT"), 1))
                    for (La, lk, Ra, rk_, bq, hf, dst, dk, mi) in specs:
                        for h in range(4):
                            k.mm(HB(bq, hf)[:, h, :], La[:, h, :], Ra[:, h, :], r=[lk, rk_], w=[hk(bq, hf)])
                    for (La, lk, Ra, rk_, bq, hf, dst, dk, mi) in specs:
                        k.tt("dve", dst[:], HB(bq, hf), m64[:, mi, :].unsqueeze(1).to_broadcast([64, 4, 64]), ALU.mult,
                             r=[hk(bq, hf), "m64"], w=[dk])
                steps.append(s4)

                def mk_sq(j):
                    def sq():
                        cur, nxt = j % 2, (j + 1) % 2
                        PTc, PTn = D_["PT"][j], D_["PT"][j + 1]
                        for h in range(4):
                            k.mm(HB(5, 0)[:, h, :], PTc[:, h, :], Pj[cur][:, h, :], r=[nm("PT%d" % j), "Pj%d" % cur], w=[hk(5, 0)])
                        for h in range(4):
                            k.mm(HB(5, 1)[:, h, :], Pj[cur][:, h, :], PTc[:, h, :], r=[nm("PT%d" % j), "Pj%d" % cur], w=[hk(5, 1)])
                        k.copy("act", Pj[nxt][:], HB(5, 0), r=[hk(5, 0)], w=["Pj%d" % nxt])
                        k.copy("act", PTn[:], HB(5, 1), r=[hk(5, 1)], w=[nm("PT%d" % (j + 1))])
                    return sq
                for j in range(5):
                    steps.append(mk_sq(j))
                return steps

            def chain_steps(ci):
                q = ci % 2
                D_ = DB[q]
                c0 = ci * 64
                ts_ = slice(c0, c0 + 64)
                eGq, Atq, Rtq = D_["eG"], D_["At"], D_["Rt"]
                Btm, Ktm, Vtm, LakT, MrbT, MrkT = D_["Btm"], D_["Ktm"], D_["Vtm"], D_["LakT"], D_["MrbT"], D_["MrkT"]
                nm = lambda x: "%s_%d" % (x, q)
                steps = []

                def c1():
                    for h in range(4):
                        k.mm(HB(4, 0)[:, h, :], Atq[:, h, :], Tb[:, h, :], start=True, stop=False, r=[nm("At"), "Tb"], w=[hk(4, 0)])
                        k.mm(HB(4, 0)[:, h, :], LakT[:, h, :], Vtm[:, h, :], start=False, stop=True, r=[nm("LakT"), nm("Vtm")], w=[hk(4, 0)])
                    k.copy("dve", Ub[:], HB(4, 0), r=[hk(4, 0)], w=["Ub"])
                    k.copy("act", U[:], HB(4, 0), r=[hk(4, 0)], w=["U"])
                steps.append(c1)

                def mk_u(j):
                    def us():
                        PTc = D_["PT"][j]
                        for h in range(4):
                            k.mm(HB(4, 1)[:, h, :], PTc[:, h, :], Ub[:, h, :], r=[nm("PT%d" % j), "Ub"], w=[hk(4, 1)])
                        k.tt("dve", Ub[:], U[:], HB(4, 1), ALU.add, r=["U", hk(4, 1)], w=["Ub"])
                        if j < 5:
                            k.tt("dve", U[:], U[:], HB(4, 1), ALU.add, r=["U", hk(4, 1)], w=["U"])
                    return us
                for j in range(6):
                    steps.append(mk_u(j))

                def c8():
                    for h in range(4):
                        k.mm(HB(6, 0)[:, h, :], Tb[:, h, :], Rtq[:, h, :], start=True, stop=False, r=["Tb", nm("Rt")], w=[hk(6, 0)])
                        k.mm(HB(6, 0)[:, h, :], Ub[:, h, :], MrbT[:, h, :], start=False, stop=False, r=["Ub", nm("MrbT")], w=[hk(6, 0)])
                        k.mm(HB(6, 0)[:, h, :], Vtm[:, h, :], MrkT[:, h, :], start=False, stop=True, r=[nm("Vtm"), nm("MrkT")], w=[hk(6, 0)])
                    k.copy("act", Yblk[:, :, ts_], HB(6, 0), r=[hk(6, 0)], w=["Yblk"])
                    for h in range(4):
                        k.mm(HB(7, 0)[:, h, :], Btm[:, h, :], Ub[:, h, :], start=True, stop=False, r=[nm("Btm"), "Ub"], w=[hk(7, 0)])
                        k.mm(HB(7, 0)[:, h, :], Ktm[:, h, :], Vtm[:, h, :], start=False, stop=True, r=[nm("Ktm"), nm("Vtm")], w=[hk(7, 0)])
                    k.tt("dve", tS[:], HB(7, 0), Tst[:], ALU.add, r=[hk(7, 0), "Tst"], w=["tS"])
                    k.tt("dve", Tb[:], tS[:], eGq[:, :, 63:64].to_broadcast([64, 4, 64]), ALU.mult, r=["tS", nm("eG")], w=["Tb"])
                    k.tt("pool", Tst[:], tS[:], eGq[:, :, 63:64].to_broadcast([64, 4, 64]), ALU.mult, r=["tS", nm("eG")], w=["Tst"])
                steps.append(c8)
                return steps

            for st in prep_steps(0):
                st()
            for ci in range(8):
                cs = chain_steps(ci)
                ps = prep_steps(ci + 1) if ci < 7 else []
                n = max(len(cs), len(ps))
                for i_ in range(n):
                    if i_ < len(cs):
                        cs[i_]()
                    if i_ < len(ps):
                        ps[i_]()

            if int(os.environ.get("A_STOP", "9")) <= 3:
                continue
            for h in range(4):
                k.mm(banks[h][0:64, :], ones64[:], Yblk[:, h, :], r=["ones64", "Yblk"], w=fb(h))
                k.stt("dve", t1[:, h, :], banks[h][0:64, :], -1.0 / 64, Yblk[:, h, :], ALU.mult, ALU.add, r=fb(h) + ["Yblk"], w=["t1"])
            k.tt("pool", t2[:], t1[:], t1[:], ALU.mult, r=["t1"], w=["t2"])
            for h in range(4):
                k.mm(banks[4 + h][0:64, :], ones64[:], t2[:, h, :], r=["ones64", "t2"], w=fb(4 + h))
            for h in range(4):
                k.act(t2[:, h, :], banks[4 + h][0:64, :], AF.Sqrt, bias=64e-5, scale=1.0 / 64, r=fb(4 + h), w=["t2"])
            k.recip(t2[:], t2[:], r=["t2"], w=["t2"])
            k.tt("pool", t1[:], t1[:], t2[:], ALU.mult, r=["t1", "t2"], w=["t1"])
            k.tt("dve", t1[:], t1[:], bc(5), ALU.mult, r=["t1", "prm"], w=["t1"])
            k.tt("pool", t1[:], t1[:], bc(6), ALU.add, r=["t1", "prm"], w=["t1"])
            k.tt("pool", t1[:], t1[:], bonus[:], ALU.add, r=["t1", "bonus"], w=["t1"])
            k.tt("dve", yo[:], t1[:], gt[:], ALU.mult, r=["t1", "gt"], w=["yo"])
            if not G.y_in:
                k.dma(R.yTA[:, tsl].rearrange("(h v) t -> v h t", v=64), yo[:], r=["yo"], w=[("yA", g)])
        P.barrier()


def attn_common(G, s, Vsrc, mix):
    k, R, I = G.k, G.R, G.I
    sb = lambda name, shape, dt=F32: G.sb(s, name, shape, dt)
    A = Ctx()
    A.kT = [sb("kT%d" % i, [64, S], BF16) for i in range(2)]
    A.V = sb("V", [128, 32, 260], BF16)
    Vv = Vsrc.rearrange("(kt p) c -> p kt c", p=128)
    for q4 in range(4):
        k.dma(A.V[:, q4 * 8:(q4 + 1) * 8, :], Vv[:, q4 * 8:(q4 + 1) * 8, :], r=[(mix + "v", g) for g in range(NG)], w=["V"])
    A.Pm = [sb("Pm%d" % i, [128, 512], BF16) for i in range(3)]
    A.rec = [sb("rec%d" % i, [128, 8]) for i in range(2)]
    A.ob = [sb("ob%d" % i, [128, 4, 64], BF16) for i in range(2)]
    return A


def phase_D(G, l):
    nc, P, k, I, R = G.nc, G.P, G.k, G.I, G.R
    banks = G.banks
    with ExitStack() as s:
        sb = lambda name, shape, dt=F32: G.sb(s, name, shape, dt)
        A = attn_common(G, s, R.vD, "D")
        Fk = sb("Fk", [128, 4, 32])
        Frow = sb("Frow", [4, S])
        sel = sb("sel", [4, 4, 128])
        negm = sb("negm", [128, 128])
        qT = [sb("qT%d" % i, [64, 512], BF16) for i in range(2)]
        FqB = [sb("FqB%d" % i, [128, 512]) for i in range(2)]
        FqD = [sb("FqD%d" % i, [128, 512]) for i in range(2)]
        tb = [sb("tb%d" % i, [128, 512]) for i in range(3)]
        allF = [("Ffm", g) for g in range(NG)]
        fkn = tb[0][0:32, :].rearrange("p (h c) -> p h c", h=4)
        for h in range(4):
            k.dma(fkn[:, h, :], R.Ffm[h].rearrange("(kt p) -> kt p", p=128), r=allF, w=["tb0"])
        for h in range(4):
            k.mm(banks[5][:, h * 32:(h + 1) * 32], fkn[:, h, :], G.ident[0:32, 0:32], r=["tb0", "ident"], w=["bank5"])
        k.copy("dve", Fk[:].rearrange("p h c -> p (h c)"), banks[5][:, 0:128], r=["bank5"], w=["Fk"])
        k.dma(Frow[:], R.Ffm, r=allF, w=["Frow"])
        k.dma(sel[:], I["sel"], w=["sel"])
        k.dma(negm[:], I["negmask"], w=["negm"])
        allqk = [("Dqk", g) for g in range(NG)]
        for h in range(4):
            kt_ = A.kT[h % 2]
            kk = "kT%d" % (h % 2)
            k.dma(kt_[:], R.kTD[h * 64:(h + 1) * 64, :], r=allqk, w=[kk])
            for g in range(NG):
                i = k.rot("Dq", 2)
                k.dma(qT[i][:], R.qTD[h * 64:(h + 1) * 64, g * TG:(g + 1) * TG], r=allqk, w=["qT%d" % i])
                k.mm(banks[5][:], sel[:, h, :], Frow[:, g * TG:(g + 1) * TG], r=["sel", "Frow"], w=["bank5"])
                k.copy("act", FqB[i][:], banks[5][:], r=["bank5"], w=["FqB%d" % i])
                k.tt("pool", FqD[i][:].rearrange("p (a b) -> p a b", a=4), FqB[i][:].rearrange("p (a b) -> p a b", a=4),
                     negm[:].unsqueeze(1).to_broadcast([128, 4, 128]), ALU.add, r=["FqB%d" % i, "negm"], w=["FqD%d" % i])
                ob_ = 3 + k.rot("DO", 2)
                O = banks[ob_][:, 0:260].rearrange("p (a e) -> p a e", a=4)
                def d_stage1(kt):
                    m = kt - 4 * g
                    c0 = max(m, 0) * 128
                    N = 512 - c0
                    sbk = k.rot("Ds", 3)
                    k.mm(banks[sbk][:, 0:N], kt_[:, kt * 128:(kt + 1) * 128], qT[i][:, c0:512], r=[kk, "qT%d" % i], w=["bank%d" % sbk])
                    ti = k.rot("Dt", 3)
                    if m < 0:
                        k.stt("dve", tb[ti][:], banks[sbk][:], Fk[:, h, kt:kt + 1], FqB[i][:], ALU.subtract, ALU.add,
                              r=["bank%d" % sbk, "Fk", "FqB%d" % i], w=["tb%d" % ti])
                    else:
                        k.stt("dve", tb[ti][:, 0:128], banks[sbk][:, 0:128], Fk[:, h, kt:kt + 1], FqD[i][:, c0:c0 + 128],
                              ALU.subtract, ALU.add, r=["bank%d" % sbk, "Fk", "FqD%d" % i], w=["tb%d" % ti])
                        if N > 128:
                            k.stt("dve", tb[ti][:, 128:N], banks[sbk][:, 128:N], Fk[:, h, kt:kt + 1], FqB[i][:, c0 + 128:512],
                                  ALU.subtract, ALU.add, r=["bank%d" % sbk, "Fk", "FqB%d" % i], w=["tb%d" % ti])
                    k.act(A.Pm[ti][:, 0:N], tb[ti][:, 0:N], AF.Exp, r=["tb%d" % ti], w=["Pm%d" % ti])
                    return (kt, m, c0, ti)

                def d_stage2(st):
                    kt, m, c0, ti = st
                    for jq in range(max(m, 0), 4):
                        k.mm(O[:, jq, :], A.Pm[ti][:, jq * 128 - c0: jq * 128 - c0 + 128], A.V[:, kt, h * 65:(h + 1) * 65],
                             start=(kt == 0 and jq == 0), stop=(kt == 4 * g + jq), r=["Pm%d" % ti, "V"], w=["bank%d" % ob_], sgc=True)
                pend = []
                for kt in range(4 * g + 4):
                    pend.append(d_stage1(kt))
                    if len(pend) > 2:
                        d_stage2(pend.pop(0))
                while pend:
                    d_stage2(pend.pop(0))
                ri = k.rot("Drec", 2)
                k.recip(A.rec[ri][:, 0:4], O[:, :, 64], r=["bank%d" % ob_], w=["rec%d" % ri])
                k.tt("dve", A.ob[ri][:], O[:, :, 0:64], A.rec[ri][:, 0:4].unsqueeze(2).to_broadcast([128, 4, 64]), ALU.mult,
                     r=["bank%d" % ob_, "rec%d" % ri], w=["ob%d" % ri])
                k.dma(R.yscr[g * TG:(g + 1) * TG, 512 + h * 64: 512 + (h + 1) * 64].rearrange("(a p) e -> p a e", p=128),
                      A.ob[ri][:], r=["ob%d" % ri], w=[("yD", g)], slow=True)
        P.barrier()


def phase_B(G, l):
    nc, P, k, I, R = G.nc, G.P, G.k, G.I, G.R
    banks = G.banks
    lambda_init = 0.8 - 0.6 * math.exp(-0.3 * l)
    with ExitStack() as s:
        sb = lambda name, shape, dt=F32: G.sb(s, name, shape, dt)
        A = attn_common(G, s, R.vB, "B")
        cmf = sb("cmf", [128, 128])
        cm = sb("cm", [128, 128], BF16)
        qp = [[sb("qp%d_%d" % (i, mp), [64, 512], BF16) for mp in range(2)] for i in range(2)]
        lamv = sb("lamv", [128, 4, 32])
        lamw = sb("lamw", [128, 2, 32])
        lams = sb("lams", [128, 4])
        subg = sb("subg", [128, 64])
        o1 = [sb("o1_%d" % i, [128, 4, 64]) for i in range(2)]
        o2 = [sb("o2_%d" % i, [128, 4, 64]) for i in range(2)]
        k.dma(cmf[:], I["cmask"], w=["cmf"])
        k.copy("dve", cm[:], cmf[:], r=["cmf"], w=["cm"])
        for i in range(2):
            for mp in range(2):
                k.memset("pool", qp[i][mp][:], 0.0, w=["qp%d" % i])
        for n, nm in enumerate(("df_lam_q1", "df_lam_k1", "df_lam_q2", "df_lam_k2")):
            bcast_load(G, lamv[:, n, :], I[nm][l:l + 1, :], 32, ["lamv"])
        bcast_load(G, subg[:], I["df_sub_g"][l:l + 1, :], 64, ["subg"])
        k.ts("dve", subg[:], subg[:], 1.0 - lambda_init, ALU.mult, r=["subg"], w=["subg"])
        k.tt("dve", lamw[:, 0, :], lamv[:, 0, :], lamv[:, 1, :], ALU.mult, r=["lamv"], w=["lamw"])
        k.tt("dve", lamw[:, 1, :], lamv[:, 2, :], lamv[:, 3, :], ALU.mult, r=["lamv"], w=["lamw"])
        k.red("dve", lams[:, 0:2], lamw[:], r=["lamw"], w=["lams"])
        k.act(lams[:, 0:2], lams[:, 0:2], AF.Exp, r=["lams"], w=["lams"])
        k.ts("dve", lams[:, 2:3], lams[:, 1:2], -lambda_init, ALU.add, r=["lams"], w=["lams"])
        k.tt("dve", lams[:, 3:4], lams[:, 2:3], lams[:, 0:1], ALU.subtract, r=["lams"], w=["lams"])
        allqk = [("Bqk", g) for g in range(NG)]
        for h in range(4):
            kt_ = A.kT[h % 2]
            kk = "kT%d" % (h % 2)
            k.dma(kt_[:], R.kTB[h * 64:(h + 1) * 64, :], r=allqk, w=[kk])
            for g in range(NG):
                i = k.rot("Bq", 2)
                k.dma(qp[i][0][0:32, :], R.qTB[h * 64:h * 64 + 32, g * TG:(g + 1) * TG], r=allqk, w=["qp%d" % i])
                k.dma(qp[i][1][32:64, :], R.qTB[h * 64 + 32:h * 64 + 64, g * TG:(g + 1) * TG], r=allqk, w=["qp%d" % i])
                oi = k.rot("BO", 2)
                Os = [banks[3 + oi][:, 0:260].rearrange("p (a e) -> p a e", a=4),
                      banks[5 + oi][:, 0:260].rearrange("p (a e) -> p a e", a=4)]
                obk = ["bank%d" % (3 + oi), "bank%d" % (5 + oi)]
                def b_stage1(kt, mp):
                    m = kt - 4 * g
                    c0 = max(m, 0) * 128
                    N = 512 - c0
                    sbk = k.rot("Bs", 3)
                    k.mm(banks[sbk][:, 0:N], kt_[:, kt * 128:(kt + 1) * 128], qp[i][mp][:, c0:512], r=[kk, "qp%d" % i],
                         w=["bank%d" % sbk])
                    ti = k.rot("Bt", 3)
                    k.act(A.Pm[ti][:, 0:N], banks[sbk][:, 0:N], AF.Exp, r=["bank%d" % sbk], w=["Pm%d" % ti])
                    if m >= 0:
                        k.tt("dve", A.Pm[ti][:, 0:128], A.Pm[ti][:, 0:128], cm[:], ALU.mult, r=["Pm%d" % ti, "cm"], w=["Pm%d" % ti])
                    return (kt, mp, m, c0, ti)

                def b_stage2(st):
                    kt, mp, m, c0, ti = st
                    for jq in range(max(m, 0), 4):
                        k.mm(Os[mp][:, jq, :], A.Pm[ti][:, jq * 128 - c0: jq * 128 - c0 + 128], A.V[:, kt, h * 65:(h + 1) * 65],
                             start=(kt == 0 and jq == 0), stop=(kt == 4 * g + jq), r=["Pm%d" % ti, "V"], w=[obk[mp]], sgc=True)
                pend = []
                for kt in range(4 * g + 4):
                    for mp in range(2):
                        pend.append(b_stage1(kt, mp))
                        if len(pend) > 2:
                            b_stage2(pend.pop(0))
                while pend:
                    b_stage2(pend.pop(0))
                ri = k.rot("Brec", 2)
                rc = A.rec[ri]
                rk = "rec%d" % ri
                k.recip(rc[:, 0:4], Os[0][:, :, 64], r=[obk[0]], w=[rk])
                k.recip(rc[:, 4:8], Os[1][:, :, 64], r=[obk[1]], w=[rk])
                k.ts("dve", rc[:, 4:8], rc[:, 4:8], lams[:, 3:4], ALU.mult, r=[rk, "lams"], w=[rk])
                k.tt("dve", o1[ri][:], Os[0][:, :, 0:64], rc[:, 0:4].unsqueeze(2).to_broadcast([128, 4, 64]), ALU.mult,
                     r=[obk[0], rk], w=["o1_%d" % ri])
                k.tt("dve", o2[ri][:], Os[1][:, :, 0:64], rc[:, 4:8].unsqueeze(2).to_broadcast([128, 4, 64]), ALU.mult,
                     r=[obk[1], rk], w=["o2_%d" % ri])
                k.tt("pool", o1[ri][:], o1[ri][:], o2[ri][:], ALU.add, r=["o1_%d" % ri, "o2_%d" % ri], w=["o1_%d" % ri])
                k.tt("pool", o2[ri][:], o1[ri][:], o1[ri][:], ALU.mult, r=["o1_%d" % ri], w=["o2_%d" % ri])
                k.red("dve", rc[:, 0:4], o2[ri][:], r=["o2_%d" % ri], w=[rk])
                k.act(rc[:, 0:4], rc[:, 0:4], AF.Sqrt, bias=EPS, scale=1.0 / 64, r=[rk], w=[rk])
                k.recip(rc[:, 0:4], rc[:, 0:4], r=[rk], w=[rk])
                k.tt("dve", o1[ri][:], o1[ri][:], rc[:, 0:4].unsqueeze(2).to_broadcast([128, 4, 64]), ALU.mult,
                     r=["o1_%d" % ri, rk], w=["o1_%d" % ri])
                k.tt("pool", A.ob[ri][:], o1[ri][:], subg[:].unsqueeze(1).to_broadcast([128, 4, 64]), ALU.mult,
                     r=["o1_%d" % ri, "subg"], w=["ob%d" % ri])
                k.dma(R.yscr[g * TG:(g + 1) * TG, h * 64:(h + 1) * 64].rearrange("(a p) e -> p a e", p=128),
                      A.ob[ri][:], r=["ob%d" % ri], w=[("yB", g)], slow=True)
        P.barrier()


_CACHE = {}


def kernel(**inputs):
    if "prog" not in _CACHE:
        _CACHE["prog"] = build()
    nc, P = _CACHE["prog"]
    consts = make_consts()
    weights = {k: np.ascontiguousarray(np.asarray(inputs[k], dtype=np.float32)) for k in WEIGHT_SHAPES}
    x = np.asarray(inputs["x"], dtype=np.float32)
    c = np.asarray(inputs["c"], dtype=np.float32)
    in_maps = []
    for b in range(8):
        m = {"x": np.ascontiguousarray(x[b]), "c": np.ascontiguousarray(c[b:b + 1])}
        m.update(weights)
        m.update(consts)
        in_maps.append(m)
    res = run_bass_kernel_spmd(nc, in_maps, core_ids=list(range(8)))
    return np.stack([np.asarray(r["out"], dtype=np.float32) for r in res.results], axis=0)
```

```python
import math
import os
import numpy as np
import concourse.bass as bass
import concourse.mybir as mybir
from concourse.bass_utils import run_bass_kernel_spmd
from contextlib import ExitStack

F32 = mybir.dt.float32
BF16 = mybir.dt.bfloat16
AF = mybir.ActivationFunctionType
ALU = mybir.AluOpType
AX = mybir.AxisListType

S = 4096
D = 1024
L = 4
NIN = 2948
DFF = 2816
NG = 8
TG = 512
EPS = 1e-6
ALPHA = math.exp(-0.5)

ENGS = ("pe", "act", "dve", "pool", "sp")
EPOCH = 30000


class Prog:
    def __init__(self, nc, es):
        self.nc = nc
        self.es = es
        self.q = {e: [] for e in ENGS}
        self.cnt = {e: 0 for e in ENGS}
        self.epoch = {e: 0 for e in ENGS}
        self.sems = {}
        self.seen = {e: {} for e in ENGS}
        self.res_w = {}
        self.res_r = {}
        self.dma_val = {}
        self.n_inst = 0
        self.rr = 0

    def _sem(self, key):
        if key not in self.sems:
            self.sems[key] = self.es.enter_context(self.nc.semaphore("s_" + "_".join(str(k) for k in key)))
        return self.sems[key]

    def _deps(self, eng, reads, writes, extra=()):
        need = {}

        def add(ev):
            if ev is None:
                return
            k, v = ev
            if eng == "pe" and k[0] == "pe":
                return
            if need.get(k, 0) < v:
                need[k] = v
        for r in reads:
            add(self.res_w.get(r))
        for w in writes:
            add(self.res_w.get(w))
            for ev in self.res_r.get(w, ()):
                add(ev)
        for ev in extra:
            add(ev)
        waits = []
        for k, v in need.items():
            if self.seen[eng].get(k, 0) >= v:
                continue
            self.seen[eng][k] = v
            waits.append((k, v))
        return waits

    def _commit(self, ev, reads, writes):
        for r in reads:
            lst = self.res_r.setdefault(r, [])
            lst.append(ev)
            if len(lst) > 64:
                mx = {}
                for k, v in lst:
                    if mx.get(k, 0) < v:
                        mx[k] = v
                self.res_r[r] = list(mx.items())
        for w in writes:
            self.res_w[w] = ev
            self.res_r[w] = []

    @staticmethod
    def _is_psum(r):
        return (isinstance(r, str) and r.startswith("bank")) or (isinstance(r, tuple) and r[0] == "hb")

    def op(self, eng, fn, reads=(), writes=()):
        pr = [r for r in reads if self._is_psum(r)]
        if pr:
            writes = list(writes) + pr
        waits = self._deps(eng, reads, writes)
        if self.cnt[eng] >= EPOCH:
            self.epoch[eng] += 1
            self.cnt[eng] = 0
        self.cnt[eng] += 1
        key = (eng, self.epoch[eng])
        ev = (key, self.cnt[eng])
        self.q[eng].append((waits, fn, key, 1))
        self._commit(ev, reads, writes)
        self.n_inst += 1
        return ev

    def dma(self, queue, pairs, reads=(), writes=(), sem=None):
        if sem is None:
            sem = ("dma", "rr%d" % (self.rr % 20))
            self.rr += 1
        key = sem
        prev = self.dma_val.get(key, 0)
        extra = [(key, prev)] if prev > 0 else []
        waits = self._deps(queue, reads, writes, extra)
        val = prev
        for i, pr in enumerate(pairs):
            out_ap, in_ap = pr[0], pr[1]
            kw = pr[2] if len(pr) > 2 else {}
            val += 16

            def fn(e, out_ap=out_ap, in_ap=in_ap, kw=kw):
                return e.dma_start(out=out_ap, in_=in_ap, **kw)
            self.q[queue].append((waits if i == 0 else [], fn, key, 16))
            self.n_inst += 1
        self.dma_val[key] = val
        ev = (key, val)
        self._commit(ev, reads, writes)
        return ev

    def barrier(self):
        evs = []
        for e in ENGS:
            for ep in range(self.epoch[e] + 1):
                k = (e, ep)
                v = self.cnt[e] if ep == self.epoch[e] else EPOCH
                if v > 0:
                    evs.append((k, v))
        for k, v in self.dma_val.items():
            evs.append((k, v))
        for e in ENGS:
            waits = []
            for k, v in evs:
                if self.seen[e].get(k, 0) < v:
                    self.seen[e][k] = v
                    waits.append((k, v))
            if waits:
                self.q[e].append((waits, None, None, 0))
        self.res_w = {}
        self.res_r = {}

    def emit(self):
        nc = self.nc
        for e in ENGS:
            for (waits, fn, key, inc) in self.q[e]:
                for k, v in waits:
                    self._sem(k)
                if key is not None:
                    self._sem(key)
        block = self.es.enter_context(nc.Block())
        engmap = {"pe": block.tensor, "act": block.scalar, "dve": block.vector, "pool": block.gpsimd,
                  "sp": block.sync}
        for e in ENGS:
            items = self.q[e]

            def body(eng, items=items):
                for (waits, fn, key, inc) in items:
                    for k, v in waits:
                        eng.wait_ge(self.sems[k], v)
                    if fn is not None:
                        ins = fn(eng)
                        ins.then_inc(self.sems[key], inc)
            engmap[e](body)


class K:
    def __init__(self, P):
        self.P = P
        self._rot = {}

    def rot(self, name, n):
        i = self._rot.get(name, 0)
        self._rot[name] = i + 1
        return i % n

    def mm(self, out, lhsT, rhs, start=True, stop=True, r=(), w=(), sgc=False):
        if sgc:
            return self.P.op("pe", lambda e: e.matmul(out, lhsT=lhsT, rhs=rhs, start=start, stop=stop, skip_group_check=True), r, w)
        return self.P.op("pe", lambda e: e.matmul(out, lhsT=lhsT, rhs=rhs, start=start, stop=stop), r, w)

    def tr(self, out, in_, ident, r=(), w=()):
        return self.P.op("pe", lambda e: e.transpose(out=out, in_=in_, identity=ident), r, w)

    def act(self, out, in_, func, bias=None, scale=None, accum_out=None, r=(), w=(), eng="act"):
        kw = {}
        if bias is not None:
            kw["bias"] = bias
        if scale is not None:
            kw["scale"] = scale
        if accum_out is not None:
            kw["accum_out"] = accum_out
        return self.P.op("act", lambda e: e.activation(out=out, in_=in_, func=func, **kw), r, w)

    def copy(self, eng, out, in_, r=(), w=()):
        if eng == "act":
            return self.P.op("act", lambda e: e.copy(out=out, in_=in_), r, w)
        return self.P.op(eng, lambda e: e.tensor_copy(out=out, in_=in_), r, w)

    def tt(self, eng, out, in0, in1, op, r=(), w=()):
        return self.P.op(eng, lambda e: e.tensor_tensor(out=out, in0=in0, in1=in1, op=op), r, w)

    def ts(self, eng, out, in0, s1, op0, s2=None, op1=None, r=(), w=()):
        if op1 is None:
            return self.P.op(eng, lambda e: e.tensor_scalar(out=out, in0=in0, scalar1=s1, scalar2=None, op0=op0), r, w)
        return self.P.op(eng, lambda e: e.tensor_scalar(out=out, in0=in0, scalar1=s1, scalar2=s2, op0=op0, op1=op1), r, w)

    def stt(self, eng, out, in0, scalar, in1, op0, op1, r=(), w=()):
        return self.P.op(eng, lambda e: e.scalar_tensor_tensor(out=out, in0=in0, scalar=scalar, in1=in1, op0=op0, op1=op1), r, w)

    def red(self, eng, out, in_, op=ALU.add, r=(), w=()):
        return self.P.op(eng, lambda e: e.tensor_reduce(out=out, in_=in_, axis=AX.X, op=op), r, w)

    def recip(self, out, in_, r=(), w=()):
        return self.P.op("dve", lambda e: e.reciprocal(out=out, in_=in_), r, w)

    def memset(self, eng, ap, val, w=()):
        return self.P.op(eng, lambda e: e.memset(ap, val), (), w)

    def scan(self, out, d0, d1, initial, op0, op1, r=(), w=()):
        return self.P.op("dve", lambda e: e.tensor_tensor_scan(out=out, data0=d0, data1=d1, initial=initial, op0=op0, op1=op1), r, w)

    def dma(self, out, in_, r=(), w=(), q="sp", slow=False, sem=None):
        kw = {"allow_slow_non_contiguous": True} if slow else {}
        return self.P.dma(q, [(out, in_, kw)], r, w, sem=sem)


def make_consts():
    c = {}
    c["ident"] = np.eye(128, dtype=np.float32)
    bo64 = np.zeros((128, 128), np.float32)
    bo64[:64, :64] = 1
    bo64[64:, 64:] = 1
    c["bo64"] = bo64
    bo32 = np.zeros((128, 128), np.float32)
    for i in range(4):
        bo32[i * 32:(i + 1) * 32, i * 32:(i + 1) * 32] = 1
    c["bo32"] = bo32
    prot = np.zeros((128, 128), np.float32)
    for b in range(4):
        for d in range(16):
            prot[b * 32 + d + 16, b * 32 + d] = -1.0
            prot[b * 32 + d, b * 32 + d + 16] = 1.0
    c["prot"] = prot
    inv = 1.0 / (10000.0 ** (np.arange(0, 32, 2, dtype=np.float32) / 32.0))
    ang = np.arange(S, dtype=np.float32)[:, None] * inv[None, :]
    cos = np.cos(ang).astype(np.float32).T
    sin = np.sin(ang).astype(np.float32).T
    c["cosT"] = np.ascontiguousarray(np.tile(cos, (8, 1)))
    c["sinT"] = np.ascontiguousarray(np.tile(sin, (8, 1)))
    k = np.arange(128)[:, None]
    q = np.arange(128)[None, :]
    c["negmask"] = np.where(k > q, -1e30, 0.0).astype(np.float32)
    c["cmask"] = ((k // 64) <= (q // 64)).astype(np.float32)
    c["triu"] = (k <= q).astype(np.float32)
    k6 = np.arange(64)[:, None]
    q6 = np.arange(64)[None, :]
    m64 = np.zeros((64, 3, 64), np.float32)
    m64[:, 0, :] = (k6 < q6)
    m64[:, 1, :] = (k6 <= q6)
    m64[:, 2, :] = (k6 > q6)
    c["m64"] = m64
    sel = np.zeros((4, 4, 128), np.float32)
    for h in range(4):
        sel[h, h, :] = 1.0
    c["sel"] = sel.transpose(1, 0, 2).copy()
    return c


CONST_SHAPES = {"ident": [128, 128], "bo64": [128, 128], "bo32": [128, 128], "prot": [128, 128],
                "cosT": [128, S], "sinT": [128, S], "negmask": [128, 128], "cmask": [128, 128],
                "triu": [128, 128], "m64": [64, 3, 64], "sel": [4, 4, 128]}

WEIGHT_SHAPES = {
    'ada_w': [L, D, 6 * D], 'ada_b': [L, 6 * D], 'norm1_g': [L, D], 'norm2_g': [L, D],
    'w_in': [L, D, NIN], 'w_out': [L, D, D],
    'rw_mu': [L, 896], 'rw_w0': [L, 256], 'rw_w_up': [L, 32, 256], 'rw_a0': [L, 256], 'rw_a_up': [L, 32, 256],
    'rw_g_up': [L, 64, 256], 'rw_k_k': [L, 256], 'rw_k_a': [L, 256], 'rw_r_k': [L, 4, 64],
    'rw_ln_g': [L, 256], 'rw_ln_b': [L, 256],
    'df_lam_q1': [L, 32], 'df_lam_k1': [L, 32], 'df_lam_q2': [L, 32], 'df_lam_k2': [L, 32],
    'df_q_g': [L, 32], 'df_k_g': [L, 32], 'df_sub_g': [L, 64],
    'sg_w': [L, 4, 128, 128], 'sg_b': [L, 4, 128], 'sg_ln_g': [L, 256], 'sg_ln_b': [L, 256],
    'fx_q_g': [L, 64], 'fx_k_g': [L, 64], 'fx_f_b': [L, 4],
    'ffn_up': [L, D, 2 * DFF], 'ffn_conv': [L, 3, 2 * DFF], 'ffn_conv_b': [L, 2 * DFF], 'ffn_down': [L, DFF, D],
}


class Ctx:
    pass


def build(layers=(0, 1, 2, 3), phases=("P1", "A", "B", "D", "P3"), debug=False, y_in=False):
    nc = bass.Bass("TRN2", target_bir_lowering=False)
    dkind = "ExternalOutput" if debug else "Internal"
    I = {}

    def din(name, shape):
        I[name] = nc.dram_tensor(name, list(shape), F32, kind="ExternalInput").ap()
    din("x", [S, D])
    din("c", [1, D])
    for k, shp in WEIGHT_SHAPES.items():
        din(k, shp)
    for k, shp in CONST_SHAPES.items():
        din(k, shp)
    out = nc.dram_tensor("out", [S, D], F32, kind="ExternalOutput").ap()

    def dscr(name, shape, dt, kind=None):
        return nc.dram_tensor(name, list(shape), dt, kind=kind or dkind).ap()
    R = Ctx()
    R.xres = dscr("xres", [S, D], F32)
    R.pmA = dscr("pmA", [896, S], F32)
    R.qTB = dscr("qTB", [256, S], BF16)
    R.kTB = dscr("kTB", [256, S], BF16)
    R.vB = dscr("vB", [S, 260], BF16)
    R.qTD = dscr("qTD", [256, S], BF16)
    R.kTD = dscr("kTD", [256, S], BF16)
    R.vD = dscr("vD", [S, 260], BF16)
    R.Ffm = dscr("Ffm", [4, S], F32)
    if y_in:
        R.yscr = nc.dram_tensor("yscr_in", [S, 768], F32, kind="ExternalInput").ap()
        R.yTA = nc.dram_tensor("yTA_in", [256, S], F32, kind="ExternalInput").ap()
    else:
        R.yscr = dscr("yscr", [S, 768], BF16)
        R.yTA = dscr("yTA", [256, S], BF16)
    R.upbf = dscr("upbf", [L, 11, 128, 8 * 512], BF16, kind="Internal")
    R.dnbf = dscr("dnbf", [L, 128, 22 * 1024], BF16, kind="Internal")

    with ExitStack() as es:
        P = Prog(nc, es)
        k = K(P)
        G = Ctx()
        G.nc, G.P, G.k, G.I, G.R, G.out = nc, P, k, I, R, out
        G.y_in = y_in
        G.dbg_x1 = nc.dram_tensor("dbg_x1", [S, D], F32, kind="ExternalOutput").ap() if debug else None

        uid = [0]

        def sb(stack, name, shape, dt=F32):
            uid[0] += 1
            return stack.enter_context(nc.sbuf_tensor("%s_u%d" % (name, uid[0]), list(shape), dt))
        G.sb = sb
        G.banks = [es.enter_context(nc.psum_tensor("bank%d" % i, [128, 512], F32)) for i in range(8)]
        G.ident = sb(es, "ident", [128, 128])
        G.identb = sb(es, "identb", [128, 128], BF16)
        G.ones_row = sb(es, "ones_row", [1, 128])
        G.condB = sb(es, "condB", [128, 8, 128])
        G.modB = sb(es, "modB", [128, 6 * D])
        k.dma(G.ident[:], I["ident"], w=["ident"])
        k.copy("dve", G.identb[:], G.ident[:], r=["ident"], w=["identb"])
        k.memset("dve", G.ones_row[:], 1.0, w=["ones_row"])
        with ExitStack() as s0:
            cT = sb(s0, "cT", [128, 8])
            cS = sb(s0, "cS", [128, 8])
            k.dma(cT[:], I["c"].rearrange("o (kc p) -> p (o kc)", p=128), w=["cT"], slow=True)
            k.act(cS[:], cT[:], AF.Silu, r=["cT"], w=["cS"])
            k.copy("dve", G.condB[:], cS[:].unsqueeze(2).to_broadcast([128, 8, 128]), r=["cS"], w=["condB"])
            P.barrier()

        for li, l in enumerate(layers):
            xsrc = I["x"] if li == 0 else R.xres
            xdst = out if li == len(layers) - 1 else R.xres
            if "P3" in phases:
                prep_ffn(G, l)
            layer_setup(G, l)
            if "P1" in phases:
                phase_p1(G, l, xsrc)
            if "A" in phases:
                phase_A(G, l)
            if "B" in phases:
                phase_B(G, l)
            if "D" in phases:
                phase_D(G, l)
            if "P3" in phases:
                phase_p3(G, l, xsrc, xdst)
        P.barrier()
        P.emit()
    return nc, P


def prep_ffn(G, l):
    nc, P, k, I, R = G.nc, G.P, G.k, G.I, G.R
    with ExitStack() as s:
        st = [G.sb(s, "pf_st%d" % i, [128, 8, 512]) for i in range(2)]
        sbf = [G.sb(s, "pf_bf%d" % i, [128, 8, 512], BF16) for i in range(2)]
        up = I["ffn_up"][l].rearrange("(kc p) n -> p kc n", p=128)
        dn = I["ffn_down"][l].rearrange("(fc p) d -> p fc d", p=128)
        engs = ["dve", "act", "dve", "act", "pool"]
        jobs = []
        for u in range(11):
            jobs.append((up[:, :, u * 512:(u + 1) * 512], R.upbf[l, u].rearrange("p (kc n) -> p kc n", kc=8), 8, 512))
        for u in range(11):
            jobs.append((dn[:, 2 * u:2 * u + 2, :], R.dnbf[l][:, 2 * u * 1024:(2 * u + 2) * 1024].rearrange("p (a d) -> p a d", a=2), 2, 1024))

        def load(i):
            src, dst, a, b = jobs[i]
            k.dma(st[i % 2][:].rearrange("p a b -> p (a b)")[:, 0:a * b].rearrange("p (a b) -> p a b", a=a), src,
                  w=["pf_st%d" % (i % 2)])
        load(0)
        load(1)
        for i in range(len(jobs)):
            src, dst, a, b = jobs[i]
            sv = st[i % 2][:].rearrange("p a b -> p (a b)")[:, 0:a * b]
            bv = sbf[i % 2][:].rearrange("p a b -> p (a b)")[:, 0:a * b]
            k.copy(engs[i % 5], bv, sv, r=["pf_st%d" % (i % 2)], w=["pf_bf%d" % (i % 2)])
            k.dma(dst, bv.rearrange("p (a b) -> p a b", a=a), r=["pf_bf%d" % (i % 2)], w=[("ffnw", l)])
            if i + 2 < len(jobs):
                load(i + 2)
        P.barrier()


def layer_setup(G, l):
    nc, P, k, I, R = G.nc, G.P, G.k, G.I, G.R
    banks = G.banks
    with ExitStack() as s:
        aw = [G.sb(s, "ls_aw%d" % i, [128, 8, 512]) for i in range(2)]
        rows = G.sb(s, "ls_rows", [1, 8 * D])
        k.dma(rows[:, 0:6 * D], I["ada_b"][l:l + 1, :], w=["ls_rows_b"])
        k.dma(rows[:, 6 * D:7 * D], I["norm1_g"][l:l + 1, :], w=["ls_rows_g"])
        k.dma(rows[:, 7 * D:8 * D], I["norm2_g"][l:l + 1, :], w=["ls_rows_g"])
        awv = I["ada_w"][l].rearrange("(kc p) n -> p kc n", p=128)
        for cc in range(12):
            b = cc % 2
            k.dma(aw[b][:], awv[:, :, cc * 512:(cc + 1) * 512], w=["ls_aw%d" % b])
            bk = banks[b]
            for kc in range(8):
                k.mm(bk[:], G.condB[:, kc, :], aw[b][:, kc, :], start=(kc == 0), stop=False,
                     r=["condB", "ls_aw%d" % b], w=["bank%d" % b])
            k.mm(bk[:], G.ones_row[0:1, :], rows[0:1, cc * 512:(cc + 1) * 512], start=False, stop=True,
                 r=["ones_row", "ls_rows_b"], w=["bank%d" % b])
            k.copy("act" if cc % 2 else "dve", G.modB[:, cc * 512:(cc + 1) * 512], bk[:], r=["bank%d" % b], w=["modB"])
        for gi, (goff, slot) in enumerate(((6 * D, 1 * D), (7 * D, 4 * D))):
            for hf in range(2):
                b = 2 + hf
                k.mm(banks[b][:], G.ones_row[0:1, :], rows[0:1, goff + hf * 512: goff + (hf + 1) * 512],
                     r=["ones_row", "ls_rows_g"], w=["bank%d" % b])
                sl = G.modB[:, slot + hf * 512: slot + (hf + 1) * 512]
                k.stt("dve", sl, sl, 1.0, banks[b][:], ALU.add, ALU.mult, r=["modB", "bank%d" % b], w=["modB"])
        P.barrier()


def norm_mod_T(G, xt, goff, shoff, T):
    k = G.k
    banks = G.banks
    for j in range(4):
        k.act(T.tmp[:], xt[:, j, :], AF.Square, r=["xt"], w=["tmp"])
        k.red("dve", T.ss[:, j:j + 1], T.tmp[:], r=["tmp"], w=["ss"])
    k.act(T.rstd[:], T.ss[:], AF.Sqrt, bias=EPS, scale=1.0 / D, r=["ss"], w=["rstd"])
    k.recip(T.rstd[:], T.rstd[:], r=["rstd"], w=["rstd"])
    for j in range(4):
        k.stt("dve", T.tmp[:], xt[:, j, :], T.rstd[:, j:j + 1], G.modB[:, goff:goff + D], ALU.mult, ALU.mult,
              r=["xt", "rstd", "modB"], w=["tmp"])
        k.tt("dve", T.hb[:, j, :], T.tmp[:], G.modB[:, shoff:shoff + D], ALU.add, r=["tmp", "modB"], w=["hb"])
    for j in range(4):
        b = 5 + (j % 2)
        pv = banks[b][:].bitcast(BF16).rearrange("p (a t) -> p a t", a=8)
        for kc in range(8):
            k.tr(pv[:, kc, :], T.hb[:, j, kc * 128:(kc + 1) * 128], G.identb[:], r=["hb", "identb"], w=["bank%d" % b])
        k.copy("act" if j % 2 else "dve", T.hT[:, :, j * 128:(j + 1) * 128], pv, r=["bank%d" % b], w=[getattr(T, "hTk", "hT")])


def bcast_load(G, tile_ap, row_ap, n, w):
    G.k.dma(tile_ap, row_ap.to_broadcast([128, n]), w=w, slow=True)


def phase_p1(G, l, xsrc):
    nc, P, k, I, R = G.nc, G.P, G.k, G.I, G.R
    banks = G.banks
    with ExitStack() as s:
        sb = lambda name, shape, dt=F32: G.sb(s, name, shape, dt)
        T = Ctx()
        w_in = sb("w_in", [128, 8, NIN], BF16)
        stg = [sb("p1_stg%d" % i, [128, 1474]) for i in range(2)]
        xt = sb("xt", [128, 4, D])
        T.sqj = sb("sqj", [128, D], BF16)
        T.ss = sb("ss", [128, 4])
        T.rstd = sb("rstd", [128, 4])
        T.tmp = sb("tmp", [128, D])
        T.hb = sb("hb", [128, 4, D], BF16)
        hTs = [sb("hT%d" % i, [128, 8, 512], BF16) for i in range(2)]
        fA = [sb("fA%d" % i, [128, 512]) for i in range(3)]
        fB = [sb("fB%d" % i, [128, 512]) for i in range(3)]
        fC = [sb("fC%d" % i, [128, 512]) for i in range(3)]
        obf = [sb("obf%d" % i, [128, 512], BF16) for i in range(3)]
        xnb = [sb("xnb%d" % i, [128, 512], BF16) for i in range(2)]
        paA = sb("paA", [128, 7, 513])
        pmo = [sb("pmo%d" % i, [128, 512]) for i in range(2)]
        cs = [sb("cos%d" % i, [128, 512]) for i in range(2)]
        sn = [sb("sin%d" % i, [128, 512]) for i in range(2)]
        bo64 = sb("bo64", [128, 128])
        bo32 = sb("bo32", [128, 128])
        protf = sb("protf", [128, 128])
        protb = sb("protb", [128, 128], BF16)
        gcol = sb("gcol", [128, 8])
        mu = sb("mu", [128, 7])
        negfb = sb("negfb", [4, 1])
        ones4 = sb("ones4", [4, 512])
        Fg = [sb("Fg%d" % i, [4, 512]) for i in range(2)]
        f4 = sb("f4", [4, 512])
        vt = [sb("vt%d" % i, [128, 4, 65], BF16) for i in range(4)]
        sgw = sb("sgw", [128, 4, 128])
        WgT = sb("WgT", [128, 4, 128], BF16)
        triu = sb("triu", [128, 128])
        sgbT = sb("sgbT", [128, 4])
        lnCg = sb("lnCg", [128, 256])
        lnCb = sb("lnCb", [128, 256])
        glC = [sb("glC%d" % i, [128, 512]) for i in range(2)]
        stC = [sb("stC%d" % i, [128, 8]) for i in range(2)]
        tmpc = [sb("tmpc%d" % i, [128, 256]) for i in range(2)]
        vnb = [sb("vnb%d" % i, [128, 256], BF16) for i in range(2)]
        ycb = [sb("ycb%d" % i, [128, 256], BF16) for i in range(2)]

        for kc in range(8):
            for hf in range(2):
                i = (kc * 2 + hf) % 2
                k.dma(stg[i][:], I["w_in"][l, kc * 128:(kc + 1) * 128, hf * 1474:(hf + 1) * 1474], w=["p1_stg%d" % i])
                k.copy(("dve", "pool", "act")[(kc * 2 + hf) % 3], w_in[:, kc, hf * 1474:(hf + 1) * 1474], stg[i][:],
                       r=["p1_stg%d" % i], w=["w_in"])
        k.dma(bo64[:], I["bo64"], w=["bo64"])
        k.dma(bo32[:], I["bo32"], w=["bo32"])
        k.dma(protf[:], I["prot"], w=["protf"])
        k.copy("dve", protb[:], protf[:], r=["protf"], w=["protb"])
        k.dma(triu[:], I["triu"], w=["triu"])
        for rep in range(2):
            k.dma(gcol[rep * 64:(rep + 1) * 64, 0:1], I["fx_q_g"][l].rearrange("(d o) -> d o", o=1), w=["gcol"], slow=True)
            k.dma(gcol[rep * 64:(rep + 1) * 64, 1:2], I["fx_k_g"][l].rearrange("(d o) -> d o", o=1), w=["gcol"], slow=True)
        for rep in range(4):
            k.dma(gcol[rep * 32:(rep + 1) * 32, 2:3], I["df_q_g"][l].rearrange("(d o) -> d o", o=1), w=["gcol"], slow=True)
            k.dma(gcol[rep * 32:(rep + 1) * 32, 3:4], I["df_k_g"][l].rearrange("(d o) -> d o", o=1), w=["gcol"], slow=True)
        k.ts("dve", gcol[:, 0:1], gcol[:, 0:1], 0.125, ALU.mult, r=["gcol"], w=["gcol"])
        k.ts("dve", gcol[:, 2:3], gcol[:, 2:3], 32.0 ** -0.5, ALU.mult, r=["gcol"], w=["gcol"])
        k.dma(mu[:], I["rw_mu"][l].rearrange("(c p) -> p c", p=128), w=["mu"], slow=True)
        k.dma(negfb[:], I["fx_f_b"][l].rearrange("(h o) -> h o", o=1), w=["negfb"], slow=True)
        k.ts("dve", negfb[:], negfb[:], -1.0, ALU.mult, r=["negfb"], w=["negfb"])
        k.memset("dve", ones4[:], 1.0, w=["ones4"])
        k.memset("pool", paA[:], 0.0, w=["paA%d" % i for i in range(7)])
        for i in range(4):
            k.memset("pool", vt[i][:], 1.0, w=["vt%d" % i])
        k.dma(sgw[:], I["sg_w"][l].rearrange("g i j -> i g j"), w=["sgw"])
        for g in range(4):
            k.tr(banks[0][:, g * 128:(g + 1) * 128], sgw[:, g, :], G.ident[:], r=["sgw", "ident"], w=["bank0"])
        k.tt("dve", WgT[:], banks[0][:].rearrange("p (g i) -> p g i", g=4),
             triu[:].unsqueeze(1).to_broadcast([128, 4, 128]), ALU.mult, r=["bank0", "triu"], w=["WgT"])
        k.dma(sgbT[:], I["sg_b"][l].rearrange("g i -> i g"), w=["sgbT"], slow=True)
        bcast_load(G, lnCg[:], I["sg_ln_g"][l:l + 1, :], 256, ["lnCg"])
        bcast_load(G, lnCb[:], I["sg_ln_b"][l:l + 1, :], 256, ["lnCb"])

        cur = Ctx()

        def fm_mm(col0, ncols, bk):
            for kc in range(8):
                k.mm(banks[bk][0:ncols, :], w_in[:, kc, col0:col0 + ncols], cur.hT[:, kc, :], start=(kc == 0), stop=(kc == 7),
                     r=["w_in", cur.hTk], w=["bank%d" % bk])

        def load_norm(g):
            tsl_ = slice(g * TG, (g + 1) * TG)
            k.dma(xt[:], xsrc[tsl_, :].rearrange("(j p) d -> p j d", p=128), r=[("x", g)], w=["xt"])
            T.hT = hTs[g % 2]
            T.hTk = "hT%d" % (g % 2)
            norm_mod_T(G, xt, 1 * D, 0, T)
        load_norm(0)

        for g in range(NG):
            tsl = slice(g * TG, (g + 1) * TG)
            k.dma(cs[g % 2][:], I["cosT"][:, tsl], w=["cos%d" % (g % 2)])
            k.dma(sn[g % 2][:], I["sinT"][:, tsl], w=["sin%d" % (g % 2)])
            cur.hT = hTs[g % 2]
            cur.hTk = "hT%d" % (g % 2)
            for ci in range(7):
                bk = k.rot("p1bank", 3)
                fm_mm(ci * 128, 128, bk)
                k.copy("pool", paA[:, ci, 0:1], paA[:, ci, 512:513], r=["paA%d" % ci], w=["paA%d" % ci])
                k.copy("act", paA[:, ci, 1:513], banks[bk][:], r=["bank%d" % bk], w=["paA%d" % ci])
                i = k.rot("fA", 3)
                k.tt("pool", fA[i][:], paA[:, ci, 0:512], paA[:, ci, 1:513], ALU.subtract, r=["paA%d" % ci], w=["fA%d" % i])
                o = k.rot("pmo", 2)
                k.stt("dve", pmo[o][:], fA[i][:], mu[:, ci:ci + 1], paA[:, ci, 1:513], ALU.mult, ALU.add,
                      r=["fA%d" % i, "mu", "paA%d" % ci], w=["pmo%d" % o])
                k.dma(R.pmA[ci * 128:(ci + 1) * 128, tsl], pmo[o][:], r=["pmo%d" % o], w=[("pmA", g)])
            if g + 1 < NG:
                load_norm(g + 1)
            for (mix, col0, gi, dst, rope) in (("B", 896, 2, R.qTB, True), ("B", 1152, 3, R.kTB, True),
                                               ("D", 2176, 0, R.qTD, False), ("D", 2432, 1, R.kTD, False)):
                for ci in range(2):
                    bk = k.rot("p1bank", 3)
                    fm_mm(col0 + ci * 128, 128, bk)
                    a = k.rot("fA", 3)
                    k.act(fA[a][:], banks[bk][:], AF.Square, r=["bank%d" % bk], w=["fA%d" % a])
                    sbk = 3 + k.rot("p1sbank", 2)
                    k.mm(banks[sbk][:], bo32[:] if mix == "B" else bo64[:], fA[a][:], r=["bo32", "bo64", "fA%d" % a],
                         w=["bank%d" % sbk])
                    b = k.rot("fB", 3)
                    nd = 32.0 if mix == "B" else 64.0
                    k.act(fB[b][:], banks[sbk][:], AF.Ln, bias=EPS, scale=1.0 / nd, r=["bank%d" % sbk], w=["fB%d" % b])
                    k.act(fB[b][:], fB[b][:], AF.Exp, scale=-0.5, r=["fB%d" % b], w=["fB%d" % b])
                    o = k.rot("obf", 3)
                    if not rope:
                        k.stt("dve", obf[o][:], banks[bk][:], gcol[:, gi:gi + 1], fB[b][:], ALU.mult, ALU.mult,
                              r=["bank%d" % bk, "gcol", "fB%d" % b], w=["obf%d" % o])
                    else:
                        c_ = k.rot("fC", 3)
                        k.stt("dve", fC[c_][:], banks[bk][:], gcol[:, gi:gi + 1], fB[b][:], ALU.mult, ALU.mult,
                              r=["bank%d" % bk, "gcol", "fB%d" % b], w=["fC%d" % c_])
                        xb = k.rot("xnb", 2)
                        k.copy("act", xnb[xb][:], fC[c_][:], r=["fC%d" % c_], w=["xnb%d" % xb])
                        rbk = 3 + k.rot("p1sbank", 2)
                        k.mm(banks[rbk][:], protb[:], xnb[xb][:], r=["protb", "xnb%d" % xb], w=["bank%d" % rbk])
                        a2 = k.rot("fA", 3)
                        k.tt("pool", fA[a2][:], fC[c_][:], cs[g % 2][:], ALU.mult, r=["fC%d" % c_, "cos%d" % (g % 2)], w=["fA%d" % a2])
                        b2 = k.rot("fB", 3)
                        k.tt("dve", fB[b2][:], banks[rbk][:], sn[g % 2][:], ALU.mult, r=["bank%d" % rbk, "sin%d" % (g % 2)],
                             w=["fB%d" % b2])
                        k.tt("pool", obf[o][:], fA[a2][:], fB[b2][:], ALU.add, r=["fA%d" % a2, "fB%d" % b2], w=["obf%d" % o])
                    k.dma(dst[ci * 128:(ci + 1) * 128, tsl], obf[o][:], r=["obf%d" % o], w=[(mix + "qk", g)])
            bk = k.rot("p1bank", 3)
            fm_mm(2944, 4, bk)
            k.act(f4[:], banks[bk][0:4, :], AF.Exp, bias=negfb[:, 0:1], scale=-1.0, r=["bank%d" % bk, "negfb"], w=["f4"])
            k.act(f4[:], f4[:], AF.Ln, bias=1.0, scale=1.0, r=["f4"], w=["f4"])
            if g == 0:
                k.scan(Fg[0][:], ones4[:], f4[:], 0.0, ALU.mult, ALU.subtract, r=["ones4", "f4"], w=["Fg0"])
            else:
                k.scan(Fg[g % 2][:], ones4[:], f4[:], Fg[(g - 1) % 2][:, 511:512], ALU.mult, ALU.subtract,
                       r=["ones4", "f4", "Fg%d" % ((g - 1) % 2)], w=["Fg%d" % (g % 2)])
            k.dma(R.Ffm[:, tsl], Fg[g % 2][:], r=["Fg%d" % (g % 2)], w=[("Ffm", g)])
            for j in range(4):
                rows = slice(g * TG + j * 128, g * TG + (j + 1) * 128)
                for (mix, col0, dst) in (("B", 1408, R.vB), ("D", 2688, R.vD)):
                    bk = k.rot("p1bank", 3)
                    for kc in range(8):
                        k.mm(banks[bk][:, 0:256], cur.hT[:, kc, j * 128:(j + 1) * 128], w_in[:, kc, col0:col0 + 256],
                             start=(kc == 0), stop=(kc == 7), r=["w_in", cur.hTk], w=["bank%d" % bk])
                    vi = k.rot("vt", 4)
                    k.copy("act" if mix == "B" else "dve", vt[vi][:, :, 0:64], banks[bk][:, 0:256].rearrange("p (h e) -> p h e", h=4),
                           r=["bank%d" % bk], w=["vt%d" % vi])
                    k.dma(dst[rows, :], vt[vi][:].rearrange("p h e -> p (h e)"), r=["vt%d" % vi], w=[(mix + "v", g)])
                bk = k.rot("p1bank", 3)
                for kc in range(8):
                    k.mm(banks[bk][:], cur.hT[:, kc, j * 128:(j + 1) * 128], w_in[:, kc, 1664:2176],
                         start=(kc == 0), stop=(kc == 7), r=["w_in", cur.hTk], w=["bank%d" % bk])
                ci = k.rot("glC", 2)
                gl, st, tc, vb, yb = glC[ci], stC[ci], tmpc[ci], vnb[ci], ycb[ci]
                kk = "C%d" % ci
                k.act(gl[:], banks[bk][:], AF.Gelu, r=["bank%d" % bk], w=[kk + "gl"])
                k.red("dve", st[:, 0:1], gl[:, 256:512], r=[kk + "gl"], w=[kk + "st"])
                k.act(tc[:], gl[:, 256:512], AF.Square, r=[kk + "gl"], w=[kk + "tc"])
                k.red("dve", st[:, 1:2], tc[:], r=[kk + "tc"], w=[kk + "st"])
                k.ts("dve", st[:, 2:3], st[:, 0:1], 1.0 / 256, ALU.mult, r=[kk + "st"], w=[kk + "st"])
                k.tt("dve", st[:, 3:4], st[:, 2:3], st[:, 2:3], ALU.mult, r=[kk + "st"], w=[kk + "st"])
                k.stt("dve", st[:, 4:5], st[:, 1:2], 1.0 / 256, st[:, 3:4], ALU.mult, ALU.subtract, r=[kk + "st"], w=[kk + "st"])
                k.act(st[:, 5:6], st[:, 4:5], AF.Sqrt, bias=EPS, scale=1.0, r=[kk + "st"], w=[kk + "st"])
                k.recip(st[:, 5:6], st[:, 5:6], r=[kk + "st"], w=[kk + "st"])
                k.ts("dve", tc[:], gl[:, 256:512], st[:, 2:3], ALU.subtract, st[:, 5:6], ALU.mult, r=[kk + "gl", kk + "st"], w=[kk + "tc"])
                k.tt("pool", tc[:], tc[:], lnCg[:], ALU.mult, r=[kk + "tc", "lnCg"], w=[kk + "tc"])
                k.tt("pool", vb[:], tc[:], lnCb[:], ALU.add, r=[kk + "tc", "lnCb"], w=[kk + "vb"])
                sbk = 3 + k.rot("p1sbank", 2)
                for hg in range(4):
                    k.mm(banks[sbk][:, hg * 64:(hg + 1) * 64], WgT[:, hg, :], vb[:, hg * 64:(hg + 1) * 64],
                         r=["WgT", kk + "vb"], w=["bank%d" % sbk])
                k.tt("dve", tc[:].rearrange("p (h e) -> p h e", h=4), banks[sbk][:, 0:256].rearrange("p (h e) -> p h e", h=4),
                     sgbT[:].unsqueeze(2).to_broadcast([128, 4, 64]), ALU.add, r=["bank%d" % sbk, "sgbT", kk + "vb"], w=[kk + "tc"])
                k.tt("pool", yb[:], tc[:], gl[:, 0:256], ALU.mult, r=[kk + "tc", kk + "gl"], w=[kk + "yb"])
                if not G.y_in:
                    k.dma(R.yscr[rows, 256:512], yb[:], r=[kk + "yb"], w=[("yC", g)])
        P.barrier()


def phase_p3(G, l, xsrc, xdst):
    nc, P, k, I, R = G.nc, G.P, G.k, G.I, G.R
    banks = G.banks
    ydt = F32 if G.y_in else BF16
    with ExitStack() as s:
        sb = lambda name, shape, dt=F32: G.sb(s, name, shape, dt)
        T = Ctx()
        w_out = sb("w_out", [128, 8, D], BF16)
        xt = sb("xt3", [128, 4, D])
        T.sqj = sb("sqj3", [128, D], BF16)
        T.ss = sb("ss3", [128, 4])
        T.rstd = sb("rstd3", [128, 4])
        T.tmp = sb("tmp3", [128, D])
        T.hb = sb("hb3", [128, 4, D], BF16)
        T.hT = sb("hT3", [128, 8, 512], BF16)
        yT = T.hT
        actT = sb("actT", [128, 22, 512], BF16)
        upw = [sb("upw%d" % i, [128, 8, 512], BF16) for i in range(2)]
        dnw = sb("dnw", [128, 22, D], BF16)
        ub = [sb("ub%d" % i, [128, 514]) for i in range(2)]
        c1 = [sb("c1_%d" % i, [128, 512]) for i in range(2)]
        c2 = [sb("c2_%d" % i, [128, 512]) for i in range(2)]
        c3 = [sb("c3_%d" % i, [128, 512]) for i in range(2)]
        sgt = [sb("sgt%d" % i, [128, 512]) for i in range(2)]
        convw = sb("convw", [128, 44, 3])
        convb = sb("convb", [128, 44])
        carryF = sb("carryF", [128, 44, 2])

        for kc in range(8):
            k.dma(T.tmp[:], I["w_out"][l, kc * 128:(kc + 1) * 128, :], w=["tmp"])
            k.copy(("dve", "pool", "act")[kc % 3], w_out[:, kc, :], T.tmp[:], r=["tmp"], w=["w_out"])
        cwn = T.tmp[0:44, 0:512].rearrange("p (t c) -> p t c", t=4)
        for t in range(3):
            k.dma(cwn[:, t, :], I["ffn_conv"][l, t].rearrange("(cc p) -> cc p", p=128), w=["tmp"])
        k.dma(cwn[:, 3, :], I["ffn_conv_b"][l].rearrange("(cc p) -> cc p", p=128), w=["tmp"])
        for t in range(4):
            k.mm(banks[0][:, t * 44:(t + 1) * 44], cwn[:, t, :], G.ident[0:44, 0:44], r=["tmp", "ident"], w=["bank0"])
        k.copy("dve", convw[:], banks[0][:, 0:132].rearrange("p (t c) -> p c t", t=3), r=["bank0"], w=["convw"])
        k.copy("dve", convb[:], banks[0][:, 132:176], r=["bank0"], w=["convb"])
        k.memset("pool", carryF[:], 0.0, w=["carryF"])

        for g in range(NG):
            tsl = slice(g * TG, (g + 1) * TG)
            k.dma(xt[:], xsrc[tsl, :].rearrange("(j p) d -> p j d", p=128), r=[("x", g)], w=["xt"])
            k.dma(dnw[:].rearrange("p a d -> p (a d)"), R.dnbf[l], r=[("ffnw", l)], w=["dnw"])
            if G.y_in:
                for j in range(4):
                    k.dma(T.tmp[:, 0:768], R.yscr[g * TG + j * 128: g * TG + (j + 1) * 128, :], w=["tmp"])
                    k.copy("pool", T.hb[:, j, 0:768], T.tmp[:, 0:768], r=["tmp"], w=["hb"])
                for kc in range(2):
                    k.dma(c1[kc][:], R.yTA[kc * 128:(kc + 1) * 128, tsl], w=["c1_%d" % kc])
                    k.copy("act", yT[:, kc, :], c1[kc][:], r=["c1_%d" % kc], w=["hT"])
            else:
                k.dma(T.hb[:, :, 0:768], R.yscr[tsl, :].rearrange("(j p) c -> p j c", p=128),
                      r=[("yC", g), ("yB", g), ("yD", g)], w=["hb"])
                k.dma(yT[:, 0:2, :], R.yTA[:, tsl].rearrange("(kc p) t -> p kc t", p=128), r=[("yA", g)], w=["hT"])
            for j in range(4):
                b = 5 + (j % 2)
                pv = banks[b][:].bitcast(BF16).rearrange("p (a t) -> p a t", a=8)
                for kc in range(6):
                    k.tr(pv[:, kc, :], T.hb[:, j, kc * 128:(kc + 1) * 128], G.identb[:], r=["hb", "identb"], w=["bank%d" % b])
                k.copy("act" if j % 2 else "dve", yT[:, 2:8, j * 128:(j + 1) * 128], pv[:, 0:6, :], r=["bank%d" % b], w=["hT"])
            for j in range(4):
                for hf in range(2):
                    bk = k.rot("p3bank", 2)
                    for kc in range(8):
                        k.mm(banks[bk][:], yT[:, kc, j * 128:(j + 1) * 128], w_out[:, kc, hf * 512:(hf + 1) * 512],
                             start=(kc == 0), stop=(kc == 7), r=["hT", "w_out"], w=["bank%d" % bk])
                    ci = k.rot("c1", 2)
                    k.tt("dve", c1[ci][:], banks[bk][:], G.modB[:, 2 * D + hf * 512: 2 * D + (hf + 1) * 512], ALU.mult,
                         r=["bank%d" % bk, "modB"], w=["c1_%d" % ci])
                    k.tt("dve", xt[:, j, hf * 512:(hf + 1) * 512], xt[:, j, hf * 512:(hf + 1) * 512], c1[ci][:], ALU.add,
                         r=["xt", "c1_%d" % ci], w=["xt"])
            if G.dbg_x1 is not None:
                k.dma(G.dbg_x1[tsl, :].rearrange("(j p) d -> p j d", p=128), xt[:], r=["xt"], w=[("dbgx1", g)])
            norm_mod_T(G, xt, 4 * D, 3 * D, T)
            for u in range(11):
                wi = u % 2
                k.dma(upw[wi][:].rearrange("p a n -> p (a n)"), R.upbf[l, u], r=[("ffnw", l)], w=["upw%d" % wi])
                for sc in range(4):
                    cc = 4 * u + sc
                    bk = 2 + k.rot("p3ubank", 3)
                    for kc in range(8):
                        k.mm(banks[bk][:], upw[wi][:, kc, sc * 128:(sc + 1) * 128], T.hT[:, kc, :], start=(kc == 0), stop=(kc == 7),
                             r=["upw%d" % wi, "hT"], w=["bank%d" % bk])
                    ui = k.rot("ub", 2)
                    k.copy("pool", ub[ui][:, 0:2], carryF[:, cc, :], r=["carryF"], w=["ub%d" % ui])
                    k.copy("act", ub[ui][:, 2:514], banks[bk][:], r=["bank%d" % bk], w=["ub%d" % ui])
                    k.copy("pool", carryF[:, cc, :], ub[ui][:, 512:514], r=["ub%d" % ui], w=["carryF"])
                    k.act(c1[ui][:], ub[ui][:, 2:514], AF.Identity, bias=convb[:, cc:cc + 1], scale=convw[:, cc, 2:3],
                          r=["ub%d" % ui, "convw", "convb"], w=["c1_%d" % ui])
                    k.stt("dve", c2[ui][:], ub[ui][:, 1:513], convw[:, cc, 1:2], c1[ui][:], ALU.mult, ALU.add,
                          r=["ub%d" % ui, "convw", "c1_%d" % ui], w=["c2_%d" % ui])
                    if cc < 22:
                        k.stt("dve", actT[:, cc, :], ub[ui][:, 0:512], convw[:, cc, 0:1], c2[ui][:], ALU.mult, ALU.add,
                              r=["ub%d" % ui, "convw", "c2_%d" % ui], w=["actT%d" % cc])
                    else:
                        cu = cc - 22
                        k.stt("dve", c3[ui][:], ub[ui][:, 0:512], convw[:, cc, 0:1], c2[ui][:], ALU.mult, ALU.add,
                              r=["ub%d" % ui, "convw", "c2_%d" % ui], w=["c3_%d" % ui])
                        k.act(sgt[ui][:], c3[ui][:], AF.Silu, r=["c3_%d" % ui], w=["sgt%d" % ui])
                        k.tt("pool", actT[:, cu, :], actT[:, cu, :], sgt[ui][:], ALU.mult, r=["actT%d" % cu, "sgt%d" % ui],
                             w=["actT%d" % cu])
            aks = ["actT%d" % i for i in range(22)]
            for j in range(4):
                for hf in range(2):
                    bk = k.rot("p3bank", 2)
                    for cu in range(22):
                        k.mm(banks[bk][:], actT[:, cu, j * 128:(j + 1) * 128], dnw[:, cu, hf * 512:(hf + 1) * 512],
                             start=(cu == 0), stop=(cu == 21), r=["actT%d" % cu, "dnw"], w=["bank%d" % bk])
                    ci = k.rot("c1", 2)
                    k.tt("dve", c1[ci][:], banks[bk][:], G.modB[:, 5 * D + hf * 512: 5 * D + (hf + 1) * 512], ALU.mult,
                         r=["bank%d" % bk, "modB"], w=["c1_%d" % ci])
                    k.tt("dve", xt[:, j, hf * 512:(hf + 1) * 512], xt[:, j, hf * 512:(hf + 1) * 512], c1[ci][:], ALU.add,
                         r=["xt", "c1_%d" % ci], w=["xt"])
            k.dma(xdst[tsl, :].rearrange("(j p) d -> p j d", p=128), xt[:], r=["xt"], w=[("x", g)])
        P.barrier()


def phase_A(G, l):
    nc, P, k, I, R = G.nc, G.P, G.k, G.I, G.R
    banks = G.banks
    with ExitStack() as s:
        sb = lambda name, shape, dt=F32: G.sb(s, name, shape, dt)
        prm = sb("prm", [64, 7, 4])
        wup = sb("wup", [32, 256])
        aup = sb("aup", [32, 256])
        gup = sb("gup", [64, 256])
        ones64 = sb("ones64", [64, 64])
        ones512 = sb("ones512", [64, 512])
        m64 = sb("m64", [64, 3, 64])
        Tst = sb("Tst", [64, 4, 64])
        big = lambda nm: sb(nm, [64, 4, 512])
        r_, k_, v_ = big("r_"), big("k_"), big("v_")
        lwt, at, gt, kkn, k2, b_, bonus, Yblk, t1, t2 = (big("lwt"), big("at"), big("gt"), big("kkn"), big("k2"), big("b_"),
                                                         big("bonus"), big("Yblk"), big("t1"), big("t2"))
        Gblk = sb("Gblk", [64, 4, 513])
        wd = sb("wd", [32, 512])
        ad = sb("ad", [32, 512])
        gd = sb("gd", [64, 512])
        yo = sb("yo", [64, 4, 512], BF16)
        sm = lambda nm: sb(nm, [64, 4, 64])
        Pc, Pp, eGn, eGp = sm("Pc"), sm("Pp"), sm("eGn"), sm("eGp")
        smb = lambda nm: sb(nm, [64, 4, 64], BF16)
        Bt, Kt = smb("Bt"), smb("Kt")
        Pj = [smb("Pj0"), smb("Pj1")]
        U, tS = sm("U"), sm("tS")
        Ub, Tb = smb("Ub"), smb("Tb")
        DB = []
        for q in range(2):
            d = {"eG": sm("eG%d" % q)}
            for nm_ in ("At", "Rt", "Btm", "Ktm", "Vtm", "LakT", "MrbT", "MrkT"):
                d[nm_] = smb("%s%d" % (nm_, q))
            d["PT"] = [smb("PT%d_%d" % (j, q)) for j in range(6)]
            DB.append(d)
        vb = sb("vb", [64, 4, 512], BF16)

        for n, nm in enumerate(("rw_w0", "rw_a0", "rw_k_k", "rw_k_a")):
            k.dma(prm[:, n, :], I[nm][l].rearrange("(h k) -> k h", k=64), w=["prm"], slow=True)
        k.dma(prm[:, 4, :], I["rw_r_k"][l].rearrange("h k -> k h"), w=["prm"], slow=True)
        k.dma(prm[:, 5, :], I["rw_ln_g"][l].rearrange("(h k) -> k h", k=64), w=["prm"], slow=True)
        k.dma(prm[:, 6, :], I["rw_ln_b"][l].rearrange("(h k) -> k h", k=64), w=["prm"], slow=True)
        k.dma(wup[:], I["rw_w_up"][l], w=["wup"])
        k.dma(aup[:], I["rw_a_up"][l], w=["aup"])
        k.dma(gup[:], I["rw_g_up"][l], w=["gup"])
        k.dma(m64[:], I["m64"], w=["m64"])
        k.memset("dve", ones64[:], 1.0, w=["ones64"])
        k.memset("dve", ones512[:], 1.0, w=["ones512"])
        k.memset("pool", Tst[:], 0.0, w=["Tst"])
        k.memset("pool", Tb[:], 0.0, w=["Tb"])
        k.memset("pool", Gblk[:], 0.0, w=["Gblk"])

        def bc(col):
            return prm[:, col, :].unsqueeze(2).to_broadcast([64, 4, 512])

        def HB(b, half):
            return banks[b][0:64, half * 256:(half + 1) * 256].rearrange("p (h t) -> p h t", h=4)

        def hk(b, half):
            return ("hb", b)

        def fb(b):
            return [("hb", b)]
        allpm = [("pmA", g) for g in range(NG)]

        for g in range(NG):
            tsl = slice(g * TG, (g + 1) * TG)
            k.dma(r_[:], R.pmA[0:256, tsl].rearrange("(h k) t -> k h t", k=64), r=allpm, w=["r_"])
            k.dma(k_[:], R.pmA[256:512, tsl].rearrange("(h k) t -> k h t", k=64), r=allpm, w=["k_"])
            k.dma(v_[:], R.pmA[512:768, tsl].rearrange("(h k) t -> k h t", k=64), r=allpm, w=["v_"])
            k.copy("pool", vb[:], v_[:], r=["v_"], w=["vb"])
            k.dma(wd[:], R.pmA[768:800, tsl], r=allpm, w=["wd"])
            k.dma(ad[:], R.pmA[800:832, tsl], r=allpm, w=["ad"])
            k.dma(gd[:], R.pmA[832:896, tsl], r=allpm, w=["gd"])
            k.act(wd[:], wd[:], AF.Tanh, r=["wd"], w=["wd"])
            k.act(gd[:], gd[:], AF.Sigmoid, r=["gd"], w=["gd"])
            for h in range(4):
                k.mm(banks[h][0:64, :], wup[:, h * 64:(h + 1) * 64], wd[:], r=["wup", "wd"], w=fb(h))
                k.act(lwt[:, h, :], banks[h][0:64, :], AF.Sigmoid, bias=prm[:, 0, h:h + 1], scale=1.0, r=fb(h) + ["prm"], w=["lwt"])
            for h in range(4):
                k.mm(banks[4 + h][0:64, :], aup[:, h * 64:(h + 1) * 64], ad[:], r=["aup", "ad"], w=fb(4 + h))
                k.act(at[:, h, :], banks[4 + h][0:64, :], AF.Sigmoid, bias=prm[:, 1, h:h + 1], scale=1.0, r=fb(4 + h) + ["prm"], w=["at"])
            for h in range(4):
                k.mm(banks[h][0:64, :], gup[:, h * 64:(h + 1) * 64], gd[:], r=["gup", "gd"], w=fb(h))
                k.copy("dve" if h % 2 else "act", gt[:, h, :], banks[h][0:64, :], r=fb(h), w=["gt"])
            k.tt("dve", kkn[:], k_[:], bc(2), ALU.mult, r=["k_", "prm"], w=["kkn"])
            k.tt("dve", t1[:], kkn[:], kkn[:], ALU.mult, r=["kkn"], w=["t1"])
            for h in range(4):
                k.mm(banks[4 + h][0:64, :], ones64[:], t1[:, h, :], r=["ones64", "t1"], w=fb(4 + h))
            for h in range(4):
                k.act(t2[:, h, :], banks[4 + h][0:64, :], AF.Ln, bias=1e-24, scale=1.0, r=fb(4 + h), w=["t2"])
            k.act(t2[:], t2[:], AF.Exp, scale=-0.5, r=["t2"], w=["t2"])
            k.tt("pool", kkn[:], kkn[:], t2[:], ALU.mult, r=["kkn", "t2"], w=["kkn"])
            k.stt("dve", t1[:], at[:], -1.0, bc(3), ALU.add, ALU.mult, r=["at", "prm", "t1"], w=["t1"])
            k.tt("pool", t1[:], t1[:], k_[:], ALU.mult, r=["t1", "k_"], w=["t1"])
            k.tt("dve", k2[:], t1[:], k_[:], ALU.add, r=["t1", "k_"], w=["k2"])
            k.tt("pool", b_[:], kkn[:], at[:], ALU.mult, r=["kkn", "at"], w=["b_"])
            k.tt("dve", t1[:], r_[:], k2[:], ALU.mult, r=["r_", "k2", "t1"], w=["t1"])
            k.tt("dve", t1[:], t1[:], bc(4), ALU.mult, r=["t1", "prm"], w=["t1"])
            for h in range(4):
                k.mm(banks[h][0:64, :], ones64[:], t1[:, h, :], r=["ones64", "t1"], w=fb(h))
                k.tt("dve", bonus[:, h, :], banks[h][0:64, :], v_[:, h, :], ALU.mult, r=fb(h) + ["v_"], w=["bonus"])
            for h in range(4):
                k.scan(Gblk[:, h, 1:513], ones512[:], lwt[:, h, :], 0.0, ALU.mult, ALU.add, r=["ones512", "lwt"], w=["Gblk"])

            if int(os.environ.get("A_STOP", "9")) <= 1:
                continue
            def prep_steps(ci):
                q = ci % 2
                D_ = DB[q]
                c0 = ci * 64
                ts_ = slice(c0, c0 + 64)
                eGq, Atq, Rtq = D_["eG"], D_["At"], D_["Rt"]
                nm = lambda x: "%s_%d" % (x, q)
                id64 = G.identb[0:64, 0:64]
                steps = []

                def s1():
                    k.tt("dve", Pc[:], Gblk[:, :, 1 + c0:1 + c0 + 64], Gblk[:, :, c0:c0 + 1].to_broadcast([64, 4, 64]), ALU.subtract,
                         r=["Gblk"], w=["Pc"])
                    k.tt("pool", Pp[:], Pc[:], lwt[:, :, ts_], ALU.subtract, r=["Pc", "lwt"], w=["Pp"])
                    k.act(eGq[:], Pc[:], AF.Exp, scale=-ALPHA, r=["Pc"], w=[nm("eG")])
                    k.act(eGn[:], Pc[:], AF.Exp, scale=ALPHA, r=["Pc"], w=["eGn"])
                    k.act(eGp[:], Pp[:], AF.Exp, scale=-ALPHA, r=["Pp"], w=["eGp"])
                steps.append(s1)

                def s2():
                    k.stt("dve", Atq[:], kkn[:, :, ts_], -1.0, eGp[:], ALU.mult, ALU.mult, r=["kkn", "eGp"], w=[nm("At")])
                    k.tt("pool", Bt[:], b_[:, :, ts_], eGn[:], ALU.mult, r=["b_", "eGn"], w=["Bt"])
                    k.tt("pool", Kt[:], k2[:, :, ts_], eGn[:], ALU.mult, r=["k2", "eGn"], w=["Kt"])
                    k.tt("dve", Rtq[:], r_[:, :, ts_], eGq[:], ALU.mult, r=["r_", nm("eG")], w=[nm("Rt")])
                steps.append(s2)

                def s3():
                    trs = ((Bt, "Bt", D_["Btm"], nm("Btm"), 2, 1, "act"), (Kt, "Kt", D_["Ktm"], nm("Ktm"), 3, 0, "act"),
                           (None, "vb", D_["Vtm"], nm("Vtm"), 3, 1, "act"))
                    for (X, xk, Xtm, xtk, bq, hf, ce) in trs:
                        for h in range(4):
                            src = vb[:, h, ts_] if X is None else X[:, h, :]
                            k.mm(HB(bq, hf)[:, h, :], src, id64, r=[xk, "identb"], w=[hk(bq, hf)])
                    for (X, xk, Xtm, xtk, bq, hf, ce) in trs:
                        k.copy(ce, Xtm[:], HB(bq, hf), r=[hk(bq, hf)], w=[xtk])
                steps.append(s3)

                def s4():
                    specs = ((Bt, "Bt", Atq, nm("At"), 0, 0, D_["PT"][0], nm("PT0"), 0), (Atq, nm("At"), Bt, "Bt", 0, 1, Pj[0], "Pj0", 2),
                             (Kt, "Kt", Atq, nm("At"), 1, 0, D_["LakT"], nm("LakT"), 0), (Bt, "Bt", Rtq, nm("Rt"), 1, 1, D_["MrbT"], nm("MrbT"), 1),
                             (Kt, "Kt", Rtq, nm("Rt"), 2, 0, D_["MrkT"], nm("MrkT"), 1))
                    for (La, lk, Ra, rk_, bq, hf, dst, dk, mi) in specs:
                        for h in range(4):
                            k.mm(HB(bq, hf)[:, h, :], La[:, h, :], Ra[:, h, :], r=[lk, rk_], w=[hk(bq, hf)])
                    for (La, lk, Ra, rk_, bq, hf, dst, dk, mi) in specs:
                        k.tt("dve", dst[:], HB(bq, hf), m64[:, mi, :].unsqueeze(1).to_broadcast([64, 4, 64]), ALU.mult,
                             r=[hk(bq, hf), "m64"], w=[dk])
                steps.append(s4)

                def mk_sq(j):
                    def sq():
                        cur, nxt = j % 2, (j + 1) % 2
                        PTc, PTn = D_["PT"][j], D_["PT"][j + 1]
                        for h in range(4):
                            k.mm(HB(5, 0)[:, h, :], PTc[:, h, :], Pj[cur][:, h, :], r=[nm("PT%d" % j), "Pj%d" % cur], w=[hk(5, 0)])
                        for h in range(4):
                            k.mm(HB(5, 1)[:, h, :], Pj[cur][:, h, :], PTc[:, h, :], r=[nm("PT%d" % j), "Pj%d" % cur], w=[hk(5, 1)])
                        k.copy("act", Pj[nxt][:], HB(5, 0), r=[hk(5, 0)], w=["Pj%d" % nxt])
                        k.copy("act", PTn[:], HB(5, 1), r=[hk(5, 1)], w=[nm("PT%d" % (j + 1))])
                    return sq
                for j in range(5):
                    steps.append(mk_sq(j))
                return steps

            def chain_steps(ci):
                q = ci % 2
                D_ = DB[q]
                c0 = ci * 64
                ts_ = slice(c0, c0 + 64)
                eGq, Atq, Rtq = D_["eG"], D_["At"], D_["Rt"]
                Btm, Ktm, Vtm, LakT, MrbT, MrkT = D_["Btm"], D_["Ktm"], D_["Vtm"], D_["LakT"], D_["MrbT"], D_["MrkT"]
                nm = lambda x: "%s_%d" % (x, q)
                steps = []

                def c1():
                    for h in range(4):
                        k.mm(HB(4, 0)[:, h, :], Atq[:, h, :], Tb[:, h, :], start=True, stop=False, r=[nm("At"), "Tb"], w=[hk(4, 0)])
                        k.mm(HB(4, 0)[:, h, :], LakT[:, h, :], Vtm[:, h, :], start=False, stop=True, r=[nm("LakT"), nm("Vtm")], w=[hk(4, 0)])
                    k.copy("dve", Ub[:], HB(4, 0), r=[hk(4, 0)], w=["Ub"])
                    k.copy("act", U[:], HB(4, 0), r=[hk(4, 0)], w=["U"])
                steps.append(c1)

                def mk_u(j):
                    def us():
                        PTc = D_["PT"][j]
                        for h in range(4):
                            k.mm(HB(4, 1)[:, h, :], PTc[:, h, :], Ub[:, h, :], r=[nm("PT%d" % j), "Ub"], w=[hk(4, 1)])
                        k.tt("dve", Ub[:], U[:], HB(4, 1), ALU.add, r=["U", hk(4, 1)], w=["Ub"])
                        if j < 5:
                            k.tt("dve", U[:], U[:], HB(4, 1), ALU.add, r=["U", hk(4, 1)], w=["U"])
                    return us
                for j in range(6):
                    steps.append(mk_u(j))

                def c8():
                    for h in range(4):
                        k.mm(HB(6, 0)[:, h, :], Tb[:, h, :], Rtq[:, h, :], start=True, stop=False, r=["Tb", nm("Rt")], w=[hk(6, 0)])
                        k.mm(HB(6, 0)[:, h, :], Ub[:, h, :], MrbT[:, h, :], start=False, stop=False, r=["Ub", nm("MrbT")], w=[hk(6, 0)])
                        k.mm(HB(6, 0)[:, h, :], Vtm[:, h, :], MrkT[:, h, :], start=False, stop=True, r=[nm("Vtm"), nm("MrkT")], w=[hk(6, 0)])
                    k.copy("act", Yblk[:, :, ts_], HB(6, 0), r=[hk(6, 0)], w=["Yblk"])
                    for h in range(4):
                        k.mm(HB(7, 0)[:, h, :], Btm[:, h, :], Ub[:, h, :], start=True, stop=False, r=[nm("Btm"), "Ub"], w=[hk(7, 0)])
                        k.mm(HB(7, 0)[:, h, :], Ktm[:, h, :], Vtm[:, h, :], start=False, stop=True, r=[nm("Ktm"), nm("Vtm")], w=[hk(7, 0)])
                    k.tt("dve", tS[:], HB(7, 0), Tst[:], ALU.add, r=[hk(7, 0), "Tst"], w=["tS"])
                    k.tt("dve", Tb[:], tS[:], eGq[:, :, 63:64].to_broadcast([64, 4, 64]), ALU.mult, r=["tS", nm("eG")], w=["Tb"])
                    k.tt("pool", Tst[:], tS[:], eGq[:, :, 63:64].to_broadcast([64, 4, 64]), ALU.mult, r=["tS", nm("eG")], w=["Tst"])
                steps.append(c8)
                return steps

            for st in prep_steps(0):
                st()
            for ci in range(8):
                cs = chain_steps(ci)
                ps = prep_steps(ci + 1) if ci < 7 else []
                n = max(len(cs), len(ps))
                for i_ in range(n):
                    if i_ < len(cs):
                        cs[i_]()
                    if i_ < len(ps):
                        ps[i_]()

            if int(os.environ.get("A_STOP", "9")) <= 3:
                continue
            for h in range(4):
                k.mm(banks[h][0:64, :], ones64[:], Yblk[:, h, :], r=["ones64", "Yblk"], w=fb(h))
                k.stt("dve", t1[:, h, :], banks[h][0:64, :], -1.0 / 64, Yblk[:, h, :], ALU.mult, ALU.add, r=fb(h) + ["Yblk"], w=["t1"])
            k.tt("dve", t2[:], t1[:], t1[:], ALU.mult, r=["t1"], w=["t2"])
            for h in range(4):
                k.mm(banks[4 + h][0:64, :], ones64[:], t2[:, h, :], r=["ones64", "t2"], w=fb(4 + h))
            for h in range(4):
                k.act(t2[:, h, :], banks[4 + h][0:64, :], AF.Ln, bias=64e-5, scale=1.0 / 64, r=fb(4 + h), w=["t2"])
            k.act(t2[:], t2[:], AF.Exp, scale=-0.5, r=["t2"], w=["t2"])
            k.tt("pool", t1[:], t1[:], t2[:], ALU.mult, r=["t1", "t2"], w=["t1"])
            k.tt("dve", t1[:], t1[:], bc(5), ALU.mult, r=["t1", "prm"], w=["t1"])
            k.tt("pool", t1[:], t1[:], bc(6), ALU.add, r=["t1", "prm"], w=["t1"])
            k.tt("dve", t1[:], t1[:], bonus[:], ALU.add, r=["t1", "bonus"], w=["t1"])
            k.tt("dve", yo[:], t1[:], gt[:], ALU.mult, r=["t1", "gt"], w=["yo"])
            if not G.y_in:
                k.dma(R.yTA[:, tsl].rearrange("(h v) t -> v h t", v=64), yo[:], r=["yo"], w=[("yA", g)])
        P.barrier()


def attn_common(G, s, Vsrc, mix):
    k, R, I = G.k, G.R, G.I
    sb = lambda name, shape, dt=F32: G.sb(s, name, shape, dt)
    A = Ctx()
    A.kT = [sb("kT%d" % i, [64, S], BF16) for i in range(2)]
    A.V = sb("V", [128, 32, 260], BF16)
    Vv = Vsrc.rearrange("(kt p) c -> p kt c", p=128)
    for q4 in range(4):
        k.dma(A.V[:, q4 * 8:(q4 + 1) * 8, :], Vv[:, q4 * 8:(q4 + 1) * 8, :], r=[(mix + "v", g) for g in range(NG)], w=["V"])
    A.Pm = [sb("Pm%d" % i, [128, 512], BF16) for i in range(3)]
    A.rec = [sb("rec%d" % i, [128, 8]) for i in range(2)]
    A.ob = [sb("ob%d" % i, [128, 4, 64], BF16) for i in range(2)]
    return A


def phase_D(G, l):
    nc, P, k, I, R = G.nc, G.P, G.k, G.I, G.R
    banks = G.banks
    with ExitStack() as s:
        sb = lambda name, shape, dt=F32: G.sb(s, name, shape, dt)
        A = attn_common(G, s, R.vD, "D")
        Fk = sb("Fk", [128, 4, 32])
        Frow = sb("Frow", [4, S])
        sel = sb("sel", [4, 4, 128])
        negm = sb("negm", [128, 128])
        qT = [sb("qT%d" % i, [64, 512], BF16) for i in range(2)]
        FqB = [sb("FqB%d" % i, [128, 512]) for i in range(2)]
        FqD = [sb("FqD%d" % i, [128, 512]) for i in range(2)]
        tb = [sb("tb%d" % i, [128, 512]) for i in range(3)]
        allF = [("Ffm", g) for g in range(NG)]
        fkn = tb[0][0:32, :].rearrange("p (h c) -> p h c", h=4)
        for h in range(4):
            k.dma(fkn[:, h, :], R.Ffm[h].rearrange("(kt p) -> kt p", p=128), r=allF, w=["tb0"])
        for h in range(4):
            k.mm(banks[5][:, h * 32:(h + 1) * 32], fkn[:, h, :], G.ident[0:32, 0:32], r=["tb0", "ident"], w=["bank5"])
        k.copy("dve", Fk[:].rearrange("p h c -> p (h c)"), banks[5][:, 0:128], r=["bank5"], w=["Fk"])
        k.dma(Frow[:], R.Ffm, r=allF, w=["Frow"])
        k.dma(sel[:], I["sel"], w=["sel"])
        k.dma(negm[:], I["negmask"], w=["negm"])
        allqk = [("Dqk", g) for g in range(NG)]
        for h in range(4):
            kt_ = A.kT[h % 2]
            kk = "kT%d" % (h % 2)
            k.dma(kt_[:], R.kTD[h * 64:(h + 1) * 64, :], r=allqk, w=[kk])
            for g in range(NG):
                i = k.rot("Dq", 2)
                k.dma(qT[i][:], R.qTD[h * 64:(h + 1) * 64, g * TG:(g + 1) * TG], r=allqk, w=["qT%d" % i])
                k.mm(banks[5][:], sel[:, h, :], Frow[:, g * TG:(g + 1) * TG], r=["sel", "Frow"], w=["bank5"])
                k.copy("act", FqB[i][:], banks[5][:], r=["bank5"], w=["FqB%d" % i])
                k.tt("pool", FqD[i][:].rearrange("p (a b) -> p a b", a=4), FqB[i][:].rearrange("p (a b) -> p a b", a=4),
                     negm[:].unsqueeze(1).to_broadcast([128, 4, 128]), ALU.add, r=["FqB%d" % i, "negm"], w=["FqD%d" % i])
                ob_ = 3 + k.rot("DO", 2)
                O = banks[ob_][:, 0:260].rearrange("p (a e) -> p a e", a=4)
                def d_stage1(kt):
                    m = kt - 4 * g
                    c0 = max(m, 0) * 128
                    N = 512 - c0
                    sbk = k.rot("Ds", 3)
                    k.mm(banks[sbk][:, 0:N], kt_[:, kt * 128:(kt + 1) * 128], qT[i][:, c0:512], r=[kk, "qT%d" % i], w=["bank%d" % sbk])
                    ti = k.rot("Dt", 3)
                    if m < 0:
                        k.stt("dve", tb[ti][:], banks[sbk][:], Fk[:, h, kt:kt + 1], FqB[i][:], ALU.subtract, ALU.add,
                              r=["bank%d" % sbk, "Fk", "FqB%d" % i], w=["tb%d" % ti])
                    else:
                        k.stt("dve", tb[ti][:, 0:128], banks[sbk][:, 0:128], Fk[:, h, kt:kt + 1], FqD[i][:, c0:c0 + 128],
                              ALU.subtract, ALU.add, r=["bank%d" % sbk, "Fk", "FqD%d" % i], w=["tb%d" % ti])
                        if N > 128:
                            k.stt("dve", tb[ti][:, 128:N], banks[sbk][:, 128:N], Fk[:, h, kt:kt + 1], FqB[i][:, c0 + 128:512],
                                  ALU.subtract, ALU.add, r=["bank%d" % sbk, "Fk", "FqB%d" % i], w=["tb%d" % ti])
                    k.act(A.Pm[ti][:, 0:N], tb[ti][:, 0:N], AF.Exp, r=["tb%d" % ti], w=["Pm%d" % ti])
                    return (kt, m, c0, ti)

                def d_stage2(st):
                    kt, m, c0, ti = st
                    for jq in range(max(m, 0), 4):
                        k.mm(O[:, jq, :], A.Pm[ti][:, jq * 128 - c0: jq * 128 - c0 + 128], A.V[:, kt, h * 65:(h + 1) * 65],
                             start=(kt == 0 and jq == 0), stop=(kt == 4 * g + jq), r=["Pm%d" % ti, "V"], w=["bank%d" % ob_], sgc=True)
                pend = []
                for kt in range(4 * g + 4):
                    pend.append(d_stage1(kt))
                    if len(pend) > 2:
                        d_stage2(pend.pop(0))
                while pend:
                    d_stage2(pend.pop(0))
                ri = k.rot("Drec", 2)
                k.recip(A.rec[ri][:, 0:4], O[:, :, 64], r=["bank%d" % ob_], w=["rec%d" % ri])
                k.tt("dve", A.ob[ri][:], O[:, :, 0:64], A.rec[ri][:, 0:4].unsqueeze(2).to_broadcast([128, 4, 64]), ALU.mult,
                     r=["bank%d" % ob_, "rec%d" % ri], w=["ob%d" % ri])
                k.dma(R.yscr[g * TG:(g + 1) * TG, 512 + h * 64: 512 + (h + 1) * 64].rearrange("(a p) e -> p a e", p=128),
                      A.ob[ri][:], r=["ob%d" % ri], w=[("yD", g)], slow=True)
        P.barrier()


def phase_B(G, l):
    nc, P, k, I, R = G.nc, G.P, G.k, G.I, G.R
    banks = G.banks
    lambda_init = 0.8 - 0.6 * math.exp(-0.3 * l)
    with ExitStack() as s:
        sb = lambda name, shape, dt=F32: G.sb(s, name, shape, dt)
        A = attn_common(G, s, R.vB, "B")
        cmf = sb("cmf", [128, 128])
        cm = sb("cm", [128, 128], BF16)
        qp = [[sb("qp%d_%d" % (i, mp), [64, 512], BF16) for mp in range(2)] for i in range(2)]
        lamv = sb("lamv", [128, 4, 32])
        lamw = sb("lamw", [128, 2, 32])
        lams = sb("lams", [128, 4])
        subg = sb("subg", [128, 64])
        o1 = [sb("o1_%d" % i, [128, 4, 64]) for i in range(2)]
        o2 = [sb("o2_%d" % i, [128, 4, 64]) for i in range(2)]
        k.dma(cmf[:], I["cmask"], w=["cmf"])
        k.copy("dve", cm[:], cmf[:], r=["cmf"], w=["cm"])
        for i in range(2):
            for mp in range(2):
                k.memset("pool", qp[i][mp][:], 0.0, w=["qp%d" % i])
        for n, nm in enumerate(("df_lam_q1", "df_lam_k1", "df_lam_q2", "df_lam_k2")):
            bcast_load(G, lamv[:, n, :], I[nm][l:l + 1, :], 32, ["lamv"])
        bcast_load(G, subg[:], I["df_sub_g"][l:l + 1, :], 64, ["subg"])
        k.ts("dve", subg[:], subg[:], 1.0 - lambda_init, ALU.mult, r=["subg"], w=["subg"])
        k.tt("dve", lamw[:, 0, :], lamv[:, 0, :], lamv[:, 1, :], ALU.mult, r=["lamv"], w=["lamw"])
        k.tt("dve", lamw[:, 1, :], lamv[:, 2, :], lamv[:, 3, :], ALU.mult, r=["lamv"], w=["lamw"])
        k.red("dve", lams[:, 0:2], lamw[:], r=["lamw"], w=["lams"])
        k.act(lams[:, 0:2], lams[:, 0:2], AF.Exp, r=["lams"], w=["lams"])
        k.ts("dve", lams[:, 2:3], lams[:, 1:2], -lambda_init, ALU.add, r=["lams"], w=["lams"])
        k.tt("dve", lams[:, 3:4], lams[:, 2:3], lams[:, 0:1], ALU.subtract, r=["lams"], w=["lams"])
        allqk = [("Bqk", g) for g in range(NG)]
        for h in range(4):
            kt_ = A.kT[h % 2]
            kk = "kT%d" % (h % 2)
            k.dma(kt_[:], R.kTB[h * 64:(h + 1) * 64, :], r=allqk, w=[kk])
            for g in range(NG):
                i = k.rot("Bq", 2)
                k.dma(qp[i][0][0:32, :], R.qTB[h * 64:h * 64 + 32, g * TG:(g + 1) * TG], r=allqk, w=["qp%d" % i])
                k.dma(qp[i][1][32:64, :], R.qTB[h * 64 + 32:h * 64 + 64, g * TG:(g + 1) * TG], r=allqk, w=["qp%d" % i])
                oi = k.rot("BO", 2)
                Os = [banks[3 + oi][:, 0:260].rearrange("p (a e) -> p a e", a=4),
                      banks[5 + oi][:, 0:260].rearrange("p (a e) -> p a e", a=4)]
                obk = ["bank%d" % (3 + oi), "bank%d" % (5 + oi)]
                def b_stage1(kt, mp):
                    m = kt - 4 * g
                    c0 = max(m, 0) * 128
                    N = 512 - c0
                    sbk = k.rot("Bs", 3)
                    k.mm(banks[sbk][:, 0:N], kt_[:, kt * 128:(kt + 1) * 128], qp[i][mp][:, c0:512], r=[kk, "qp%d" % i],
                         w=["bank%d" % sbk])
                    ti = k.rot("Bt", 3)
                    k.act(A.Pm[ti][:, 0:N], banks[sbk][:, 0:N], AF.Exp, r=["bank%d" % sbk], w=["Pm%d" % ti])
                    if m >= 0:
                        k.tt("dve", A.Pm[ti][:, 0:128], A.Pm[ti][:, 0:128], cm[:], ALU.mult, r=["Pm%d" % ti, "cm"], w=["Pm%d" % ti])
                    return (kt, mp, m, c0, ti)

                def b_stage2(st):
                    kt, mp, m, c0, ti = st
                    for jq in range(max(m, 0), 4):
                        k.mm(Os[mp][:, jq, :], A.Pm[ti][:, jq * 128 - c0: jq * 128 - c0 + 128], A.V[:, kt, h * 65:(h + 1) * 65],
                             start=(kt == 0 and jq == 0), stop=(kt == 4 * g + jq), r=["Pm%d" % ti, "V"], w=[obk[mp]], sgc=True)
                pend = []
                for kt in range(4 * g + 4):
                    for mp in range(2):
                        pend.append(b_stage1(kt, mp))
                        if len(pend) > 2:
                            b_stage2(pend.pop(0))
                while pend:
                    b_stage2(pend.pop(0))
                ri = k.rot("Brec", 2)
                rc = A.rec[ri]
                rk = "rec%d" % ri
                k.recip(rc[:, 0:4], Os[0][:, :, 64], r=[obk[0]], w=[rk])
                k.recip(rc[:, 4:8], Os[1][:, :, 64], r=[obk[1]], w=[rk])
                k.ts("dve", rc[:, 4:8], rc[:, 4:8], lams[:, 3:4], ALU.mult, r=[rk, "lams"], w=[rk])
                k.tt("dve", o1[ri][:], Os[0][:, :, 0:64], rc[:, 0:4].unsqueeze(2).to_broadcast([128, 4, 64]), ALU.mult,
                     r=[obk[0], rk], w=["o1_%d" % ri])
                k.tt("dve", o2[ri][:], Os[1][:, :, 0:64], rc[:, 4:8].unsqueeze(2).to_broadcast([128, 4, 64]), ALU.mult,
                     r=[obk[1], rk], w=["o2_%d" % ri])
                k.tt("pool", o1[ri][:], o1[ri][:], o2[ri][:], ALU.add, r=["o1_%d" % ri, "o2_%d" % ri], w=["o1_%d" % ri])
                k.tt("pool", o2[ri][:], o1[ri][:], o1[ri][:], ALU.mult, r=["o1_%d" % ri], w=["o2_%d" % ri])
                k.red("dve", rc[:, 0:4], o2[ri][:], r=["o2_%d" % ri], w=[rk])
                k.act(rc[:, 0:4], rc[:, 0:4], AF.Sqrt, bias=EPS, scale=1.0 / 64, r=[rk], w=[rk])
                k.recip(rc[:, 0:4], rc[:, 0:4], r=[rk], w=[rk])
                k.tt("dve", o1[ri][:], o1[ri][:], rc[:, 0:4].unsqueeze(2).to_broadcast([128, 4, 64]), ALU.mult,
                     r=["o1_%d" % ri, rk], w=["o1_%d" % ri])
                k.tt("pool", A.ob[ri][:], o1[ri][:], subg[:].unsqueeze(1).to_broadcast([128, 4, 64]), ALU.mult,
                     r=["o1_%d" % ri, "subg"], w=["ob%d" % ri])
                k.dma(R.yscr[g * TG:(g + 1) * TG, h * 64:(h + 1) * 64].rearrange("(a p) e -> p a e", p=128),
                      A.ob[ri][:], r=["ob%d" % ri], w=[("yB", g)], slow=True)
        P.barrier()


_CACHE = {}


def kernel(**inputs):
    if "prog" not in _CACHE:
        _CACHE["prog"] = build()
    nc, P = _CACHE["prog"]
    consts = make_consts()
    weights = {k: np.ascontiguousarray(np.asarray(inputs[k], dtype=np.float32)) for k in WEIGHT_SHAPES}
    x = np.asarray(inputs["x"], dtype=np.float32)
    c = np.asarray(inputs["c"], dtype=np.float32)
    in_maps = []
    for b in range(8):
        m = {"x": np.ascontiguousarray(x[b]), "c": np.ascontiguousarray(c[b:b + 1])}
        m.update(weights)
        m.update(consts)
        in_maps.append(m)
    res = run_bass_kernel_spmd(nc, in_maps, core_ids=list(range(8)))
    return np.stack([np.asarray(r["out"], dtype=np.float32) for r in res.results], axis=0)
```

```python
import math
import os
import numpy as np
import concourse.bass as bass
import concourse.mybir as mybir
from concourse.bass_utils import run_bass_kernel_spmd
from contextlib import ExitStack

F32 = mybir.dt.float32
BF16 = mybir.dt.bfloat16
AF = mybir.ActivationFunctionType
ALU = mybir.AluOpType
AX = mybir.AxisListType

S = 4096
D = 1024
L = 4
NIN = 2948
DFF = 2816
NG = 8
TG = 512
EPS = 1e-6
ALPHA = math.exp(-0.5)

ENGS = ("pe", "act", "dve", "pool", "sp")
EPOCH = 30000


class Prog:
    def __init__(self, nc, es):
        self.nc = nc
        self.es = es
        self.q = {e: [] for e in ENGS}
        self.cnt = {e: 0 for e in ENGS}
        self.epoch = {e: 0 for e in ENGS}
        self.sems = {}
        self.seen = {e: {} for e in ENGS}
        self.res_w = {}
        self.res_r = {}
        self.dma_val = {}
        self.n_inst = 0
        self.rr = 0

    def _sem(self, key):
        if key not in self.sems:
            self.sems[key] = self.es.enter_context(self.nc.semaphore("s_" + "_".join(str(k) for k in key)))
        return self.sems[key]

    def _deps(self, eng, reads, writes, extra=()):
        need = {}

        def add(ev):
            if ev is None:
                return
            k, v = ev
            if eng == "pe" and k[0] == "pe":
                return
            if need.get(k, 0) < v:
                need[k] = v
        for r in reads:
            add(self.res_w.get(r))
        for w in writes:
            add(self.res_w.get(w))
            for ev in self.res_r.get(w, ()):
                add(ev)
        for ev in extra:
            add(ev)
        waits = []
        for k, v in need.items():
            if self.seen[eng].get(k, 0) >= v:
                continue
            self.seen[eng][k] = v
            waits.append((k, v))
        return waits

    def _commit(self, ev, reads, writes):
        for r in reads:
            lst = self.res_r.setdefault(r, [])
            lst.append(ev)
            if len(lst) > 64:
                mx = {}
                for k, v in lst:
                    if mx.get(k, 0) < v:
                        mx[k] = v
                self.res_r[r] = list(mx.items())
        for w in writes:
            self.res_w[w] = ev
            self.res_r[w] = []

    @staticmethod
    def _is_psum(r):
        return (isinstance(r, str) and r.startswith("bank")) or (isinstance(r, tuple) and r[0] == "hb")

    def op(self, eng, fn, reads=(), writes=()):
        pr = [r for r in reads if self._is_psum(r)]
        if pr:
            writes = list(writes) + pr
        waits = self._deps(eng, reads, writes)
        if self.cnt[eng] >= EPOCH:
            self.epoch[eng] += 1
            self.cnt[eng] = 0
        self.cnt[eng] += 1
        key = (eng, self.epoch[eng])
        ev = (key, self.cnt[eng])
        self.q[eng].append((waits, fn, key, 1))
        self._commit(ev, reads, writes)
        self.n_inst += 1
        return ev

    def dma(self, queue, pairs, reads=(), writes=(), sem=None):
        if sem is None:
            sem = ("dma", "rr%d" % (self.rr % 20))
            self.rr += 1
        key = sem
        prev = self.dma_val.get(key, 0)
        extra = [(key, prev)] if prev > 0 else []
        waits = self._deps(queue, reads, writes, extra)
        val = prev
        for i, pr in enumerate(pairs):
            out_ap, in_ap = pr[0], pr[1]
            kw = pr[2] if len(pr) > 2 else {}
            val += 16

            def fn(e, out_ap=out_ap, in_ap=in_ap, kw=kw):
                return e.dma_start(out=out_ap, in_=in_ap, **kw)
            self.q[queue].append((waits if i == 0 else [], fn, key, 16))
            self.n_inst += 1
        self.dma_val[key] = val
        ev = (key, val)
        self._commit(ev, reads, writes)
        return ev

    def barrier(self):
        evs = []
        for e in ENGS:
            for ep in range(self.epoch[e] + 1):
                k = (e, ep)
                v = self.cnt[e] if ep == self.epoch[e] else EPOCH
                if v > 0:
                    evs.append((k, v))
        for k, v in self.dma_val.items():
            evs.append((k, v))
        for e in ENGS:
            waits = []
            for k, v in evs:
                if self.seen[e].get(k, 0) < v:
                    self.seen[e][k] = v
                    waits.append((k, v))
            if waits:
                self.q[e].append((waits, None, None, 0))
        self.res_w = {}
        self.res_r = {}

    def emit(self):
        nc = self.nc
        for e in ENGS:
            for (waits, fn, key, inc) in self.q[e]:
                for k, v in waits:
                    self._sem(k)
                if key is not None:
                    self._sem(key)
        block = self.es.enter_context(nc.Block())
        engmap = {"pe": block.tensor, "act": block.scalar, "dve": block.vector, "pool": block.gpsimd,
                  "sp": block.sync}
        for e in ENGS:
            items = self.q[e]

            def body(eng, items=items):
                for (waits, fn, key, inc) in items:
                    for k, v in waits:
                        eng.wait_ge(self.sems[k], v)
                    if fn is not None:
                        ins = fn(eng)
                        ins.then_inc(self.sems[key], inc)
            engmap[e](body)


class K:
    def __init__(self, P):
        self.P = P
        self._rot = {}

    def rot(self, name, n):
        i = self._rot.get(name, 0)
        self._rot[name] = i + 1
        return i % n

    def mm(self, out, lhsT, rhs, start=True, stop=True, r=(), w=(), sgc=False):
        if sgc:
            return self.P.op("pe", lambda e: e.matmul(out, lhsT=lhsT, rhs=rhs, start=start, stop=stop, skip_group_check=True), r, w)
        return self.P.op("pe", lambda e: e.matmul(out, lhsT=lhsT, rhs=rhs, start=start, stop=stop), r, w)

    def tr(self, out, in_, ident, r=(), w=()):
        return self.P.op("pe", lambda e: e.transpose(out=out, in_=in_, identity=ident), r, w)

    def act(self, out, in_, func, bias=None, scale=None, accum_out=None, r=(), w=(), eng="act"):
        kw = {}
        if bias is not None:
            kw["bias"] = bias
        if scale is not None:
            kw["scale"] = scale
        if accum_out is not None:
            kw["accum_out"] = accum_out
        return self.P.op("act", lambda e: e.activation(out=out, in_=in_, func=func, **kw), r, w)

    def copy(self, eng, out, in_, r=(), w=()):
        if eng == "act":
            return self.P.op("act", lambda e: e.copy(out=out, in_=in_), r, w)
        return self.P.op(eng, lambda e: e.tensor_copy(out=out, in_=in_), r, w)

    def tt(self, eng, out, in0, in1, op, r=(), w=()):
        return self.P.op(eng, lambda e: e.tensor_tensor(out=out, in0=in0, in1=in1, op=op), r, w)

    def ts(self, eng, out, in0, s1, op0, s2=None, op1=None, r=(), w=()):
        if op1 is None:
            return self.P.op(eng, lambda e: e.tensor_scalar(out=out, in0=in0, scalar1=s1, scalar2=None, op0=op0), r, w)
        return self.P.op(eng, lambda e: e.tensor_scalar(out=out, in0=in0, scalar1=s1, scalar2=s2, op0=op0, op1=op1), r, w)

    def stt(self, eng, out, in0, scalar, in1, op0, op1, r=(), w=()):
        return self.P.op(eng, lambda e: e.scalar_tensor_tensor(out=out, in0=in0, scalar=scalar, in1=in1, op0=op0, op1=op1), r, w)

    def red(self, eng, out, in_, op=ALU.add, r=(), w=()):
        return self.P.op(eng, lambda e: e.tensor_reduce(out=out, in_=in_, axis=AX.X, op=op), r, w)

    def recip(self, out, in_, r=(), w=()):
        return self.P.op("dve", lambda e: e.reciprocal(out=out, in_=in_), r, w)

    def memset(self, eng, ap, val, w=()):
        return self.P.op(eng, lambda e: e.memset(ap, val), (), w)

    def scan(self, out, d0, d1, initial, op0, op1, r=(), w=()):
        return self.P.op("dve", lambda e: e.tensor_tensor_scan(out=out, data0=d0, data1=d1, initial=initial, op0=op0, op1=op1), r, w)

    def dma(self, out, in_, r=(), w=(), q="sp", slow=False, sem=None):
        kw = {"allow_slow_non_contiguous": True} if slow else {}
        return self.P.dma(q, [(out, in_, kw)], r, w, sem=sem)


def make_consts():
    c = {}
    c["ident"] = np.eye(128, dtype=np.float32)
    bo64 = np.zeros((128, 128), np.float32)
    bo64[:64, :64] = 1
    bo64[64:, 64:] = 1
    c["bo64"] = bo64
    bo32 = np.zeros((128, 128), np.float32)
    for i in range(4):
        bo32[i * 32:(i + 1) * 32, i * 32:(i + 1) * 32] = 1
    c["bo32"] = bo32
    prot = np.zeros((128, 128), np.float32)
    for b in range(4):
        for d in range(16):
            prot[b * 32 + d + 16, b * 32 + d] = -1.0
            prot[b * 32 + d, b * 32 + d + 16] = 1.0
    c["prot"] = prot
    inv = 1.0 / (10000.0 ** (np.arange(0, 32, 2, dtype=np.float32) / 32.0))
    ang = np.arange(S, dtype=np.float32)[:, None] * inv[None, :]
    cos = np.cos(ang).astype(np.float32).T
    sin = np.sin(ang).astype(np.float32).T
    c["cosT"] = np.ascontiguousarray(np.tile(cos, (8, 1)))
    c["sinT"] = np.ascontiguousarray(np.tile(sin, (8, 1)))
    k = np.arange(128)[:, None]
    q = np.arange(128)[None, :]
    c["negmask"] = np.where(k > q, -1e30, 0.0).astype(np.float32)
    c["cmask"] = ((k // 64) <= (q // 64)).astype(np.float32)
    c["triu"] = (k <= q).astype(np.float32)
    k6 = np.arange(64)[:, None]
    q6 = np.arange(64)[None, :]
    m64 = np.zeros((64, 3, 64), np.float32)
    m64[:, 0, :] = (k6 < q6)
    m64[:, 1, :] = (k6 <= q6)
    m64[:, 2, :] = (k6 > q6)
    c["m64"] = m64
    sel = np.zeros((4, 4, 128), np.float32)
    for h in range(4):
        sel[h, h, :] = 1.0
    c["sel"] = sel.transpose(1, 0, 2).copy()
    return c


CONST_SHAPES = {"ident": [128, 128], "bo64": [128, 128], "bo32": [128, 128], "prot": [128, 128],
                "cosT": [128, S], "sinT": [128, S], "negmask": [128, 128], "cmask": [128, 128],
                "triu": [128, 128], "m64": [64, 3, 64], "sel": [4, 4, 128]}

WEIGHT_SHAPES = {
    'ada_w': [L, D, 6 * D], 'ada_b': [L, 6 * D], 'norm1_g': [L, D], 'norm2_g': [L, D],
    'w_in': [L, D, NIN], 'w_out': [L, D, D],
    'rw_mu': [L, 896], 'rw_w0': [L, 256], 'rw_w_up': [L, 32, 256], 'rw_a0': [L, 256], 'rw_a_up': [L, 32, 256],
    'rw_g_up': [L, 64, 256], 'rw_k_k': [L, 256], 'rw_k_a': [L, 256], 'rw_r_k': [L, 4, 64],
    'rw_ln_g': [L, 256], 'rw_ln_b': [L, 256],
    'df_lam_q1': [L, 32], 'df_lam_k1': [L, 32], 'df_lam_q2': [L, 32], 'df_lam_k2': [L, 32],
    'df_q_g': [L, 32], 'df_k_g': [L, 32], 'df_sub_g': [L, 64],
    'sg_w': [L, 4, 128, 128], 'sg_b': [L, 4, 128], 'sg_ln_g': [L, 256], 'sg_ln_b': [L, 256],
    'fx_q_g': [L, 64], 'fx_k_g': [L, 64], 'fx_f_b': [L, 4],
    'ffn_up': [L, D, 2 * DFF], 'ffn_conv': [L, 3, 2 * DFF], 'ffn_conv_b': [L, 2 * DFF], 'ffn_down': [L, DFF, D],
}


class Ctx:
    pass


def build(layers=(0, 1, 2, 3), phases=("P1", "A", "B", "D", "P3"), debug=False, y_in=False):
    nc = bass.Bass("TRN2", target_bir_lowering=False)
    dkind = "ExternalOutput" if debug else "Internal"
    I = {}

    def din(name, shape):
        I[name] = nc.dram_tensor(name, list(shape), F32, kind="ExternalInput").ap()
    din("x", [S, D])
    din("c", [1, D])
    for k, shp in WEIGHT_SHAPES.items():
        din(k, shp)
    for k, shp in CONST_SHAPES.items():
        din(k, shp)
    out = nc.dram_tensor("out", [S, D], F32, kind="ExternalOutput").ap()

    def dscr(name, shape, dt, kind=None):
        return nc.dram_tensor(name, list(shape), dt, kind=kind or dkind).ap()
    R = Ctx()
    R.xres = dscr("xres", [S, D], F32)
    R.pmA = dscr("pmA", [896, S], F32)
    R.qTB = dscr("qTB", [256, S], BF16)
    R.kTB = dscr("kTB", [256, S], BF16)
    R.vB = dscr("vB", [S, 260], BF16)
    R.qTD = dscr("qTD", [256, S], BF16)
    R.kTD = dscr("kTD", [256, S], BF16)
    R.vD = dscr("vD", [S, 260], BF16)
    R.Ffm = dscr("Ffm", [4, S], F32)
    if y_in:
        R.yscr = nc.dram_tensor("yscr_in", [S, 768], F32, kind="ExternalInput").ap()
        R.yTA = nc.dram_tensor("yTA_in", [256, S], F32, kind="ExternalInput").ap()
    else:
        R.yscr = dscr("yscr", [S, 768], BF16)
        R.yTA = dscr("yTA", [256, S], BF16)
    R.upbf = dscr("upbf", [L, 11, 128, 8 * 512], BF16, kind="Internal")
    R.dnbf = dscr("dnbf", [L, 128, 22 * 1024], BF16, kind="Internal")

    with ExitStack() as es:
        P = Prog(nc, es)
        k = K(P)
        G = Ctx()
        G.nc, G.P, G.k, G.I, G.R, G.out = nc, P, k, I, R, out
        G.y_in = y_in
        G.dbg_x1 = nc.dram_tensor("dbg_x1", [S, D], F32, kind="ExternalOutput").ap() if debug else None

        uid = [0]

        def sb(stack, name, shape, dt=F32):
            uid[0] += 1
            return stack.enter_context(nc.sbuf_tensor("%s_u%d" % (name, uid[0]), list(shape), dt))
        G.sb = sb
        G.banks = [es.enter_context(nc.psum_tensor("bank%d" % i, [128, 512], F32)) for i in range(8)]
        G.ident = sb(es, "ident", [128, 128])
        G.identb = sb(es, "identb", [128, 128], BF16)
        G.ones_row = sb(es, "ones_row", [1, 128])
        G.condB = sb(es, "condB", [128, 8, 128])
        G.modB = sb(es, "modB", [128, 6 * D])
        k.dma(G.ident[:], I["ident"], w=["ident"])
        k.copy("dve", G.identb[:], G.ident[:], r=["ident"], w=["identb"])
        k.memset("dve", G.ones_row[:], 1.0, w=["ones_row"])
        with ExitStack() as s0:
            cT = sb(s0, "cT", [128, 8])
            cS = sb(s0, "cS", [128, 8])
            k.dma(cT[:], I["c"].rearrange("o (kc p) -> p (o kc)", p=128), w=["cT"], slow=True)
            k.act(cS[:], cT[:], AF.Silu, r=["cT"], w=["cS"])
            k.copy("dve", G.condB[:], cS[:].unsqueeze(2).to_broadcast([128, 8, 128]), r=["cS"], w=["condB"])
            P.barrier()

        for li, l in enumerate(layers):
            xsrc = I["x"] if li == 0 else R.xres
            xdst = out if li == len(layers) - 1 else R.xres
            G.prep_in_B = ("P3" in phases) and ("B" in phases)
            if "P3" in phases and not G.prep_in_B:
                prep_ffn(G, l)
            layer_setup(G, l)
            if "P1" in phases:
                phase_p1(G, l, xsrc)
            if "A" in phases:
                phase_A(G, l)
            if "B" in phases:
                phase_B(G, l)
            if "D" in phases:
                phase_D(G, l)
            if "P3" in phases:
                phase_p3(G, l, xsrc, xdst)
        P.barrier()
        P.emit()
    return nc, P


def prep_ffn(G, l):
    nc, P, k, I, R = G.nc, G.P, G.k, G.I, G.R
    with ExitStack() as s:
        st = [G.sb(s, "pf_st%d" % i, [128, 8, 512]) for i in range(2)]
        sbf = [G.sb(s, "pf_bf%d" % i, [128, 8, 512], BF16) for i in range(2)]
        up = I["ffn_up"][l].rearrange("(kc p) n -> p kc n", p=128)
        dn = I["ffn_down"][l].rearrange("(fc p) d -> p fc d", p=128)
        engs = ["dve", "act", "dve", "act", "pool"]
        jobs = []
        for u in range(11):
            jobs.append((up[:, :, u * 512:(u + 1) * 512], R.upbf[l, u].rearrange("p (kc n) -> p kc n", kc=8), 8, 512))
        for u in range(11):
            jobs.append((dn[:, 2 * u:2 * u + 2, :], R.dnbf[l][:, 2 * u * 1024:(2 * u + 2) * 1024].rearrange("p (a d) -> p a d", a=2), 2, 1024))

        def load(i):
            src, dst, a, b = jobs[i]
            k.dma(st[i % 2][:].rearrange("p a b -> p (a b)")[:, 0:a * b].rearrange("p (a b) -> p a b", a=a), src,
                  w=["pf_st%d" % (i % 2)])
        load(0)
        load(1)
        for i in range(len(jobs)):
            src, dst, a, b = jobs[i]
            sv = st[i % 2][:].rearrange("p a b -> p (a b)")[:, 0:a * b]
            bv = sbf[i % 2][:].rearrange("p a b -> p (a b)")[:, 0:a * b]
            k.copy(engs[i % 5], bv, sv, r=["pf_st%d" % (i % 2)], w=["pf_bf%d" % (i % 2)])
            k.dma(dst, bv.rearrange("p (a b) -> p a b", a=a), r=["pf_bf%d" % (i % 2)], w=[("ffnw", l)])
            if i + 2 < len(jobs):
                load(i + 2)
        P.barrier()


def layer_setup(G, l):
    nc, P, k, I, R = G.nc, G.P, G.k, G.I, G.R
    banks = G.banks
    with ExitStack() as s:
        aw = [G.sb(s, "ls_aw%d" % i, [128, 8, 512]) for i in range(2)]
        rows = G.sb(s, "ls_rows", [1, 8 * D])
        k.dma(rows[:, 0:6 * D], I["ada_b"][l:l + 1, :], w=["ls_rows_b"])
        k.dma(rows[:, 6 * D:7 * D], I["norm1_g"][l:l + 1, :], w=["ls_rows_g"])
        k.dma(rows[:, 7 * D:8 * D], I["norm2_g"][l:l + 1, :], w=["ls_rows_g"])
        awv = I["ada_w"][l].rearrange("(kc p) n -> p kc n", p=128)
        for cc in range(12):
            b = cc % 2
            k.dma(aw[b][:], awv[:, :, cc * 512:(cc + 1) * 512], w=["ls_aw%d" % b])
            bk = banks[b]
            for kc in range(8):
                k.mm(bk[:], G.condB[:, kc, :], aw[b][:, kc, :], start=(kc == 0), stop=False,
                     r=["condB", "ls_aw%d" % b], w=["bank%d" % b])
            k.mm(bk[:], G.ones_row[0:1, :], rows[0:1, cc * 512:(cc + 1) * 512], start=False, stop=True,
                 r=["ones_row", "ls_rows_b"], w=["bank%d" % b])
            k.copy("act" if cc % 2 else "dve", G.modB[:, cc * 512:(cc + 1) * 512], bk[:], r=["bank%d" % b], w=["modB"])
        for gi, (goff, slot) in enumerate(((6 * D, 1 * D), (7 * D, 4 * D))):
            for hf in range(2):
                b = 2 + hf
                k.mm(banks[b][:], G.ones_row[0:1, :], rows[0:1, goff + hf * 512: goff + (hf + 1) * 512],
                     r=["ones_row", "ls_rows_g"], w=["bank%d" % b])
                sl = G.modB[:, slot + hf * 512: slot + (hf + 1) * 512]
                k.stt("dve", sl, sl, 1.0, banks[b][:], ALU.add, ALU.mult, r=["modB", "bank%d" % b], w=["modB"])
        P.barrier()


def norm_mod_T(G, xt, goff, shoff, T):
    k = G.k
    banks = G.banks
    for j in range(4):
        k.act(T.tmp[:], xt[:, j, :], AF.Square, r=["xt"], w=["tmp"])
        k.red("dve", T.ss[:, j:j + 1], T.tmp[:], r=["tmp"], w=["ss"])
    k.act(T.rstd[:], T.ss[:], AF.Sqrt, bias=EPS, scale=1.0 / D, r=["ss"], w=["rstd"])
    k.recip(T.rstd[:], T.rstd[:], r=["rstd"], w=["rstd"])
    for j in range(4):
        k.stt("dve", T.tmp[:], xt[:, j, :], T.rstd[:, j:j + 1], G.modB[:, goff:goff + D], ALU.mult, ALU.mult,
              r=["xt", "rstd", "modB"], w=["tmp"])
        k.tt("dve", T.hb[:, j, :], T.tmp[:], G.modB[:, shoff:shoff + D], ALU.add, r=["tmp", "modB"], w=["hb"])
    for j in range(4):
        b = 5 + (j % 2)
        pv = banks[b][:].bitcast(BF16).rearrange("p (a t) -> p a t", a=8)
        for kc in range(8):
            k.tr(pv[:, kc, :], T.hb[:, j, kc * 128:(kc + 1) * 128], G.identb[:], r=["hb", "identb"], w=["bank%d" % b])
        k.copy("act" if j % 2 else "dve", T.hT[:, :, j * 128:(j + 1) * 128], pv, r=["bank%d" % b], w=[getattr(T, "hTk", "hT")])


def bcast_load(G, tile_ap, row_ap, n, w):
    G.k.dma(tile_ap, row_ap.to_broadcast([128, n]), w=w, slow=True)


def phase_p1(G, l, xsrc):
    nc, P, k, I, R = G.nc, G.P, G.k, G.I, G.R
    banks = G.banks
    with ExitStack() as s:
        sb = lambda name, shape, dt=F32: G.sb(s, name, shape, dt)
        T = Ctx()
        w_in = sb("w_in", [128, 8, NIN], BF16)
        stg = [sb("p1_stg%d" % i, [128, 1474]) for i in range(2)]
        xt = sb("xt", [128, 4, D])
        T.sqj = sb("sqj", [128, D], BF16)
        T.ss = sb("ss", [128, 4])
        T.rstd = sb("rstd", [128, 4])
        T.tmp = sb("tmp", [128, D])
        T.hb = sb("hb", [128, 4, D], BF16)
        hTs = [sb("hT%d" % i, [128, 8, 512], BF16) for i in range(2)]
        fA = [sb("fA%d" % i, [128, 512]) for i in range(3)]
        fB = [sb("fB%d" % i, [128, 512]) for i in range(3)]
        fC = [sb("fC%d" % i, [128, 512]) for i in range(3)]
        obf = [sb("obf%d" % i, [128, 512], BF16) for i in range(3)]
        xnb = [sb("xnb%d" % i, [128, 512], BF16) for i in range(2)]
        paA = sb("paA", [128, 7, 513])
        pmo = [sb("pmo%d" % i, [128, 512]) for i in range(2)]
        cs = [sb("cos%d" % i, [128, 512]) for i in range(2)]
        sn = [sb("sin%d" % i, [128, 512]) for i in range(2)]
        bo64 = sb("bo64", [128, 128])
        bo32 = sb("bo32", [128, 128])
        protf = sb("protf", [128, 128])
        protb = sb("protb", [128, 128], BF16)
        gcol = sb("gcol", [128, 8])
        mu = sb("mu", [128, 7])
        negfb = sb("negfb", [4, 1])
        ones4 = sb("ones4", [4, 512])
        Fg = [sb("Fg%d" % i, [4, 512]) for i in range(2)]
        f4 = sb("f4", [4, 512])
        vt = [sb("vt%d" % i, [128, 4, 65], BF16) for i in range(4)]
        sgw = sb("sgw", [128, 4, 128])
        WgT = sb("WgT", [128, 4, 128], BF16)
        triu = sb("triu", [128, 128])
        sgbT = sb("sgbT", [128, 4])
        lnCg = sb("lnCg", [128, 256])
        lnCb = sb("lnCb", [128, 256])
        glC = [sb("glC%d" % i, [128, 512]) for i in range(2)]
        stC = [sb("stC%d" % i, [128, 8]) for i in range(2)]
        tmpc = [sb("tmpc%d" % i, [128, 256]) for i in range(2)]
        vnb = [sb("vnb%d" % i, [128, 256], BF16) for i in range(2)]
        ycb = [sb("ycb%d" % i, [128, 256], BF16) for i in range(2)]

        for kc in range(8):
            for hf in range(2):
                i = (kc * 2 + hf) % 2
                k.dma(stg[i][:], I["w_in"][l, kc * 128:(kc + 1) * 128, hf * 1474:(hf + 1) * 1474], w=["p1_stg%d" % i])
                k.copy(("dve", "pool", "act")[(kc * 2 + hf) % 3], w_in[:, kc, hf * 1474:(hf + 1) * 1474], stg[i][:],
                       r=["p1_stg%d" % i], w=["w_in"])
        k.dma(bo64[:], I["bo64"], w=["bo64"])
        k.dma(bo32[:], I["bo32"], w=["bo32"])
        k.dma(protf[:], I["prot"], w=["protf"])
        k.copy("dve", protb[:], protf[:], r=["protf"], w=["protb"])
        k.dma(triu[:], I["triu"], w=["triu"])
        for rep in range(2):
            k.dma(gcol[rep * 64:(rep + 1) * 64, 0:1], I["fx_q_g"][l].rearrange("(d o) -> d o", o=1), w=["gcol"], slow=True)
            k.dma(gcol[rep * 64:(rep + 1) * 64, 1:2], I["fx_k_g"][l].rearrange("(d o) -> d o", o=1), w=["gcol"], slow=True)
        for rep in range(4):
            k.dma(gcol[rep * 32:(rep + 1) * 32, 2:3], I["df_q_g"][l].rearrange("(d o) -> d o", o=1), w=["gcol"], slow=True)
            k.dma(gcol[rep * 32:(rep + 1) * 32, 3:4], I["df_k_g"][l].rearrange("(d o) -> d o", o=1), w=["gcol"], slow=True)
        k.ts("dve", gcol[:, 0:1], gcol[:, 0:1], 0.125, ALU.mult, r=["gcol"], w=["gcol"])
        k.ts("dve", gcol[:, 2:3], gcol[:, 2:3], 32.0 ** -0.5, ALU.mult, r=["gcol"], w=["gcol"])
        k.dma(mu[:], I["rw_mu"][l].rearrange("(c p) -> p c", p=128), w=["mu"], slow=True)
        k.dma(negfb[:], I["fx_f_b"][l].rearrange("(h o) -> h o", o=1), w=["negfb"], slow=True)
        k.ts("dve", negfb[:], negfb[:], -1.0, ALU.mult, r=["negfb"], w=["negfb"])
        k.memset("dve", ones4[:], 1.0, w=["ones4"])
        k.memset("pool", paA[:], 0.0, w=["paA%d" % i for i in range(7)])
        for i in range(4):
            k.memset("pool", vt[i][:], 1.0, w=["vt%d" % i])
        k.dma(sgw[:], I["sg_w"][l].rearrange("g i j -> i g j"), w=["sgw"])
        for g in range(4):
            k.tr(banks[0][:, g * 128:(g + 1) * 128], sgw[:, g, :], G.ident[:], r=["sgw", "ident"], w=["bank0"])
        k.tt("dve", WgT[:], banks[0][:].rearrange("p (g i) -> p g i", g=4),
             triu[:].unsqueeze(1).to_broadcast([128, 4, 128]), ALU.mult, r=["bank0", "triu"], w=["WgT"])
        k.dma(sgbT[:], I["sg_b"][l].rearrange("g i -> i g"), w=["sgbT"], slow=True)
        bcast_load(G, lnCg[:], I["sg_ln_g"][l:l + 1, :], 256, ["lnCg"])
        bcast_load(G, lnCb[:], I["sg_ln_b"][l:l + 1, :], 256, ["lnCb"])

        cur = Ctx()

        def fm_mm(col0, ncols, bk):
            for kc in range(8):
                k.mm(banks[bk][0:ncols, :], w_in[:, kc, col0:col0 + ncols], cur.hT[:, kc, :], start=(kc == 0), stop=(kc == 7),
                     r=["w_in", cur.hTk], w=["bank%d" % bk])

        def load_norm(g):
            tsl_ = slice(g * TG, (g + 1) * TG)
            k.dma(xt[:], xsrc[tsl_, :].rearrange("(j p) d -> p j d", p=128), r=[("x", g)], w=["xt"])
            T.hT = hTs[g % 2]
            T.hTk = "hT%d" % (g % 2)
            norm_mod_T(G, xt, 1 * D, 0, T)
        load_norm(0)

        for g in range(NG):
            tsl = slice(g * TG, (g + 1) * TG)
            k.dma(cs[g % 2][:], I["cosT"][:, tsl], w=["cos%d" % (g % 2)])
            k.dma(sn[g % 2][:], I["sinT"][:, tsl], w=["sin%d" % (g % 2)])
            cur.hT = hTs[g % 2]
            cur.hTk = "hT%d" % (g % 2)
            for ci in range(7):
                bk = k.rot("p1bank", 3)
                fm_mm(ci * 128, 128, bk)
                k.copy("pool", paA[:, ci, 0:1], paA[:, ci, 512:513], r=["paA%d" % ci], w=["paA%d" % ci])
                k.copy("act", paA[:, ci, 1:513], banks[bk][:], r=["bank%d" % bk], w=["paA%d" % ci])
                i = k.rot("fA", 3)
                k.tt("pool", fA[i][:], paA[:, ci, 0:512], paA[:, ci, 1:513], ALU.subtract, r=["paA%d" % ci], w=["fA%d" % i])
                o = k.rot("pmo", 2)
                k.stt("dve", pmo[o][:], fA[i][:], mu[:, ci:ci + 1], paA[:, ci, 1:513], ALU.mult, ALU.add,
                      r=["fA%d" % i, "mu", "paA%d" % ci], w=["pmo%d" % o])
                k.dma(R.pmA[ci * 128:(ci + 1) * 128, tsl], pmo[o][:], r=["pmo%d" % o], w=[("pmA", g)])
            if g + 1 < NG:
                load_norm(g + 1)
            for (mix, col0, gi, dst, rope) in (("B", 896, 2, R.qTB, True), ("B", 1152, 3, R.kTB, True),
                                               ("D", 2176, 0, R.qTD, False), ("D", 2432, 1, R.kTD, False)):
                for ci in range(2):
                    bk = k.rot("p1bank", 3)
                    fm_mm(col0 + ci * 128, 128, bk)
                    a = k.rot("fA", 3)
                    k.act(fA[a][:], banks[bk][:], AF.Square, r=["bank%d" % bk], w=["fA%d" % a])
                    sbk = 3 + k.rot("p1sbank", 2)
                    k.mm(banks[sbk][:], bo32[:] if mix == "B" else bo64[:], fA[a][:], r=["bo32", "bo64", "fA%d" % a],
                         w=["bank%d" % sbk])
                    b = k.rot("fB", 3)
                    nd = 32.0 if mix == "B" else 64.0
                    k.act(fB[b][:], banks[sbk][:], AF.Ln, bias=EPS, scale=1.0 / nd, r=["bank%d" % sbk], w=["fB%d" % b])
                    k.act(fB[b][:], fB[b][:], AF.Exp, scale=-0.5, r=["fB%d" % b], w=["fB%d" % b])
                    o = k.rot("obf", 3)
                    if not rope:
                        k.stt("dve", obf[o][:], banks[bk][:], gcol[:, gi:gi + 1], fB[b][:], ALU.mult, ALU.mult,
                              r=["bank%d" % bk, "gcol", "fB%d" % b], w=["obf%d" % o])
                    else:
                        c_ = k.rot("fC", 3)
                        k.stt("dve", fC[c_][:], banks[bk][:], gcol[:, gi:gi + 1], fB[b][:], ALU.mult, ALU.mult,
                              r=["bank%d" % bk, "gcol", "fB%d" % b], w=["fC%d" % c_])
                        xb = k.rot("xnb", 2)
                        k.copy("act", xnb[xb][:], fC[c_][:], r=["fC%d" % c_], w=["xnb%d" % xb])
                        rbk = 3 + k.rot("p1sbank", 2)
                        k.mm(banks[rbk][:], protb[:], xnb[xb][:], r=["protb", "xnb%d" % xb], w=["bank%d" % rbk])
                        a2 = k.rot("fA", 3)
                        k.tt("pool", fA[a2][:], fC[c_][:], cs[g % 2][:], ALU.mult, r=["fC%d" % c_, "cos%d" % (g % 2)], w=["fA%d" % a2])
                        b2 = k.rot("fB", 3)
                        k.tt("dve", fB[b2][:], banks[rbk][:], sn[g % 2][:], ALU.mult, r=["bank%d" % rbk, "sin%d" % (g % 2)],
                             w=["fB%d" % b2])
                        k.tt("pool", obf[o][:], fA[a2][:], fB[b2][:], ALU.add, r=["fA%d" % a2, "fB%d" % b2], w=["obf%d" % o])
                    k.dma(dst[ci * 128:(ci + 1) * 128, tsl], obf[o][:], r=["obf%d" % o], w=[(mix + "qk", g)])
            bk = k.rot("p1bank", 3)
            fm_mm(2944, 4, bk)
            k.act(f4[:], banks[bk][0:4, :], AF.Exp, bias=negfb[:, 0:1], scale=-1.0, r=["bank%d" % bk, "negfb"], w=["f4"])
            k.act(f4[:], f4[:], AF.Ln, bias=1.0, scale=1.0, r=["f4"], w=["f4"])
            if g == 0:
                k.scan(Fg[0][:], ones4[:], f4[:], 0.0, ALU.mult, ALU.subtract, r=["ones4", "f4"], w=["Fg0"])
            else:
                k.scan(Fg[g % 2][:], ones4[:], f4[:], Fg[(g - 1) % 2][:, 511:512], ALU.mult, ALU.subtract,
                       r=["ones4", "f4", "Fg%d" % ((g - 1) % 2)], w=["Fg%d" % (g % 2)])
            k.dma(R.Ffm[:, tsl], Fg[g % 2][:], r=["Fg%d" % (g % 2)], w=[("Ffm", g)])
            for j in range(4):
                rows = slice(g * TG + j * 128, g * TG + (j + 1) * 128)
                for (mix, col0, dst) in (("B", 1408, R.vB), ("D", 2688, R.vD)):
                    bk = k.rot("p1bank", 3)
                    for kc in range(8):
                        k.mm(banks[bk][:, 0:256], cur.hT[:, kc, j * 128:(j + 1) * 128], w_in[:, kc, col0:col0 + 256],
                             start=(kc == 0), stop=(kc == 7), r=["w_in", cur.hTk], w=["bank%d" % bk])
                    vi = k.rot("vt", 4)
                    k.copy("act" if mix == "B" else "dve", vt[vi][:, :, 0:64], banks[bk][:, 0:256].rearrange("p (h e) -> p h e", h=4),
                           r=["bank%d" % bk], w=["vt%d" % vi])
                    k.dma(dst[rows, :], vt[vi][:].rearrange("p h e -> p (h e)"), r=["vt%d" % vi], w=[(mix + "v", g)])
                bk = k.rot("p1bank", 3)
                for kc in range(8):
                    k.mm(banks[bk][:], cur.hT[:, kc, j * 128:(j + 1) * 128], w_in[:, kc, 1664:2176],
                         start=(kc == 0), stop=(kc == 7), r=["w_in", cur.hTk], w=["bank%d" % bk])
                ci = k.rot("glC", 2)
                gl, st, tc, vb, yb = glC[ci], stC[ci], tmpc[ci], vnb[ci], ycb[ci]
                kk = "C%d" % ci
                k.act(gl[:], banks[bk][:], AF.Gelu, r=["bank%d" % bk], w=[kk + "gl"])
                k.red("dve", st[:, 0:1], gl[:, 256:512], r=[kk + "gl"], w=[kk + "st"])
                k.act(tc[:], gl[:, 256:512], AF.Square, r=[kk + "gl"], w=[kk + "tc"])
                k.red("dve", st[:, 1:2], tc[:], r=[kk + "tc"], w=[kk + "st"])
                k.ts("dve", st[:, 2:3], st[:, 0:1], 1.0 / 256, ALU.mult, r=[kk + "st"], w=[kk + "st"])
                k.tt("dve", st[:, 3:4], st[:, 2:3], st[:, 2:3], ALU.mult, r=[kk + "st"], w=[kk + "st"])
                k.stt("dve", st[:, 4:5], st[:, 1:2], 1.0 / 256, st[:, 3:4], ALU.mult, ALU.subtract, r=[kk + "st"], w=[kk + "st"])
                k.act(st[:, 5:6], st[:, 4:5], AF.Sqrt, bias=EPS, scale=1.0, r=[kk + "st"], w=[kk + "st"])
                k.recip(st[:, 5:6], st[:, 5:6], r=[kk + "st"], w=[kk + "st"])
                k.ts("dve", tc[:], gl[:, 256:512], st[:, 2:3], ALU.subtract, st[:, 5:6], ALU.mult, r=[kk + "gl", kk + "st"], w=[kk + "tc"])
                k.tt("pool", tc[:], tc[:], lnCg[:], ALU.mult, r=[kk + "tc", "lnCg"], w=[kk + "tc"])
                k.tt("pool", vb[:], tc[:], lnCb[:], ALU.add, r=[kk + "tc", "lnCb"], w=[kk + "vb"])
                sbk = 3 + k.rot("p1sbank", 2)
                for hg in range(4):
                    k.mm(banks[sbk][:, hg * 64:(hg + 1) * 64], WgT[:, hg, :], vb[:, hg * 64:(hg + 1) * 64],
                         r=["WgT", kk + "vb"], w=["bank%d" % sbk])
                k.tt("dve", tc[:].rearrange("p (h e) -> p h e", h=4), banks[sbk][:, 0:256].rearrange("p (h e) -> p h e", h=4),
                     sgbT[:].unsqueeze(2).to_broadcast([128, 4, 64]), ALU.add, r=["bank%d" % sbk, "sgbT", kk + "vb"], w=[kk + "tc"])
                k.tt("pool", yb[:], tc[:], gl[:, 0:256], ALU.mult, r=[kk + "tc", kk + "gl"], w=[kk + "yb"])
                if not G.y_in:
                    k.dma(R.yscr[rows, 256:512], yb[:], r=[kk + "yb"], w=[("yC", g)])
        P.barrier()


def phase_p3(G, l, xsrc, xdst):
    nc, P, k, I, R = G.nc, G.P, G.k, G.I, G.R
    banks = G.banks
    ydt = F32 if G.y_in else BF16
    with ExitStack() as s:
        sb = lambda name, shape, dt=F32: G.sb(s, name, shape, dt)
        T = Ctx()
        w_out = sb("w_out", [128, 8, D], BF16)
        xt = sb("xt3", [128, 4, D])
        T.sqj = sb("sqj3", [128, D], BF16)
        T.ss = sb("ss3", [128, 4])
        T.rstd = sb("rstd3", [128, 4])
        T.tmp = sb("tmp3", [128, D])
        T.hb = sb("hb3", [128, 4, D], BF16)
        T.hT = sb("hT3", [128, 8, 512], BF16)
        yT = T.hT
        actT = sb("actT", [128, 22, 512], BF16)
        upw = [sb("upw%d" % i, [128, 8, 512], BF16) for i in range(2)]
        dnw = sb("dnw", [128, 22, D], BF16)
        ub = [sb("ub%d" % i, [128, 514]) for i in range(3)]
        c1 = [sb("c1_%d" % i, [128, 512]) for i in range(3)]
        c2 = [sb("c2_%d" % i, [128, 512]) for i in range(3)]
        c3 = [sb("c3_%d" % i, [128, 512]) for i in range(2)]
        sgt = [sb("sgt%d" % i, [128, 512]) for i in range(2)]
        convw = sb("convw", [128, 44, 3])
        convb = sb("convb", [128, 44])
        carryF = sb("carryF", [128, 44, 2])

        for kc in range(8):
            k.dma(T.tmp[:], I["w_out"][l, kc * 128:(kc + 1) * 128, :], w=["tmp"])
            k.copy(("dve", "pool", "act")[kc % 3], w_out[:, kc, :], T.tmp[:], r=["tmp"], w=["w_out"])
        cwn = T.tmp[0:44, 0:512].rearrange("p (t c) -> p t c", t=4)
        for t in range(3):
            k.dma(cwn[:, t, :], I["ffn_conv"][l, t].rearrange("(cc p) -> cc p", p=128), w=["tmp"])
        k.dma(cwn[:, 3, :], I["ffn_conv_b"][l].rearrange("(cc p) -> cc p", p=128), w=["tmp"])
        for t in range(4):
            k.mm(banks[0][:, t * 44:(t + 1) * 44], cwn[:, t, :], G.ident[0:44, 0:44], r=["tmp", "ident"], w=["bank0"])
        k.copy("dve", convw[:], banks[0][:, 0:132].rearrange("p (t c) -> p c t", t=3), r=["bank0"], w=["convw"])
        k.copy("dve", convb[:], banks[0][:, 132:176], r=["bank0"], w=["convb"])
        k.memset("pool", carryF[:], 0.0, w=["carryF"])

        for g in range(NG):
            tsl = slice(g * TG, (g + 1) * TG)
            k.dma(xt[:], xsrc[tsl, :].rearrange("(j p) d -> p j d", p=128), r=[("x", g)], w=["xt"])
            k.dma(dnw[:].rearrange("p a d -> p (a d)"), R.dnbf[l], r=[("ffnw", l)], w=["dnw"])
            if G.y_in:
                for j in range(4):
                    k.dma(T.tmp[:, 0:768], R.yscr[g * TG + j * 128: g * TG + (j + 1) * 128, :], w=["tmp"])
                    k.copy("pool", T.hb[:, j, 0:768], T.tmp[:, 0:768], r=["tmp"], w=["hb"])
                for kc in range(2):
                    k.dma(c1[kc][:], R.yTA[kc * 128:(kc + 1) * 128, tsl], w=["c1_%d" % kc])
                    k.copy("act", yT[:, kc, :], c1[kc][:], r=["c1_%d" % kc], w=["hT"])
            else:
                k.dma(T.hb[:, :, 0:768], R.yscr[tsl, :].rearrange("(j p) c -> p j c", p=128),
                      r=[("yC", g), ("yB", g), ("yD", g)], w=["hb"])
                k.dma(yT[:, 0:2, :], R.yTA[:, tsl].rearrange("(kc p) t -> p kc t", p=128), r=[("yA", g)], w=["hT"])
            for j in range(4):
                b = 5 + (j % 2)
                pv = banks[b][:].bitcast(BF16).rearrange("p (a t) -> p a t", a=8)
                for kc in range(6):
                    k.tr(pv[:, kc, :], T.hb[:, j, kc * 128:(kc + 1) * 128], G.identb[:], r=["hb", "identb"], w=["bank%d" % b])
                k.copy("act" if j % 2 else "dve", yT[:, 2:8, j * 128:(j + 1) * 128], pv[:, 0:6, :], r=["bank%d" % b], w=["hT"])
            for j in range(4):
                for hf in range(2):
                    bk = k.rot("p3bank", 2)
                    for kc in range(8):
                        k.mm(banks[bk][:], yT[:, kc, j * 128:(j + 1) * 128], w_out[:, kc, hf * 512:(hf + 1) * 512],
                             start=(kc == 0), stop=(kc == 7), r=["hT", "w_out"], w=["bank%d" % bk])
                    ci = k.rot("c1", 2)
                    k.tt("dve", c1[ci][:], banks[bk][:], G.modB[:, 2 * D + hf * 512: 2 * D + (hf + 1) * 512], ALU.mult,
                         r=["bank%d" % bk, "modB"], w=["c1_%d" % ci])
                    k.tt("dve", xt[:, j, hf * 512:(hf + 1) * 512], xt[:, j, hf * 512:(hf + 1) * 512], c1[ci][:], ALU.add,
                         r=["xt", "c1_%d" % ci], w=["xt"])
            if G.dbg_x1 is not None:
                k.dma(G.dbg_x1[tsl, :].rearrange("(j p) d -> p j d", p=128), xt[:], r=["xt"], w=[("dbgx1", g)])
            norm_mod_T(G, xt, 4 * D, 3 * D, T)
            for u in range(11):
                wi = u % 2
                k.dma(upw[wi][:].rearrange("p a n -> p (a n)"), R.upbf[l, u], r=[("ffnw", l)], w=["upw%d" % wi])
                for sc in range(4):
                    cc = 4 * u + sc
                    bk = 2 + k.rot("p3ubank", 3)
                    for kc in range(8):
                        k.mm(banks[bk][:], upw[wi][:, kc, sc * 128:(sc + 1) * 128], T.hT[:, kc, :], start=(kc == 0), stop=(kc == 7),
                             r=["upw%d" % wi, "hT"], w=["bank%d" % bk])
                    ui = k.rot("ub", 3)
                    k.copy("pool", ub[ui][:, 0:2], carryF[:, cc, :], r=["carryF"], w=["ubc%d" % ui])
                    k.copy("act", ub[ui][:, 2:514], banks[bk][:], r=["bank%d" % bk], w=["ub%d" % ui])
                    k.copy("pool", carryF[:, cc, :], ub[ui][:, 512:514], r=["ub%d" % ui], w=["carryF"])
                    k.act(c1[ui][:], ub[ui][:, 2:514], AF.Identity, bias=convb[:, cc:cc + 1], scale=convw[:, cc, 2:3],
                          r=["ub%d" % ui, "convw", "convb"], w=["c1_%d" % ui])
                    k.stt("dve", c2[ui][:], ub[ui][:, 1:513], convw[:, cc, 1:2], c1[ui][:], ALU.mult, ALU.add,
                          r=["ub%d" % ui, "ubc%d" % ui, "convw", "c1_%d" % ui], w=["c2_%d" % ui])
                    if cc < 22:
                        k.stt("dve", actT[:, cc, :], ub[ui][:, 0:512], convw[:, cc, 0:1], c2[ui][:], ALU.mult, ALU.add,
                              r=["ub%d" % ui, "ubc%d" % ui, "convw", "c2_%d" % ui], w=["actT%d" % cc])
                    else:
                        cu = cc - 22
                        k.stt("dve", c3[ui % 2][:], ub[ui][:, 0:512], convw[:, cc, 0:1], c2[ui][:], ALU.mult, ALU.add,
                              r=["ub%d" % ui, "ubc%d" % ui, "convw", "c2_%d" % ui], w=["c3_%d" % (ui % 2)])
                        k.act(sgt[ui % 2][:], c3[ui % 2][:], AF.Silu, r=["c3_%d" % (ui % 2)], w=["sgt%d" % (ui % 2)])
                        k.tt("pool", actT[:, cu, :], actT[:, cu, :], sgt[ui % 2][:], ALU.mult, r=["actT%d" % cu, "sgt%d" % (ui % 2)],
                             w=["actT%d" % cu])
            aks = ["actT%d" % i for i in range(22)]
            for j in range(4):
                for hf in range(2):
                    bk = k.rot("p3bank", 2)
                    for cu in range(22):
                        k.mm(banks[bk][:], actT[:, cu, j * 128:(j + 1) * 128], dnw[:, cu, hf * 512:(hf + 1) * 512],
                             start=(cu == 0), stop=(cu == 21), r=["actT%d" % cu, "dnw"], w=["bank%d" % bk])
                    ci = k.rot("c1", 2)
                    k.tt("dve", c1[ci][:], banks[bk][:], G.modB[:, 5 * D + hf * 512: 5 * D + (hf + 1) * 512], ALU.mult,
                         r=["bank%d" % bk, "modB"], w=["c1_%d" % ci])
                    k.tt("dve", xt[:, j, hf * 512:(hf + 1) * 512], xt[:, j, hf * 512:(hf + 1) * 512], c1[ci][:], ALU.add,
                         r=["xt", "c1_%d" % ci], w=["xt"])
            k.dma(xdst[tsl, :].rearrange("(j p) d -> p j d", p=128), xt[:], r=["xt"], w=[("x", g)])
        P.barrier()


def phase_A(G, l):
    nc, P, k, I, R = G.nc, G.P, G.k, G.I, G.R
    banks = G.banks
    with ExitStack() as s:
        sb = lambda name, shape, dt=F32: G.sb(s, name, shape, dt)
        prm = sb("prm", [64, 7, 4])
        wup = sb("wup", [32, 256])
        aup = sb("aup", [32, 256])
        gup = sb("gup", [64, 256])
        ones64 = sb("ones64", [64, 64])
        ones512 = sb("ones512", [64, 512])
        m64 = sb("m64", [64, 3, 64])
        Tst = sb("Tst", [64, 4, 64])
        big = lambda nm: sb(nm, [64, 4, 512])
        r_, k_, v_ = big("r_"), big("k_"), big("v_")
        lwt, at, gt, kkn, k2, b_, bonus, Yblk, t1, t2 = (big("lwt"), big("at"), big("gt"), big("kkn"), big("k2"), big("b_"),
                                                         big("bonus"), big("Yblk"), big("t1"), big("t2"))
        Gblk = sb("Gblk", [64, 4, 513])
        wd = sb("wd", [32, 512])
        ad = sb("ad", [32, 512])
        gd = sb("gd", [64, 512])
        yo = sb("yo", [64, 4, 512], BF16)
        sm = lambda nm: sb(nm, [64, 4, 64])
        Pc, Pp, eGn, eGp = sm("Pc"), sm("Pp"), sm("eGn"), sm("eGp")
        smb = lambda nm: sb(nm, [64, 4, 64], BF16)
        Bt, Kt = smb("Bt"), smb("Kt")
        Pj = [smb("Pj0"), smb("Pj1")]
        U, tS = sm("U"), sm("tS")
        Ub, Tb = smb("Ub"), smb("Tb")
        DB = []
        for q in range(2):
            d = {"eG": sm("eG%d" % q)}
            for nm_ in ("At", "Rt", "Btm", "Ktm", "Vtm", "LakT", "MrbT", "MrkT"):
                d[nm_] = smb("%s%d" % (nm_, q))
            d["PT"] = [smb("PT%d_%d" % (j, q)) for j in range(6)]
            DB.append(d)
        vb = sb("vb", [64, 4, 512], BF16)

        for n, nm in enumerate(("rw_w0", "rw_a0", "rw_k_k", "rw_k_a")):
            k.dma(prm[:, n, :], I[nm][l].rearrange("(h k) -> k h", k=64), w=["prm"], slow=True)
        k.dma(prm[:, 4, :], I["rw_r_k"][l].rearrange("h k -> k h"), w=["prm"], slow=True)
        k.dma(prm[:, 5, :], I["rw_ln_g"][l].rearrange("(h k) -> k h", k=64), w=["prm"], slow=True)
        k.dma(prm[:, 6, :], I["rw_ln_b"][l].rearrange("(h k) -> k h", k=64), w=["prm"], slow=True)
        k.dma(wup[:], I["rw_w_up"][l], w=["wup"])
        k.dma(aup[:], I["rw_a_up"][l], w=["aup"])
        k.dma(gup[:], I["rw_g_up"][l], w=["gup"])
        k.dma(m64[:], I["m64"], w=["m64"])
        k.memset("dve", ones64[:], 1.0, w=["ones64"])
        k.memset("dve", ones512[:], 1.0, w=["ones512"])
        k.memset("pool", Tst[:], 0.0, w=["Tst"])
        k.memset("pool", Tb[:], 0.0, w=["Tb"])
        k.memset("pool", Gblk[:], 0.0, w=["Gblk"])

        def bc(col):
            return prm[:, col, :].unsqueeze(2).to_broadcast([64, 4, 512])

        def HB(b, half):
            return banks[b][0:64, half * 256:(half + 1) * 256].rearrange("p (h t) -> p h t", h=4)

        def hk(b, half):
            return ("hb", b)

        def fb(b):
            return [("hb", b)]
        allpm = [("pmA", g) for g in range(NG)]

        for g in range(NG):
            tsl = slice(g * TG, (g + 1) * TG)
            k.dma(r_[:], R.pmA[0:256, tsl].rearrange("(h k) t -> k h t", k=64), r=allpm, w=["r_"])
            k.dma(k_[:], R.pmA[256:512, tsl].rearrange("(h k) t -> k h t", k=64), r=allpm, w=["k_"])
            k.dma(v_[:], R.pmA[512:768, tsl].rearrange("(h k) t -> k h t", k=64), r=allpm, w=["v_"])
            k.copy("pool", vb[:], v_[:], r=["v_"], w=["vb"])
            k.dma(wd[:], R.pmA[768:800, tsl], r=allpm, w=["wd"])
            k.dma(ad[:], R.pmA[800:832, tsl], r=allpm, w=["ad"])
            k.dma(gd[:], R.pmA[832:896, tsl], r=allpm, w=["gd"])
            k.act(wd[:], wd[:], AF.Tanh, r=["wd"], w=["wd"])
            k.act(gd[:], gd[:], AF.Sigmoid, r=["gd"], w=["gd"])
            for h in range(4):
                k.mm(banks[h][0:64, :], wup[:, h * 64:(h + 1) * 64], wd[:], r=["wup", "wd"], w=fb(h))
                k.act(lwt[:, h, :], banks[h][0:64, :], AF.Sigmoid, bias=prm[:, 0, h:h + 1], scale=1.0, r=fb(h) + ["prm"], w=["lwt"])
            for h in range(4):
                k.mm(banks[4 + h][0:64, :], aup[:, h * 64:(h + 1) * 64], ad[:], r=["aup", "ad"], w=fb(4 + h))
                k.act(at[:, h, :], banks[4 + h][0:64, :], AF.Sigmoid, bias=prm[:, 1, h:h + 1], scale=1.0, r=fb(4 + h) + ["prm"], w=["at"])
            for h in range(4):
                k.mm(banks[h][0:64, :], gup[:, h * 64:(h + 1) * 64], gd[:], r=["gup", "gd"], w=fb(h))
                k.copy("dve" if h % 2 else "act", gt[:, h, :], banks[h][0:64, :], r=fb(h), w=["gt"])
            k.tt("dve", kkn[:], k_[:], bc(2), ALU.mult, r=["k_", "prm"], w=["kkn"])
            k.tt("dve", t1[:], kkn[:], kkn[:], ALU.mult, r=["kkn"], w=["t1"])
            for h in range(4):
                k.mm(banks[4 + h][0:64, :], ones64[:], t1[:, h, :], r=["ones64", "t1"], w=fb(4 + h))
            for h in range(4):
                k.act(t2[:, h, :], banks[4 + h][0:64, :], AF.Ln, bias=1e-24, scale=1.0, r=fb(4 + h), w=["t2"])
            k.act(t2[:], t2[:], AF.Exp, scale=-0.5, r=["t2"], w=["t2"])
            k.tt("pool", kkn[:], kkn[:], t2[:], ALU.mult, r=["kkn", "t2"], w=["kkn"])
            k.stt("dve", t1[:], at[:], -1.0, bc(3), ALU.add, ALU.mult, r=["at", "prm", "t1"], w=["t1"])
            k.tt("pool", t1[:], t1[:], k_[:], ALU.mult, r=["t1", "k_"], w=["t1"])
            k.tt("dve", k2[:], t1[:], k_[:], ALU.add, r=["t1", "k_"], w=["k2"])
            k.tt("pool", b_[:], kkn[:], at[:], ALU.mult, r=["kkn", "at"], w=["b_"])
            k.tt("dve", t1[:], r_[:], k2[:], ALU.mult, r=["r_", "k2", "t1"], w=["t1"])
            k.tt("dve", t1[:], t1[:], bc(4), ALU.mult, r=["t1", "prm"], w=["t1"])
            for h in range(4):
                k.mm(banks[h][0:64, :], ones64[:], t1[:, h, :], r=["ones64", "t1"], w=fb(h))
                k.tt("dve", bonus[:, h, :], banks[h][0:64, :], v_[:, h, :], ALU.mult, r=fb(h) + ["v_"], w=["bonus"])
            for h in range(4):
                k.scan(Gblk[:, h, 1:513], ones512[:], lwt[:, h, :], 0.0, ALU.mult, ALU.add, r=["ones512", "lwt"], w=["Gblk"])

            if int(os.environ.get("A_STOP", "9")) <= 1:
                continue
            def prep_steps(ci):
                q = ci % 2
                D_ = DB[q]
                c0 = ci * 64
                ts_ = slice(c0, c0 + 64)
                eGq, Atq, Rtq = D_["eG"], D_["At"], D_["Rt"]
                nm = lambda x: "%s_%d" % (x, q)
                id64 = G.identb[0:64, 0:64]
                steps = []

                def s1():
                    k.tt("dve", Pc[:], Gblk[:, :, 1 + c0:1 + c0 + 64], Gblk[:, :, c0:c0 + 1].to_broadcast([64, 4, 64]), ALU.subtract,
                         r=["Gblk"], w=["Pc"])
                    k.tt("pool", Pp[:], Pc[:], lwt[:, :, ts_], ALU.subtract, r=["Pc", "lwt"], w=["Pp"])
                    k.act(eGq[:], Pc[:], AF.Exp, scale=-ALPHA, r=["Pc"], w=[nm("eG")])
                    k.act(eGn[:], Pc[:], AF.Exp, scale=ALPHA, r=["Pc"], w=["eGn"])
                    k.act(eGp[:], Pp[:], AF.Exp, scale=-ALPHA, r=["Pp"], w=["eGp"])
                steps.append(s1)

                def s2():
                    k.stt("dve", Atq[:], kkn[:, :, ts_], -1.0, eGp[:], ALU.mult, ALU.mult, r=["kkn", "eGp"], w=[nm("At")])
                    k.tt("pool", Bt[:], b_[:, :, ts_], eGn[:], ALU.mult, r=["b_", "eGn"], w=["Bt"])
                    k.tt("pool", Kt[:], k2[:, :, ts_], eGn[:], ALU.mult, r=["k2", "eGn"], w=["Kt"])
                    k.tt("dve", Rtq[:], r_[:, :, ts_], eGq[:], ALU.mult, r=["r_", nm("eG")], w=[nm("Rt")])
                steps.append(s2)

                def s3():
                    trs = ((Bt, "Bt", D_["Btm"], nm("Btm"), 2, 1, "act"), (Kt, "Kt", D_["Ktm"], nm("Ktm"), 3, 0, "act"),
                           (None, "vb", D_["Vtm"], nm("Vtm"), 3, 1, "act"))
                    for (X, xk, Xtm, xtk, bq, hf, ce) in trs:
                        for h in range(4):
                            src = vb[:, h, ts_] if X is None else X[:, h, :]
                            k.mm(HB(bq, hf)[:, h, :], src, id64, r=[xk, "identb"], w=[hk(bq, hf)])
                    for (X, xk, Xtm, xtk, bq, hf, ce) in trs:
                        k.copy(ce, Xtm[:], HB(bq, hf), r=[hk(bq, hf)], w=[xtk])
                steps.append(s3)

                def s4():
                    specs = ((Bt, "Bt", Atq, nm("At"), 0, 0, D_["PT"][0], nm("PT0"), 0), (Atq, nm("At"), Bt, "Bt", 0, 1, Pj[0], "Pj0", 2),
                             (Kt, "Kt", Atq, nm("At"), 1, 0, D_["LakT"], nm("LakT"), 0), (Bt, "Bt", Rtq, nm("Rt"), 1, 1, D_["MrbT"], nm("MrbT"), 1),
                             (Kt, "Kt", Rtq, nm("Rt"), 2, 0, D_["MrkT"], nm("MrkT"), 1))
                    for (La, lk, Ra, rk_, bq, hf, dst, dk, mi) in specs:
                        for h in range(4):
                            k.mm(HB(bq, hf)[:, h, :], La[:, h, :], Ra[:, h, :], r=[lk, rk_], w=[hk(bq, hf)])
                    for (La, lk, Ra, rk_, bq, hf, dst, dk, mi) in specs:
                        k.tt("dve", dst[:], HB(bq, hf), m64[:, mi, :].unsqueeze(1).to_broadcast([64, 4, 64]), ALU.mult,
                             r=[hk(bq, hf), "m64"], w=[dk])
                steps.append(s4)

                def mk_sq(j):
                    def sq():
                        cur, nxt = j % 2, (j + 1) % 2
                        PTc, PTn = D_["PT"][j], D_["PT"][j + 1]
                        for h in range(4):
                            k.mm(HB(5, 0)[:, h, :], PTc[:, h, :], Pj[cur][:, h, :], r=[nm("PT%d" % j), "Pj%d" % cur], w=[hk(5, 0)])
                        for h in range(4):
                            k.mm(HB(5, 1)[:, h, :], Pj[cur][:, h, :], PTc[:, h, :], r=[nm("PT%d" % j), "Pj%d" % cur], w=[hk(5, 1)])
                        k.copy("act", Pj[nxt][:], HB(5, 0), r=[hk(5, 0)], w=["Pj%d" % nxt])
                        k.copy("act", PTn[:], HB(5, 1), r=[hk(5, 1)], w=[nm("PT%d" % (j + 1))])
                    return sq
                for j in range(5):
                    steps.append(mk_sq(j))
                return steps

            def chain_steps(ci):
                q = ci % 2
                D_ = DB[q]
                c0 = ci * 64
                ts_ = slice(c0, c0 + 64)
                eGq, Atq, Rtq = D_["eG"], D_["At"], D_["Rt"]
                Btm, Ktm, Vtm, LakT, MrbT, MrkT = D_["Btm"], D_["Ktm"], D_["Vtm"], D_["LakT"], D_["MrbT"], D_["MrkT"]
                nm = lambda x: "%s_%d" % (x, q)
                steps = []

                def c1():
                    for h in range(4):
                        k.mm(HB(4, 0)[:, h, :], Atq[:, h, :], Tb[:, h, :], start=True, stop=False, r=[nm("At"), "Tb"], w=[hk(4, 0)])
                        k.mm(HB(4, 0)[:, h, :], LakT[:, h, :], Vtm[:, h, :], start=False, stop=True, r=[nm("LakT"), nm("Vtm")], w=[hk(4, 0)])
                    k.copy("dve", Ub[:], HB(4, 0), r=[hk(4, 0)], w=["Ub"])
                    k.copy("act", U[:], HB(4, 0), r=[hk(4, 0)], w=["U"])
                steps.append(c1)

                def mk_u(j):
                    def us():
                        PTc = D_["PT"][j]
                        for h in range(4):
                            k.mm(HB(4, 1)[:, h, :], PTc[:, h, :], Ub[:, h, :], r=[nm("PT%d" % j), "Ub"], w=[hk(4, 1)])
                        k.tt("dve", Ub[:], U[:], HB(4, 1), ALU.add, r=["U", hk(4, 1)], w=["Ub"])
                        if j < 5:
                            k.tt("dve", U[:], U[:], HB(4, 1), ALU.add, r=["U", hk(4, 1)], w=["U"])
                    return us
                for j in range(6):
                    steps.append(mk_u(j))

                def c8():
                    for h in range(4):
                        k.mm(HB(6, 0)[:, h, :], Tb[:, h, :], Rtq[:, h, :], start=True, stop=False, r=["Tb", nm("Rt")], w=[hk(6, 0)])
                        k.mm(HB(6, 0)[:, h, :], Ub[:, h, :], MrbT[:, h, :], start=False, stop=False, r=["Ub", nm("MrbT")], w=[hk(6, 0)])
                        k.mm(HB(6, 0)[:, h, :], Vtm[:, h, :], MrkT[:, h, :], start=False, stop=True, r=[nm("Vtm"), nm("MrkT")], w=[hk(6, 0)])
                    k.copy("act", Yblk[:, :, ts_], HB(6, 0), r=[hk(6, 0)], w=["Yblk"])
                    for h in range(4):
                        k.mm(HB(7, 0)[:, h, :], Btm[:, h, :], Ub[:, h, :], start=True, stop=False, r=[nm("Btm"), "Ub"], w=[hk(7, 0)])
                        k.mm(HB(7, 0)[:, h, :], Ktm[:, h, :], Vtm[:, h, :], start=False, stop=True, r=[nm("Ktm"), nm("Vtm")], w=[hk(7, 0)])
                    k.tt("dve", tS[:], HB(7, 0), Tst[:], ALU.add, r=[hk(7, 0), "Tst"], w=["tS"])
                    k.tt("dve", Tb[:], tS[:], eGq[:, :, 63:64].to_broadcast([64, 4, 64]), ALU.mult, r=["tS", nm("eG")], w=["Tb"])
                    k.tt("pool", Tst[:], tS[:], eGq[:, :, 63:64].to_broadcast([64, 4, 64]), ALU.mult, r=["tS", nm("eG")], w=["Tst"])
                steps.append(c8)
                return steps

            for st in prep_steps(0):
                st()
            for ci in range(8):
                cs = chain_steps(ci)
                ps = prep_steps(ci + 1) if ci < 7 else []
                n = max(len(cs), len(ps))
                for i_ in range(n):
                    if i_ < len(cs):
                        cs[i_]()
                    if i_ < len(ps):
                        ps[i_]()

            if int(os.environ.get("A_STOP", "9")) <= 3:
                continue
            for h in range(4):
                k.mm(banks[h][0:64, :], ones64[:], Yblk[:, h, :], r=["ones64", "Yblk"], w=fb(h))
                k.stt("dve", t1[:, h, :], banks[h][0:64, :], -1.0 / 64, Yblk[:, h, :], ALU.mult, ALU.add, r=fb(h) + ["Yblk"], w=["t1"])
            k.tt("dve", t2[:], t1[:], t1[:], ALU.mult, r=["t1"], w=["t2"])
            for h in range(4):
                k.mm(banks[4 + h][0:64, :], ones64[:], t2[:, h, :], r=["ones64", "t2"], w=fb(4 + h))
            for h in range(4):
                k.act(t2[:, h, :], banks[4 + h][0:64, :], AF.Ln, bias=64e-5, scale=1.0 / 64, r=fb(4 + h), w=["t2"])
            k.act(t2[:], t2[:], AF.Exp, scale=-0.5, r=["t2"], w=["t2"])
            k.tt("pool", t1[:], t1[:], t2[:], ALU.mult, r=["t1", "t2"], w=["t1"])
            k.tt("dve", t1[:], t1[:], bc(5), ALU.mult, r=["t1", "prm"], w=["t1"])
            k.tt("pool", t1[:], t1[:], bc(6), ALU.add, r=["t1", "prm"], w=["t1"])
            k.tt("dve", t1[:], t1[:], bonus[:], ALU.add, r=["t1", "bonus"], w=["t1"])
            k.tt("dve", yo[:], t1[:], gt[:], ALU.mult, r=["t1", "gt"], w=["yo"])
            if not G.y_in:
                k.dma(R.yTA[:, tsl].rearrange("(h v) t -> v h t", v=64), yo[:], r=["yo"], w=[("yA", g)])
        P.barrier()


def attn_common(G, s, Vsrc, mix):
    k, R, I = G.k, G.R, G.I
    sb = lambda name, shape, dt=F32: G.sb(s, name, shape, dt)
    A = Ctx()
    A.kT = [sb("kT%d" % i, [64, S], BF16) for i in range(2)]
    A.V = sb("V", [128, 32, 260], BF16)
    Vv = Vsrc.rearrange("(kt p) c -> p kt c", p=128)
    for q4 in range(4):
        k.dma(A.V[:, q4 * 8:(q4 + 1) * 8, :], Vv[:, q4 * 8:(q4 + 1) * 8, :], r=[(mix + "v", g) for g in range(NG)], w=["V"])
    A.Pm = [sb("Pm%d" % i, [128, 512], BF16) for i in range(3)]
    A.rec = [sb("rec%d" % i, [128, 8]) for i in range(2)]
    A.ob = [sb("ob%d" % i, [128, 4, 64], BF16) for i in range(2)]
    return A


def phase_D(G, l):
    nc, P, k, I, R = G.nc, G.P, G.k, G.I, G.R
    banks = G.banks
    with ExitStack() as s:
        sb = lambda name, shape, dt=F32: G.sb(s, name, shape, dt)
        A = attn_common(G, s, R.vD, "D")
        Fk = sb("Fk", [128, 4, 32])
        Frow = sb("Frow", [4, S])
        sel = sb("sel", [4, 4, 128])
        negm = sb("negm", [128, 128])
        qT = [sb("qT%d" % i, [64, 512], BF16) for i in range(2)]
        FqB = [sb("FqB%d" % i, [128, 512]) for i in range(2)]
        FqD = [sb("FqD%d" % i, [128, 512]) for i in range(2)]
        tb = [sb("tb%d" % i, [128, 512]) for i in range(3)]
        allF = [("Ffm", g) for g in range(NG)]
        fkn = tb[0][0:32, :].rearrange("p (h c) -> p h c", h=4)
        for h in range(4):
            k.dma(fkn[:, h, :], R.Ffm[h].rearrange("(kt p) -> kt p", p=128), r=allF, w=["tb0"])
        for h in range(4):
            k.mm(banks[5][:, h * 32:(h + 1) * 32], fkn[:, h, :], G.ident[0:32, 0:32], r=["tb0", "ident"], w=["bank5"])
        k.copy("dve", Fk[:].rearrange("p h c -> p (h c)"), banks[5][:, 0:128], r=["bank5"], w=["Fk"])
        k.dma(Frow[:], R.Ffm, r=allF, w=["Frow"])
        k.dma(sel[:], I["sel"], w=["sel"])
        k.dma(negm[:], I["negmask"], w=["negm"])
        allqk = [("Dqk", g) for g in range(NG)]
        for h in range(4):
            kt_ = A.kT[h % 2]
            kk = "kT%d" % (h % 2)
            k.dma(kt_[:], R.kTD[h * 64:(h + 1) * 64, :], r=allqk, w=[kk])
            for g in range(NG):
                i = k.rot("Dq", 2)
                k.dma(qT[i][:], R.qTD[h * 64:(h + 1) * 64, g * TG:(g + 1) * TG], r=allqk, w=["qT%d" % i])
                k.mm(banks[5][:], sel[:, h, :], Frow[:, g * TG:(g + 1) * TG], r=["sel", "Frow"], w=["bank5"])
                k.copy("act", FqB[i][:], banks[5][:], r=["bank5"], w=["FqB%d" % i])
                k.tt("pool", FqD[i][:].rearrange("p (a b) -> p a b", a=4), FqB[i][:].rearrange("p (a b) -> p a b", a=4),
                     negm[:].unsqueeze(1).to_broadcast([128, 4, 128]), ALU.add, r=["FqB%d" % i, "negm"], w=["FqD%d" % i])
                ob_ = 3 + k.rot("DO", 2)
                O = banks[ob_][:, 0:260].rearrange("p (a e) -> p a e", a=4)
                def d_stage1(kt):
                    m = kt - 4 * g
                    c0 = max(m, 0) * 128
                    N = 512 - c0
                    sbk = k.rot("Ds", 3)
                    k.mm(banks[sbk][:, 0:N], kt_[:, kt * 128:(kt + 1) * 128], qT[i][:, c0:512], r=[kk, "qT%d" % i], w=["bank%d" % sbk])
                    ti = k.rot("Dt", 3)
                    if m < 0:
                        k.stt("dve", tb[ti][:], banks[sbk][:], Fk[:, h, kt:kt + 1], FqB[i][:], ALU.subtract, ALU.add,
                              r=["bank%d" % sbk, "Fk", "FqB%d" % i], w=["tb%d" % ti])
                    else:
                        k.stt("dve", tb[ti][:, 0:128], banks[sbk][:, 0:128], Fk[:, h, kt:kt + 1], FqD[i][:, c0:c0 + 128],
                              ALU.subtract, ALU.add, r=["bank%d" % sbk, "Fk", "FqD%d" % i], w=["tb%d" % ti])
                        if N > 128:
                            k.stt("dve", tb[ti][:, 128:N], banks[sbk][:, 128:N], Fk[:, h, kt:kt + 1], FqB[i][:, c0 + 128:512],
                                  ALU.subtract, ALU.add, r=["bank%d" % sbk, "Fk", "FqB%d" % i], w=["tb%d" % ti])
                    k.act(A.Pm[ti][:, 0:N], tb[ti][:, 0:N], AF.Exp, r=["tb%d" % ti], w=["Pm%d" % ti])
                    return (kt, m, c0, ti)

                def d_stage2(st):
                    kt, m, c0, ti = st
                    for jq in range(max(m, 0), 4):
                        k.mm(O[:, jq, :], A.Pm[ti][:, jq * 128 - c0: jq * 128 - c0 + 128], A.V[:, kt, h * 65:(h + 1) * 65],
                             start=(kt == 0 and jq == 0), stop=(kt == 4 * g + jq), r=["Pm%d" % ti, "V"], w=["bank%d" % ob_], sgc=True)
                pend = []
                for kt in range(4 * g + 4):
                    pend.append(d_stage1(kt))
                    if len(pend) > 2:
                        d_stage2(pend.pop(0))
                while pend:
                    d_stage2(pend.pop(0))
                ri = k.rot("Drec", 2)
                k.recip(A.rec[ri][:, 0:4], O[:, :, 64], r=["bank%d" % ob_], w=["rec%d" % ri])
                k.tt("dve", A.ob[ri][:], O[:, :, 0:64], A.rec[ri][:, 0:4].unsqueeze(2).to_broadcast([128, 4, 64]), ALU.mult,
                     r=["bank%d" % ob_, "rec%d" % ri], w=["ob%d" % ri])
                k.dma(R.yscr[g * TG:(g + 1) * TG, 512 + h * 64: 512 + (h + 1) * 64].rearrange("(a p) e -> p a e", p=128),
                      A.ob[ri][:], r=["ob%d" % ri], w=[("yD", g)], slow=True)
        P.barrier()


def phase_B(G, l):
    nc, P, k, I, R = G.nc, G.P, G.k, G.I, G.R
    banks = G.banks
    lambda_init = 0.8 - 0.6 * math.exp(-0.3 * l)
    with ExitStack() as s:
        sb = lambda name, shape, dt=F32: G.sb(s, name, shape, dt)
        A = attn_common(G, s, R.vB, "B")
        cmf = sb("cmf", [128, 128])
        cm = sb("cm", [128, 128], BF16)
        qp = [[sb("qp%d_%d" % (i, mp), [64, 512], BF16) for mp in range(2)] for i in range(2)]
        lamv = sb("lamv", [128, 4, 32])
        lamw = sb("lamw", [128, 2, 32])
        lams = sb("lams", [128, 4])
        subg = sb("subg", [128, 64])
        o1 = [sb("o1_%d" % i, [128, 4, 64]) for i in range(2)]
        o2 = [sb("o2_%d" % i, [128, 4, 64]) for i in range(2)]
        k.dma(cmf[:], I["cmask"], w=["cmf"])
        k.copy("dve", cm[:], cmf[:], r=["cmf"], w=["cm"])
        for i in range(2):
            for mp in range(2):
                k.memset("pool", qp[i][mp][:], 0.0, w=["qp%d" % i])
        for n, nm in enumerate(("df_lam_q1", "df_lam_k1", "df_lam_q2", "df_lam_k2")):
            bcast_load(G, lamv[:, n, :], I[nm][l:l + 1, :], 32, ["lamv"])
        bcast_load(G, subg[:], I["df_sub_g"][l:l + 1, :], 64, ["subg"])
        k.ts("dve", subg[:], subg[:], 1.0 - lambda_init, ALU.mult, r=["subg"], w=["subg"])
        k.tt("dve", lamw[:, 0, :], lamv[:, 0, :], lamv[:, 1, :], ALU.mult, r=["lamv"], w=["lamw"])
        k.tt("dve", lamw[:, 1, :], lamv[:, 2, :], lamv[:, 3, :], ALU.mult, r=["lamv"], w=["lamw"])
        k.red("dve", lams[:, 0:2], lamw[:], r=["lamw"], w=["lams"])
        k.act(lams[:, 0:2], lams[:, 0:2], AF.Exp, r=["lams"], w=["lams"])
        k.ts("dve", lams[:, 2:3], lams[:, 1:2], -lambda_init, ALU.add, r=["lams"], w=["lams"])
        k.tt("dve", lams[:, 3:4], lams[:, 2:3], lams[:, 0:1], ALU.subtract, r=["lams"], w=["lams"])
        allqk = [("Bqk", g) for g in range(NG)]
        jobs = []
        if G.prep_in_B:
            pst = [sb("pf_st%d" % i_, [128, 8, 512]) for i_ in range(2)]
            pbf = [sb("pf_bf%d" % i_, [128, 8, 512], BF16) for i_ in range(2)]
            up = I["ffn_up"][l].rearrange("(kc p) n -> p kc n", p=128)
            dn = I["ffn_down"][l].rearrange("(fc p) d -> p fc d", p=128)
            for u in range(11):
                jobs.append((up[:, :, u * 512:(u + 1) * 512], R.upbf[l, u].rearrange("p (kc n) -> p kc n", kc=8), 8, 512))
            for u in range(11):
                jobs.append((dn[:, 2 * u:2 * u + 2, :], R.dnbf[l][:, 2 * u * 1024:(2 * u + 2) * 1024].rearrange("p (a d) -> p a d", a=2), 2, 1024))

        def pf_load(i_):
            src, dst, a_, b_ = jobs[i_]
            k.dma(pst[i_ % 2][:].rearrange("p a b -> p (a b)")[:, 0:a_ * b_].rearrange("p (a b) -> p a b", a=a_), src,
                  w=["pf_st%d" % (i_ % 2)], q="pool")

        def pf_job(i_):
            src, dst, a_, b_ = jobs[i_]
            sv = pst[i_ % 2][:].rearrange("p a b -> p (a b)")[:, 0:a_ * b_]
            bv = pbf[i_ % 2][:].rearrange("p a b -> p (a b)")[:, 0:a_ * b_]
            k.copy("dve" if i_ % 3 else "pool", bv, sv, r=["pf_st%d" % (i_ % 2)], w=["pf_bf%d" % (i_ % 2)])
            k.dma(dst, bv.rearrange("p (a b) -> p a b", a=a_), r=["pf_bf%d" % (i_ % 2)], w=[("ffnw", l)], q="pool")
            if i_ + 2 < len(jobs):
                pf_load(i_ + 2)
        if jobs:
            pf_load(0)
            pf_load(1)
        jn = [0]
        for h in range(4):
            kt_ = A.kT[h % 2]
            kk = "kT%d" % (h % 2)
            k.dma(kt_[:], R.kTB[h * 64:(h + 1) * 64, :], r=allqk, w=[kk])
            for g in range(NG):
                if jobs and jn[0] < len(jobs) and (h * NG + g) >= 2:
                    pf_job(jn[0])
                    jn[0] += 1
                i = k.rot("Bq", 2)
                k.dma(qp[i][0][0:32, :], R.qTB[h * 64:h * 64 + 32, g * TG:(g + 1) * TG], r=allqk, w=["qp%d" % i])
                k.dma(qp[i][1][32:64, :], R.qTB[h * 64 + 32:h * 64 + 64, g * TG:(g + 1) * TG], r=allqk, w=["qp%d" % i])
                oi = k.rot("BO", 2)
                Os = [banks[3 + oi][:, 0:260].rearrange("p (a e) -> p a e", a=4),
                      banks[5 + oi][:, 0:260].rearrange("p (a e) -> p a e", a=4)]
                obk = ["bank%d" % (3 + oi), "bank%d" % (5 + oi)]
                def b_stage1(kt, mp):
                    m = kt - 4 * g
                    c0 = max(m, 0) * 128
                    N = 512 - c0
                    sbk = k.rot("Bs", 3)
                    k.mm(banks[sbk][:, 0:N], kt_[:, kt * 128:(kt + 1) * 128], qp[i][mp][:, c0:512], r=[kk, "qp%d" % i],
                         w=["bank%d" % sbk])
                    ti = k.rot("Bt", 3)
                    k.act(A.Pm[ti][:, 0:N], banks[sbk][:, 0:N], AF.Exp, r=["bank%d" % sbk], w=["Pm%d" % ti])
                    if m >= 0:
                        k.tt("dve", A.Pm[ti][:, 0:128], A.Pm[ti][:, 0:128], cm[:], ALU.mult, r=["Pm%d" % ti, "cm"], w=["Pm%d" % ti])
                    return (kt, mp, m, c0, ti)

                def b_stage2(st):
                    kt, mp, m, c0, ti = st
                    for jq in range(max(m, 0), 4):
                        k.mm(Os[mp][:, jq, :], A.Pm[ti][:, jq * 128 - c0: jq * 128 - c0 + 128], A.V[:, kt, h * 65:(h + 1) * 65],
                             start=(kt == 0 and jq == 0), stop=(kt == 4 * g + jq), r=["Pm%d" % ti, "V"], w=[obk[mp]], sgc=True)
                pend = []
                for kt in range(4 * g + 4):
                    for mp in range(2):
                        pend.append(b_stage1(kt, mp))
                        if len(pend) > 2:
                            b_stage2(pend.pop(0))
                while pend:
                    b_stage2(pend.pop(0))
                ri = k.rot("Brec", 2)
                rc = A.rec[ri]
                rk = "rec%d" % ri
                k.recip(rc[:, 0:4], Os[0][:, :, 64], r=[obk[0]], w=[rk])
                k.recip(rc[:, 4:8], Os[1][:, :, 64], r=[obk[1]], w=[rk])
                k.ts("dve", rc[:, 4:8], rc[:, 4:8], lams[:, 3:4], ALU.mult, r=[rk, "lams"], w=[rk])
                k.tt("dve", o1[ri][:], Os[0][:, :, 0:64], rc[:, 0:4].unsqueeze(2).to_broadcast([128, 4, 64]), ALU.mult,
                     r=[obk[0], rk], w=["o1_%d" % ri])
                k.tt("dve", o2[ri][:], Os[1][:, :, 0:64], rc[:, 4:8].unsqueeze(2).to_broadcast([128, 4, 64]), ALU.mult,
                     r=[obk[1], rk], w=["o2_%d" % ri])
                k.tt("pool", o1[ri][:], o1[ri][:], o2[ri][:], ALU.add, r=["o1_%d" % ri, "o2_%d" % ri], w=["o1_%d" % ri])
                k.tt("pool", o2[ri][:], o1[ri][:], o1[ri][:], ALU.mult, r=["o1_%d" % ri], w=["o2_%d" % ri])
                k.red("dve", rc[:, 0:4], o2[ri][:], r=["o2_%d" % ri], w=[rk])
                k.act(rc[:, 0:4], rc[:, 0:4], AF.Sqrt, bias=EPS, scale=1.0 / 64, r=[rk], w=[rk])
                k.recip(rc[:, 0:4], rc[:, 0:4], r=[rk], w=[rk])
                k.tt("dve", o1[ri][:], o1[ri][:], rc[:, 0:4].unsqueeze(2).to_broadcast([128, 4, 64]), ALU.mult,
                     r=["o1_%d" % ri, rk], w=["o1_%d" % ri])
                k.tt("pool", A.ob[ri][:], o1[ri][:], subg[:].unsqueeze(1).to_broadcast([128, 4, 64]), ALU.mult,
                     r=["o1_%d" % ri, "subg"], w=["ob%d" % ri])
                k.dma(R.yscr[g * TG:(g + 1) * TG, h * 64:(h + 1) * 64].rearrange("(a p) e -> p a e", p=128),
                      A.ob[ri][:], r=["ob%d" % ri], w=[("yB", g)], slow=True)
        while jobs and jn[0] < len(jobs):
            pf_job(jn[0])
            jn[0] += 1
        P.barrier()


_CACHE = {}


def kernel(**inputs):
    if "prog" not in _CACHE:
        _CACHE["prog"] = build()
    nc, P = _CACHE["prog"]
    consts = make_consts()
    weights = {k: np.ascontiguousarray(np.asarray(inputs[k], dtype=np.float32)) for k in WEIGHT_SHAPES}
    x = np.asarray(inputs["x"], dtype=np.float32)
    c = np.asarray(inputs["c"], dtype=np.float32)
    in_maps = []
    for b in range(8):
        m = {"x": np.ascontiguousarray(x[b]), "c": np.ascontiguousarray(c[b:b + 1])}
        m.update(weights)
        m.update(consts)
        in_maps.append(m)
    res = run_bass_kernel_spmd(nc, in_maps, core_ids=list(range(8)))
    return np.stack([np.asarray(r["out"], dtype=np.float32) for r in res.results], axis=0)
```

```python
import math
import os
import numpy as np
import concourse.bass as bass
import concourse.mybir as mybir
from concourse.bass_utils import run_bass_kernel_spmd
from contextlib import ExitStack

F32 = mybir.dt.float32
BF16 = mybir.dt.bfloat16
AF = mybir.ActivationFunctionType
ALU = mybir.AluOpType
AX = mybir.AxisListType

S = 4096
D = 1024
L = 4
NIN = 2948
DFF = 2816
NG = 8
TG = 512
EPS = 1e-6
ALPHA = math.exp(-0.5)

ENGS = ("pe", "act", "dve", "pool", "sp")
EPOCH = 30000


class Prog:
    def __init__(self, nc, es):
        self.nc = nc
        self.es = es
        self.q = {e: [] for e in ENGS}
        self.cnt = {e: 0 for e in ENGS}
        self.epoch = {e: 0 for e in ENGS}
        self.sems = {}
        self.seen = {e: {} for e in ENGS}
        self.res_w = {}
        self.res_r = {}
        self.dma_val = {}
        self.n_inst = 0
        self.rr = 0

    def _sem(self, key):
        if key not in self.sems:
            self.sems[key] = self.es.enter_context(self.nc.semaphore("s_" + "_".join(str(k) for k in key)))
        return self.sems[key]

    def _deps(self, eng, reads, writes, extra=()):
        need = {}

        def add(ev):
            if ev is None:
                return
            k, v = ev
            if eng == "pe" and k[0] == "pe":
                return
            if need.get(k, 0) < v:
                need[k] = v
        for r in reads:
            add(self.res_w.get(r))
        for w in writes:
            add(self.res_w.get(w))
            for ev in self.res_r.get(w, ()):
                add(ev)
        for ev in extra:
            add(ev)
        waits = []
        for k, v in need.items():
            if self.seen[eng].get(k, 0) >= v:
                continue
            self.seen[eng][k] = v
            waits.append((k, v))
        return waits

    def _commit(self, ev, reads, writes):
        for r in reads:
            lst = self.res_r.setdefault(r, [])
            lst.append(ev)
            if len(lst) > 64:
                mx = {}
                for k, v in lst:
                    if mx.get(k, 0) < v:
                        mx[k] = v
                self.res_r[r] = list(mx.items())
        for w in writes:
            self.res_w[w] = ev
            self.res_r[w] = []

    @staticmethod
    def _is_psum(r):
        return (isinstance(r, str) and r.startswith("bank")) or (isinstance(r, tuple) and r[0] == "hb")

    def op(self, eng, fn, reads=(), writes=()):
        pr = [r for r in reads if self._is_psum(r)]
        if pr:
            writes = list(writes) + pr
        waits = self._deps(eng, reads, writes)
        if self.cnt[eng] >= EPOCH:
            self.epoch[eng] += 1
            self.cnt[eng] = 0
        self.cnt[eng] += 1
        key = (eng, self.epoch[eng])
        ev = (key, self.cnt[eng])
        self.q[eng].append((waits, fn, key, 1))
        self._commit(ev, reads, writes)
        self.n_inst += 1
        return ev

    def dma(self, queue, pairs, reads=(), writes=(), sem=None):
        if sem is None:
            sem = ("dma", "rr%d" % (self.rr % 20))
            self.rr += 1
        key = sem
        prev = self.dma_val.get(key, 0)
        extra = [(key, prev)] if prev > 0 else []
        waits = self._deps(queue, reads, writes, extra)
        val = prev
        for i, pr in enumerate(pairs):
            out_ap, in_ap = pr[0], pr[1]
            kw = pr[2] if len(pr) > 2 else {}
            val += 16

            def fn(e, out_ap=out_ap, in_ap=in_ap, kw=kw):
                return e.dma_start(out=out_ap, in_=in_ap, **kw)
            self.q[queue].append((waits if i == 0 else [], fn, key, 16))
            self.n_inst += 1
        self.dma_val[key] = val
        ev = (key, val)
        self._commit(ev, reads, writes)
        return ev

    def barrier(self):
        evs = []
        for e in ENGS:
            for ep in range(self.epoch[e] + 1):
                k = (e, ep)
                v = self.cnt[e] if ep == self.epoch[e] else EPOCH
                if v > 0:
                    evs.append((k, v))
        for k, v in self.dma_val.items():
            evs.append((k, v))
        for e in ENGS:
            waits = []
            for k, v in evs:
                if self.seen[e].get(k, 0) < v:
                    self.seen[e][k] = v
                    waits.append((k, v))
            if waits:
                self.q[e].append((waits, None, None, 0))
        self.res_w = {}
        self.res_r = {}

    def emit(self):
        nc = self.nc
        for e in ENGS:
            for (waits, fn, key, inc) in self.q[e]:
                for k, v in waits:
                    self._sem(k)
                if key is not None:
                    self._sem(key)
        block = self.es.enter_context(nc.Block())
        engmap = {"pe": block.tensor, "act": block.scalar, "dve": block.vector, "pool": block.gpsimd,
                  "sp": block.sync}
        for e in ENGS:
            items = self.q[e]

            def body(eng, items=items):
                for (waits, fn, key, inc) in items:
                    for k, v in waits:
                        eng.wait_ge(self.sems[k], v)
                    if fn is not None:
                        ins = fn(eng)
                        ins.then_inc(self.sems[key], inc)
            engmap[e](body)


class K:
    def __init__(self, P):
        self.P = P
        self._rot = {}

    def rot(self, name, n):
        i = self._rot.get(name, 0)
        self._rot[name] = i + 1
        return i % n

    def mm(self, out, lhsT, rhs, start=True, stop=True, r=(), w=(), sgc=False):
        if sgc:
            return self.P.op("pe", lambda e: e.matmul(out, lhsT=lhsT, rhs=rhs, start=start, stop=stop, skip_group_check=True), r, w)
        return self.P.op("pe", lambda e: e.matmul(out, lhsT=lhsT, rhs=rhs, start=start, stop=stop), r, w)

    def tr(self, out, in_, ident, r=(), w=()):
        return self.P.op("pe", lambda e: e.transpose(out=out, in_=in_, identity=ident), r, w)

    def act(self, out, in_, func, bias=None, scale=None, accum_out=None, r=(), w=(), eng="act"):
        kw = {}
        if bias is not None:
            kw["bias"] = bias
        if scale is not None:
            kw["scale"] = scale
        if accum_out is not None:
            kw["accum_out"] = accum_out
        return self.P.op("act", lambda e: e.activation(out=out, in_=in_, func=func, **kw), r, w)

    def copy(self, eng, out, in_, r=(), w=()):
        if eng == "act":
            return self.P.op("act", lambda e: e.copy(out=out, in_=in_), r, w)
        return self.P.op(eng, lambda e: e.tensor_copy(out=out, in_=in_), r, w)

    def tt(self, eng, out, in0, in1, op, r=(), w=()):
        return self.P.op(eng, lambda e: e.tensor_tensor(out=out, in0=in0, in1=in1, op=op), r, w)

    def ts(self, eng, out, in0, s1, op0, s2=None, op1=None, r=(), w=()):
        if op1 is None:
            return self.P.op(eng, lambda e: e.tensor_scalar(out=out, in0=in0, scalar1=s1, scalar2=None, op0=op0), r, w)
        return self.P.op(eng, lambda e: e.tensor_scalar(out=out, in0=in0, scalar1=s1, scalar2=s2, op0=op0, op1=op1), r, w)

    def stt(self, eng, out, in0, scalar, in1, op0, op1, r=(), w=()):
        return self.P.op(eng, lambda e: e.scalar_tensor_tensor(out=out, in0=in0, scalar=scalar, in1=in1, op0=op0, op1=op1), r, w)

    def red(self, eng, out, in_, op=ALU.add, r=(), w=()):
        return self.P.op(eng, lambda e: e.tensor_reduce(out=out, in_=in_, axis=AX.X, op=op), r, w)

    def recip(self, out, in_, r=(), w=()):
        return self.P.op("dve", lambda e: e.reciprocal(out=out, in_=in_), r, w)

    def memset(self, eng, ap, val, w=()):
        return self.P.op(eng, lambda e: e.memset(ap, val), (), w)

    def scan(self, out, d0, d1, initial, op0, op1, r=(), w=()):
        return self.P.op("dve", lambda e: e.tensor_tensor_scan(out=out, data0=d0, data1=d1, initial=initial, op0=op0, op1=op1), r, w)

    def dma(self, out, in_, r=(), w=(), q="sp", slow=False, sem=None):
        kw = {"allow_slow_non_contiguous": True} if slow else {}
        return self.P.dma(q, [(out, in_, kw)], r, w, sem=sem)


def make_consts():
    c = {}
    c["ident"] = np.eye(128, dtype=np.float32)
    bo64 = np.zeros((128, 128), np.float32)
    bo64[:64, :64] = 1
    bo64[64:, 64:] = 1
    c["bo64"] = bo64
    bo32 = np.zeros((128, 128), np.float32)
    for i in range(4):
        bo32[i * 32:(i + 1) * 32, i * 32:(i + 1) * 32] = 1
    c["bo32"] = bo32
    prot = np.zeros((128, 128), np.float32)
    for b in range(4):
        for d in range(16):
            prot[b * 32 + d + 16, b * 32 + d] = -1.0
            prot[b * 32 + d, b * 32 + d + 16] = 1.0
    c["prot"] = prot
    inv = 1.0 / (10000.0 ** (np.arange(0, 32, 2, dtype=np.float32) / 32.0))
    ang = np.arange(S, dtype=np.float32)[:, None] * inv[None, :]
    cos = np.cos(ang).astype(np.float32).T
    sin = np.sin(ang).astype(np.float32).T
    c["cosT"] = np.ascontiguousarray(np.tile(cos, (8, 1)))
    c["sinT"] = np.ascontiguousarray(np.tile(sin, (8, 1)))
    k = np.arange(128)[:, None]
    q = np.arange(128)[None, :]
    c["negmask"] = np.where(k > q, -1e30, 0.0).astype(np.float32)
    c["cmask"] = ((k // 64) <= (q // 64)).astype(np.float32)
    c["triu"] = (k <= q).astype(np.float32)
    k6 = np.arange(64)[:, None]
    q6 = np.arange(64)[None, :]
    m64 = np.zeros((64, 3, 64), np.float32)
    m64[:, 0, :] = (k6 < q6)
    m64[:, 1, :] = (k6 <= q6)
    m64[:, 2, :] = (k6 > q6)
    c["m64"] = m64
    sel = np.zeros((4, 4, 128), np.float32)
    for h in range(4):
        sel[h, h, :] = 1.0
    c["sel"] = sel.transpose(1, 0, 2).copy()
    return c


CONST_SHAPES = {"ident": [128, 128], "bo64": [128, 128], "bo32": [128, 128], "prot": [128, 128],
                "cosT": [128, S], "sinT": [128, S], "negmask": [128, 128], "cmask": [128, 128],
                "triu": [128, 128], "m64": [64, 3, 64], "sel": [4, 4, 128]}

WEIGHT_SHAPES = {
    'ada_w': [L, D, 6 * D], 'ada_b': [L, 6 * D], 'norm1_g': [L, D], 'norm2_g': [L, D],
    'w_in': [L, D, NIN], 'w_out': [L, D, D],
    'rw_mu': [L, 896], 'rw_w0': [L, 256], 'rw_w_up': [L, 32, 256], 'rw_a0': [L, 256], 'rw_a_up': [L, 32, 256],
    'rw_g_up': [L, 64, 256], 'rw_k_k': [L, 256], 'rw_k_a': [L, 256], 'rw_r_k': [L, 4, 64],
    'rw_ln_g': [L, 256], 'rw_ln_b': [L, 256],
    'df_lam_q1': [L, 32], 'df_lam_k1': [L, 32], 'df_lam_q2': [L, 32], 'df_lam_k2': [L, 32],
    'df_q_g': [L, 32], 'df_k_g': [L, 32], 'df_sub_g': [L, 64],
    'sg_w': [L, 4, 128, 128], 'sg_b': [L, 4, 128], 'sg_ln_g': [L, 256], 'sg_ln_b': [L, 256],
    'fx_q_g': [L, 64], 'fx_k_g': [L, 64], 'fx_f_b': [L, 4],
    'ffn_up': [L, D, 2 * DFF], 'ffn_conv': [L, 3, 2 * DFF], 'ffn_conv_b': [L, 2 * DFF], 'ffn_down': [L, DFF, D],
}


class Ctx:
    pass


def build(layers=(0, 1, 2, 3), phases=("P1", "A", "B", "D", "P3"), debug=False, y_in=False):
    nc = bass.Bass("TRN2", target_bir_lowering=False)
    dkind = "ExternalOutput" if debug else "Internal"
    I = {}

    def din(name, shape):
        I[name] = nc.dram_tensor(name, list(shape), F32, kind="ExternalInput").ap()
    din("x", [S, D])
    din("c", [1, D])
    for k, shp in WEIGHT_SHAPES.items():
        din(k, shp)
    for k, shp in CONST_SHAPES.items():
        din(k, shp)
    out = nc.dram_tensor("out", [S, D], F32, kind="ExternalOutput").ap()

    def dscr(name, shape, dt, kind=None):
        return nc.dram_tensor(name, list(shape), dt, kind=kind or dkind).ap()
    R = Ctx()
    R.xres = dscr("xres", [S, D], F32)
    R.pmA = dscr("pmA", [896, S], F32)
    R.qTB = dscr("qTB", [256, S], BF16)
    R.kTB = dscr("kTB", [256, S], BF16)
    R.vB = dscr("vB", [S, 260], BF16)
    R.qTD = dscr("qTD", [256, S], BF16)
    R.kTD = dscr("kTD", [256, S], BF16)
    R.vD = dscr("vD", [S, 260], BF16)
    R.Ffm = dscr("Ffm", [4, S], F32)
    if y_in:
        R.yscr = nc.dram_tensor("yscr_in", [S, 768], F32, kind="ExternalInput").ap()
        R.yTA = nc.dram_tensor("yTA_in", [256, S], F32, kind="ExternalInput").ap()
    else:
        R.yscr = dscr("yscr", [S, 768], BF16)
        R.yTA = dscr("yTA", [256, S], BF16)
    R.upbf = dscr("upbf", [L, 11, 128, 8 * 512], BF16, kind="Internal")
    R.dnbf = dscr("dnbf", [L, 128, 22 * 1024], BF16, kind="Internal")

    with ExitStack() as es:
        P = Prog(nc, es)
        k = K(P)
        G = Ctx()
        G.nc, G.P, G.k, G.I, G.R, G.out = nc, P, k, I, R, out
        G.y_in = y_in
        G.dbg_x1 = nc.dram_tensor("dbg_x1", [S, D], F32, kind="ExternalOutput").ap() if debug else None

        uid = [0]

        def sb(stack, name, shape, dt=F32):
            uid[0] += 1
            return stack.enter_context(nc.sbuf_tensor("%s_u%d" % (name, uid[0]), list(shape), dt))
        G.sb = sb
        G.banks = [es.enter_context(nc.psum_tensor("bank%d" % i, [128, 512], F32)) for i in range(8)]
        G.ident = sb(es, "ident", [128, 128])
        G.identb = sb(es, "identb", [128, 128], BF16)
        G.ones_row = sb(es, "ones_row", [1, 128])
        G.condB = sb(es, "condB", [128, 8, 128])
        G.modB = sb(es, "modB", [128, 6 * D])
        k.dma(G.ident[:], I["ident"], w=["ident"])
        k.copy("dve", G.identb[:], G.ident[:], r=["ident"], w=["identb"])
        k.memset("dve", G.ones_row[:], 1.0, w=["ones_row"])
        with ExitStack() as s0:
            cT = sb(s0, "cT", [128, 8])
            cS = sb(s0, "cS", [128, 8])
            k.dma(cT[:], I["c"].rearrange("o (kc p) -> p (o kc)", p=128), w=["cT"], slow=True)
            k.act(cS[:], cT[:], AF.Silu, r=["cT"], w=["cS"])
            k.copy("dve", G.condB[:], cS[:].unsqueeze(2).to_broadcast([128, 8, 128]), r=["cS"], w=["condB"])
            P.barrier()

        for li, l in enumerate(layers):
            xsrc = I["x"] if li == 0 else R.xres
            xdst = out if li == len(layers) - 1 else R.xres
            G.prep_in_B = ("P3" in phases) and ("B" in phases)
            if "P3" in phases and not G.prep_in_B:
                prep_ffn(G, l)
            layer_setup(G, l)
            if "P1" in phases:
                phase_p1(G, l, xsrc)
            if "A" in phases:
                phase_A(G, l)
            if "B" in phases:
                phase_B(G, l)
            if "D" in phases:
                phase_D(G, l)
            if "P3" in phases:
                phase_p3(G, l, xsrc, xdst)
        P.barrier()
        P.emit()
    return nc, P


def prep_ffn(G, l):
    nc, P, k, I, R = G.nc, G.P, G.k, G.I, G.R
    with ExitStack() as s:
        st = [G.sb(s, "pf_st%d" % i, [128, 8, 512]) for i in range(2)]
        sbf = [G.sb(s, "pf_bf%d" % i, [128, 8, 512], BF16) for i in range(2)]
        up = I["ffn_up"][l].rearrange("(kc p) n -> p kc n", p=128)
        dn = I["ffn_down"][l].rearrange("(fc p) d -> p fc d", p=128)
        engs = ["dve", "act", "dve", "act", "pool"]
        jobs = []
        for u in range(11):
            jobs.append((up[:, :, u * 512:(u + 1) * 512], R.upbf[l, u].rearrange("p (kc n) -> p kc n", kc=8), 8, 512))
        for u in range(11):
            jobs.append((dn[:, 2 * u:2 * u + 2, :], R.dnbf[l][:, 2 * u * 1024:(2 * u + 2) * 1024].rearrange("p (a d) -> p a d", a=2), 2, 1024))

        def load(i):
            src, dst, a, b = jobs[i]
            k.dma(st[i % 2][:].rearrange("p a b -> p (a b)")[:, 0:a * b].rearrange("p (a b) -> p a b", a=a), src,
                  w=["pf_st%d" % (i % 2)])
        load(0)
        load(1)
        for i in range(len(jobs)):
            src, dst, a, b = jobs[i]
            sv = st[i % 2][:].rearrange("p a b -> p (a b)")[:, 0:a * b]
            bv = sbf[i % 2][:].rearrange("p a b -> p (a b)")[:, 0:a * b]
            k.copy(engs[i % 5], bv, sv, r=["pf_st%d" % (i % 2)], w=["pf_bf%d" % (i % 2)])
            k.dma(dst, bv.rearrange("p (a b) -> p a b", a=a), r=["pf_bf%d" % (i % 2)], w=[("ffnw", l)])
            if i + 2 < len(jobs):
                load(i + 2)
        P.barrier()


def layer_setup(G, l):
    nc, P, k, I, R = G.nc, G.P, G.k, G.I, G.R
    banks = G.banks
    with ExitStack() as s:
        aw = [G.sb(s, "ls_aw%d" % i, [128, 8, 512]) for i in range(2)]
        rows = G.sb(s, "ls_rows", [1, 8 * D])
        k.dma(rows[:, 0:6 * D], I["ada_b"][l:l + 1, :], w=["ls_rows_b"])
        k.dma(rows[:, 6 * D:7 * D], I["norm1_g"][l:l + 1, :], w=["ls_rows_g"])
        k.dma(rows[:, 7 * D:8 * D], I["norm2_g"][l:l + 1, :], w=["ls_rows_g"])
        awv = I["ada_w"][l].rearrange("(kc p) n -> p kc n", p=128)
        for cc in range(12):
            b = cc % 2
            k.dma(aw[b][:], awv[:, :, cc * 512:(cc + 1) * 512], w=["ls_aw%d" % b])
            bk = banks[b]
            for kc in range(8):
                k.mm(bk[:], G.condB[:, kc, :], aw[b][:, kc, :], start=(kc == 0), stop=False,
                     r=["condB", "ls_aw%d" % b], w=["bank%d" % b])
            k.mm(bk[:], G.ones_row[0:1, :], rows[0:1, cc * 512:(cc + 1) * 512], start=False, stop=True,
                 r=["ones_row", "ls_rows_b"], w=["bank%d" % b])
            k.copy("act" if cc % 2 else "dve", G.modB[:, cc * 512:(cc + 1) * 512], bk[:], r=["bank%d" % b], w=["modB"])
        for gi, (goff, slot) in enumerate(((6 * D, 1 * D), (7 * D, 4 * D))):
            for hf in range(2):
                b = 2 + hf
                k.mm(banks[b][:], G.ones_row[0:1, :], rows[0:1, goff + hf * 512: goff + (hf + 1) * 512],
                     r=["ones_row", "ls_rows_g"], w=["bank%d" % b])
                sl = G.modB[:, slot + hf * 512: slot + (hf + 1) * 512]
                k.stt("dve", sl, sl, 1.0, banks[b][:], ALU.add, ALU.mult, r=["modB", "bank%d" % b], w=["modB"])
        P.barrier()


def norm_mod_T(G, xt, goff, shoff, T):
    k = G.k
    banks = G.banks
    for j in range(4):
        k.act(T.tmp[:], xt[:, j, :], AF.Square, r=["xt"], w=["tmp"])
        k.red("dve", T.ss[:, j:j + 1], T.tmp[:], r=["tmp"], w=["ss"])
    k.act(T.rstd[:], T.ss[:], AF.Sqrt, bias=EPS, scale=1.0 / D, r=["ss"], w=["rstd"])
    k.recip(T.rstd[:], T.rstd[:], r=["rstd"], w=["rstd"])
    for j in range(4):
        k.stt("dve", T.tmp[:], xt[:, j, :], T.rstd[:, j:j + 1], G.modB[:, goff:goff + D], ALU.mult, ALU.mult,
              r=["xt", "rstd", "modB"], w=["tmp"])
        k.tt("dve", T.hb[:, j, :], T.tmp[:], G.modB[:, shoff:shoff + D], ALU.add, r=["tmp", "modB"], w=["hb"])
    for j in range(4):
        b = 5 + (j % 2)
        pv = banks[b][:].bitcast(BF16).rearrange("p (a t) -> p a t", a=8)
        for kc in range(8):
            k.tr(pv[:, kc, :], T.hb[:, j, kc * 128:(kc + 1) * 128], G.identb[:], r=["hb", "identb"], w=["bank%d" % b])
        k.copy("act" if j % 2 else "dve", T.hT[:, :, j * 128:(j + 1) * 128], pv, r=["bank%d" % b], w=[getattr(T, "hTk", "hT")])


def bcast_load(G, tile_ap, row_ap, n, w):
    G.k.dma(tile_ap, row_ap.to_broadcast([128, n]), w=w, slow=True)


def phase_p1(G, l, xsrc):
    nc, P, k, I, R = G.nc, G.P, G.k, G.I, G.R
    banks = G.banks
    with ExitStack() as s:
        sb = lambda name, shape, dt=F32: G.sb(s, name, shape, dt)
        T = Ctx()
        w_in = sb("w_in", [128, 8, NIN], BF16)
        stg = [sb("p1_stg%d" % i, [128, 1474]) for i in range(2)]
        xt = sb("xt", [128, 4, D])
        T.sqj = sb("sqj", [128, D], BF16)
        T.ss = sb("ss", [128, 4])
        T.rstd = sb("rstd", [128, 4])
        T.tmp = sb("tmp", [128, D])
        T.hb = sb("hb", [128, 4, D], BF16)
        hTs = [sb("hT%d" % i, [128, 8, 512], BF16) for i in range(2)]
        fA = [sb("fA%d" % i, [128, 512]) for i in range(3)]
        fB = [sb("fB%d" % i, [128, 512]) for i in range(3)]
        fC = [sb("fC%d" % i, [128, 512]) for i in range(3)]
        obf = [sb("obf%d" % i, [128, 512], BF16) for i in range(3)]
        xnb = [sb("xnb%d" % i, [128, 512], BF16) for i in range(2)]
        paA = sb("paA", [128, 7, 513])
        pmo = [sb("pmo%d" % i, [128, 512]) for i in range(2)]
        cs = [sb("cos%d" % i, [128, 512]) for i in range(2)]
        sn = [sb("sin%d" % i, [128, 512]) for i in range(2)]
        bo64 = sb("bo64", [128, 128])
        bo32 = sb("bo32", [128, 128])
        protf = sb("protf", [128, 128])
        protb = sb("protb", [128, 128], BF16)
        gcol = sb("gcol", [128, 8])
        mu = sb("mu", [128, 7])
        negfb = sb("negfb", [4, 1])
        ones4 = sb("ones4", [4, 512])
        Fg = [sb("Fg%d" % i, [4, 512]) for i in range(2)]
        f4 = sb("f4", [4, 512])
        vt = [sb("vt%d" % i, [128, 4, 65], BF16) for i in range(4)]
        sgw = sb("sgw", [128, 4, 128])
        WgT = sb("WgT", [128, 4, 128], BF16)
        triu = sb("triu", [128, 128])
        sgbT = sb("sgbT", [128, 4])
        lnCg = sb("lnCg", [128, 256])
        lnCb = sb("lnCb", [128, 256])
        glC = [sb("glC%d" % i, [128, 512]) for i in range(2)]
        stC = [sb("stC%d" % i, [128, 8]) for i in range(2)]
        tmpc = [sb("tmpc%d" % i, [128, 256]) for i in range(2)]
        vnb = [sb("vnb%d" % i, [128, 256], BF16) for i in range(2)]
        ycb = [sb("ycb%d" % i, [128, 256], BF16) for i in range(2)]

        for kc in range(8):
            for hf in range(2):
                i = (kc * 2 + hf) % 2
                k.dma(stg[i][:], I["w_in"][l, kc * 128:(kc + 1) * 128, hf * 1474:(hf + 1) * 1474], w=["p1_stg%d" % i])
                k.copy(("dve", "pool", "act")[(kc * 2 + hf) % 3], w_in[:, kc, hf * 1474:(hf + 1) * 1474], stg[i][:],
                       r=["p1_stg%d" % i], w=["w_in"])
        k.dma(bo64[:], I["bo64"], w=["bo64"])
        k.dma(bo32[:], I["bo32"], w=["bo32"])
        k.dma(protf[:], I["prot"], w=["protf"])
        k.copy("dve", protb[:], protf[:], r=["protf"], w=["protb"])
        k.dma(triu[:], I["triu"], w=["triu"])
        for rep in range(2):
            k.dma(gcol[rep * 64:(rep + 1) * 64, 0:1], I["fx_q_g"][l].rearrange("(d o) -> d o", o=1), w=["gcol"], slow=True)
            k.dma(gcol[rep * 64:(rep + 1) * 64, 1:2], I["fx_k_g"][l].rearrange("(d o) -> d o", o=1), w=["gcol"], slow=True)
        for rep in range(4):
            k.dma(gcol[rep * 32:(rep + 1) * 32, 2:3], I["df_q_g"][l].rearrange("(d o) -> d o", o=1), w=["gcol"], slow=True)
            k.dma(gcol[rep * 32:(rep + 1) * 32, 3:4], I["df_k_g"][l].rearrange("(d o) -> d o", o=1), w=["gcol"], slow=True)
        k.ts("dve", gcol[:, 0:1], gcol[:, 0:1], 0.125, ALU.mult, r=["gcol"], w=["gcol"])
        k.ts("dve", gcol[:, 2:3], gcol[:, 2:3], 32.0 ** -0.5, ALU.mult, r=["gcol"], w=["gcol"])
        k.dma(mu[:], I["rw_mu"][l].rearrange("(c p) -> p c", p=128), w=["mu"], slow=True)
        k.dma(negfb[:], I["fx_f_b"][l].rearrange("(h o) -> h o", o=1), w=["negfb"], slow=True)
        k.ts("dve", negfb[:], negfb[:], -1.0, ALU.mult, r=["negfb"], w=["negfb"])
        k.memset("dve", ones4[:], 1.0, w=["ones4"])
        k.memset("pool", paA[:], 0.0, w=["paA%d" % i for i in range(7)])
        for i in range(4):
            k.memset("pool", vt[i][:], 1.0, w=["vt%d" % i])
        k.dma(sgw[:], I["sg_w"][l].rearrange("g i j -> i g j"), w=["sgw"])
        for g in range(4):
            k.tr(banks[0][:, g * 128:(g + 1) * 128], sgw[:, g, :], G.ident[:], r=["sgw", "ident"], w=["bank0"])
        k.tt("dve", WgT[:], banks[0][:].rearrange("p (g i) -> p g i", g=4),
             triu[:].unsqueeze(1).to_broadcast([128, 4, 128]), ALU.mult, r=["bank0", "triu"], w=["WgT"])
        k.dma(sgbT[:], I["sg_b"][l].rearrange("g i -> i g"), w=["sgbT"], slow=True)
        bcast_load(G, lnCg[:], I["sg_ln_g"][l:l + 1, :], 256, ["lnCg"])
        bcast_load(G, lnCb[:], I["sg_ln_b"][l:l + 1, :], 256, ["lnCb"])

        cur = Ctx()

        def fm_mm(col0, ncols, bk):
            for kc in range(8):
                k.mm(banks[bk][0:ncols, :], w_in[:, kc, col0:col0 + ncols], cur.hT[:, kc, :], start=(kc == 0), stop=(kc == 7),
                     r=["w_in", cur.hTk], w=["bank%d" % bk])

        def load_norm(g):
            tsl_ = slice(g * TG, (g + 1) * TG)
            k.dma(xt[:], xsrc[tsl_, :].rearrange("(j p) d -> p j d", p=128), r=[("x", g)], w=["xt"])
            T.hT = hTs[g % 2]
            T.hTk = "hT%d" % (g % 2)
            norm_mod_T(G, xt, 1 * D, 0, T)
        load_norm(0)

        for g in range(NG):
            tsl = slice(g * TG, (g + 1) * TG)
            k.dma(cs[g % 2][:], I["cosT"][:, tsl], w=["cos%d" % (g % 2)])
            k.dma(sn[g % 2][:], I["sinT"][:, tsl], w=["sin%d" % (g % 2)])
            cur.hT = hTs[g % 2]
            cur.hTk = "hT%d" % (g % 2)
            for ci in range(7):
                bk = k.rot("p1bank", 3)
                fm_mm(ci * 128, 128, bk)
                k.copy("pool", paA[:, ci, 0:1], paA[:, ci, 512:513], r=["paA%d" % ci], w=["paA%d" % ci])
                k.copy("act", paA[:, ci, 1:513], banks[bk][:], r=["bank%d" % bk], w=["paA%d" % ci])
                i = k.rot("fA", 3)
                k.tt("pool", fA[i][:], paA[:, ci, 0:512], paA[:, ci, 1:513], ALU.subtract, r=["paA%d" % ci], w=["fA%d" % i])
                o = k.rot("pmo", 2)
                k.stt("dve", pmo[o][:], fA[i][:], mu[:, ci:ci + 1], paA[:, ci, 1:513], ALU.mult, ALU.add,
                      r=["fA%d" % i, "mu", "paA%d" % ci], w=["pmo%d" % o])
                k.dma(R.pmA[ci * 128:(ci + 1) * 128, tsl], pmo[o][:], r=["pmo%d" % o], w=[("pmA", g)])
            if g + 1 < NG:
                load_norm(g + 1)
            for (mix, col0, gi, dst, rope) in (("B", 896, 2, R.qTB, True), ("B", 1152, 3, R.kTB, True),
                                               ("D", 2176, 0, R.qTD, False), ("D", 2432, 1, R.kTD, False)):
                for ci in range(2):
                    bk = k.rot("p1bank", 3)
                    fm_mm(col0 + ci * 128, 128, bk)
                    a = k.rot("fA", 3)
                    k.act(fA[a][:], banks[bk][:], AF.Square, r=["bank%d" % bk], w=["fA%d" % a])
                    sbk = 3 + k.rot("p1sbank", 2)
                    k.mm(banks[sbk][:], bo32[:] if mix == "B" else bo64[:], fA[a][:], r=["bo32", "bo64", "fA%d" % a],
                         w=["bank%d" % sbk])
                    b = k.rot("fB", 3)
                    nd = 32.0 if mix == "B" else 64.0
                    k.act(fB[b][:], banks[sbk][:], AF.Ln, bias=EPS, scale=1.0 / nd, r=["bank%d" % sbk], w=["fB%d" % b])
                    k.act(fB[b][:], fB[b][:], AF.Exp, scale=-0.5, r=["fB%d" % b], w=["fB%d" % b])
                    o = k.rot("obf", 3)
                    if not rope:
                        k.stt("dve", obf[o][:], banks[bk][:], gcol[:, gi:gi + 1], fB[b][:], ALU.mult, ALU.mult,
                              r=["bank%d" % bk, "gcol", "fB%d" % b], w=["obf%d" % o])
                    else:
                        c_ = k.rot("fC", 3)
                        k.stt("dve", fC[c_][:], banks[bk][:], gcol[:, gi:gi + 1], fB[b][:], ALU.mult, ALU.mult,
                              r=["bank%d" % bk, "gcol", "fB%d" % b], w=["fC%d" % c_])
                        xb = k.rot("xnb", 2)
                        k.copy("act", xnb[xb][:], fC[c_][:], r=["fC%d" % c_], w=["xnb%d" % xb])
                        rbk = 3 + k.rot("p1sbank", 2)
                        k.mm(banks[rbk][:], protb[:], xnb[xb][:], r=["protb", "xnb%d" % xb], w=["bank%d" % rbk])
                        a2 = k.rot("fA", 3)
                        k.tt("pool", fA[a2][:], fC[c_][:], cs[g % 2][:], ALU.mult, r=["fC%d" % c_, "cos%d" % (g % 2)], w=["fA%d" % a2])
                        b2 = k.rot("fB", 3)
                        k.tt("dve", fB[b2][:], banks[rbk][:], sn[g % 2][:], ALU.mult, r=["bank%d" % rbk, "sin%d" % (g % 2)],
                             w=["fB%d" % b2])
                        k.tt("pool", obf[o][:], fA[a2][:], fB[b2][:], ALU.add, r=["fA%d" % a2, "fB%d" % b2], w=["obf%d" % o])
                    k.dma(dst[ci * 128:(ci + 1) * 128, tsl], obf[o][:], r=["obf%d" % o], w=[(mix + "qk", g)])
            bk = k.rot("p1bank", 3)
            fm_mm(2944, 4, bk)
            k.act(f4[:], banks[bk][0:4, :], AF.Exp, bias=negfb[:, 0:1], scale=-1.0, r=["bank%d" % bk, "negfb"], w=["f4"])
            k.act(f4[:], f4[:], AF.Ln, bias=1.0, scale=1.0, r=["f4"], w=["f4"])
            if g == 0:
                k.scan(Fg[0][:], ones4[:], f4[:], 0.0, ALU.mult, ALU.subtract, r=["ones4", "f4"], w=["Fg0"])
            else:
                k.scan(Fg[g % 2][:], ones4[:], f4[:], Fg[(g - 1) % 2][:, 511:512], ALU.mult, ALU.subtract,
                       r=["ones4", "f4", "Fg%d" % ((g - 1) % 2)], w=["Fg%d" % (g % 2)])
            k.dma(R.Ffm[:, tsl], Fg[g % 2][:], r=["Fg%d" % (g % 2)], w=[("Ffm", g)])
            for j in range(4):
                rows = slice(g * TG + j * 128, g * TG + (j + 1) * 128)
                for (mix, col0, dst) in (("B", 1408, R.vB), ("D", 2688, R.vD)):
                    bk = k.rot("p1bank", 3)
                    for kc in range(8):
                        k.mm(banks[bk][:, 0:256], cur.hT[:, kc, j * 128:(j + 1) * 128], w_in[:, kc, col0:col0 + 256],
                             start=(kc == 0), stop=(kc == 7), r=["w_in", cur.hTk], w=["bank%d" % bk])
                    vi = k.rot("vt", 4)
                    k.copy("act" if mix == "B" else "dve", vt[vi][:, :, 0:64], banks[bk][:, 0:256].rearrange("p (h e) -> p h e", h=4),
                           r=["bank%d" % bk], w=["vt%d" % vi])
                    k.dma(dst[rows, :], vt[vi][:].rearrange("p h e -> p (h e)"), r=["vt%d" % vi], w=[(mix + "v", g)])
                bk = k.rot("p1bank", 3)
                for kc in range(8):
                    k.mm(banks[bk][:], cur.hT[:, kc, j * 128:(j + 1) * 128], w_in[:, kc, 1664:2176],
                         start=(kc == 0), stop=(kc == 7), r=["w_in", cur.hTk], w=["bank%d" % bk])
                ci = k.rot("glC", 2)
                gl, st, tc, vb, yb = glC[ci], stC[ci], tmpc[ci], vnb[ci], ycb[ci]
                kk = "C%d" % ci
                k.act(gl[:], banks[bk][:], AF.Gelu, r=["bank%d" % bk], w=[kk + "gl"])
                k.red("dve", st[:, 0:1], gl[:, 256:512], r=[kk + "gl"], w=[kk + "st"])
                k.act(tc[:], gl[:, 256:512], AF.Square, r=[kk + "gl"], w=[kk + "tc"])
                k.red("dve", st[:, 1:2], tc[:], r=[kk + "tc"], w=[kk + "st"])
                k.ts("dve", st[:, 2:3], st[:, 0:1], 1.0 / 256, ALU.mult, r=[kk + "st"], w=[kk + "st"])
                k.tt("dve", st[:, 3:4], st[:, 2:3], st[:, 2:3], ALU.mult, r=[kk + "st"], w=[kk + "st"])
                k.stt("dve", st[:, 4:5], st[:, 1:2], 1.0 / 256, st[:, 3:4], ALU.mult, ALU.subtract, r=[kk + "st"], w=[kk + "st"])
                k.act(st[:, 5:6], st[:, 4:5], AF.Sqrt, bias=EPS, scale=1.0, r=[kk + "st"], w=[kk + "st"])
                k.recip(st[:, 5:6], st[:, 5:6], r=[kk + "st"], w=[kk + "st"])
                k.ts("dve", tc[:], gl[:, 256:512], st[:, 2:3], ALU.subtract, st[:, 5:6], ALU.mult, r=[kk + "gl", kk + "st"], w=[kk + "tc"])
                k.tt("pool", tc[:], tc[:], lnCg[:], ALU.mult, r=[kk + "tc", "lnCg"], w=[kk + "tc"])
                k.tt("pool", vb[:], tc[:], lnCb[:], ALU.add, r=[kk + "tc", "lnCb"], w=[kk + "vb"])
                sbk = 3 + k.rot("p1sbank", 2)
                for hg in range(4):
                    k.mm(banks[sbk][:, hg * 64:(hg + 1) * 64], WgT[:, hg, :], vb[:, hg * 64:(hg + 1) * 64],
                         r=["WgT", kk + "vb"], w=["bank%d" % sbk])
                k.tt("dve", tc[:].rearrange("p (h e) -> p h e", h=4), banks[sbk][:, 0:256].rearrange("p (h e) -> p h e", h=4),
                     sgbT[:].unsqueeze(2).to_broadcast([128, 4, 64]), ALU.add, r=["bank%d" % sbk, "sgbT", kk + "vb"], w=[kk + "tc"])
                k.tt("pool", yb[:], tc[:], gl[:, 0:256], ALU.mult, r=[kk + "tc", kk + "gl"], w=[kk + "yb"])
                if not G.y_in:
                    k.dma(R.yscr[rows, 256:512], yb[:], r=[kk + "yb"], w=[("yC", g)])
        P.barrier()


def phase_p3(G, l, xsrc, xdst):
    nc, P, k, I, R = G.nc, G.P, G.k, G.I, G.R
    banks = G.banks
    ydt = F32 if G.y_in else BF16
    with ExitStack() as s:
        sb = lambda name, shape, dt=F32: G.sb(s, name, shape, dt)
        T = Ctx()
        w_out = sb("w_out", [128, 8, D], BF16)
        xt = sb("xt3", [128, 4, D])
        T.sqj = sb("sqj3", [128, D], BF16)
        T.ss = sb("ss3", [128, 4])
        T.rstd = sb("rstd3", [128, 4])
        T.tmp = sb("tmp3", [128, D])
        T.hb = sb("hb3", [128, 4, D], BF16)
        T.hT = sb("hT3", [128, 8, 512], BF16)
        yT = T.hT
        actT = sb("actT", [128, 22, 512], BF16)
        upw = [sb("upw%d" % i, [128, 8, 512], BF16) for i in range(2)]
        dnw = sb("dnw", [128, 22, D], BF16)
        ub = [sb("ub%d" % i, [128, 514]) for i in range(3)]
        c1 = [sb("c1_%d" % i, [128, 512]) for i in range(3)]
        c2 = [sb("c2_%d" % i, [128, 512]) for i in range(3)]
        c3 = [sb("c3_%d" % i, [128, 512]) for i in range(2)]
        sgt = [sb("sgt%d" % i, [128, 512]) for i in range(2)]
        convw = sb("convw", [128, 44, 3])
        convb = sb("convb", [128, 44])
        carryF = sb("carryF", [128, 44, 2])

        for kc in range(8):
            k.dma(T.tmp[:], I["w_out"][l, kc * 128:(kc + 1) * 128, :], w=["tmp"])
            k.copy(("dve", "pool", "act")[kc % 3], w_out[:, kc, :], T.tmp[:], r=["tmp"], w=["w_out"])
        cwn = T.tmp[0:44, 0:512].rearrange("p (t c) -> p t c", t=4)
        for t in range(3):
            k.dma(cwn[:, t, :], I["ffn_conv"][l, t].rearrange("(cc p) -> cc p", p=128), w=["tmp"])
        k.dma(cwn[:, 3, :], I["ffn_conv_b"][l].rearrange("(cc p) -> cc p", p=128), w=["tmp"])
        for t in range(4):
            k.mm(banks[0][:, t * 44:(t + 1) * 44], cwn[:, t, :], G.ident[0:44, 0:44], r=["tmp", "ident"], w=["bank0"])
        k.copy("dve", convw[:], banks[0][:, 0:132].rearrange("p (t c) -> p c t", t=3), r=["bank0"], w=["convw"])
        k.copy("dve", convb[:], banks[0][:, 132:176], r=["bank0"], w=["convb"])
        k.memset("pool", carryF[:], 0.0, w=["carryF"])

        for g in range(NG):
            tsl = slice(g * TG, (g + 1) * TG)
            k.dma(xt[:], xsrc[tsl, :].rearrange("(j p) d -> p j d", p=128), r=[("x", g)], w=["xt"])
            k.dma(dnw[:].rearrange("p a d -> p (a d)"), R.dnbf[l], r=[("ffnw", l)], w=["dnw"])
            if G.y_in:
                for j in range(4):
                    k.dma(T.tmp[:, 0:768], R.yscr[g * TG + j * 128: g * TG + (j + 1) * 128, :], w=["tmp"])
                    k.copy("pool", T.hb[:, j, 0:768], T.tmp[:, 0:768], r=["tmp"], w=["hb"])
                for kc in range(2):
                    k.dma(c1[kc][:], R.yTA[kc * 128:(kc + 1) * 128, tsl], w=["c1_%d" % kc])
                    k.copy("act", yT[:, kc, :], c1[kc][:], r=["c1_%d" % kc], w=["hT"])
            else:
                k.dma(T.hb[:, :, 0:768], R.yscr[tsl, :].rearrange("(j p) c -> p j c", p=128),
                      r=[("yC", g), ("yB", g), ("yD", g)], w=["hb"])
                k.dma(yT[:, 0:2, :], R.yTA[:, tsl].rearrange("(kc p) t -> p kc t", p=128), r=[("yA", g)], w=["hT"])
            for j in range(4):
                b = 5 + (j % 2)
                pv = banks[b][:].bitcast(BF16).rearrange("p (a t) -> p a t", a=8)
                for kc in range(6):
                    k.tr(pv[:, kc, :], T.hb[:, j, kc * 128:(kc + 1) * 128], G.identb[:], r=["hb", "identb"], w=["bank%d" % b])
                k.copy("act" if j % 2 else "dve", yT[:, 2:8, j * 128:(j + 1) * 128], pv[:, 0:6, :], r=["bank%d" % b], w=["hT"])
            for j in range(4):
                for hf in range(2):
                    bk = k.rot("p3bank", 2)
                    for kc in range(8):
                        k.mm(banks[bk][:], yT[:, kc, j * 128:(j + 1) * 128], w_out[:, kc, hf * 512:(hf + 1) * 512],
                             start=(kc == 0), stop=(kc == 7), r=["hT", "w_out"], w=["bank%d" % bk])
                    ci = k.rot("c1", 2)
                    k.tt("dve", c1[ci][:], banks[bk][:], G.modB[:, 2 * D + hf * 512: 2 * D + (hf + 1) * 512], ALU.mult,
                         r=["bank%d" % bk, "modB"], w=["c1_%d" % ci])
                    k.tt("dve", xt[:, j, hf * 512:(hf + 1) * 512], xt[:, j, hf * 512:(hf + 1) * 512], c1[ci][:], ALU.add,
                         r=["xt", "c1_%d" % ci], w=["xt"])
            if G.dbg_x1 is not None:
                k.dma(G.dbg_x1[tsl, :].rearrange("(j p) d -> p j d", p=128), xt[:], r=["xt"], w=[("dbgx1", g)])
            norm_mod_T(G, xt, 4 * D, 3 * D, T)
            for u in range(11):
                wi = u % 2
                k.dma(upw[wi][:].rearrange("p a n -> p (a n)"), R.upbf[l, u], r=[("ffnw", l)], w=["upw%d" % wi])
                for sc in range(4):
                    cc = 4 * u + sc
                    bk = 2 + k.rot("p3ubank", 3)
                    for kc in range(8):
                        k.mm(banks[bk][:], upw[wi][:, kc, sc * 128:(sc + 1) * 128], T.hT[:, kc, :], start=(kc == 0), stop=(kc == 7),
                             r=["upw%d" % wi, "hT"], w=["bank%d" % bk])
                    ui = k.rot("ub", 3)
                    k.copy("pool", ub[ui][:, 0:2], carryF[:, cc, :], r=["carryF"], w=["ubc%d" % ui])
                    k.copy("act", ub[ui][:, 2:514], banks[bk][:], r=["bank%d" % bk], w=["ub%d" % ui])
                    k.copy("pool", carryF[:, cc, :], ub[ui][:, 512:514], r=["ub%d" % ui], w=["carryF"])
                    k.act(c1[ui][:], ub[ui][:, 2:514], AF.Identity, bias=convb[:, cc:cc + 1], scale=convw[:, cc, 2:3],
                          r=["ub%d" % ui, "convw", "convb"], w=["c1_%d" % ui])
                    k.stt("dve", c2[ui][:], ub[ui][:, 1:513], convw[:, cc, 1:2], c1[ui][:], ALU.mult, ALU.add,
                          r=["ub%d" % ui, "ubc%d" % ui, "convw", "c1_%d" % ui], w=["c2_%d" % ui])
                    if cc < 22:
                        k.stt("dve", actT[:, cc, :], ub[ui][:, 0:512], convw[:, cc, 0:1], c2[ui][:], ALU.mult, ALU.add,
                              r=["ub%d" % ui, "ubc%d" % ui, "convw", "c2_%d" % ui], w=["actT%d" % cc])
                    else:
                        cu = cc - 22
                        k.stt("dve", c3[ui % 2][:], ub[ui][:, 0:512], convw[:, cc, 0:1], c2[ui][:], ALU.mult, ALU.add,
                              r=["ub%d" % ui, "ubc%d" % ui, "convw", "c2_%d" % ui], w=["c3_%d" % (ui % 2)])
                        k.act(sgt[ui % 2][:], c3[ui % 2][:], AF.Silu, r=["c3_%d" % (ui % 2)], w=["sgt%d" % (ui % 2)])
                        k.tt("pool", actT[:, cu, :], actT[:, cu, :], sgt[ui % 2][:], ALU.mult, r=["actT%d" % cu, "sgt%d" % (ui % 2)],
                             w=["actT%d" % cu])
            aks = ["actT%d" % i for i in range(22)]
            for j in range(4):
                for hf in range(2):
                    bk = k.rot("p3bank", 2)
                    for cu in range(22):
                        k.mm(banks[bk][:], actT[:, cu, j * 128:(j + 1) * 128], dnw[:, cu, hf * 512:(hf + 1) * 512],
                             start=(cu == 0), stop=(cu == 21), r=["actT%d" % cu, "dnw"], w=["bank%d" % bk])
                    ci = k.rot("c1", 2)
                    k.tt("dve", c1[ci][:], banks[bk][:], G.modB[:, 5 * D + hf * 512: 5 * D + (hf + 1) * 512], ALU.mult,
                         r=["bank%d" % bk, "modB"], w=["c1_%d" % ci])
                    k.tt("dve", xt[:, j, hf * 512:(hf + 1) * 512], xt[:, j, hf * 512:(hf + 1) * 512], c1[ci][:], ALU.add,
                         r=["xt", "c1_%d" % ci], w=["xt"])
            k.dma(xdst[tsl, :].rearrange("(j p) d -> p j d", p=128), xt[:], r=["xt"], w=[("x", g)])
        P.barrier()


def phase_A(G, l):
    nc, P, k, I, R = G.nc, G.P, G.k, G.I, G.R
    banks = G.banks
    with ExitStack() as s:
        sb = lambda name, shape, dt=F32: G.sb(s, name, shape, dt)
        prm = sb("prm", [64, 7, 4])
        wup = sb("wup", [32, 256])
        aup = sb("aup", [32, 256])
        gup = sb("gup", [64, 256])
        ones64 = sb("ones64", [64, 64])
        ones512 = sb("ones512", [64, 512])
        m64 = sb("m64", [64, 3, 64])
        Tst = sb("Tst", [64, 4, 64])
        big = lambda nm: sb(nm, [64, 4, 512])
        r_, k_, v_ = big("r_"), big("k_"), big("v_")
        lwt, at, gt, kkn, k2, b_, bonus, Yblk, t1, t2 = (big("lwt"), big("at"), big("gt"), big("kkn"), big("k2"), big("b_"),
                                                         big("bonus"), big("Yblk"), big("t1"), big("t2"))
        Gblk = sb("Gblk", [64, 4, 513])
        wd = sb("wd", [32, 512])
        ad = sb("ad", [32, 512])
        gd = sb("gd", [64, 512])
        yo = sb("yo", [64, 4, 512], BF16)
        sm = lambda nm: sb(nm, [64, 4, 64])
        Pc, Pp, eGn, eGp = sm("Pc"), sm("Pp"), sm("eGn"), sm("eGp")
        smb = lambda nm: sb(nm, [64, 4, 64], BF16)
        Bt, Kt = smb("Bt"), smb("Kt")
        Pj = [smb("Pj0"), smb("Pj1")]
        U, tS = sm("U"), sm("tS")
        Ub, Tb = smb("Ub"), smb("Tb")
        DB = []
        for q in range(2):
            d = {"eG": sm("eG%d" % q)}
            for nm_ in ("At", "Rt", "Btm", "Ktm", "Vtm", "LakT", "MrbT", "MrkT"):
                d[nm_] = smb("%s%d" % (nm_, q))
            d["PT"] = [smb("PT%d_%d" % (j, q)) for j in range(6)]
            DB.append(d)
        vb = sb("vb", [64, 4, 512], BF16)

        for n, nm in enumerate(("rw_w0", "rw_a0", "rw_k_k", "rw_k_a")):
            k.dma(prm[:, n, :], I[nm][l].rearrange("(h k) -> k h", k=64), w=["prm"], slow=True)
        k.dma(prm[:, 4, :], I["rw_r_k"][l].rearrange("h k -> k h"), w=["prm"], slow=True)
        k.dma(prm[:, 5, :], I["rw_ln_g"][l].rearrange("(h k) -> k h", k=64), w=["prm"], slow=True)
        k.dma(prm[:, 6, :], I["rw_ln_b"][l].rearrange("(h k) -> k h", k=64), w=["prm"], slow=True)
        k.dma(wup[:], I["rw_w_up"][l], w=["wup"])
        k.dma(aup[:], I["rw_a_up"][l], w=["aup"])
        k.dma(gup[:], I["rw_g_up"][l], w=["gup"])
        k.dma(m64[:], I["m64"], w=["m64"])
        k.memset("dve", ones64[:], 1.0, w=["ones64"])
        k.memset("dve", ones512[:], 1.0, w=["ones512"])
        k.memset("pool", Tst[:], 0.0, w=["Tst"])
        k.memset("pool", Tb[:], 0.0, w=["Tb"])
        k.memset("pool", Gblk[:], 0.0, w=["Gblk"])

        def bc(col):
            return prm[:, col, :].unsqueeze(2).to_broadcast([64, 4, 512])

        def HB(b, half):
            return banks[b][0:64, half * 256:(half + 1) * 256].rearrange("p (h t) -> p h t", h=4)

        def hk(b, half):
            return ("hb", b)

        def fb(b):
            return [("hb", b)]
        allpm = [("pmA", g) for g in range(NG)]

        for g in range(NG):
            tsl = slice(g * TG, (g + 1) * TG)
            k.dma(r_[:], R.pmA[0:256, tsl].rearrange("(h k) t -> k h t", k=64), r=allpm, w=["r_"])
            k.dma(k_[:], R.pmA[256:512, tsl].rearrange("(h k) t -> k h t", k=64), r=allpm, w=["k_"])
            k.dma(v_[:], R.pmA[512:768, tsl].rearrange("(h k) t -> k h t", k=64), r=allpm, w=["v_"])
            k.copy("pool", vb[:], v_[:], r=["v_"], w=["vb"])
            k.dma(wd[:], R.pmA[768:800, tsl], r=allpm, w=["wd"])
            k.dma(ad[:], R.pmA[800:832, tsl], r=allpm, w=["ad"])
            k.dma(gd[:], R.pmA[832:896, tsl], r=allpm, w=["gd"])
            k.act(wd[:], wd[:], AF.Tanh, r=["wd"], w=["wd"])
            k.act(gd[:], gd[:], AF.Sigmoid, r=["gd"], w=["gd"])
            for h in range(4):
                k.mm(banks[h][0:64, :], wup[:, h * 64:(h + 1) * 64], wd[:], r=["wup", "wd"], w=fb(h))
                k.act(lwt[:, h, :], banks[h][0:64, :], AF.Sigmoid, bias=prm[:, 0, h:h + 1], scale=1.0, r=fb(h) + ["prm"], w=["lwt"])
            for h in range(4):
                k.mm(banks[4 + h][0:64, :], aup[:, h * 64:(h + 1) * 64], ad[:], r=["aup", "ad"], w=fb(4 + h))
                k.act(at[:, h, :], banks[4 + h][0:64, :], AF.Sigmoid, bias=prm[:, 1, h:h + 1], scale=1.0, r=fb(4 + h) + ["prm"], w=["at"])
            for h in range(4):
                k.mm(banks[h][0:64, :], gup[:, h * 64:(h + 1) * 64], gd[:], r=["gup", "gd"], w=fb(h))
                k.copy("dve" if h % 2 else "act", gt[:, h, :], banks[h][0:64, :], r=fb(h), w=["gt"])
            k.tt("dve", kkn[:], k_[:], bc(2), ALU.mult, r=["k_", "prm"], w=["kkn"])
            k.tt("dve", t1[:], kkn[:], kkn[:], ALU.mult, r=["kkn"], w=["t1"])
            for h in range(4):
                k.mm(banks[4 + h][0:64, :], ones64[:], t1[:, h, :], r=["ones64", "t1"], w=fb(4 + h))
            for h in range(4):
                k.act(t2[:, h, :], banks[4 + h][0:64, :], AF.Ln, bias=1e-24, scale=1.0, r=fb(4 + h), w=["t2"])
            k.act(t2[:], t2[:], AF.Exp, scale=-0.5, r=["t2"], w=["t2"])
            k.tt("pool", kkn[:], kkn[:], t2[:], ALU.mult, r=["kkn", "t2"], w=["kkn"])
            k.stt("dve", t1[:], at[:], -1.0, bc(3), ALU.add, ALU.mult, r=["at", "prm", "t1"], w=["t1"])
            k.tt("pool", t1[:], t1[:], k_[:], ALU.mult, r=["t1", "k_"], w=["t1"])
            k.tt("dve", k2[:], t1[:], k_[:], ALU.add, r=["t1", "k_"], w=["k2"])
            k.tt("pool", b_[:], kkn[:], at[:], ALU.mult, r=["kkn", "at"], w=["b_"])
            k.tt("dve", t1[:], r_[:], k2[:], ALU.mult, r=["r_", "k2", "t1"], w=["t1"])
            k.tt("dve", t1[:], t1[:], bc(4), ALU.mult, r=["t1", "prm"], w=["t1"])
            for h in range(4):
                k.mm(banks[h][0:64, :], ones64[:], t1[:, h, :], r=["ones64", "t1"], w=fb(h))
                k.tt("dve", bonus[:, h, :], banks[h][0:64, :], v_[:, h, :], ALU.mult, r=fb(h) + ["v_"], w=["bonus"])
            for h in range(4):
                k.scan(Gblk[:, h, 1:513], ones512[:], lwt[:, h, :], 0.0, ALU.mult, ALU.add, r=["ones512", "lwt"], w=["Gblk"])

            if int(os.environ.get("A_STOP", "9")) <= 1:
                continue
            def prep_steps(ci):
                q = ci % 2
                D_ = DB[q]
                c0 = ci * 64
                ts_ = slice(c0, c0 + 64)
                eGq, Atq, Rtq = D_["eG"], D_["At"], D_["Rt"]
                nm = lambda x: "%s_%d" % (x, q)
                id64 = G.identb[0:64, 0:64]
                steps = []

                def s1():
                    k.tt("dve", Pc[:], Gblk[:, :, 1 + c0:1 + c0 + 64], Gblk[:, :, c0:c0 + 1].to_broadcast([64, 4, 64]), ALU.subtract,
                         r=["Gblk"], w=["Pc"])
                    k.tt("pool", Pp[:], Pc[:], lwt[:, :, ts_], ALU.subtract, r=["Pc", "lwt"], w=["Pp"])
                    k.act(eGq[:], Pc[:], AF.Exp, scale=-ALPHA, r=["Pc"], w=[nm("eG")])
                    k.act(eGn[:], Pc[:], AF.Exp, scale=ALPHA, r=["Pc"], w=["eGn"])
                    k.act(eGp[:], Pp[:], AF.Exp, scale=-ALPHA, r=["Pp"], w=["eGp"])
                steps.append(s1)

                def s2():
                    k.stt("dve", Atq[:], kkn[:, :, ts_], -1.0, eGp[:], ALU.mult, ALU.mult, r=["kkn", "eGp"], w=[nm("At")])
                    k.tt("pool", Bt[:], b_[:, :, ts_], eGn[:], ALU.mult, r=["b_", "eGn"], w=["Bt"])
                    k.tt("pool", Kt[:], k2[:, :, ts_], eGn[:], ALU.mult, r=["k2", "eGn"], w=["Kt"])
                    k.tt("dve", Rtq[:], r_[:, :, ts_], eGq[:], ALU.mult, r=["r_", nm("eG")], w=[nm("Rt")])
                steps.append(s2)

                def s3():
                    trs = ((Bt, "Bt", D_["Btm"], nm("Btm"), 2, 1, "act"), (Kt, "Kt", D_["Ktm"], nm("Ktm"), 3, 0, "act"),
                           (None, "vb", D_["Vtm"], nm("Vtm"), 3, 1, "act"))
                    for (X, xk, Xtm, xtk, bq, hf, ce) in trs:
                        for h in range(4):
                            src = vb[:, h, ts_] if X is None else X[:, h, :]
                            k.mm(HB(bq, hf)[:, h, :], src, id64, r=[xk, "identb"], w=[hk(bq, hf)])
                    for (X, xk, Xtm, xtk, bq, hf, ce) in trs:
                        k.copy(ce, Xtm[:], HB(bq, hf), r=[hk(bq, hf)], w=[xtk])
                steps.append(s3)

                def s4():
                    specs = ((Bt, "Bt", Atq, nm("At"), 0, 0, D_["PT"][0], nm("PT0"), 0), (Atq, nm("At"), Bt, "Bt", 0, 1, Pj[0], "Pj0", 2),
                             (Kt, "Kt", Atq, nm("At"), 1, 0, D_["LakT"], nm("LakT"), 0), (Bt, "Bt", Rtq, nm("Rt"), 1, 1, D_["MrbT"], nm("MrbT"), 1),
                             (Kt, "Kt", Rtq, nm("Rt"), 2, 0, D_["MrkT"], nm("MrkT"), 1))
                    for (La, lk, Ra, rk_, bq, hf, dst, dk, mi) in specs:
                        for h in range(4):
                            k.mm(HB(bq, hf)[:, h, :], La[:, h, :], Ra[:, h, :], r=[lk, rk_], w=[hk(bq, hf)])
                    for (La, lk, Ra, rk_, bq, hf, dst, dk, mi) in specs:
                        k.tt("dve", dst[:], HB(bq, hf), m64[:, mi, :].unsqueeze(1).to_broadcast([64, 4, 64]), ALU.mult,
                             r=[hk(bq, hf), "m64"], w=[dk])
                steps.append(s4)

                def mk_sq(j):
                    def sq():
                        cur, nxt = j % 2, (j + 1) % 2
                        PTc, PTn = D_["PT"][j], D_["PT"][j + 1]
                        for h in range(4):
                            k.mm(HB(5, 0)[:, h, :], PTc[:, h, :], Pj[cur][:, h, :], r=[nm("PT%d" % j), "Pj%d" % cur], w=[hk(5, 0)])
                        for h in range(4):
                            k.mm(HB(5, 1)[:, h, :], Pj[cur][:, h, :], PTc[:, h, :], r=[nm("PT%d" % j), "Pj%d" % cur], w=[hk(5, 1)])
                        k.copy("act", Pj[nxt][:], HB(5, 0), r=[hk(5, 0)], w=["Pj%d" % nxt])
                        k.copy("act", PTn[:], HB(5, 1), r=[hk(5, 1)], w=[nm("PT%d" % (j + 1))])
                    return sq
                for j in range(5):
                    steps.append(mk_sq(j))
                return steps

            def chain_steps(ci):
                q = ci % 2
                D_ = DB[q]
                c0 = ci * 64
                ts_ = slice(c0, c0 + 64)
                eGq, Atq, Rtq = D_["eG"], D_["At"], D_["Rt"]
                Btm, Ktm, Vtm, LakT, MrbT, MrkT = D_["Btm"], D_["Ktm"], D_["Vtm"], D_["LakT"], D_["MrbT"], D_["MrkT"]
                nm = lambda x: "%s_%d" % (x, q)
                steps = []

                def c1():
                    for h in range(4):
                        k.mm(HB(4, 0)[:, h, :], Atq[:, h, :], Tb[:, h, :], start=True, stop=False, r=[nm("At"), "Tb"], w=[hk(4, 0)])
                        k.mm(HB(4, 0)[:, h, :], LakT[:, h, :], Vtm[:, h, :], start=False, stop=True, r=[nm("LakT"), nm("Vtm")], w=[hk(4, 0)])
                    k.copy("dve", Ub[:], HB(4, 0), r=[hk(4, 0)], w=["Ub"])
                    k.copy("act", U[:], HB(4, 0), r=[hk(4, 0)], w=["U"])
                steps.append(c1)

                def mk_u(j):
                    def us():
                        PTc = D_["PT"][j]
                        for h in range(4):
                            k.mm(HB(4, 1)[:, h, :], PTc[:, h, :], Ub[:, h, :], r=[nm("PT%d" % j), "Ub"], w=[hk(4, 1)])
                        k.tt("dve", Ub[:], U[:], HB(4, 1), ALU.add, r=["U", hk(4, 1)], w=["Ub"])
                        if j < 5:
                            k.tt("dve", U[:], U[:], HB(4, 1), ALU.add, r=["U", hk(4, 1)], w=["U"])
                    return us
                for j in range(6):
                    steps.append(mk_u(j))

                def c8():
                    for h in range(4):
                        k.mm(HB(6, 0)[:, h, :], Tb[:, h, :], Rtq[:, h, :], start=True, stop=False, r=["Tb", nm("Rt")], w=[hk(6, 0)])
                        k.mm(HB(6, 0)[:, h, :], Ub[:, h, :], MrbT[:, h, :], start=False, stop=False, r=["Ub", nm("MrbT")], w=[hk(6, 0)])
                        k.mm(HB(6, 0)[:, h, :], Vtm[:, h, :], MrkT[:, h, :], start=False, stop=True, r=[nm("Vtm"), nm("MrkT")], w=[hk(6, 0)])
                    k.copy("act", Yblk[:, :, ts_], HB(6, 0), r=[hk(6, 0)], w=["Yblk"])
                    for h in range(4):
                        k.mm(HB(7, 0)[:, h, :], Btm[:, h, :], Ub[:, h, :], start=True, stop=False, r=[nm("Btm"), "Ub"], w=[hk(7, 0)])
                        k.mm(HB(7, 0)[:, h, :], Ktm[:, h, :], Vtm[:, h, :], start=False, stop=True, r=[nm("Ktm"), nm("Vtm")], w=[hk(7, 0)])
                    k.tt("dve", tS[:], HB(7, 0), Tst[:], ALU.add, r=[hk(7, 0), "Tst"], w=["tS"])
                    k.tt("dve", Tb[:], tS[:], eGq[:, :, 63:64].to_broadcast([64, 4, 64]), ALU.mult, r=["tS", nm("eG")], w=["Tb"])
                    k.tt("pool", Tst[:], tS[:], eGq[:, :, 63:64].to_broadcast([64, 4, 64]), ALU.mult, r=["tS", nm("eG")], w=["Tst"])
                steps.append(c8)
                return steps

            for st in prep_steps(0):
                st()
            for ci in range(8):
                cs = chain_steps(ci)
                ps = prep_steps(ci + 1) if ci < 7 else []
                n = max(len(cs), len(ps))
                for i_ in range(n):
                    if i_ < len(cs):
                        cs[i_]()
                    if i_ < len(ps):
                        ps[i_]()

            if int(os.environ.get("A_STOP", "9")) <= 3:
                continue
            for h in range(4):
                k.mm(banks[h][0:64, :], ones64[:], Yblk[:, h, :], r=["ones64", "Yblk"], w=fb(h))
                k.stt("dve", t1[:, h, :], banks[h][0:64, :], -1.0 / 64, Yblk[:, h, :], ALU.mult, ALU.add, r=fb(h) + ["Yblk"], w=["t1"])
            k.tt("dve", t2[:], t1[:], t1[:], ALU.mult, r=["t1"], w=["t2"])
            for h in range(4):
                k.mm(banks[4 + h][0:64, :], ones64[:], t2[:, h, :], r=["ones64", "t2"], w=fb(4 + h))
            for h in range(4):
                k.act(t2[:, h, :], banks[4 + h][0:64, :], AF.Ln, bias=64e-5, scale=1.0 / 64, r=fb(4 + h), w=["t2"])
            k.act(t2[:], t2[:], AF.Exp, scale=-0.5, r=["t2"], w=["t2"])
            k.tt("pool", t1[:], t1[:], t2[:], ALU.mult, r=["t1", "t2"], w=["t1"])
            k.tt("dve", t1[:], t1[:], bc(5), ALU.mult, r=["t1", "prm"], w=["t1"])
            k.tt("pool", t1[:], t1[:], bc(6), ALU.add, r=["t1", "prm"], w=["t1"])
            k.tt("dve", t1[:], t1[:], bonus[:], ALU.add, r=["t1", "bonus"], w=["t1"])
            k.tt("dve", yo[:], t1[:], gt[:], ALU.mult, r=["t1", "gt"], w=["yo"])
            if not G.y_in:
                k.dma(R.yTA[:, tsl].rearrange("(h v) t -> v h t", v=64), yo[:], r=["yo"], w=[("yA", g)])
        P.barrier()


def attn_common(G, s, Vsrc, mix):
    k, R, I = G.k, G.R, G.I
    sb = lambda name, shape, dt=F32: G.sb(s, name, shape, dt)
    A = Ctx()
    A.kT = [sb("kT%d" % i, [64, S], BF16) for i in range(2)]
    A.V = sb("V", [128, 32, 260], BF16)
    Vv = Vsrc.rearrange("(kt p) c -> p kt c", p=128)
    for q4 in range(4):
        k.dma(A.V[:, q4 * 8:(q4 + 1) * 8, :], Vv[:, q4 * 8:(q4 + 1) * 8, :], r=[(mix + "v", g) for g in range(NG)], w=["V"])
    A.Pm = [sb("Pm%d" % i, [128, 512], BF16) for i in range(5)]
    A.rec = [sb("rec%d" % i, [128, 8]) for i in range(2)]
    A.ob = [sb("ob%d" % i, [128, 4, 64], BF16) for i in range(2)]
    return A


def phase_D(G, l):
    nc, P, k, I, R = G.nc, G.P, G.k, G.I, G.R
    banks = G.banks
    with ExitStack() as s:
        sb = lambda name, shape, dt=F32: G.sb(s, name, shape, dt)
        A = attn_common(G, s, R.vD, "D")
        Fk = sb("Fk", [128, 4, 32])
        Frow = sb("Frow", [4, S])
        sel = sb("sel", [4, 4, 128])
        negm = sb("negm", [128, 128])
        qT = [sb("qT%d" % i, [64, 512], BF16) for i in range(2)]
        FqB = [sb("FqB%d" % i, [128, 512]) for i in range(2)]
        FqD = [sb("FqD%d" % i, [128, 512]) for i in range(2)]
        tb = [sb("tb%d" % i, [128, 512]) for i in range(5)]
        sbanks = [0, 1, 2, 6, 7]
        allF = [("Ffm", g) for g in range(NG)]
        fkn = tb[0][0:32, :].rearrange("p (h c) -> p h c", h=4)
        for h in range(4):
            k.dma(fkn[:, h, :], R.Ffm[h].rearrange("(kt p) -> kt p", p=128), r=allF, w=["tb0"])
        for h in range(4):
            k.mm(banks[5][:, h * 32:(h + 1) * 32], fkn[:, h, :], G.ident[0:32, 0:32], r=["tb0", "ident"], w=["bank5"])
        k.copy("dve", Fk[:].rearrange("p h c -> p (h c)"), banks[5][:, 0:128], r=["bank5"], w=["Fk"])
        k.dma(Frow[:], R.Ffm, r=allF, w=["Frow"])
        k.dma(sel[:], I["sel"], w=["sel"])
        k.dma(negm[:], I["negmask"], w=["negm"])
        allqk = [("Dqk", g) for g in range(NG)]
        for h in range(4):
            kt_ = A.kT[h % 2]
            kk = "kT%d" % (h % 2)
            k.dma(kt_[:], R.kTD[h * 64:(h + 1) * 64, :], r=allqk, w=[kk])
            for g in range(NG):
                i = k.rot("Dq", 2)
                k.dma(qT[i][:], R.qTD[h * 64:(h + 1) * 64, g * TG:(g + 1) * TG], r=allqk, w=["qT%d" % i])
                k.mm(banks[5][:], sel[:, h, :], Frow[:, g * TG:(g + 1) * TG], r=["sel", "Frow"], w=["bank5"])
                k.copy("act", FqB[i][:], banks[5][:], r=["bank5"], w=["FqB%d" % i])
                k.tt("pool", FqD[i][:].rearrange("p (a b) -> p a b", a=4), FqB[i][:].rearrange("p (a b) -> p a b", a=4),
                     negm[:].unsqueeze(1).to_broadcast([128, 4, 128]), ALU.add, r=["FqB%d" % i, "negm"], w=["FqD%d" % i])
                ob_ = 3 + k.rot("DO", 2)
                O = banks[ob_][:, 0:260].rearrange("p (a e) -> p a e", a=4)
                def d_stage1(kt):
                    m = kt - 4 * g
                    c0 = max(m, 0) * 128
                    N = 512 - c0
                    sbk = sbanks[k.rot("Ds", 5)]
                    k.mm(banks[sbk][:, 0:N], kt_[:, kt * 128:(kt + 1) * 128], qT[i][:, c0:512], r=[kk, "qT%d" % i], w=["bank%d" % sbk])
                    ti = k.rot("Dt", 5)
                    if m < 0:
                        k.stt("dve", tb[ti][:], banks[sbk][:], Fk[:, h, kt:kt + 1], FqB[i][:], ALU.subtract, ALU.add,
                              r=["bank%d" % sbk, "Fk", "FqB%d" % i], w=["tb%d" % ti])
                    else:
                        k.stt("dve", tb[ti][:, 0:128], banks[sbk][:, 0:128], Fk[:, h, kt:kt + 1], FqD[i][:, c0:c0 + 128],
                              ALU.subtract, ALU.add, r=["bank%d" % sbk, "Fk", "FqD%d" % i], w=["tb%d" % ti])
                        if N > 128:
                            k.stt("dve", tb[ti][:, 128:N], banks[sbk][:, 128:N], Fk[:, h, kt:kt + 1], FqB[i][:, c0 + 128:512],
                                  ALU.subtract, ALU.add, r=["bank%d" % sbk, "Fk", "FqB%d" % i], w=["tb%d" % ti])
                    k.act(A.Pm[ti][:, 0:N], tb[ti][:, 0:N], AF.Exp, r=["tb%d" % ti], w=["Pm%d" % ti])
                    return (kt, m, c0, ti)

                def d_stage2(st):
                    kt, m, c0, ti = st
                    for jq in range(max(m, 0), 4):
                        k.mm(O[:, jq, :], A.Pm[ti][:, jq * 128 - c0: jq * 128 - c0 + 128], A.V[:, kt, h * 65:(h + 1) * 65],
                             start=(kt == 0 and jq == 0), stop=(kt == 4 * g + jq), r=["Pm%d" % ti, "V"], w=["bank%d" % ob_], sgc=True)
                pend = []
                for kt in range(4 * g + 4):
                    pend.append(d_stage1(kt))
                    if len(pend) > 3:
                        d_stage2(pend.pop(0))
                while pend:
                    d_stage2(pend.pop(0))
                ri = k.rot("Drec", 2)
                k.recip(A.rec[ri][:, 0:4], O[:, :, 64], r=["bank%d" % ob_], w=["rec%d" % ri])
                k.tt("dve", A.ob[ri][:], O[:, :, 0:64], A.rec[ri][:, 0:4].unsqueeze(2).to_broadcast([128, 4, 64]), ALU.mult,
                     r=["bank%d" % ob_, "rec%d" % ri], w=["ob%d" % ri])
                k.dma(R.yscr[g * TG:(g + 1) * TG, 512 + h * 64: 512 + (h + 1) * 64].rearrange("(a p) e -> p a e", p=128),
                      A.ob[ri][:], r=["ob%d" % ri], w=[("yD", g)], slow=True)
        P.barrier()


def phase_B(G, l):
    nc, P, k, I, R = G.nc, G.P, G.k, G.I, G.R
    banks = G.banks
    lambda_init = 0.8 - 0.6 * math.exp(-0.3 * l)
    with ExitStack() as s:
        sb = lambda name, shape, dt=F32: G.sb(s, name, shape, dt)
        A = attn_common(G, s, R.vB, "B")
        cmf = sb("cmf", [128, 128])
        cm = sb("cm", [128, 128], BF16)
        qp = [[sb("qp%d_%d" % (i, mp), [64, 512], BF16) for mp in range(2)] for i in range(2)]
        lamv = sb("lamv", [128, 4, 32])
        lamw = sb("lamw", [128, 2, 32])
        lams = sb("lams", [128, 4])
        subg = sb("subg", [128, 64])
        o1 = [sb("o1_%d" % i, [128, 4, 64]) for i in range(2)]
        o2 = [sb("o2_%d" % i, [128, 4, 64]) for i in range(2)]
        k.dma(cmf[:], I["cmask"], w=["cmf"])
        k.copy("dve", cm[:], cmf[:], r=["cmf"], w=["cm"])
        for i in range(2):
            for mp in range(2):
                k.memset("pool", qp[i][mp][:], 0.0, w=["qp%d" % i])
        for n, nm in enumerate(("df_lam_q1", "df_lam_k1", "df_lam_q2", "df_lam_k2")):
            bcast_load(G, lamv[:, n, :], I[nm][l:l + 1, :], 32, ["lamv"])
        bcast_load(G, subg[:], I["df_sub_g"][l:l + 1, :], 64, ["subg"])
        k.ts("dve", subg[:], subg[:], 1.0 - lambda_init, ALU.mult, r=["subg"], w=["subg"])
        k.tt("dve", lamw[:, 0, :], lamv[:, 0, :], lamv[:, 1, :], ALU.mult, r=["lamv"], w=["lamw"])
        k.tt("dve", lamw[:, 1, :], lamv[:, 2, :], lamv[:, 3, :], ALU.mult, r=["lamv"], w=["lamw"])
        k.red("dve", lams[:, 0:2], lamw[:], r=["lamw"], w=["lams"])
        k.act(lams[:, 0:2], lams[:, 0:2], AF.Exp, r=["lams"], w=["lams"])
        k.ts("dve", lams[:, 2:3], lams[:, 1:2], -lambda_init, ALU.add, r=["lams"], w=["lams"])
        k.tt("dve", lams[:, 3:4], lams[:, 2:3], lams[:, 0:1], ALU.subtract, r=["lams"], w=["lams"])
        allqk = [("Bqk", g) for g in range(NG)]
        jobs = []
        if G.prep_in_B:
            pst = [sb("pf_st%d" % i_, [128, 8, 512]) for i_ in range(2)]
            pbf = [sb("pf_bf%d" % i_, [128, 8, 512], BF16) for i_ in range(2)]
            up = I["ffn_up"][l].rearrange("(kc p) n -> p kc n", p=128)
            dn = I["ffn_down"][l].rearrange("(fc p) d -> p fc d", p=128)
            for u in range(11):
                jobs.append((up[:, :, u * 512:(u + 1) * 512], R.upbf[l, u].rearrange("p (kc n) -> p kc n", kc=8), 8, 512))
            for u in range(11):
                jobs.append((dn[:, 2 * u:2 * u + 2, :], R.dnbf[l][:, 2 * u * 1024:(2 * u + 2) * 1024].rearrange("p (a d) -> p a d", a=2), 2, 1024))

        def pf_load(i_):
            src, dst, a_, b_ = jobs[i_]
            k.dma(pst[i_ % 2][:].rearrange("p a b -> p (a b)")[:, 0:a_ * b_].rearrange("p (a b) -> p a b", a=a_), src,
                  w=["pf_st%d" % (i_ % 2)], q="pool")

        def pf_job(i_):
            src, dst, a_, b_ = jobs[i_]
            sv = pst[i_ % 2][:].rearrange("p a b -> p (a b)")[:, 0:a_ * b_]
            bv = pbf[i_ % 2][:].rearrange("p a b -> p (a b)")[:, 0:a_ * b_]
            k.copy("dve" if i_ % 3 else "pool", bv, sv, r=["pf_st%d" % (i_ % 2)], w=["pf_bf%d" % (i_ % 2)])
            k.dma(dst, bv.rearrange("p (a b) -> p a b", a=a_), r=["pf_bf%d" % (i_ % 2)], w=[("ffnw", l)], q="pool")
            if i_ + 2 < len(jobs):
                pf_load(i_ + 2)
        if jobs:
            pf_load(0)
            pf_load(1)
        jn = [0]
        for h in range(4):
            kt_ = A.kT[h % 2]
            kk = "kT%d" % (h % 2)
            k.dma(kt_[:], R.kTB[h * 64:(h + 1) * 64, :], r=allqk, w=[kk])
            for g in range(NG):
                if jobs and jn[0] < len(jobs) and (h * NG + g) >= 2:
                    pf_job(jn[0])
                    jn[0] += 1
                i = k.rot("Bq", 2)
                k.dma(qp[i][0][0:32, :], R.qTB[h * 64:h * 64 + 32, g * TG:(g + 1) * TG], r=allqk, w=["qp%d" % i])
                k.dma(qp[i][1][32:64, :], R.qTB[h * 64 + 32:h * 64 + 64, g * TG:(g + 1) * TG], r=allqk, w=["qp%d" % i])
                oi = k.rot("BO", 2)
                Os = [banks[3 + oi][:, 0:260].rearrange("p (a e) -> p a e", a=4),
                      banks[5 + oi][:, 0:260].rearrange("p (a e) -> p a e", a=4)]
                obk = ["bank%d" % (3 + oi), "bank%d" % (5 + oi)]
                def b_stage1(kt, mp):
                    m = kt - 4 * g
                    c0 = max(m, 0) * 128
                    N = 512 - c0
                    sbk = (0, 1, 2, 7)[k.rot("Bs", 4)]
                    k.mm(banks[sbk][:, 0:N], kt_[:, kt * 128:(kt + 1) * 128], qp[i][mp][:, c0:512], r=[kk, "qp%d" % i],
                         w=["bank%d" % sbk])
                    ti = k.rot("Bt", 4)
                    k.act(A.Pm[ti][:, 0:N], banks[sbk][:, 0:N], AF.Exp, r=["bank%d" % sbk], w=["Pm%d" % ti])
                    if m >= 0:
                        k.tt("dve", A.Pm[ti][:, 0:128], A.Pm[ti][:, 0:128], cm[:], ALU.mult, r=["Pm%d" % ti, "cm"], w=["Pm%d" % ti])
                    return (kt, mp, m, c0, ti)

                def b_stage2(st):
                    kt, mp, m, c0, ti = st
                    for jq in range(max(m, 0), 4):
                        k.mm(Os[mp][:, jq, :], A.Pm[ti][:, jq * 128 - c0: jq * 128 - c0 + 128], A.V[:, kt, h * 65:(h + 1) * 65],
                             start=(kt == 0 and jq == 0), stop=(kt == 4 * g + jq), r=["Pm%d" % ti, "V"], w=[obk[mp]], sgc=True)
                pend = []
                for kt in range(4 * g + 4):
                    for mp in range(2):
                        pend.append(b_stage1(kt, mp))
                        if len(pend) > 3:
                            b_stage2(pend.pop(0))
                while pend:
                    b_stage2(pend.pop(0))
                ri = k.rot("Brec", 2)
                rc = A.rec[ri]
                rk = "rec%d" % ri
                k.recip(rc[:, 0:4], Os[0][:, :, 64], r=[obk[0]], w=[rk])
                k.recip(rc[:, 4:8], Os[1][:, :, 64], r=[obk[1]], w=[rk])
                k.ts("dve", rc[:, 4:8], rc[:, 4:8], lams[:, 3:4], ALU.mult, r=[rk, "lams"], w=[rk])
                k.tt("dve", o1[ri][:], Os[0][:, :, 0:64], rc[:, 0:4].unsqueeze(2).to_broadcast([128, 4, 64]), ALU.mult,
                     r=[obk[0], rk], w=["o1_%d" % ri])
                k.tt("dve", o2[ri][:], Os[1][:, :, 0:64], rc[:, 4:8].unsqueeze(2).to_broadcast([128, 4, 64]), ALU.mult,
                     r=[obk[1], rk], w=["o2_%d" % ri])
                k.tt("pool", o1[ri][:], o1[ri][:], o2[ri][:], ALU.add, r=["o1_%d" % ri, "o2_%d" % ri], w=["o1_%d" % ri])
                k.tt("pool", o2[ri][:], o1[ri][:], o1[ri][:], ALU.mult, r=["o1_%d" % ri], w=["o2_%d" % ri])
                k.red("dve", rc[:, 0:4], o2[ri][:], r=["o2_%d" % ri], w=[rk])
                k.act(rc[:, 0:4], rc[:, 0:4], AF.Sqrt, bias=EPS, scale=1.0 / 64, r=[rk], w=[rk])
                k.recip(rc[:, 0:4], rc[:, 0:4], r=[rk], w=[rk])
                k.tt("dve", o1[ri][:], o1[ri][:], rc[:, 0:4].unsqueeze(2).to_broadcast([128, 4, 64]), ALU.mult,
                     r=["o1_%d" % ri, rk], w=["o1_%d" % ri])
                k.tt("pool", A.ob[ri][:], o1[ri][:], subg[:].unsqueeze(1).to_broadcast([128, 4, 64]), ALU.mult,
                     r=["o1_%d" % ri, "subg"], w=["ob%d" % ri])
                k.dma(R.yscr[g * TG:(g + 1) * TG, h * 64:(h + 1) * 64].rearrange("(a p) e -> p a e", p=128),
                      A.ob[ri][:], r=["ob%d" % ri], w=[("yB", g)], slow=True)
        while jobs and jn[0] < len(jobs):
            pf_job(jn[0])
            jn[0] += 1
        P.barrier()


_CACHE = {}


def kernel(**inputs):
    if "prog" not in _CACHE:
        _CACHE["prog"] = build()
    nc, P = _CACHE["prog"]
    consts = make_consts()
    weights = {k: np.ascontiguousarray(np.asarray(inputs[k], dtype=np.float32)) for k in WEIGHT_SHAPES}
    x = np.asarray(inputs["x"], dtype=np.float32)
    c = np.asarray(inputs["c"], dtype=np.float32)
    in_maps = []
    for b in range(8):
        m = {"x": np.ascontiguousarray(x[b]), "c": np.ascontiguousarray(c[b:b + 1])}
        m.update(weights)
        m.update(consts)
        in_maps.append(m)
    res = run_bass_kernel_spmd(nc, in_maps, core_ids=list(range(8)))
    return np.stack([np.asarray(r["out"], dtype=np.float32) for r in res.results], axis=0)
```

```python
import math
import os
import numpy as np
import concourse.bass as bass
import concourse.mybir as mybir
from concourse.bass_utils import run_bass_kernel_spmd
from contextlib import ExitStack

F32 = mybir.dt.float32
BF16 = mybir.dt.bfloat16
AF = mybir.ActivationFunctionType
ALU = mybir.AluOpType
AX = mybir.AxisListType

S = 4096
D = 1024
L = 4
NIN = 2948
DFF = 2816
NG = 8
TG = 512
EPS = 1e-6
ALPHA = math.exp(-0.5)

ENGS = ("pe", "act", "dve", "pool", "sp")
EPOCH = 30000


class Prog:
    def __init__(self, nc, es):
        self.nc = nc
        self.es = es
        self.q = {e: [] for e in ENGS}
        self.cnt = {e: 0 for e in ENGS}
        self.epoch = {e: 0 for e in ENGS}
        self.sems = {}
        self.seen = {e: {} for e in ENGS}
        self.res_w = {}
        self.res_r = {}
        self.dma_val = {}
        self.n_inst = 0
        self.rr = 0

    def _sem(self, key):
        if key not in self.sems:
            self.sems[key] = self.es.enter_context(self.nc.semaphore("s_" + "_".join(str(k) for k in key)))
        return self.sems[key]

    def _deps(self, eng, reads, writes, extra=()):
        need = {}

        def add(ev):
            if ev is None:
                return
            k, v = ev
            if eng == "pe" and k[0] == "pe":
                return
            if need.get(k, 0) < v:
                need[k] = v
        for r in reads:
            add(self.res_w.get(r))
        for w in writes:
            add(self.res_w.get(w))
            for ev in self.res_r.get(w, ()):
                add(ev)
        for ev in extra:
            add(ev)
        waits = []
        for k, v in need.items():
            if self.seen[eng].get(k, 0) >= v:
                continue
            self.seen[eng][k] = v
            waits.append((k, v))
        return waits

    def _commit(self, ev, reads, writes):
        for r in reads:
            lst = self.res_r.setdefault(r, [])
            lst.append(ev)
            if len(lst) > 64:
                mx = {}
                for k, v in lst:
                    if mx.get(k, 0) < v:
                        mx[k] = v
                self.res_r[r] = list(mx.items())
        for w in writes:
            self.res_w[w] = ev
            self.res_r[w] = []

    @staticmethod
    def _is_psum(r):
        return (isinstance(r, str) and r.startswith("bank")) or (isinstance(r, tuple) and r[0] == "hb")

    def op(self, eng, fn, reads=(), writes=()):
        pr = [r for r in reads if self._is_psum(r)]
        if pr:
            writes = list(writes) + pr
        waits = self._deps(eng, reads, writes)
        if self.cnt[eng] >= EPOCH:
            self.epoch[eng] += 1
            self.cnt[eng] = 0
        self.cnt[eng] += 1
        key = (eng, self.epoch[eng])
        ev = (key, self.cnt[eng])
        self.q[eng].append((waits, fn, key, 1))
        self._commit(ev, reads, writes)
        self.n_inst += 1
        return ev

    def dma(self, queue, pairs, reads=(), writes=(), sem=None):
        if sem is None:
            sem = ("dma", "rr%d" % (self.rr % 20))
            self.rr += 1
        key = sem
        prev = self.dma_val.get(key, 0)
        extra = [(key, prev)] if prev > 0 else []
        waits = self._deps(queue, reads, writes, extra)
        val = prev
        for i, pr in enumerate(pairs):
            out_ap, in_ap = pr[0], pr[1]
            kw = pr[2] if len(pr) > 2 else {}
            val += 16

            def fn(e, out_ap=out_ap, in_ap=in_ap, kw=kw):
                return e.dma_start(out=out_ap, in_=in_ap, **kw)
            self.q[queue].append((waits if i == 0 else [], fn, key, 16))
            self.n_inst += 1
        self.dma_val[key] = val
        ev = (key, val)
        self._commit(ev, reads, writes)
        return ev

    def barrier(self):
        evs = []
        for e in ENGS:
            for ep in range(self.epoch[e] + 1):
                k = (e, ep)
                v = self.cnt[e] if ep == self.epoch[e] else EPOCH
                if v > 0:
                    evs.append((k, v))
        for k, v in self.dma_val.items():
            evs.append((k, v))
        for e in ENGS:
            waits = []
            for k, v in evs:
                if self.seen[e].get(k, 0) < v:
                    self.seen[e][k] = v
                    waits.append((k, v))
            if waits:
                self.q[e].append((waits, None, None, 0))
        self.res_w = {}
        self.res_r = {}

    def emit(self):
        nc = self.nc
        for e in ENGS:
            for (waits, fn, key, inc) in self.q[e]:
                for k, v in waits:
                    self._sem(k)
                if key is not None:
                    self._sem(key)
        block = self.es.enter_context(nc.Block())
        engmap = {"pe": block.tensor, "act": block.scalar, "dve": block.vector, "pool": block.gpsimd,
                  "sp": block.sync}
        for e in ENGS:
            items = self.q[e]

            def body(eng, items=items):
                for (waits, fn, key, inc) in items:
                    for k, v in waits:
                        eng.wait_ge(self.sems[k], v)
                    if fn is not None:
                        ins = fn(eng)
                        ins.then_inc(self.sems[key], inc)
            engmap[e](body)


class K:
    def __init__(self, P):
        self.P = P
        self._rot = {}

    def rot(self, name, n):
        i = self._rot.get(name, 0)
        self._rot[name] = i + 1
        return i % n

    def mm(self, out, lhsT, rhs, start=True, stop=True, r=(), w=(), sgc=False):
        if sgc:
            return self.P.op("pe", lambda e: e.matmul(out, lhsT=lhsT, rhs=rhs, start=start, stop=stop, skip_group_check=True), r, w)
        return self.P.op("pe", lambda e: e.matmul(out, lhsT=lhsT, rhs=rhs, start=start, stop=stop), r, w)

    def tr(self, out, in_, ident, r=(), w=()):
        return self.P.op("pe", lambda e: e.transpose(out=out, in_=in_, identity=ident), r, w)

    def act(self, out, in_, func, bias=None, scale=None, accum_out=None, r=(), w=(), eng="act"):
        kw = {}
        if bias is not None:
            kw["bias"] = bias
        if scale is not None:
            kw["scale"] = scale
        if accum_out is not None:
            kw["accum_out"] = accum_out
        return self.P.op("act", lambda e: e.activation(out=out, in_=in_, func=func, **kw), r, w)

    def copy(self, eng, out, in_, r=(), w=()):
        if eng == "act":
            return self.P.op("act", lambda e: e.copy(out=out, in_=in_), r, w)
        return self.P.op(eng, lambda e: e.tensor_copy(out=out, in_=in_), r, w)

    def tt(self, eng, out, in0, in1, op, r=(), w=()):
        return self.P.op(eng, lambda e: e.tensor_tensor(out=out, in0=in0, in1=in1, op=op), r, w)

    def ts(self, eng, out, in0, s1, op0, s2=None, op1=None, r=(), w=()):
        if op1 is None:
            return self.P.op(eng, lambda e: e.tensor_scalar(out=out, in0=in0, scalar1=s1, scalar2=None, op0=op0), r, w)
        return self.P.op(eng, lambda e: e.tensor_scalar(out=out, in0=in0, scalar1=s1, scalar2=s2, op0=op0, op1=op1), r, w)

    def stt(self, eng, out, in0, scalar, in1, op0, op1, r=(), w=()):
        return self.P.op(eng, lambda e: e.scalar_tensor_tensor(out=out, in0=in0, scalar=scalar, in1=in1, op0=op0, op1=op1), r, w)

    def red(self, eng, out, in_, op=ALU.add, r=(), w=()):
        return self.P.op(eng, lambda e: e.tensor_reduce(out=out, in_=in_, axis=AX.X, op=op), r, w)

    def recip(self, out, in_, r=(), w=()):
        return self.P.op("dve", lambda e: e.reciprocal(out=out, in_=in_), r, w)

    def memset(self, eng, ap, val, w=()):
        return self.P.op(eng, lambda e: e.memset(ap, val), (), w)

    def scan(self, out, d0, d1, initial, op0, op1, r=(), w=()):
        return self.P.op("dve", lambda e: e.tensor_tensor_scan(out=out, data0=d0, data1=d1, initial=initial, op0=op0, op1=op1), r, w)

    def dma(self, out, in_, r=(), w=(), q="sp", slow=False, sem=None):
        kw = {"allow_slow_non_contiguous": True} if slow else {}
        return self.P.dma(q, [(out, in_, kw)], r, w, sem=sem)


def make_consts():
    c = {}
    c["ident"] = np.eye(128, dtype=np.float32)
    bo64 = np.zeros((128, 128), np.float32)
    bo64[:64, :64] = 1
    bo64[64:, 64:] = 1
    c["bo64"] = bo64
    bo32 = np.zeros((128, 128), np.float32)
    for i in range(4):
        bo32[i * 32:(i + 1) * 32, i * 32:(i + 1) * 32] = 1
    c["bo32"] = bo32
    prot = np.zeros((128, 128), np.float32)
    for b in range(4):
        for d in range(16):
            prot[b * 32 + d + 16, b * 32 + d] = -1.0
            prot[b * 32 + d, b * 32 + d + 16] = 1.0
    c["prot"] = prot
    inv = 1.0 / (10000.0 ** (np.arange(0, 32, 2, dtype=np.float32) / 32.0))
    ang = np.arange(S, dtype=np.float32)[:, None] * inv[None, :]
    cos = np.cos(ang).astype(np.float32).T
    sin = np.sin(ang).astype(np.float32).T
    c["cosT"] = np.ascontiguousarray(np.tile(cos, (8, 1)))
    c["sinT"] = np.ascontiguousarray(np.tile(sin, (8, 1)))
    k = np.arange(128)[:, None]
    q = np.arange(128)[None, :]
    c["negmask"] = np.where(k > q, -1e30, 0.0).astype(np.float32)
    c["cmask"] = ((k // 64) <= (q // 64)).astype(np.float32)
    c["triu"] = (k <= q).astype(np.float32)
    k6 = np.arange(64)[:, None]
    q6 = np.arange(64)[None, :]
    m64 = np.zeros((64, 3, 64), np.float32)
    m64[:, 0, :] = (k6 < q6)
    m64[:, 1, :] = (k6 <= q6)
    m64[:, 2, :] = (k6 > q6)
    c["m64"] = m64
    sel = np.zeros((4, 4, 128), np.float32)
    for h in range(4):
        sel[h, h, :] = 1.0
    c["sel"] = sel.transpose(1, 0, 2).copy()
    return c


CONST_SHAPES = {"ident": [128, 128], "bo64": [128, 128], "bo32": [128, 128], "prot": [128, 128],
                "cosT": [128, S], "sinT": [128, S], "negmask": [128, 128], "cmask": [128, 128],
                "triu": [128, 128], "m64": [64, 3, 64], "sel": [4, 4, 128]}

WEIGHT_SHAPES = {
    'ada_w': [L, D, 6 * D], 'ada_b': [L, 6 * D], 'norm1_g': [L, D], 'norm2_g': [L, D],
    'w_in': [L, D, NIN], 'w_out': [L, D, D],
    'rw_mu': [L, 896], 'rw_w0': [L, 256], 'rw_w_up': [L, 32, 256], 'rw_a0': [L, 256], 'rw_a_up': [L, 32, 256],
    'rw_g_up': [L, 64, 256], 'rw_k_k': [L, 256], 'rw_k_a': [L, 256], 'rw_r_k': [L, 4, 64],
    'rw_ln_g': [L, 256], 'rw_ln_b': [L, 256],
    'df_lam_q1': [L, 32], 'df_lam_k1': [L, 32], 'df_lam_q2': [L, 32], 'df_lam_k2': [L, 32],
    'df_q_g': [L, 32], 'df_k_g': [L, 32], 'df_sub_g': [L, 64],
    'sg_w': [L, 4, 128, 128], 'sg_b': [L, 4, 128], 'sg_ln_g': [L, 256], 'sg_ln_b': [L, 256],
    'fx_q_g': [L, 64], 'fx_k_g': [L, 64], 'fx_f_b': [L, 4],
    'ffn_up': [L, D, 2 * DFF], 'ffn_conv': [L, 3, 2 * DFF], 'ffn_conv_b': [L, 2 * DFF], 'ffn_down': [L, DFF, D],
}


class Ctx:
    pass


def build(layers=(0, 1, 2, 3), phases=("P1", "A", "B", "D", "P3"), debug=False, y_in=False):
    nc = bass.Bass("TRN2", target_bir_lowering=False)
    dkind = "ExternalOutput" if debug else "Internal"
    I = {}

    def din(name, shape):
        I[name] = nc.dram_tensor(name, list(shape), F32, kind="ExternalInput").ap()
    din("x", [S, D])
    din("c", [1, D])
    for k, shp in WEIGHT_SHAPES.items():
        din(k, shp)
    for k, shp in CONST_SHAPES.items():
        din(k, shp)
    out = nc.dram_tensor("out", [S, D], F32, kind="ExternalOutput").ap()

    def dscr(name, shape, dt, kind=None):
        return nc.dram_tensor(name, list(shape), dt, kind=kind or dkind).ap()
    R = Ctx()
    R.xres = dscr("xres", [S, D], F32)
    R.pmA = dscr("pmA", [896, S], F32)
    R.qTB = dscr("qTB", [256, S], BF16)
    R.kTB = dscr("kTB", [256, S], BF16)
    R.vB = dscr("vB", [S, 260], BF16)
    R.qTD = dscr("qTD", [256, S], BF16)
    R.kTD = dscr("kTD", [256, S], BF16)
    R.vD = dscr("vD", [S, 260], BF16)
    R.Ffm = dscr("Ffm", [4, S], F32)
    if y_in:
        R.yscr = nc.dram_tensor("yscr_in", [S, 768], F32, kind="ExternalInput").ap()
        R.yTA = nc.dram_tensor("yTA_in", [256, S], F32, kind="ExternalInput").ap()
    else:
        R.yscr = dscr("yscr", [S, 768], BF16)
        R.yTA = dscr("yTA", [256, S], BF16)
    R.upbf = dscr("upbf", [L, 11, 128, 8 * 512], BF16, kind="Internal")
    R.dnbf = dscr("dnbf", [L, 128, 22 * 1024], BF16, kind="Internal")

    with ExitStack() as es:
        P = Prog(nc, es)
        k = K(P)
        G = Ctx()
        G.nc, G.P, G.k, G.I, G.R, G.out = nc, P, k, I, R, out
        G.y_in = y_in
        G.dbg_x1 = nc.dram_tensor("dbg_x1", [S, D], F32, kind="ExternalOutput").ap() if debug else None

        uid = [0]

        def sb(stack, name, shape, dt=F32):
            uid[0] += 1
            return stack.enter_context(nc.sbuf_tensor("%s_u%d" % (name, uid[0]), list(shape), dt))
        G.sb = sb
        G.banks = [es.enter_context(nc.psum_tensor("bank%d" % i, [128, 512], F32)) for i in range(8)]
        G.ident = sb(es, "ident", [128, 128])
        G.identb = sb(es, "identb", [128, 128], BF16)
        G.ones_row = sb(es, "ones_row", [1, 128])
        G.condB = sb(es, "condB", [128, 8, 128])
        G.modB = sb(es, "modB", [128, 6 * D])
        k.dma(G.ident[:], I["ident"], w=["ident"])
        k.copy("dve", G.identb[:], G.ident[:], r=["ident"], w=["identb"])
        k.memset("dve", G.ones_row[:], 1.0, w=["ones_row"])
        with ExitStack() as s0:
            cT = sb(s0, "cT", [128, 8])
            cS = sb(s0, "cS", [128, 8])
            k.dma(cT[:], I["c"].rearrange("o (kc p) -> p (o kc)", p=128), w=["cT"], slow=True)
            k.act(cS[:], cT[:], AF.Silu, r=["cT"], w=["cS"])
            k.copy("dve", G.condB[:], cS[:].unsqueeze(2).to_broadcast([128, 8, 128]), r=["cS"], w=["condB"])
            P.barrier()

        for li, l in enumerate(layers):
            xsrc = I["x"] if li == 0 else R.xres
            xdst = out if li == len(layers) - 1 else R.xres
            G.prep_in_B = ("P3" in phases) and ("B" in phases)
            if "P3" in phases and not G.prep_in_B:
                prep_ffn(G, l)
            layer_setup(G, l)
            if "P1" in phases:
                phase_p1(G, l, xsrc)
            if "A" in phases:
                phase_A(G, l)
            if "B" in phases:
                phase_B(G, l)
            if "D" in phases:
                phase_D(G, l)
            if "P3" in phases:
                phase_p3(G, l, xsrc, xdst)
        P.barrier()
        P.emit()
    return nc, P


def prep_ffn(G, l):
    nc, P, k, I, R = G.nc, G.P, G.k, G.I, G.R
    with ExitStack() as s:
        st = [G.sb(s, "pf_st%d" % i, [128, 8, 512]) for i in range(2)]
        sbf = [G.sb(s, "pf_bf%d" % i, [128, 8, 512], BF16) for i in range(2)]
        up = I["ffn_up"][l].rearrange("(kc p) n -> p kc n", p=128)
        dn = I["ffn_down"][l].rearrange("(fc p) d -> p fc d", p=128)
        engs = ["dve", "act", "dve", "act", "pool"]
        jobs = []
        for u in range(11):
            jobs.append((up[:, :, u * 512:(u + 1) * 512], R.upbf[l, u].rearrange("p (kc n) -> p kc n", kc=8), 8, 512))
        for u in range(11):
            jobs.append((dn[:, 2 * u:2 * u + 2, :], R.dnbf[l][:, 2 * u * 1024:(2 * u + 2) * 1024].rearrange("p (a d) -> p a d", a=2), 2, 1024))

        def load(i):
            src, dst, a, b = jobs[i]
            k.dma(st[i % 2][:].rearrange("p a b -> p (a b)")[:, 0:a * b].rearrange("p (a b) -> p a b", a=a), src,
                  w=["pf_st%d" % (i % 2)])
        load(0)
        load(1)
        for i in range(len(jobs)):
            src, dst, a, b = jobs[i]
            sv = st[i % 2][:].rearrange("p a b -> p (a b)")[:, 0:a * b]
            bv = sbf[i % 2][:].rearrange("p a b -> p (a b)")[:, 0:a * b]
            k.copy(engs[i % 5], bv, sv, r=["pf_st%d" % (i % 2)], w=["pf_bf%d" % (i % 2)])
            k.dma(dst, bv.rearrange("p (a b) -> p a b", a=a), r=["pf_bf%d" % (i % 2)], w=[("ffnw", l)])
            if i + 2 < len(jobs):
                load(i + 2)
        P.barrier()


def layer_setup(G, l):
    nc, P, k, I, R = G.nc, G.P, G.k, G.I, G.R
    banks = G.banks
    with ExitStack() as s:
        aw = [G.sb(s, "ls_aw%d" % i, [128, 8, 512]) for i in range(2)]
        rows = G.sb(s, "ls_rows", [1, 8 * D])
        k.dma(rows[:, 0:6 * D], I["ada_b"][l:l + 1, :], w=["ls_rows_b"])
        k.dma(rows[:, 6 * D:7 * D], I["norm1_g"][l:l + 1, :], w=["ls_rows_g"])
        k.dma(rows[:, 7 * D:8 * D], I["norm2_g"][l:l + 1, :], w=["ls_rows_g"])
        awv = I["ada_w"][l].rearrange("(kc p) n -> p kc n", p=128)
        for cc in range(12):
            b = cc % 2
            k.dma(aw[b][:], awv[:, :, cc * 512:(cc + 1) * 512], w=["ls_aw%d" % b])
            bk = banks[b]
            for kc in range(8):
                k.mm(bk[:], G.condB[:, kc, :], aw[b][:, kc, :], start=(kc == 0), stop=False,
                     r=["condB", "ls_aw%d" % b], w=["bank%d" % b])
            k.mm(bk[:], G.ones_row[0:1, :], rows[0:1, cc * 512:(cc + 1) * 512], start=False, stop=True,
                 r=["ones_row", "ls_rows_b"], w=["bank%d" % b])
            k.copy("act" if cc % 2 else "dve", G.modB[:, cc * 512:(cc + 1) * 512], bk[:], r=["bank%d" % b], w=["modB"])
        for gi, (goff, slot) in enumerate(((6 * D, 1 * D), (7 * D, 4 * D))):
            for hf in range(2):
                b = 2 + hf
                k.mm(banks[b][:], G.ones_row[0:1, :], rows[0:1, goff + hf * 512: goff + (hf + 1) * 512],
                     r=["ones_row", "ls_rows_g"], w=["bank%d" % b])
                sl = G.modB[:, slot + hf * 512: slot + (hf + 1) * 512]
                k.stt("dve", sl, sl, 1.0, banks[b][:], ALU.add, ALU.mult, r=["modB", "bank%d" % b], w=["modB"])
        P.barrier()


def norm_mod_T(G, xt, goff, shoff, T):
    k = G.k
    banks = G.banks
    for j in range(4):
        k.act(T.tmp[:], xt[:, j, :], AF.Square, r=["xt"], w=["tmp"])
        k.red("dve", T.ss[:, j:j + 1], T.tmp[:], r=["tmp"], w=["ss"])
    k.act(T.rstd[:], T.ss[:], AF.Sqrt, bias=EPS, scale=1.0 / D, r=["ss"], w=["rstd"])
    k.recip(T.rstd[:], T.rstd[:], r=["rstd"], w=["rstd"])
    for j in range(4):
        k.stt("dve", T.tmp[:], xt[:, j, :], T.rstd[:, j:j + 1], G.modB[:, goff:goff + D], ALU.mult, ALU.mult,
              r=["xt", "rstd", "modB"], w=["tmp"])
        k.tt("dve", T.hb[:, j, :], T.tmp[:], G.modB[:, shoff:shoff + D], ALU.add, r=["tmp", "modB"], w=["hb"])
    for j in range(4):
        b = 5 + (j % 2)
        pv = banks[b][:].bitcast(BF16).rearrange("p (a t) -> p a t", a=8)
        for kc in range(8):
            k.tr(pv[:, kc, :], T.hb[:, j, kc * 128:(kc + 1) * 128], G.identb[:], r=["hb", "identb"], w=["bank%d" % b])
        k.copy("act" if j % 2 else "dve", T.hT[:, :, j * 128:(j + 1) * 128], pv, r=["bank%d" % b], w=[getattr(T, "hTk", "hT")])


def bcast_load(G, tile_ap, row_ap, n, w):
    G.k.dma(tile_ap, row_ap.to_broadcast([128, n]), w=w, slow=True)


def phase_p1(G, l, xsrc):
    nc, P, k, I, R = G.nc, G.P, G.k, G.I, G.R
    banks = G.banks
    with ExitStack() as s:
        sb = lambda name, shape, dt=F32: G.sb(s, name, shape, dt)
        T = Ctx()
        w_in = sb("w_in", [128, 8, NIN], BF16)
        stg = [sb("p1_stg0", [128, 1474])] * 2
        xt = sb("xt", [128, 4, D])
        T.sqj = sb("sqj", [128, D], BF16)
        T.ss = sb("ss", [128, 4])
        T.rstd = sb("rstd", [128, 4])
        T.tmp = sb("tmp", [128, D])
        T.hb = sb("hb", [128, 4, D], BF16)
        hTs = [sb("hT%d" % i, [128, 8, 512], BF16) for i in range(2)]
        fA = [sb("fA%d" % i, [128, 512]) for i in range(3)]
        fB = [sb("fB%d" % i, [128, 512]) for i in range(3)]
        fC = [sb("fC%d" % i, [128, 512]) for i in range(3)]
        obf = [sb("obf%d" % i, [128, 512], BF16) for i in range(3)]
        xnb = [sb("xnb%d" % i, [128, 512], BF16) for i in range(2)]
        paA = sb("paA", [128, 7, 513])
        pmo = [sb("pmo%d" % i, [128, 512]) for i in range(2)]
        cs = [sb("cos%d" % i, [128, 512]) for i in range(2)]
        sn = [sb("sin%d" % i, [128, 512]) for i in range(2)]
        bo64 = sb("bo64", [128, 128])
        bo32 = sb("bo32", [128, 128])
        protf = sb("protf", [128, 128])
        protb = sb("protb", [128, 128], BF16)
        bo64b = sb("bo64b", [128, 128], BF16)
        bo32b = sb("bo32b", [128, 128], BF16)
        sqb = [sb("sqb%d" % i, [128, 512], BF16) for i in range(2)]
        gcol = sb("gcol", [128, 8])
        mu = sb("mu", [128, 7])
        negfb = sb("negfb", [4, 1])
        ones4 = sb("ones4", [4, 512])
        Fg = [sb("Fg%d" % i, [4, 512]) for i in range(2)]
        f4 = sb("f4", [4, 512])
        vt = [sb("vt%d" % i, [128, 4, 65], BF16) for i in range(4)]
        sgw = sb("sgw", [128, 4, 128])
        WgT = sb("WgT", [128, 4, 128], BF16)
        triu = sb("triu", [128, 128])
        sgbT = sb("sgbT", [128, 4])
        lnCg = sb("lnCg", [128, 256])
        lnCb = sb("lnCb", [128, 256])
        glC = [sb("glC%d" % i, [128, 512]) for i in range(2)]
        stC = [sb("stC%d" % i, [128, 8]) for i in range(2)]
        tmpc = [sb("tmpc%d" % i, [128, 256]) for i in range(2)]
        vnb = [sb("vnb%d" % i, [128, 256], BF16) for i in range(2)]
        ycb = [sb("ycb%d" % i, [128, 256], BF16) for i in range(2)]

        for kc in range(8):
            for hf in range(2):
                i = 0
                k.dma(stg[i][:], I["w_in"][l, kc * 128:(kc + 1) * 128, hf * 1474:(hf + 1) * 1474], w=["p1_stg%d" % i])
                k.copy(("dve", "pool", "act")[(kc * 2 + hf) % 3], w_in[:, kc, hf * 1474:(hf + 1) * 1474], stg[i][:],
                       r=["p1_stg%d" % i], w=["w_in"])
        k.dma(bo64[:], I["bo64"], w=["bo64"])
        k.dma(bo32[:], I["bo32"], w=["bo32"])
        k.dma(protf[:], I["prot"], w=["protf"])
        k.copy("dve", protb[:], protf[:], r=["protf"], w=["protb"])
        k.copy("dve", bo64b[:], bo64[:], r=["bo64"], w=["bo64b"])
        k.copy("dve", bo32b[:], bo32[:], r=["bo32"], w=["bo32b"])
        k.dma(triu[:], I["triu"], w=["triu"])
        for rep in range(2):
            k.dma(gcol[rep * 64:(rep + 1) * 64, 0:1], I["fx_q_g"][l].rearrange("(d o) -> d o", o=1), w=["gcol"], slow=True)
            k.dma(gcol[rep * 64:(rep + 1) * 64, 1:2], I["fx_k_g"][l].rearrange("(d o) -> d o", o=1), w=["gcol"], slow=True)
        for rep in range(4):
            k.dma(gcol[rep * 32:(rep + 1) * 32, 2:3], I["df_q_g"][l].rearrange("(d o) -> d o", o=1), w=["gcol"], slow=True)
            k.dma(gcol[rep * 32:(rep + 1) * 32, 3:4], I["df_k_g"][l].rearrange("(d o) -> d o", o=1), w=["gcol"], slow=True)
        k.ts("dve", gcol[:, 0:1], gcol[:, 0:1], 0.125, ALU.mult, r=["gcol"], w=["gcol"])
        k.ts("dve", gcol[:, 2:3], gcol[:, 2:3], 32.0 ** -0.5, ALU.mult, r=["gcol"], w=["gcol"])
        k.dma(mu[:], I["rw_mu"][l].rearrange("(c p) -> p c", p=128), w=["mu"], slow=True)
        k.dma(negfb[:], I["fx_f_b"][l].rearrange("(h o) -> h o", o=1), w=["negfb"], slow=True)
        k.ts("dve", negfb[:], negfb[:], -1.0, ALU.mult, r=["negfb"], w=["negfb"])
        k.memset("dve", ones4[:], 1.0, w=["ones4"])
        k.memset("pool", paA[:], 0.0, w=["paA%d" % i for i in range(7)])
        for i in range(4):
            k.memset("pool", vt[i][:], 1.0, w=["vt%d" % i])
        k.dma(sgw[:], I["sg_w"][l].rearrange("g i j -> i g j"), w=["sgw"])
        for g in range(4):
            k.tr(banks[0][:, g * 128:(g + 1) * 128], sgw[:, g, :], G.ident[:], r=["sgw", "ident"], w=["bank0"])
        k.tt("dve", WgT[:], banks[0][:].rearrange("p (g i) -> p g i", g=4),
             triu[:].unsqueeze(1).to_broadcast([128, 4, 128]), ALU.mult, r=["bank0", "triu"], w=["WgT"])
        k.dma(sgbT[:], I["sg_b"][l].rearrange("g i -> i g"), w=["sgbT"], slow=True)
        bcast_load(G, lnCg[:], I["sg_ln_g"][l:l + 1, :], 256, ["lnCg"])
        bcast_load(G, lnCb[:], I["sg_ln_b"][l:l + 1, :], 256, ["lnCb"])

        cur = Ctx()

        def fm_mm(col0, ncols, bk):
            for kc in range(8):
                k.mm(banks[bk][0:ncols, :], w_in[:, kc, col0:col0 + ncols], cur.hT[:, kc, :], start=(kc == 0), stop=(kc == 7),
                     r=["w_in", cur.hTk], w=["bank%d" % bk])

        def load_norm(g):
            tsl_ = slice(g * TG, (g + 1) * TG)
            k.dma(xt[:], xsrc[tsl_, :].rearrange("(j p) d -> p j d", p=128), r=[("x", g)], w=["xt"])
            T.hT = hTs[g % 2]
            T.hTk = "hT%d" % (g % 2)
            norm_mod_T(G, xt, 1 * D, 0, T)
        load_norm(0)

        for g in range(NG):
            tsl = slice(g * TG, (g + 1) * TG)
            k.dma(cs[g % 2][:], I["cosT"][:, tsl], w=["cos%d" % (g % 2)])
            k.dma(sn[g % 2][:], I["sinT"][:, tsl], w=["sin%d" % (g % 2)])
            cur.hT = hTs[g % 2]
            cur.hTk = "hT%d" % (g % 2)
            for ci in range(7):
                bk = k.rot("p1bank", 3)
                fm_mm(ci * 128, 128, bk)
                k.copy("pool", paA[:, ci, 0:1], paA[:, ci, 512:513], r=["paA%d" % ci], w=["paA%d" % ci])
                k.copy("act", paA[:, ci, 1:513], banks[bk][:], r=["bank%d" % bk], w=["paA%d" % ci])
                i = k.rot("fA", 3)
                k.tt("pool", fA[i][:], paA[:, ci, 0:512], paA[:, ci, 1:513], ALU.subtract, r=["paA%d" % ci], w=["fA%d" % i])
                o = k.rot("pmo", 2)
                k.stt("dve", pmo[o][:], fA[i][:], mu[:, ci:ci + 1], paA[:, ci, 1:513], ALU.mult, ALU.add,
                      r=["fA%d" % i, "mu", "paA%d" % ci], w=["pmo%d" % o])
                k.dma(R.pmA[ci * 128:(ci + 1) * 128, tsl], pmo[o][:], r=["pmo%d" % o], w=[("pmA", g)])
            if g + 1 < NG:
                load_norm(g + 1)
            for (mix, col0, gi, dst, rope) in (("B", 896, 2, R.qTB, True), ("B", 1152, 3, R.kTB, True),
                                               ("D", 2176, 0, R.qTD, False), ("D", 2432, 1, R.kTD, False)):
                for ci in range(2):
                    bk = k.rot("p1bank", 3)
                    fm_mm(col0 + ci * 128, 128, bk)
                    a = k.rot("sqb", 2)
                    k.act(sqb[a][:], banks[bk][:], AF.Square, r=["bank%d" % bk], w=["sqb%d" % a])
                    sbk = 3 + k.rot("p1sbank", 2)
                    k.mm(banks[sbk][:], bo32b[:] if mix == "B" else bo64b[:], sqb[a][:], r=["bo32b", "bo64b", "sqb%d" % a],
                         w=["bank%d" % sbk])
                    b = k.rot("fB", 3)
                    nd = 32.0 if mix == "B" else 64.0
                    k.act(fB[b][:], banks[sbk][:], AF.Ln, bias=EPS, scale=1.0 / nd, r=["bank%d" % sbk], w=["fB%d" % b])
                    k.act(fB[b][:], fB[b][:], AF.Exp, scale=-0.5, r=["fB%d" % b], w=["fB%d" % b])
                    o = k.rot("obf", 3)
                    if not rope:
                        k.stt("dve", obf[o][:], banks[bk][:], gcol[:, gi:gi + 1], fB[b][:], ALU.mult, ALU.mult,
                              r=["bank%d" % bk, "gcol", "fB%d" % b], w=["obf%d" % o])
                    else:
                        c_ = k.rot("fC", 3)
                        k.stt("dve", fC[c_][:], banks[bk][:], gcol[:, gi:gi + 1], fB[b][:], ALU.mult, ALU.mult,
                              r=["bank%d" % bk, "gcol", "fB%d" % b], w=["fC%d" % c_])
                        xb = k.rot("xnb", 2)
                        k.copy("act", xnb[xb][:], fC[c_][:], r=["fC%d" % c_], w=["xnb%d" % xb])
                        rbk = 3 + k.rot("p1sbank", 2)
                        k.mm(banks[rbk][:], protb[:], xnb[xb][:], r=["protb", "xnb%d" % xb], w=["bank%d" % rbk])
                        a2 = k.rot("fA", 3)
                        k.tt("pool", fA[a2][:], fC[c_][:], cs[g % 2][:], ALU.mult, r=["fC%d" % c_, "cos%d" % (g % 2)], w=["fA%d" % a2])
                        b2 = k.rot("fB", 3)
                        k.tt("dve", fB[b2][:], banks[rbk][:], sn[g % 2][:], ALU.mult, r=["bank%d" % rbk, "sin%d" % (g % 2)],
                             w=["fB%d" % b2])
                        k.tt("pool", obf[o][:], fA[a2][:], fB[b2][:], ALU.add, r=["fA%d" % a2, "fB%d" % b2], w=["obf%d" % o])
                    k.dma(dst[ci * 128:(ci + 1) * 128, tsl], obf[o][:], r=["obf%d" % o], w=[(mix + "qk", g)])
            bk = k.rot("p1bank", 3)
            fm_mm(2944, 4, bk)
            k.act(f4[:], banks[bk][0:4, :], AF.Exp, bias=negfb[:, 0:1], scale=-1.0, r=["bank%d" % bk, "negfb"], w=["f4"])
            k.act(f4[:], f4[:], AF.Ln, bias=1.0, scale=1.0, r=["f4"], w=["f4"])
            if g == 0:
                k.scan(Fg[0][:], ones4[:], f4[:], 0.0, ALU.mult, ALU.subtract, r=["ones4", "f4"], w=["Fg0"])
            else:
                k.scan(Fg[g % 2][:], ones4[:], f4[:], Fg[(g - 1) % 2][:, 511:512], ALU.mult, ALU.subtract,
                       r=["ones4", "f4", "Fg%d" % ((g - 1) % 2)], w=["Fg%d" % (g % 2)])
            k.dma(R.Ffm[:, tsl], Fg[g % 2][:], r=["Fg%d" % (g % 2)], w=[("Ffm", g)])
            for j in range(4):
                rows = slice(g * TG + j * 128, g * TG + (j + 1) * 128)
                for (mix, col0, dst) in (("B", 1408, R.vB), ("D", 2688, R.vD)):
                    bk = k.rot("p1bank", 3)
                    for kc in range(8):
                        k.mm(banks[bk][:, 0:256], cur.hT[:, kc, j * 128:(j + 1) * 128], w_in[:, kc, col0:col0 + 256],
                             start=(kc == 0), stop=(kc == 7), r=["w_in", cur.hTk], w=["bank%d" % bk])
                    vi = k.rot("vt", 4)
                    k.copy("act" if mix == "B" else "dve", vt[vi][:, :, 0:64], banks[bk][:, 0:256].rearrange("p (h e) -> p h e", h=4),
                           r=["bank%d" % bk], w=["vt%d" % vi])
                    k.dma(dst[rows, :], vt[vi][:].rearrange("p h e -> p (h e)"), r=["vt%d" % vi], w=[(mix + "v", g)])
                bk = k.rot("p1bank", 3)
                for kc in range(8):
                    k.mm(banks[bk][:], cur.hT[:, kc, j * 128:(j + 1) * 128], w_in[:, kc, 1664:2176],
                         start=(kc == 0), stop=(kc == 7), r=["w_in", cur.hTk], w=["bank%d" % bk])
                ci = k.rot("glC", 2)
                gl, st, tc, vb, yb = glC[ci], stC[ci], tmpc[ci], vnb[ci], ycb[ci]
                kk = "C%d" % ci
                k.act(gl[:], banks[bk][:], AF.Gelu, r=["bank%d" % bk], w=[kk + "gl"])
                k.red("dve", st[:, 0:1], gl[:, 256:512], r=[kk + "gl"], w=[kk + "st"])
                k.act(tc[:], gl[:, 256:512], AF.Square, r=[kk + "gl"], w=[kk + "tc"])
                k.red("dve", st[:, 1:2], tc[:], r=[kk + "tc"], w=[kk + "st"])
                k.ts("dve", st[:, 2:3], st[:, 0:1], 1.0 / 256, ALU.mult, r=[kk + "st"], w=[kk + "st"])
                k.tt("dve", st[:, 3:4], st[:, 2:3], st[:, 2:3], ALU.mult, r=[kk + "st"], w=[kk + "st"])
                k.stt("dve", st[:, 4:5], st[:, 1:2], 1.0 / 256, st[:, 3:4], ALU.mult, ALU.subtract, r=[kk + "st"], w=[kk + "st"])
                k.act(st[:, 5:6], st[:, 4:5], AF.Sqrt, bias=EPS, scale=1.0, r=[kk + "st"], w=[kk + "st"])
                k.recip(st[:, 5:6], st[:, 5:6], r=[kk + "st"], w=[kk + "st"])
                k.ts("dve", tc[:], gl[:, 256:512], st[:, 2:3], ALU.subtract, st[:, 5:6], ALU.mult, r=[kk + "gl", kk + "st"], w=[kk + "tc"])
                k.tt("pool", tc[:], tc[:], lnCg[:], ALU.mult, r=[kk + "tc", "lnCg"], w=[kk + "tc"])
                k.tt("pool", vb[:], tc[:], lnCb[:], ALU.add, r=[kk + "tc", "lnCb"], w=[kk + "vb"])
                sbk = 3 + k.rot("p1sbank", 2)
                for hg in range(4):
                    k.mm(banks[sbk][:, hg * 64:(hg + 1) * 64], WgT[:, hg, :], vb[:, hg * 64:(hg + 1) * 64],
                         r=["WgT", kk + "vb"], w=["bank%d" % sbk])
                k.tt("dve", tc[:].rearrange("p (h e) -> p h e", h=4), banks[sbk][:, 0:256].rearrange("p (h e) -> p h e", h=4),
                     sgbT[:].unsqueeze(2).to_broadcast([128, 4, 64]), ALU.add, r=["bank%d" % sbk, "sgbT", kk + "vb"], w=[kk + "tc"])
                k.tt("pool", yb[:], tc[:], gl[:, 0:256], ALU.mult, r=[kk + "tc", kk + "gl"], w=[kk + "yb"])
                if not G.y_in:
                    k.dma(R.yscr[rows, 256:512], yb[:], r=[kk + "yb"], w=[("yC", g)])
        P.barrier()


def phase_p3(G, l, xsrc, xdst):
    nc, P, k, I, R = G.nc, G.P, G.k, G.I, G.R
    banks = G.banks
    ydt = F32 if G.y_in else BF16
    with ExitStack() as s:
        sb = lambda name, shape, dt=F32: G.sb(s, name, shape, dt)
        T = Ctx()
        w_out = sb("w_out", [128, 8, D], BF16)
        xt = sb("xt3", [128, 4, D])
        T.sqj = sb("sqj3", [128, D], BF16)
        T.ss = sb("ss3", [128, 4])
        T.rstd = sb("rstd3", [128, 4])
        T.tmp = sb("tmp3", [128, D])
        T.hb = sb("hb3", [128, 4, D], BF16)
        T.hT = sb("hT3", [128, 8, 512], BF16)
        yT = T.hT
        actT = sb("actT", [128, 22, 512], BF16)
        upw = [sb("upw%d" % i, [128, 8, 512], BF16) for i in range(2)]
        dnw = sb("dnw", [128, 22, D], BF16)
        ub = [sb("ub%d" % i, [128, 514]) for i in range(3)]
        c1 = [sb("c1_%d" % i, [128, 512]) for i in range(3)]
        c2 = [sb("c2_%d" % i, [128, 512]) for i in range(3)]
        c3 = [sb("c3_%d" % i, [128, 512]) for i in range(2)]
        sgt = [sb("sgt%d" % i, [128, 512]) for i in range(2)]
        convw = sb("convw", [128, 44, 3])
        convb = sb("convb", [128, 44])
        carryF = sb("carryF", [128, 44, 2])

        for kc in range(8):
            k.dma(T.tmp[:], I["w_out"][l, kc * 128:(kc + 1) * 128, :], w=["tmp"])
            k.copy(("dve", "pool", "act")[kc % 3], w_out[:, kc, :], T.tmp[:], r=["tmp"], w=["w_out"])
        cwn = T.tmp[0:44, 0:512].rearrange("p (t c) -> p t c", t=4)
        for t in range(3):
            k.dma(cwn[:, t, :], I["ffn_conv"][l, t].rearrange("(cc p) -> cc p", p=128), w=["tmp"])
        k.dma(cwn[:, 3, :], I["ffn_conv_b"][l].rearrange("(cc p) -> cc p", p=128), w=["tmp"])
        for t in range(4):
            k.mm(banks[0][:, t * 44:(t + 1) * 44], cwn[:, t, :], G.ident[0:44, 0:44], r=["tmp", "ident"], w=["bank0"])
        k.copy("dve", convw[:], banks[0][:, 0:132].rearrange("p (t c) -> p c t", t=3), r=["bank0"], w=["convw"])
        k.copy("dve", convb[:], banks[0][:, 132:176], r=["bank0"], w=["convb"])
        k.memset("pool", carryF[:], 0.0, w=["carryF"])

        for g in range(NG):
            tsl = slice(g * TG, (g + 1) * TG)
            k.dma(xt[:], xsrc[tsl, :].rearrange("(j p) d -> p j d", p=128), r=[("x", g)], w=["xt"])
            k.dma(dnw[:].rearrange("p a d -> p (a d)"), R.dnbf[l], r=[("ffnw", l)], w=["dnw"])
            if G.y_in:
                for j in range(4):
                    k.dma(T.tmp[:, 0:768], R.yscr[g * TG + j * 128: g * TG + (j + 1) * 128, :], w=["tmp"])
                    k.copy("pool", T.hb[:, j, 0:768], T.tmp[:, 0:768], r=["tmp"], w=["hb"])
                for kc in range(2):
                    k.dma(c1[kc][:], R.yTA[kc * 128:(kc + 1) * 128, tsl], w=["c1_%d" % kc])
                    k.copy("act", yT[:, kc, :], c1[kc][:], r=["c1_%d" % kc], w=["hT"])
            else:
                k.dma(T.hb[:, :, 0:768], R.yscr[tsl, :].rearrange("(j p) c -> p j c", p=128),
                      r=[("yC", g), ("yB", g), ("yD", g)], w=["hb"])
                k.dma(yT[:, 0:2, :], R.yTA[:, tsl].rearrange("(kc p) t -> p kc t", p=128), r=[("yA", g)], w=["hT"])
            for j in range(4):
                b = 5 + (j % 2)
                pv = banks[b][:].bitcast(BF16).rearrange("p (a t) -> p a t", a=8)
                for kc in range(6):
                    k.tr(pv[:, kc, :], T.hb[:, j, kc * 128:(kc + 1) * 128], G.identb[:], r=["hb", "identb"], w=["bank%d" % b])
                k.copy("act" if j % 2 else "dve", yT[:, 2:8, j * 128:(j + 1) * 128], pv[:, 0:6, :], r=["bank%d" % b], w=["hT"])
            for j in range(4):
                for hf in range(2):
                    bk = k.rot("p3bank", 2)
                    for kc in range(8):
                        k.mm(banks[bk][:], yT[:, kc, j * 128:(j + 1) * 128], w_out[:, kc, hf * 512:(hf + 1) * 512],
                             start=(kc == 0), stop=(kc == 7), r=["hT", "w_out"], w=["bank%d" % bk])
                    ci = k.rot("c1", 2)
                    k.tt("dve", c1[ci][:], banks[bk][:], G.modB[:, 2 * D + hf * 512: 2 * D + (hf + 1) * 512], ALU.mult,
                         r=["bank%d" % bk, "modB"], w=["c1_%d" % ci])
                    k.tt("dve", xt[:, j, hf * 512:(hf + 1) * 512], xt[:, j, hf * 512:(hf + 1) * 512], c1[ci][:], ALU.add,
                         r=["xt", "c1_%d" % ci], w=["xt"])
            if G.dbg_x1 is not None:
                k.dma(G.dbg_x1[tsl, :].rearrange("(j p) d -> p j d", p=128), xt[:], r=["xt"], w=[("dbgx1", g)])
            norm_mod_T(G, xt, 4 * D, 3 * D, T)
            for u in range(11):
                wi = u % 2
                k.dma(upw[wi][:].rearrange("p a n -> p (a n)"), R.upbf[l, u], r=[("ffnw", l)], w=["upw%d" % wi])
                for sc in range(4):
                    cc = 4 * u + sc
                    bk = 2 + k.rot("p3ubank", 3)
                    for kc in range(8):
                        k.mm(banks[bk][:], upw[wi][:, kc, sc * 128:(sc + 1) * 128], T.hT[:, kc, :], start=(kc == 0), stop=(kc == 7),
                             r=["upw%d" % wi, "hT"], w=["bank%d" % bk])
                    ui = k.rot("ub", 3)
                    k.copy("pool", ub[ui][:, 0:2], carryF[:, cc, :], r=["carryF"], w=["ubc%d" % ui])
                    k.copy("act", ub[ui][:, 2:514], banks[bk][:], r=["bank%d" % bk], w=["ub%d" % ui])
                    k.copy("pool", carryF[:, cc, :], ub[ui][:, 512:514], r=["ub%d" % ui], w=["carryF"])
                    k.act(c1[ui][:], ub[ui][:, 2:514], AF.Identity, bias=convb[:, cc:cc + 1], scale=convw[:, cc, 2:3],
                          r=["ub%d" % ui, "convw", "convb"], w=["c1_%d" % ui])
                    k.stt("dve", c2[ui][:], ub[ui][:, 1:513], convw[:, cc, 1:2], c1[ui][:], ALU.mult, ALU.add,
                          r=["ub%d" % ui, "ubc%d" % ui, "convw", "c1_%d" % ui], w=["c2_%d" % ui])
                    if cc < 22:
                        k.stt("dve", actT[:, cc, :], ub[ui][:, 0:512], convw[:, cc, 0:1], c2[ui][:], ALU.mult, ALU.add,
                              r=["ub%d" % ui, "ubc%d" % ui, "convw", "c2_%d" % ui], w=["actT%d" % cc])
                    else:
                        cu = cc - 22
                        k.stt("dve", c3[ui % 2][:], ub[ui][:, 0:512], convw[:, cc, 0:1], c2[ui][:], ALU.mult, ALU.add,
                              r=["ub%d" % ui, "ubc%d" % ui, "convw", "c2_%d" % ui], w=["c3_%d" % (ui % 2)])
                        k.act(sgt[ui % 2][:], c3[ui % 2][:], AF.Silu, r=["c3_%d" % (ui % 2)], w=["sgt%d" % (ui % 2)])
                        k.tt("pool", actT[:, cu, :], actT[:, cu, :], sgt[ui % 2][:], ALU.mult, r=["actT%d" % cu, "sgt%d" % (ui % 2)],
                             w=["actT%d" % cu])
            aks = ["actT%d" % i for i in range(22)]
            for j in range(4):
                for hf in range(2):
                    bk = k.rot("p3bank", 2)
                    for cu in range(22):
                        k.mm(banks[bk][:], actT[:, cu, j * 128:(j + 1) * 128], dnw[:, cu, hf * 512:(hf + 1) * 512],
                             start=(cu == 0), stop=(cu == 21), r=["actT%d" % cu, "dnw"], w=["bank%d" % bk])
                    ci = k.rot("c1", 2)
                    k.tt("dve", c1[ci][:], banks[bk][:], G.modB[:, 5 * D + hf * 512: 5 * D + (hf + 1) * 512], ALU.mult,
                         r=["bank%d" % bk, "modB"], w=["c1_%d" % ci])
                    k.tt("dve", xt[:, j, hf * 512:(hf + 1) * 512], xt[:, j, hf * 512:(hf + 1) * 512], c1[ci][:], ALU.add,
                         r=["xt", "c1_%d" % ci], w=["xt"])
            k.dma(xdst[tsl, :].rearrange("(j p) d -> p j d", p=128), xt[:], r=["xt"], w=[("x", g)])
        P.barrier()


def phase_A(G, l):
    nc, P, k, I, R = G.nc, G.P, G.k, G.I, G.R
    banks = G.banks
    with ExitStack() as s:
        sb = lambda name, shape, dt=F32: G.sb(s, name, shape, dt)
        prm = sb("prm", [64, 7, 4])
        wup = sb("wup", [32, 256])
        aup = sb("aup", [32, 256])
        gup = sb("gup", [64, 256])
        ones64 = sb("ones64", [64, 64])
        ones64b = sb("ones64b", [64, 64], BF16)
        wupb = sb("wupb", [32, 256], BF16)
        aupb = sb("aupb", [32, 256], BF16)
        gupb = sb("gupb", [64, 256], BF16)
        wdb = sb("wdb", [32, 512], BF16)
        adb = sb("adb", [32, 512], BF16)
        gdb = sb("gdb", [64, 512], BF16)
        t1b = sb("t1b", [64, 4, 512], BF16)
        ones512 = sb("ones512", [64, 512])
        m64 = sb("m64", [64, 3, 64])
        Tst = sb("Tst", [64, 4, 64])
        big = lambda nm: sb(nm, [64, 4, 512])
        r_, k_, v_ = big("r_"), big("k_"), big("v_")
        lwt, at, gt, kkn, k2, b_, bonus, Yblk, t1, t2 = (big("lwt"), big("at"), big("gt"), big("kkn"), big("k2"), big("b_"),
                                                         big("bonus"), big("Yblk"), big("t1"), big("t2"))
        Gblk = sb("Gblk", [64, 4, 513])
        wd = sb("wd", [32, 512])
        ad = sb("ad", [32, 512])
        gd = sb("gd", [64, 512])
        yo = sb("yo", [64, 4, 512], BF16)
        sm = lambda nm: sb(nm, [64, 4, 64])
        Pc, Pp, eGn, eGp = sm("Pc"), sm("Pp"), sm("eGn"), sm("eGp")
        smb = lambda nm: sb(nm, [64, 4, 64], BF16)
        Bt, Kt = smb("Bt"), smb("Kt")
        Pj = [smb("Pj0"), smb("Pj1")]
        U, tS = sm("U"), sm("tS")
        Ub, Tb = smb("Ub"), smb("Tb")
        DB = []
        for q in range(2):
            d = {"eG": sm("eG%d" % q)}
            for nm_ in ("At", "Rt", "Btm", "Ktm", "Vtm", "LakT", "MrbT", "MrkT"):
                d[nm_] = smb("%s%d" % (nm_, q))
            d["PT"] = [smb("PT%d_%d" % (j, q)) for j in range(6)]
            DB.append(d)
        vb = sb("vb", [64, 4, 512], BF16)

        for n, nm in enumerate(("rw_w0", "rw_a0", "rw_k_k", "rw_k_a")):
            k.dma(prm[:, n, :], I[nm][l].rearrange("(h k) -> k h", k=64), w=["prm"], slow=True)
        k.dma(prm[:, 4, :], I["rw_r_k"][l].rearrange("h k -> k h"), w=["prm"], slow=True)
        k.dma(prm[:, 5, :], I["rw_ln_g"][l].rearrange("(h k) -> k h", k=64), w=["prm"], slow=True)
        k.dma(prm[:, 6, :], I["rw_ln_b"][l].rearrange("(h k) -> k h", k=64), w=["prm"], slow=True)
        k.dma(wup[:], I["rw_w_up"][l], w=["wup"])
        k.dma(aup[:], I["rw_a_up"][l], w=["aup"])
        k.dma(gup[:], I["rw_g_up"][l], w=["gup"])
        k.dma(m64[:], I["m64"], w=["m64"])
        k.memset("dve", ones64[:], 1.0, w=["ones64"])
        k.memset("dve", ones64b[:], 1.0, w=["ones64b"])
        k.copy("dve", wupb[:], wup[:], r=["wup"], w=["wupb"])
        k.copy("dve", aupb[:], aup[:], r=["aup"], w=["aupb"])
        k.copy("dve", gupb[:], gup[:], r=["gup"], w=["gupb"])
        k.memset("dve", ones512[:], 1.0, w=["ones512"])
        k.memset("pool", Tst[:], 0.0, w=["Tst"])
        k.memset("pool", Tb[:], 0.0, w=["Tb"])
        k.memset("pool", Gblk[:], 0.0, w=["Gblk"])

        def bc(col):
            return prm[:, col, :].unsqueeze(2).to_broadcast([64, 4, 512])

        def HB(b, half):
            return banks[b][0:64, half * 256:(half + 1) * 256].rearrange("p (h t) -> p h t", h=4)

        def hk(b, half):
            return ("hb", b)

        def fb(b):
            return [("hb", b)]
        allpm = [("pmA", g) for g in range(NG)]

        for g in range(NG):
            tsl = slice(g * TG, (g + 1) * TG)
            k.dma(r_[:], R.pmA[0:256, tsl].rearrange("(h k) t -> k h t", k=64), r=allpm, w=["r_"])
            k.dma(k_[:], R.pmA[256:512, tsl].rearrange("(h k) t -> k h t", k=64), r=allpm, w=["k_"])
            k.dma(v_[:], R.pmA[512:768, tsl].rearrange("(h k) t -> k h t", k=64), r=allpm, w=["v_"])
            k.copy("pool", vb[:], v_[:], r=["v_"], w=["vb"])
            k.dma(wd[:], R.pmA[768:800, tsl], r=allpm, w=["wd"])
            k.dma(ad[:], R.pmA[800:832, tsl], r=allpm, w=["ad"])
            k.dma(gd[:], R.pmA[832:896, tsl], r=allpm, w=["gd"])
            k.act(wdb[:], wd[:], AF.Tanh, r=["wd"], w=["wdb"])
            k.act(gdb[:], gd[:], AF.Sigmoid, r=["gd"], w=["gdb"])
            k.copy("pool", adb[:], ad[:], r=["ad"], w=["adb"])
            for h in range(4):
                k.mm(banks[h][0:64, :], wupb[:, h * 64:(h + 1) * 64], wdb[:], r=["wupb", "wdb"], w=fb(h))
                k.act(lwt[:, h, :], banks[h][0:64, :], AF.Sigmoid, bias=prm[:, 0, h:h + 1], scale=1.0, r=fb(h) + ["prm"], w=["lwt"])
            for h in range(4):
                k.mm(banks[4 + h][0:64, :], aupb[:, h * 64:(h + 1) * 64], adb[:], r=["aupb", "adb"], w=fb(4 + h))
                k.act(at[:, h, :], banks[4 + h][0:64, :], AF.Sigmoid, bias=prm[:, 1, h:h + 1], scale=1.0, r=fb(4 + h) + ["prm"], w=["at"])
            for h in range(4):
                k.mm(banks[h][0:64, :], gupb[:, h * 64:(h + 1) * 64], gdb[:], r=["gupb", "gdb"], w=fb(h))
                k.copy("dve" if h % 2 else "act", gt[:, h, :], banks[h][0:64, :], r=fb(h), w=["gt"])
            k.tt("dve", kkn[:], k_[:], bc(2), ALU.mult, r=["k_", "prm"], w=["kkn"])
            k.tt("dve", t1b[:], kkn[:], kkn[:], ALU.mult, r=["kkn"], w=["t1b"])
            for h in range(4):
                k.mm(banks[4 + h][0:64, :], ones64b[:], t1b[:, h, :], r=["ones64b", "t1b"], w=fb(4 + h))
            for h in range(4):
                k.act(t2[:, h, :], banks[4 + h][0:64, :], AF.Ln, bias=1e-24, scale=1.0, r=fb(4 + h), w=["t2"])
            k.act(t2[:], t2[:], AF.Exp, scale=-0.5, r=["t2"], w=["t2"])
            k.tt("pool", kkn[:], kkn[:], t2[:], ALU.mult, r=["kkn", "t2"], w=["kkn"])
            k.stt("dve", t1[:], at[:], -1.0, bc(3), ALU.add, ALU.mult, r=["at", "prm", "t1"], w=["t1"])
            k.tt("pool", t1[:], t1[:], k_[:], ALU.mult, r=["t1", "k_"], w=["t1"])
            k.tt("dve", k2[:], t1[:], k_[:], ALU.add, r=["t1", "k_"], w=["k2"])
            k.tt("pool", b_[:], kkn[:], at[:], ALU.mult, r=["kkn", "at"], w=["b_"])
            k.tt("dve", t1[:], r_[:], k2[:], ALU.mult, r=["r_", "k2", "t1"], w=["t1"])
            k.tt("dve", t1b[:], t1[:], bc(4), ALU.mult, r=["t1", "prm"], w=["t1b"])
            for h in range(4):
                k.mm(banks[h][0:64, :], ones64b[:], t1b[:, h, :], r=["ones64b", "t1b"], w=fb(h))
                k.tt("dve", bonus[:, h, :], banks[h][0:64, :], v_[:, h, :], ALU.mult, r=fb(h) + ["v_"], w=["bonus"])
            for h in range(4):
                k.scan(Gblk[:, h, 1:513], ones512[:], lwt[:, h, :], 0.0, ALU.mult, ALU.add, r=["ones512", "lwt"], w=["Gblk"])

            if int(os.environ.get("A_STOP", "9")) <= 1:
                continue
            def prep_steps(ci):
                q = ci % 2
                D_ = DB[q]
                c0 = ci * 64
                ts_ = slice(c0, c0 + 64)
                eGq, Atq, Rtq = D_["eG"], D_["At"], D_["Rt"]
                nm = lambda x: "%s_%d" % (x, q)
                id64 = G.identb[0:64, 0:64]
                steps = []

                def s1():
                    k.tt("dve", Pc[:], Gblk[:, :, 1 + c0:1 + c0 + 64], Gblk[:, :, c0:c0 + 1].to_broadcast([64, 4, 64]), ALU.subtract,
                         r=["Gblk"], w=["Pc"])
                    k.tt("pool", Pp[:], Pc[:], lwt[:, :, ts_], ALU.subtract, r=["Pc", "lwt"], w=["Pp"])
                    k.act(eGq[:], Pc[:], AF.Exp, scale=-ALPHA, r=["Pc"], w=[nm("eG")])
                    k.act(eGn[:], Pc[:], AF.Exp, scale=ALPHA, r=["Pc"], w=["eGn"])
                    k.act(eGp[:], Pp[:], AF.Exp, scale=-ALPHA, r=["Pp"], w=["eGp"])
                steps.append(s1)

                def s2():
                    k.stt("dve", Atq[:], kkn[:, :, ts_], -1.0, eGp[:], ALU.mult, ALU.mult, r=["kkn", "eGp"], w=[nm("At")])
                    k.tt("pool", Bt[:], b_[:, :, ts_], eGn[:], ALU.mult, r=["b_", "eGn"], w=["Bt"])
                    k.tt("pool", Kt[:], k2[:, :, ts_], eGn[:], ALU.mult, r=["k2", "eGn"], w=["Kt"])
                    k.tt("dve", Rtq[:], r_[:, :, ts_], eGq[:], ALU.mult, r=["r_", nm("eG")], w=[nm("Rt")])
                steps.append(s2)

                def s3():
                    trs = ((Bt, "Bt", D_["Btm"], nm("Btm"), 2, 1, "act"), (Kt, "Kt", D_["Ktm"], nm("Ktm"), 3, 0, "act"),
                           (None, "vb", D_["Vtm"], nm("Vtm"), 3, 1, "act"))
                    for (X, xk, Xtm, xtk, bq, hf, ce) in trs:
                        for h in range(4):
                            src = vb[:, h, ts_] if X is None else X[:, h, :]
                            k.mm(HB(bq, hf)[:, h, :], src, id64, r=[xk, "identb"], w=[hk(bq, hf)])
                    for (X, xk, Xtm, xtk, bq, hf, ce) in trs:
                        k.copy(ce, Xtm[:], HB(bq, hf), r=[hk(bq, hf)], w=[xtk])
                steps.append(s3)

                def s4():
                    specs = ((Bt, "Bt", Atq, nm("At"), 0, 0, D_["PT"][0], nm("PT0"), 0), (Atq, nm("At"), Bt, "Bt", 0, 1, Pj[0], "Pj0", 2),
                             (Kt, "Kt", Atq, nm("At"), 1, 0, D_["LakT"], nm("LakT"), 0), (Bt, "Bt", Rtq, nm("Rt"), 1, 1, D_["MrbT"], nm("MrbT"), 1),
                             (Kt, "Kt", Rtq, nm("Rt"), 2, 0, D_["MrkT"], nm("MrkT"), 1))
                    for (La, lk, Ra, rk_, bq, hf, dst, dk, mi) in specs:
                        for h in range(4):
                            k.mm(HB(bq, hf)[:, h, :], La[:, h, :], Ra[:, h, :], r=[lk, rk_], w=[hk(bq, hf)])
                    for (La, lk, Ra, rk_, bq, hf, dst, dk, mi) in specs:
                        k.tt("dve", dst[:], HB(bq, hf), m64[:, mi, :].unsqueeze(1).to_broadcast([64, 4, 64]), ALU.mult,
                             r=[hk(bq, hf), "m64"], w=[dk])
                steps.append(s4)

                def mk_sq(j):
                    def sq():
                        cur, nxt = j % 2, (j + 1) % 2
                        PTc, PTn = D_["PT"][j], D_["PT"][j + 1]
                        for h in range(4):
                            k.mm(HB(5, 0)[:, h, :], PTc[:, h, :], Pj[cur][:, h, :], r=[nm("PT%d" % j), "Pj%d" % cur], w=[hk(5, 0)])
                        for h in range(4):
                            k.mm(HB(5, 1)[:, h, :], Pj[cur][:, h, :], PTc[:, h, :], r=[nm("PT%d" % j), "Pj%d" % cur], w=[hk(5, 1)])
                        k.copy("act", Pj[nxt][:], HB(5, 0), r=[hk(5, 0)], w=["Pj%d" % nxt])
                        k.copy("act", PTn[:], HB(5, 1), r=[hk(5, 1)], w=[nm("PT%d" % (j + 1))])
                    return sq
                for j in range(5):
                    steps.append(mk_sq(j))
                return steps

            def chain_steps(ci):
                q = ci % 2
                D_ = DB[q]
                c0 = ci * 64
                ts_ = slice(c0, c0 + 64)
                eGq, Atq, Rtq = D_["eG"], D_["At"], D_["Rt"]
                Btm, Ktm, Vtm, LakT, MrbT, MrkT = D_["Btm"], D_["Ktm"], D_["Vtm"], D_["LakT"], D_["MrbT"], D_["MrkT"]
                nm = lambda x: "%s_%d" % (x, q)
                steps = []

                def c1():
                    for h in range(4):
                        k.mm(HB(4, 0)[:, h, :], Atq[:, h, :], Tb[:, h, :], start=True, stop=False, r=[nm("At"), "Tb"], w=[hk(4, 0)])
                        k.mm(HB(4, 0)[:, h, :], LakT[:, h, :], Vtm[:, h, :], start=False, stop=True, r=[nm("LakT"), nm("Vtm")], w=[hk(4, 0)])
                    k.copy("dve", Ub[:], HB(4, 0), r=[hk(4, 0)], w=["Ub"])
                    k.copy("act", U[:], HB(4, 0), r=[hk(4, 0)], w=["U"])
                steps.append(c1)

                def mk_u(j):
                    def us():
                        PTc = D_["PT"][j]
                        for h in range(4):
                            k.mm(HB(4, 1)[:, h, :], PTc[:, h, :], Ub[:, h, :], r=[nm("PT%d" % j), "Ub"], w=[hk(4, 1)])
                        k.tt("dve", Ub[:], U[:], HB(4, 1), ALU.add, r=["U", hk(4, 1)], w=["Ub"])
                        if j < 5:
                            k.tt("dve", U[:], U[:], HB(4, 1), ALU.add, r=["U", hk(4, 1)], w=["U"])
                    return us
                for j in range(6):
                    steps.append(mk_u(j))

                def c8():
                    for h in range(4):
                        k.mm(HB(6, 0)[:, h, :], Tb[:, h, :], Rtq[:, h, :], start=True, stop=False, r=["Tb", nm("Rt")], w=[hk(6, 0)])
                        k.mm(HB(6, 0)[:, h, :], Ub[:, h, :], MrbT[:, h, :], start=False, stop=False, r=["Ub", nm("MrbT")], w=[hk(6, 0)])
                        k.mm(HB(6, 0)[:, h, :], Vtm[:, h, :], MrkT[:, h, :], start=False, stop=True, r=[nm("Vtm"), nm("MrkT")], w=[hk(6, 0)])
                    k.copy("act", Yblk[:, :, ts_], HB(6, 0), r=[hk(6, 0)], w=["Yblk"])
                    for h in range(4):
                        k.mm(HB(7, 0)[:, h, :], Btm[:, h, :], Ub[:, h, :], start=True, stop=False, r=[nm("Btm"), "Ub"], w=[hk(7, 0)])
                        k.mm(HB(7, 0)[:, h, :], Ktm[:, h, :], Vtm[:, h, :], start=False, stop=True, r=[nm("Ktm"), nm("Vtm")], w=[hk(7, 0)])
                    k.tt("dve", tS[:], HB(7, 0), Tst[:], ALU.add, r=[hk(7, 0), "Tst"], w=["tS"])
                    k.tt("dve", Tb[:], tS[:], eGq[:, :, 63:64].to_broadcast([64, 4, 64]), ALU.mult, r=["tS", nm("eG")], w=["Tb"])
                    k.tt("pool", Tst[:], tS[:], eGq[:, :, 63:64].to_broadcast([64, 4, 64]), ALU.mult, r=["tS", nm("eG")], w=["Tst"])
                steps.append(c8)
                return steps

            for st in prep_steps(0):
                st()
            for ci in range(8):
                cs = chain_steps(ci)
                ps = prep_steps(ci + 1) if ci < 7 else []
                n = max(len(cs), len(ps))
                for i_ in range(n):
                    if i_ < len(cs):
                        cs[i_]()
                    if i_ < len(ps):
                        ps[i_]()

            if int(os.environ.get("A_STOP", "9")) <= 3:
                continue
            for h in range(4):
                k.mm(banks[h][0:64, :], ones64[:], Yblk[:, h, :], r=["ones64", "Yblk"], w=fb(h))
                k.stt("dve", t1[:, h, :], banks[h][0:64, :], -1.0 / 64, Yblk[:, h, :], ALU.mult, ALU.add, r=fb(h) + ["Yblk"], w=["t1"])
            k.tt("dve", t1b[:], t1[:], t1[:], ALU.mult, r=["t1"], w=["t1b"])
            for h in range(4):
                k.mm(banks[4 + h][0:64, :], ones64b[:], t1b[:, h, :], r=["ones64b", "t1b"], w=fb(4 + h))
            for h in range(4):
                k.act(t2[:, h, :], banks[4 + h][0:64, :], AF.Ln, bias=64e-5, scale=1.0 / 64, r=fb(4 + h), w=["t2"])
            k.act(t2[:], t2[:], AF.Exp, scale=-0.5, r=["t2"], w=["t2"])
            k.tt("pool", t1[:], t1[:], t2[:], ALU.mult, r=["t1", "t2"], w=["t1"])
            k.tt("dve", t1[:], t1[:], bc(5), ALU.mult, r=["t1", "prm"], w=["t1"])
            k.tt("pool", t1[:], t1[:], bc(6), ALU.add, r=["t1", "prm"], w=["t1"])
            k.tt("dve", t1[:], t1[:], bonus[:], ALU.add, r=["t1", "bonus"], w=["t1"])
            k.tt("dve", yo[:], t1[:], gt[:], ALU.mult, r=["t1", "gt"], w=["yo"])
            if not G.y_in:
                k.dma(R.yTA[:, tsl].rearrange("(h v) t -> v h t", v=64), yo[:], r=["yo"], w=[("yA", g)])
        P.barrier()


def attn_common(G, s, Vsrc, mix):
    k, R, I = G.k, G.R, G.I
    sb = lambda name, shape, dt=F32: G.sb(s, name, shape, dt)
    A = Ctx()
    A.kT = [sb("kT%d" % i, [64, S], BF16) for i in range(2)]
    A.V = sb("V", [128, 32, 260], BF16)
    Vv = Vsrc.rearrange("(kt p) c -> p kt c", p=128)
    for q4 in range(4):
        k.dma(A.V[:, q4 * 8:(q4 + 1) * 8, :], Vv[:, q4 * 8:(q4 + 1) * 8, :], r=[(mix + "v", g) for g in range(NG)], w=["V"])
    A.Pm = [sb("Pm%d" % i, [128, 512], BF16) for i in range(5)]
    A.rec = [sb("rec%d" % i, [128, 8]) for i in range(2)]
    A.ob = [sb("ob%d" % i, [128, 4, 64], BF16) for i in range(2)]
    return A


def phase_D(G, l):
    nc, P, k, I, R = G.nc, G.P, G.k, G.I, G.R
    banks = G.banks
    with ExitStack() as s:
        sb = lambda name, shape, dt=F32: G.sb(s, name, shape, dt)
        A = attn_common(G, s, R.vD, "D")
        Fk = sb("Fk", [128, 4, 32])
        Frow = sb("Frow", [4, S])
        sel = sb("sel", [4, 4, 128])
        negm = sb("negm", [128, 128])
        qT = [sb("qT%d" % i, [64, 512], BF16) for i in range(2)]
        FqB = [sb("FqB%d" % i, [128, 512]) for i in range(2)]
        FqD = [sb("FqD%d" % i, [128, 512]) for i in range(2)]
        tb = [sb("tb%d" % i, [128, 512]) for i in range(5)]
        sbanks = [0, 1, 2, 6, 7]
        allF = [("Ffm", g) for g in range(NG)]
        fkn = tb[0][0:32, :].rearrange("p (h c) -> p h c", h=4)
        for h in range(4):
            k.dma(fkn[:, h, :], R.Ffm[h].rearrange("(kt p) -> kt p", p=128), r=allF, w=["tb0"])
        for h in range(4):
            k.mm(banks[5][:, h * 32:(h + 1) * 32], fkn[:, h, :], G.ident[0:32, 0:32], r=["tb0", "ident"], w=["bank5"])
        k.copy("dve", Fk[:].rearrange("p h c -> p (h c)"), banks[5][:, 0:128], r=["bank5"], w=["Fk"])
        k.dma(Frow[:], R.Ffm, r=allF, w=["Frow"])
        k.dma(sel[:], I["sel"], w=["sel"])
        k.dma(negm[:], I["negmask"], w=["negm"])
        allqk = [("Dqk", g) for g in range(NG)]
        for h in range(4):
            kt_ = A.kT[h % 2]
            kk = "kT%d" % (h % 2)
            k.dma(kt_[:], R.kTD[h * 64:(h + 1) * 64, :], r=allqk, w=[kk])
            for g in range(NG):
                i = k.rot("Dq", 2)
                k.dma(qT[i][:], R.qTD[h * 64:(h + 1) * 64, g * TG:(g + 1) * TG], r=allqk, w=["qT%d" % i])
                k.mm(banks[5][:], sel[:, h, :], Frow[:, g * TG:(g + 1) * TG], r=["sel", "Frow"], w=["bank5"])
                k.copy("act", FqB[i][:], banks[5][:], r=["bank5"], w=["FqB%d" % i])
                k.tt("pool", FqD[i][:].rearrange("p (a b) -> p a b", a=4), FqB[i][:].rearrange("p (a b) -> p a b", a=4),
                     negm[:].unsqueeze(1).to_broadcast([128, 4, 128]), ALU.add, r=["FqB%d" % i, "negm"], w=["FqD%d" % i])
                ob_ = 3 + k.rot("DO", 2)
                O = banks[ob_][:, 0:260].rearrange("p (a e) -> p a e", a=4)
                def d_stage1(kt):
                    m = kt - 4 * g
                    c0 = max(m, 0) * 128
                    N = 512 - c0
                    sbk = sbanks[k.rot("Ds", 5)]
                    k.mm(banks[sbk][:, 0:N], kt_[:, kt * 128:(kt + 1) * 128], qT[i][:, c0:512], r=[kk, "qT%d" % i], w=["bank%d" % sbk])
                    ti = k.rot("Dt", 5)
                    if m < 0:
                        k.stt("dve", tb[ti][:], banks[sbk][:], Fk[:, h, kt:kt + 1], FqB[i][:], ALU.subtract, ALU.add,
                              r=["bank%d" % sbk, "Fk", "FqB%d" % i], w=["tb%d" % ti])
                    else:
                        k.stt("dve", tb[ti][:, 0:128], banks[sbk][:, 0:128], Fk[:, h, kt:kt + 1], FqD[i][:, c0:c0 + 128],
                              ALU.subtract, ALU.add, r=["bank%d" % sbk, "Fk", "FqD%d" % i], w=["tb%d" % ti])
                        if N > 128:
                            k.stt("dve", tb[ti][:, 128:N], banks[sbk][:, 128:N], Fk[:, h, kt:kt + 1], FqB[i][:, c0 + 128:512],
                                  ALU.subtract, ALU.add, r=["bank%d" % sbk, "Fk", "FqB%d" % i], w=["tb%d" % ti])
                    k.act(A.Pm[ti][:, 0:N], tb[ti][:, 0:N], AF.Exp, r=["tb%d" % ti], w=["Pm%d" % ti])
                    return (kt, m, c0, ti)

                def d_stage2(st):
                    kt, m, c0, ti = st
                    for jq in range(max(m, 0), 4):
                        k.mm(O[:, jq, :], A.Pm[ti][:, jq * 128 - c0: jq * 128 - c0 + 128], A.V[:, kt, h * 65:(h + 1) * 65],
                             start=(kt == 0 and jq == 0), stop=(kt == 4 * g + jq), r=["Pm%d" % ti, "V"], w=["bank%d" % ob_], sgc=True)
                pend = []
                for kt in range(4 * g + 4):
                    pend.append(d_stage1(kt))
                    if len(pend) > 3:
                        d_stage2(pend.pop(0))
                while pend:
                    d_stage2(pend.pop(0))
                ri = k.rot("Drec", 2)
                k.recip(A.rec[ri][:, 0:4], O[:, :, 64], r=["bank%d" % ob_], w=["rec%d" % ri])
                k.tt("dve", A.ob[ri][:], O[:, :, 0:64], A.rec[ri][:, 0:4].unsqueeze(2).to_broadcast([128, 4, 64]), ALU.mult,
                     r=["bank%d" % ob_, "rec%d" % ri], w=["ob%d" % ri])
                k.dma(R.yscr[g * TG:(g + 1) * TG, 512 + h * 64: 512 + (h + 1) * 64].rearrange("(a p) e -> p a e", p=128),
                      A.ob[ri][:], r=["ob%d" % ri], w=[("yD", g)], slow=True)
        P.barrier()


def phase_B(G, l):
    nc, P, k, I, R = G.nc, G.P, G.k, G.I, G.R
    banks = G.banks
    lambda_init = 0.8 - 0.6 * math.exp(-0.3 * l)
    with ExitStack() as s:
        sb = lambda name, shape, dt=F32: G.sb(s, name, shape, dt)
        A = attn_common(G, s, R.vB, "B")
        cmf = sb("cmf", [128, 128])
        cm = sb("cm", [128, 128], BF16)
        qp = [[sb("qp%d_%d" % (i, mp), [64, 512], BF16) for mp in range(2)] for i in range(2)]
        lamv = sb("lamv", [128, 4, 32])
        lamw = sb("lamw", [128, 2, 32])
        lams = sb("lams", [128, 4])
        subg = sb("subg", [128, 64])
        o1 = [sb("o1_%d" % i, [128, 4, 64]) for i in range(2)]
        o2 = [sb("o2_%d" % i, [128, 4, 64]) for i in range(2)]
        k.dma(cmf[:], I["cmask"], w=["cmf"])
        k.copy("dve", cm[:], cmf[:], r=["cmf"], w=["cm"])
        for i in range(2):
            for mp in range(2):
                k.memset("pool", qp[i][mp][:], 0.0, w=["qp%d" % i])
        for n, nm in enumerate(("df_lam_q1", "df_lam_k1", "df_lam_q2", "df_lam_k2")):
            bcast_load(G, lamv[:, n, :], I[nm][l:l + 1, :], 32, ["lamv"])
        bcast_load(G, subg[:], I["df_sub_g"][l:l + 1, :], 64, ["subg"])
        k.ts("dve", subg[:], subg[:], 1.0 - lambda_init, ALU.mult, r=["subg"], w=["subg"])
        k.tt("dve", lamw[:, 0, :], lamv[:, 0, :], lamv[:, 1, :], ALU.mult, r=["lamv"], w=["lamw"])
        k.tt("dve", lamw[:, 1, :], lamv[:, 2, :], lamv[:, 3, :], ALU.mult, r=["lamv"], w=["lamw"])
        k.red("dve", lams[:, 0:2], lamw[:], r=["lamw"], w=["lams"])
        k.act(lams[:, 0:2], lams[:, 0:2], AF.Exp, r=["lams"], w=["lams"])
        k.ts("dve", lams[:, 2:3], lams[:, 1:2], -lambda_init, ALU.add, r=["lams"], w=["lams"])
        k.tt("dve", lams[:, 3:4], lams[:, 2:3], lams[:, 0:1], ALU.subtract, r=["lams"], w=["lams"])
        allqk = [("Bqk", g) for g in range(NG)]
        jobs = []
        if G.prep_in_B:
            pst = [sb("pf_st%d" % i_, [128, 8, 512]) for i_ in range(2)]
            pbf = [sb("pf_bf%d" % i_, [128, 8, 512], BF16) for i_ in range(2)]
            up = I["ffn_up"][l].rearrange("(kc p) n -> p kc n", p=128)
            dn = I["ffn_down"][l].rearrange("(fc p) d -> p fc d", p=128)
            for u in range(11):
                jobs.append((up[:, :, u * 512:(u + 1) * 512], R.upbf[l, u].rearrange("p (kc n) -> p kc n", kc=8), 8, 512))
            for u in range(11):
                jobs.append((dn[:, 2 * u:2 * u + 2, :], R.dnbf[l][:, 2 * u * 1024:(2 * u + 2) * 1024].rearrange("p (a d) -> p a d", a=2), 2, 1024))

        def pf_load(i_):
            src, dst, a_, b_ = jobs[i_]
            k.dma(pst[i_ % 2][:].rearrange("p a b -> p (a b)")[:, 0:a_ * b_].rearrange("p (a b) -> p a b", a=a_), src,
                  w=["pf_st%d" % (i_ % 2)], q="pool")

        def pf_job(i_):
            src, dst, a_, b_ = jobs[i_]
            sv = pst[i_ % 2][:].rearrange("p a b -> p (a b)")[:, 0:a_ * b_]
            bv = pbf[i_ % 2][:].rearrange("p a b -> p (a b)")[:, 0:a_ * b_]
            k.copy("dve" if i_ % 3 else "pool", bv, sv, r=["pf_st%d" % (i_ % 2)], w=["pf_bf%d" % (i_ % 2)])
            k.dma(dst, bv.rearrange("p (a b) -> p a b", a=a_), r=["pf_bf%d" % (i_ % 2)], w=[("ffnw", l)], q="pool")
            if i_ + 2 < len(jobs):
                pf_load(i_ + 2)
        if jobs:
            pf_load(0)
            pf_load(1)
        jn = [0]
        for h in range(4):
            kt_ = A.kT[h % 2]
            kk = "kT%d" % (h % 2)
            k.dma(kt_[:], R.kTB[h * 64:(h + 1) * 64, :], r=allqk, w=[kk])
            for g in range(NG):
                if jobs and jn[0] < len(jobs) and (h * NG + g) >= 2:
                    pf_job(jn[0])
                    jn[0] += 1
                i = k.rot("Bq", 2)
                k.dma(qp[i][0][0:32, :], R.qTB[h * 64:h * 64 + 32, g * TG:(g + 1) * TG], r=allqk, w=["qp%d" % i])
                k.dma(qp[i][1][32:64, :], R.qTB[h * 64 + 32:h * 64 + 64, g * TG:(g + 1) * TG], r=allqk, w=["qp%d" % i])
                oi = k.rot("BO", 2)
                Os = [banks[3 + oi][:, 0:260].rearrange("p (a e) -> p a e", a=4),
                      banks[5 + oi][:, 0:260].rearrange("p (a e) -> p a e", a=4)]
                obk = ["bank%d" % (3 + oi), "bank%d" % (5 + oi)]
                def b_stage1(kt, mp):
                    m = kt - 4 * g
                    c0 = max(m, 0) * 128
                    N = 512 - c0
                    sbk = (0, 1, 2, 7)[k.rot("Bs", 4)]
                    k.mm(banks[sbk][:, 0:N], kt_[:, kt * 128:(kt + 1) * 128], qp[i][mp][:, c0:512], r=[kk, "qp%d" % i],
                         w=["bank%d" % sbk])
                    ti = k.rot("Bt", 4)
                    k.act(A.Pm[ti][:, 0:N], banks[sbk][:, 0:N], AF.Exp, r=["bank%d" % sbk], w=["Pm%d" % ti])
                    if m >= 0:
                        k.tt("dve", A.Pm[ti][:, 0:128], A.Pm[ti][:, 0:128], cm[:], ALU.mult, r=["Pm%d" % ti, "cm"], w=["Pm%d" % ti])
                    return (kt, mp, m, c0, ti)

                def b_stage2(st):
                    kt, mp, m, c0, ti = st
                    for jq in range(max(m, 0), 4):
                        k.mm(Os[mp][:, jq, :], A.Pm[ti][:, jq * 128 - c0: jq * 128 - c0 + 128], A.V[:, kt, h * 65:(h + 1) * 65],
                             start=(kt == 0 and jq == 0), stop=(kt == 4 * g + jq), r=["Pm%d" % ti, "V"], w=[obk[mp]], sgc=True)
                pend = []
                for kt in range(4 * g + 4):
                    for mp in range(2):
                        pend.append(b_stage1(kt, mp))
                        if len(pend) > 3:
                            b_stage2(pend.pop(0))
                while pend:
                    b_stage2(pend.pop(0))
                ri = k.rot("Brec", 2)
                rc = A.rec[ri]
                rk = "rec%d" % ri
                k.recip(rc[:, 0:4], Os[0][:, :, 64], r=[obk[0]], w=[rk])
                k.recip(rc[:, 4:8], Os[1][:, :, 64], r=[obk[1]], w=[rk])
                k.ts("dve", rc[:, 4:8], rc[:, 4:8], lams[:, 3:4], ALU.mult, r=[rk, "lams"], w=[rk])
                k.tt("dve", o1[ri][:], Os[0][:, :, 0:64], rc[:, 0:4].unsqueeze(2).to_broadcast([128, 4, 64]), ALU.mult,
                     r=[obk[0], rk], w=["o1_%d" % ri])
                k.tt("dve", o2[ri][:], Os[1][:, :, 0:64], rc[:, 4:8].unsqueeze(2).to_broadcast([128, 4, 64]), ALU.mult,
                     r=[obk[1], rk], w=["o2_%d" % ri])
                k.tt("pool", o1[ri][:], o1[ri][:], o2[ri][:], ALU.add, r=["o1_%d" % ri, "o2_%d" % ri], w=["o1_%d" % ri])
                k.tt("pool", o2[ri][:], o1[ri][:], o1[ri][:], ALU.mult, r=["o1_%d" % ri], w=["o2_%d" % ri])
                k.red("dve", rc[:, 0:4], o2[ri][:], r=["o2_%d" % ri], w=[rk])
                k.act(rc[:, 0:4], rc[:, 0:4], AF.Sqrt, bias=EPS, scale=1.0 / 64, r=[rk], w=[rk])
                k.recip(rc[:, 0:4], rc[:, 0:4], r=[rk], w=[rk])
                k.tt("dve", o1[ri][:], o1[ri][:], rc[:, 0:4].unsqueeze(2).to_broadcast([128, 4, 64]), ALU.mult,
                     r=["o1_%d" % ri, rk], w=["o1_%d" % ri])
                k.tt("pool", A.ob[ri][:], o1[ri][:], subg[:].unsqueeze(1).to_broadcast([128, 4, 64]), ALU.mult,
                     r=["o1_%d" % ri, "subg"], w=["ob%d" % ri])
                k.dma(R.yscr[g * TG:(g + 1) * TG, h * 64:(h + 1) * 64].rearrange("(a p) e -> p a e", p=128),
                      A.ob[ri][:], r=["ob%d" % ri], w=[("yB", g)], slow=True)
        while jobs and jn[0] < len(jobs):
            pf_job(jn[0])
            jn[0] += 1
        P.barrier()


_CACHE = {}


def kernel(**inputs):
    if "prog" not in _CACHE:
        _CACHE["prog"] = build()
    nc, P = _CACHE["prog"]
    consts = make_consts()
    weights = {k: np.ascontiguousarray(np.asarray(inputs[k], dtype=np.float32)) for k in WEIGHT_SHAPES}
    x = np.asarray(inputs["x"], dtype=np.float32)
    c = np.asarray(inputs["c"], dtype=np.float32)
    in_maps = []
    for b in range(8):
        m = {"x": np.ascontiguousarray(x[b]), "c": np.ascontiguousarray(c[b:b + 1])}
        m.update(weights)
        m.update(consts)
        in_maps.append(m)
    res = run_bass_kernel_spmd(nc, in_maps, core_ids=list(range(8)))
    return np.stack([np.asarray(r["out"], dtype=np.float32) for r in res.results], axis=0)
```

```python
import math
import os
import numpy as np
import concourse.bass as bass
import concourse.mybir as mybir
from concourse.bass_utils import run_bass_kernel_spmd
from contextlib import ExitStack

F32 = mybir.dt.float32
BF16 = mybir.dt.bfloat16
AF = mybir.ActivationFunctionType
ALU = mybir.AluOpType
AX = mybir.AxisListType

S = 4096
D = 1024
L = 4
NIN = 2948
DFF = 2816
NG = 8
TG = 512
EPS = 1e-6
ALPHA = math.exp(-0.5)

ENGS = ("pe", "act", "dve", "pool", "sp")
EPOCH = 30000


class Prog:
    def __init__(self, nc, es):
        self.nc = nc
        self.es = es
        self.q = {e: [] for e in ENGS}
        self.cnt = {e: 0 for e in ENGS}
        self.epoch = {e: 0 for e in ENGS}
        self.sems = {}
        self.seen = {e: {} for e in ENGS}
        self.res_w = {}
        self.res_r = {}
        self.dma_val = {}
        self.n_inst = 0
        self.rr = 0

    def _sem(self, key):
        if key not in self.sems:
            self.sems[key] = self.es.enter_context(self.nc.semaphore("s_" + "_".join(str(k) for k in key)))
        return self.sems[key]

    def _deps(self, eng, reads, writes, extra=()):
        need = {}

        def add(ev):
            if ev is None:
                return
            k, v = ev
            if eng == "pe" and k[0] == "pe":
                return
            if need.get(k, 0) < v:
                need[k] = v
        for r in reads:
            add(self.res_w.get(r))
        for w in writes:
            add(self.res_w.get(w))
            for ev in self.res_r.get(w, ()):
                add(ev)
        for ev in extra:
            add(ev)
        waits = []
        for k, v in need.items():
            if self.seen[eng].get(k, 0) >= v:
                continue
            self.seen[eng][k] = v
            waits.append((k, v))
        return waits

    def _commit(self, ev, reads, writes):
        for r in reads:
            lst = self.res_r.setdefault(r, [])
            lst.append(ev)
            if len(lst) > 64:
                mx = {}
                for k, v in lst:
                    if mx.get(k, 0) < v:
                        mx[k] = v
                self.res_r[r] = list(mx.items())
        for w in writes:
            self.res_w[w] = ev
            self.res_r[w] = []

    @staticmethod
    def _is_psum(r):
        return (isinstance(r, str) and r.startswith("bank")) or (isinstance(r, tuple) and r[0] == "hb")

    def op(self, eng, fn, reads=(), writes=()):
        pr = [r for r in reads if self._is_psum(r)]
        if pr:
            writes = list(writes) + pr
        waits = self._deps(eng, reads, writes)
        if self.cnt[eng] >= EPOCH:
            self.epoch[eng] += 1
            self.cnt[eng] = 0
        self.cnt[eng] += 1
        key = (eng, self.epoch[eng])
        ev = (key, self.cnt[eng])
        self.q[eng].append((waits, fn, key, 1))
        self._commit(ev, reads, writes)
        self.n_inst += 1
        return ev

    def dma(self, queue, pairs, reads=(), writes=(), sem=None):
        if sem is None:
            sem = ("dma", "rr%d" % (self.rr % 20))
            self.rr += 1
        key = sem
        prev = self.dma_val.get(key, 0)
        extra = [(key, prev)] if prev > 0 else []
        waits = self._deps(queue, reads, writes, extra)
        val = prev
        for i, pr in enumerate(pairs):
            out_ap, in_ap = pr[0], pr[1]
            kw = pr[2] if len(pr) > 2 else {}
            val += 16

            def fn(e, out_ap=out_ap, in_ap=in_ap, kw=kw):
                return e.dma_start(out=out_ap, in_=in_ap, **kw)
            self.q[queue].append((waits if i == 0 else [], fn, key, 16))
            self.n_inst += 1
        self.dma_val[key] = val
        ev = (key, val)
        self._commit(ev, reads, writes)
        return ev

    def barrier(self):
        evs = []
        for e in ENGS:
            for ep in range(self.epoch[e] + 1):
                k = (e, ep)
                v = self.cnt[e] if ep == self.epoch[e] else EPOCH
                if v > 0:
                    evs.append((k, v))
        for k, v in self.dma_val.items():
            evs.append((k, v))
        for e in ENGS:
            waits = []
            for k, v in evs:
                if self.seen[e].get(k, 0) < v:
                    self.seen[e][k] = v
                    waits.append((k, v))
            if waits:
                self.q[e].append((waits, None, None, 0))
        self.res_w = {}
        self.res_r = {}

    def emit(self):
        nc = self.nc
        for e in ENGS:
            for (waits, fn, key, inc) in self.q[e]:
                for k, v in waits:
                    self._sem(k)
                if key is not None:
                    self._sem(key)
        block = self.es.enter_context(nc.Block())
        engmap = {"pe": block.tensor, "act": block.scalar, "dve": block.vector, "pool": block.gpsimd,
                  "sp": block.sync}
        for e in ENGS:
            items = self.q[e]

            def body(eng, items=items):
                for (waits, fn, key, inc) in items:
                    for k, v in waits:
                        eng.wait_ge(self.sems[k], v)
                    if fn is not None:
                        ins = fn(eng)
                        ins.then_inc(self.sems[key], inc)
            engmap[e](body)


class K:
    def __init__(self, P):
        self.P = P
        self._rot = {}

    def rot(self, name, n):
        i = self._rot.get(name, 0)
        self._rot[name] = i + 1
        return i % n

    def mm(self, out, lhsT, rhs, start=True, stop=True, r=(), w=(), sgc=False):
        if sgc:
            return self.P.op("pe", lambda e: e.matmul(out, lhsT=lhsT, rhs=rhs, start=start, stop=stop, skip_group_check=True), r, w)
        return self.P.op("pe", lambda e: e.matmul(out, lhsT=lhsT, rhs=rhs, start=start, stop=stop), r, w)

    def tr(self, out, in_, ident, r=(), w=()):
        return self.P.op("pe", lambda e: e.transpose(out=out, in_=in_, identity=ident), r, w)

    def act(self, out, in_, func, bias=None, scale=None, accum_out=None, r=(), w=(), eng="act"):
        kw = {}
        if bias is not None:
            kw["bias"] = bias
        if scale is not None:
            kw["scale"] = scale
        if accum_out is not None:
            kw["accum_out"] = accum_out
        return self.P.op("act", lambda e: e.activation(out=out, in_=in_, func=func, **kw), r, w)

    def copy(self, eng, out, in_, r=(), w=()):
        if eng == "act":
            return self.P.op("act", lambda e: e.copy(out=out, in_=in_), r, w)
        return self.P.op(eng, lambda e: e.tensor_copy(out=out, in_=in_), r, w)

    def tt(self, eng, out, in0, in1, op, r=(), w=()):
        return self.P.op(eng, lambda e: e.tensor_tensor(out=out, in0=in0, in1=in1, op=op), r, w)

    def ts(self, eng, out, in0, s1, op0, s2=None, op1=None, r=(), w=()):
        if op1 is None:
            return self.P.op(eng, lambda e: e.tensor_scalar(out=out, in0=in0, scalar1=s1, scalar2=None, op0=op0), r, w)
        return self.P.op(eng, lambda e: e.tensor_scalar(out=out, in0=in0, scalar1=s1, scalar2=s2, op0=op0, op1=op1), r, w)

    def stt(self, eng, out, in0, scalar, in1, op0, op1, r=(), w=()):
        return self.P.op(eng, lambda e: e.scalar_tensor_tensor(out=out, in0=in0, scalar=scalar, in1=in1, op0=op0, op1=op1), r, w)

    def red(self, eng, out, in_, op=ALU.add, r=(), w=()):
        return self.P.op(eng, lambda e: e.tensor_reduce(out=out, in_=in_, axis=AX.X, op=op), r, w)

    def recip(self, out, in_, r=(), w=()):
        return self.P.op("dve", lambda e: e.reciprocal(out=out, in_=in_), r, w)

    def memset(self, eng, ap, val, w=()):
        return self.P.op(eng, lambda e: e.memset(ap, val), (), w)

    def scan(self, out, d0, d1, initial, op0, op1, r=(), w=()):
        return self.P.op("dve", lambda e: e.tensor_tensor_scan(out=out, data0=d0, data1=d1, initial=initial, op0=op0, op1=op1), r, w)

    def dma(self, out, in_, r=(), w=(), q="sp", slow=False, sem=None):
        kw = {"allow_slow_non_contiguous": True} if slow else {}
        return self.P.dma(q, [(out, in_, kw)], r, w, sem=sem)


def make_consts():
    c = {}
    c["ident"] = np.eye(128, dtype=np.float32)
    bo64 = np.zeros((128, 128), np.float32)
    bo64[:64, :64] = 1
    bo64[64:, 64:] = 1
    c["bo64"] = bo64
    bo32 = np.zeros((128, 128), np.float32)
    for i in range(4):
        bo32[i * 32:(i + 1) * 32, i * 32:(i + 1) * 32] = 1
    c["bo32"] = bo32
    prot = np.zeros((128, 128), np.float32)
    for b in range(4):
        for d in range(16):
            prot[b * 32 + d + 16, b * 32 + d] = -1.0
            prot[b * 32 + d, b * 32 + d + 16] = 1.0
    c["prot"] = prot
    inv = 1.0 / (10000.0 ** (np.arange(0, 32, 2, dtype=np.float32) / 32.0))
    ang = np.arange(S, dtype=np.float32)[:, None] * inv[None, :]
    cos = np.cos(ang).astype(np.float32).T
    sin = np.sin(ang).astype(np.float32).T
    c["cosT"] = np.ascontiguousarray(np.tile(cos, (8, 1)))
    c["sinT"] = np.ascontiguousarray(np.tile(sin, (8, 1)))
    k = np.arange(128)[:, None]
    q = np.arange(128)[None, :]
    c["negmask"] = np.where(k > q, -1e30, 0.0).astype(np.float32)
    c["cmask"] = ((k // 64) <= (q // 64)).astype(np.float32)
    c["triu"] = (k <= q).astype(np.float32)
    k6 = np.arange(64)[:, None]
    q6 = np.arange(64)[None, :]
    m64 = np.zeros((64, 3, 64), np.float32)
    m64[:, 0, :] = (k6 < q6)
    m64[:, 1, :] = (k6 <= q6)
    m64[:, 2, :] = (k6 > q6)
    c["m64"] = m64
    sel = np.zeros((4, 4, 128), np.float32)
    for h in range(4):
        sel[h, h, :] = 1.0
    c["sel"] = sel.transpose(1, 0, 2).copy()
    return c


CONST_SHAPES = {"ident": [128, 128], "bo64": [128, 128], "bo32": [128, 128], "prot": [128, 128],
                "cosT": [128, S], "sinT": [128, S], "negmask": [128, 128], "cmask": [128, 128],
                "triu": [128, 128], "m64": [64, 3, 64], "sel": [4, 4, 128]}

WEIGHT_SHAPES = {
    'ada_w': [L, D, 6 * D], 'ada_b': [L, 6 * D], 'norm1_g': [L, D], 'norm2_g': [L, D],
    'w_in': [L, D, NIN], 'w_out': [L, D, D],
    'rw_mu': [L, 896], 'rw_w0': [L, 256], 'rw_w_up': [L, 32, 256], 'rw_a0': [L, 256], 'rw_a_up': [L, 32, 256],
    'rw_g_up': [L, 64, 256], 'rw_k_k': [L, 256], 'rw_k_a': [L, 256], 'rw_r_k': [L, 4, 64],
    'rw_ln_g': [L, 256], 'rw_ln_b': [L, 256],
    'df_lam_q1': [L, 32], 'df_lam_k1': [L, 32], 'df_lam_q2': [L, 32], 'df_lam_k2': [L, 32],
    'df_q_g': [L, 32], 'df_k_g': [L, 32], 'df_sub_g': [L, 64],
    'sg_w': [L, 4, 128, 128], 'sg_b': [L, 4, 128], 'sg_ln_g': [L, 256], 'sg_ln_b': [L, 256],
    'fx_q_g': [L, 64], 'fx_k_g': [L, 64], 'fx_f_b': [L, 4],
    'ffn_up': [L, D, 2 * DFF], 'ffn_conv': [L, 3, 2 * DFF], 'ffn_conv_b': [L, 2 * DFF], 'ffn_down': [L, DFF, D],
}


class Ctx:
    pass


def build(layers=(0, 1, 2, 3), phases=("P1", "A", "B", "D", "P3"), debug=False, y_in=False):
    nc = bass.Bass("TRN2", target_bir_lowering=False)
    dkind = "ExternalOutput" if debug else "Internal"
    I = {}

    def din(name, shape):
        I[name] = nc.dram_tensor(name, list(shape), F32, kind="ExternalInput").ap()
    din("x", [S, D])
    din("c", [1, D])
    for k, shp in WEIGHT_SHAPES.items():
        din(k, shp)
    for k, shp in CONST_SHAPES.items():
        din(k, shp)
    out = nc.dram_tensor("out", [S, D], F32, kind="ExternalOutput").ap()

    def dscr(name, shape, dt, kind=None):
        return nc.dram_tensor(name, list(shape), dt, kind=kind or dkind).ap()
    R = Ctx()
    R.xres = dscr("xres", [S, D], F32)
    R.pmA = dscr("pmA", [896, S], F32)
    R.qTB = dscr("qTB", [256, S], BF16)
    R.kTB = dscr("kTB", [256, S], BF16)
    R.vB = dscr("vB", [S, 260], BF16)
    R.qTD = dscr("qTD", [256, S], BF16)
    R.kTD = dscr("kTD", [256, S], BF16)
    R.vD = dscr("vD", [S, 260], BF16)
    R.Ffm = dscr("Ffm", [4, S], F32)
    if y_in:
        R.yscr = nc.dram_tensor("yscr_in", [S, 768], F32, kind="ExternalInput").ap()
        R.yTA = nc.dram_tensor("yTA_in", [256, S], F32, kind="ExternalInput").ap()
    else:
        R.yscr = dscr("yscr", [S, 768], BF16)
        R.yTA = dscr("yTA", [256, S], BF16)
    R.upbf = dscr("upbf", [L, 11, 128, 8 * 512], BF16, kind="Internal")
    R.dnbf = dscr("dnbf", [L, 128, 22 * 1024], BF16, kind="Internal")

    with ExitStack() as es:
        P = Prog(nc, es)
        k = K(P)
        G = Ctx()
        G.nc, G.P, G.k, G.I, G.R, G.out = nc, P, k, I, R, out
        G.y_in = y_in
        G.dbg_x1 = nc.dram_tensor("dbg_x1", [S, D], F32, kind="ExternalOutput").ap() if debug else None

        uid = [0]

        def sb(stack, name, shape, dt=F32):
            uid[0] += 1
            return stack.enter_context(nc.sbuf_tensor("%s_u%d" % (name, uid[0]), list(shape), dt))
        G.sb = sb
        G.banks = [es.enter_context(nc.psum_tensor("bank%d" % i, [128, 512], F32)) for i in range(8)]
        G.ident = sb(es, "ident", [128, 128])
        G.identb = sb(es, "identb", [128, 128], BF16)
        G.ones_row = sb(es, "ones_row", [1, 128])
        G.condB = sb(es, "condB", [128, 8, 128])
        G.modB = sb(es, "modB", [128, 6 * D])
        k.dma(G.ident[:], I["ident"], w=["ident"])
        k.copy("dve", G.identb[:], G.ident[:], r=["ident"], w=["identb"])
        k.memset("dve", G.ones_row[:], 1.0, w=["ones_row"])
        with ExitStack() as s0:
            cT = sb(s0, "cT", [128, 8])
            cS = sb(s0, "cS", [128, 8])
            k.dma(cT[:], I["c"].rearrange("o (kc p) -> p (o kc)", p=128), w=["cT"], slow=True)
            k.act(cS[:], cT[:], AF.Silu, r=["cT"], w=["cS"])
            k.copy("dve", G.condB[:], cS[:].unsqueeze(2).to_broadcast([128, 8, 128]), r=["cS"], w=["condB"])
            P.barrier()

        for li, l in enumerate(layers):
            xsrc = I["x"] if li == 0 else R.xres
            xdst = out if li == len(layers) - 1 else R.xres
            G.prep_in_B = ("P3" in phases) and ("B" in phases)
            if "P3" in phases and not G.prep_in_B:
                prep_ffn(G, l)
            layer_setup(G, l)
            if "P1" in phases:
                phase_p1(G, l, xsrc)
            if "A" in phases:
                phase_A(G, l)
            if "B" in phases:
                phase_B(G, l)
            if "D" in phases:
                phase_D(G, l)
            if "P3" in phases:
                phase_p3(G, l, xsrc, xdst)
        P.barrier()
        P.emit()
    return nc, P


def prep_ffn(G, l):
    nc, P, k, I, R = G.nc, G.P, G.k, G.I, G.R
    with ExitStack() as s:
        st = [G.sb(s, "pf_st%d" % i, [128, 8, 512]) for i in range(2)]
        sbf = [G.sb(s, "pf_bf%d" % i, [128, 8, 512], BF16) for i in range(2)]
        up = I["ffn_up"][l].rearrange("(kc p) n -> p kc n", p=128)
        dn = I["ffn_down"][l].rearrange("(fc p) d -> p fc d", p=128)
        engs = ["dve", "act", "dve", "act", "pool"]
        jobs = []
        for u in range(11):
            jobs.append((up[:, :, u * 512:(u + 1) * 512], R.upbf[l, u].rearrange("p (kc n) -> p kc n", kc=8), 8, 512))
        for u in range(11):
            jobs.append((dn[:, 2 * u:2 * u + 2, :], R.dnbf[l][:, 2 * u * 1024:(2 * u + 2) * 1024].rearrange("p (a d) -> p a d", a=2), 2, 1024))

        def load(i):
            src, dst, a, b = jobs[i]
            k.dma(st[i % 2][:].rearrange("p a b -> p (a b)")[:, 0:a * b].rearrange("p (a b) -> p a b", a=a), src,
                  w=["pf_st%d" % (i % 2)])
        load(0)
        load(1)
        for i in range(len(jobs)):
            src, dst, a, b = jobs[i]
            sv = st[i % 2][:].rearrange("p a b -> p (a b)")[:, 0:a * b]
            bv = sbf[i % 2][:].rearrange("p a b -> p (a b)")[:, 0:a * b]
            k.copy(engs[i % 5], bv, sv, r=["pf_st%d" % (i % 2)], w=["pf_bf%d" % (i % 2)])
            k.dma(dst, bv.rearrange("p (a b) -> p a b", a=a), r=["pf_bf%d" % (i % 2)], w=[("ffnw", l)])
            if i + 2 < len(jobs):
                load(i + 2)
        P.barrier()


def layer_setup(G, l):
    nc, P, k, I, R = G.nc, G.P, G.k, G.I, G.R
    banks = G.banks
    with ExitStack() as s:
        aw = [G.sb(s, "ls_aw%d" % i, [128, 8, 512]) for i in range(3)]
        rows = G.sb(s, "ls_rows", [1, 8 * D])
        k.dma(rows[:, 0:6 * D], I["ada_b"][l:l + 1, :], w=["ls_rows_b"])
        k.dma(rows[:, 6 * D:7 * D], I["norm1_g"][l:l + 1, :], w=["ls_rows_g"])
        k.dma(rows[:, 7 * D:8 * D], I["norm2_g"][l:l + 1, :], w=["ls_rows_g"])
        awv = I["ada_w"][l].rearrange("(kc p) n -> p kc n", p=128)
        for cc in range(12):
            b = cc % 2
            a3 = cc % 3
            k.dma(aw[a3][:], awv[:, :, cc * 512:(cc + 1) * 512], w=["ls_aw%d" % a3])
            bk = banks[b]
            for kc in range(8):
                k.mm(bk[:], G.condB[:, kc, :], aw[a3][:, kc, :], start=(kc == 0), stop=False,
                     r=["condB", "ls_aw%d" % a3], w=["bank%d" % b])
            k.mm(bk[:], G.ones_row[0:1, :], rows[0:1, cc * 512:(cc + 1) * 512], start=False, stop=True,
                 r=["ones_row", "ls_rows_b"], w=["bank%d" % b])
            k.copy("act" if cc % 2 else "dve", G.modB[:, cc * 512:(cc + 1) * 512], bk[:], r=["bank%d" % b], w=["modB"])
        for gi, (goff, slot) in enumerate(((6 * D, 1 * D), (7 * D, 4 * D))):
            for hf in range(2):
                b = 2 + hf
                k.mm(banks[b][:], G.ones_row[0:1, :], rows[0:1, goff + hf * 512: goff + (hf + 1) * 512],
                     r=["ones_row", "ls_rows_g"], w=["bank%d" % b])
                sl = G.modB[:, slot + hf * 512: slot + (hf + 1) * 512]
                k.stt("dve", sl, sl, 1.0, banks[b][:], ALU.add, ALU.mult, r=["modB", "bank%d" % b], w=["modB"])
        P.barrier()


def norm_mod_T(G, xt, goff, shoff, T):
    k = G.k
    banks = G.banks
    for j in range(4):
        k.act(T.tmp[:], xt[:, j, :], AF.Square, r=["xt"], w=["tmp"])
        k.red("dve", T.ss[:, j:j + 1], T.tmp[:], r=["tmp"], w=["ss"])
    k.act(T.rstd[:], T.ss[:], AF.Sqrt, bias=EPS, scale=1.0 / D, r=["ss"], w=["rstd"])
    k.recip(T.rstd[:], T.rstd[:], r=["rstd"], w=["rstd"])
    for j in range(4):
        k.stt("dve", T.tmp[:], xt[:, j, :], T.rstd[:, j:j + 1], G.modB[:, goff:goff + D], ALU.mult, ALU.mult,
              r=["xt", "rstd", "modB"], w=["tmp"])
        k.tt("dve", T.hb[:, j, :], T.tmp[:], G.modB[:, shoff:shoff + D], ALU.add, r=["tmp", "modB"], w=["hb"])
    for j in range(4):
        b = 5 + (j % 2)
        pv = banks[b][:].bitcast(BF16).rearrange("p (a t) -> p a t", a=8)
        for kc in range(8):
            k.tr(pv[:, kc, :], T.hb[:, j, kc * 128:(kc + 1) * 128], G.identb[:], r=["hb", "identb"], w=["bank%d" % b])
        k.copy("act" if j % 2 else "dve", T.hT[:, :, j * 128:(j + 1) * 128], pv, r=["bank%d" % b], w=[getattr(T, "hTk", "hT")])


def bcast_load(G, tile_ap, row_ap, n, w):
    G.k.dma(tile_ap, row_ap.to_broadcast([128, n]), w=w, slow=True)


def phase_p1(G, l, xsrc):
    nc, P, k, I, R = G.nc, G.P, G.k, G.I, G.R
    banks = G.banks
    with ExitStack() as s:
        sb = lambda name, shape, dt=F32: G.sb(s, name, shape, dt)
        T = Ctx()
        w_in = sb("w_in", [128, 8, NIN], BF16)
        stg = [sb("p1_stg0", [128, 1474])] * 2
        xt = sb("xt", [128, 4, D])
        T.sqj = sb("sqj", [128, D], BF16)
        T.ss = sb("ss", [128, 4])
        T.rstd = sb("rstd", [128, 4])
        T.tmp = sb("tmp", [128, D])
        T.hb = sb("hb", [128, 4, D], BF16)
        hTs = [sb("hT%d" % i, [128, 8, 512], BF16) for i in range(2)]
        fA = [sb("fA%d" % i, [128, 512]) for i in range(3)]
        fB = [sb("fB%d" % i, [128, 512]) for i in range(3)]
        fC = [sb("fC%d" % i, [128, 512]) for i in range(3)]
        obf = [sb("obf%d" % i, [128, 512], BF16) for i in range(3)]
        xnb = [sb("xnb%d" % i, [128, 512], BF16) for i in range(2)]
        paA = sb("paA", [128, 7, 513])
        pmo = [sb("pmo%d" % i, [128, 512]) for i in range(2)]
        cs = [sb("cos%d" % i, [128, 512]) for i in range(2)]
        sn = [sb("sin%d" % i, [128, 512]) for i in range(2)]
        bo64 = sb("bo64", [128, 128])
        bo32 = sb("bo32", [128, 128])
        protf = sb("protf", [128, 128])
        protb = sb("protb", [128, 128], BF16)
        bo64b = sb("bo64b", [128, 128], BF16)
        bo32b = sb("bo32b", [128, 128], BF16)
        sqb = [sb("sqb%d" % i, [128, 512], BF16) for i in range(2)]
        gcol = sb("gcol", [128, 8])
        mu = sb("mu", [128, 7])
        negfb = sb("negfb", [4, 1])
        ones4 = sb("ones4", [4, 512])
        Fg = [sb("Fg%d" % i, [4, 512]) for i in range(2)]
        f4 = sb("f4", [4, 512])
        vt = [sb("vt%d" % i, [128, 4, 65], BF16) for i in range(4)]
        sgw = sb("sgw", [128, 4, 128])
        WgT = sb("WgT", [128, 4, 128], BF16)
        triu = sb("triu", [128, 128])
        sgbT = sb("sgbT", [128, 4])
        lnCg = sb("lnCg", [128, 256])
        lnCb = sb("lnCb", [128, 256])
        glC = [sb("glC%d" % i, [128, 512]) for i in range(2)]
        stC = [sb("stC%d" % i, [128, 8]) for i in range(2)]
        tmpc = [sb("tmpc%d" % i, [128, 256]) for i in range(2)]
        vnb = [sb("vnb%d" % i, [128, 256], BF16) for i in range(2)]
        ycb = [sb("ycb%d" % i, [128, 256], BF16) for i in range(2)]

        xtf = xt[:].rearrange("p j d -> p (j d)")
        wslots = [(stg[0][:], "p1_stg0"), (xtf[:, 0:1474], "xt")]
        for kc in range(8):
            for hf in range(2):
                sap, skey = wslots[(kc * 2 + hf) % 2]
                k.dma(sap, I["w_in"][l, kc * 128:(kc + 1) * 128, hf * 1474:(hf + 1) * 1474], w=[skey])
                k.copy(("dve", "act")[(kc * 2 + hf) % 2], w_in[:, kc, hf * 1474:(hf + 1) * 1474], sap,
                       r=[skey], w=["w_in"])
        k.dma(bo64[:], I["bo64"], w=["bo64"])
        k.dma(bo32[:], I["bo32"], w=["bo32"])
        k.dma(protf[:], I["prot"], w=["protf"])
        k.copy("dve", protb[:], protf[:], r=["protf"], w=["protb"])
        k.copy("dve", bo64b[:], bo64[:], r=["bo64"], w=["bo64b"])
        k.copy("dve", bo32b[:], bo32[:], r=["bo32"], w=["bo32b"])
        k.dma(triu[:], I["triu"], w=["triu"])
        for rep in range(2):
            k.dma(gcol[rep * 64:(rep + 1) * 64, 0:1], I["fx_q_g"][l].rearrange("(d o) -> d o", o=1), w=["gcol"], slow=True)
            k.dma(gcol[rep * 64:(rep + 1) * 64, 1:2], I["fx_k_g"][l].rearrange("(d o) -> d o", o=1), w=["gcol"], slow=True)
        for rep in range(4):
            k.dma(gcol[rep * 32:(rep + 1) * 32, 2:3], I["df_q_g"][l].rearrange("(d o) -> d o", o=1), w=["gcol"], slow=True)
            k.dma(gcol[rep * 32:(rep + 1) * 32, 3:4], I["df_k_g"][l].rearrange("(d o) -> d o", o=1), w=["gcol"], slow=True)
        k.ts("dve", gcol[:, 0:1], gcol[:, 0:1], 0.125, ALU.mult, r=["gcol"], w=["gcol"])
        k.ts("dve", gcol[:, 2:3], gcol[:, 2:3], 32.0 ** -0.5, ALU.mult, r=["gcol"], w=["gcol"])
        k.dma(mu[:], I["rw_mu"][l].rearrange("(c p) -> p c", p=128), w=["mu"], slow=True)
        k.dma(negfb[:], I["fx_f_b"][l].rearrange("(h o) -> h o", o=1), w=["negfb"], slow=True)
        k.ts("dve", negfb[:], negfb[:], -1.0, ALU.mult, r=["negfb"], w=["negfb"])
        k.memset("dve", ones4[:], 1.0, w=["ones4"])
        k.memset("pool", paA[:], 0.0, w=["paA%d" % i for i in range(7)])
        for i in range(4):
            k.memset("pool", vt[i][:], 1.0, w=["vt%d" % i])
        k.dma(sgw[:], I["sg_w"][l].rearrange("g i j -> i g j"), w=["sgw"])
        for g in range(4):
            k.tr(banks[0][:, g * 128:(g + 1) * 128], sgw[:, g, :], G.ident[:], r=["sgw", "ident"], w=["bank0"])
        k.tt("dve", WgT[:], banks[0][:].rearrange("p (g i) -> p g i", g=4),
             triu[:].unsqueeze(1).to_broadcast([128, 4, 128]), ALU.mult, r=["bank0", "triu"], w=["WgT"])
        k.dma(sgbT[:], I["sg_b"][l].rearrange("g i -> i g"), w=["sgbT"], slow=True)
        bcast_load(G, lnCg[:], I["sg_ln_g"][l:l + 1, :], 256, ["lnCg"])
        bcast_load(G, lnCb[:], I["sg_ln_b"][l:l + 1, :], 256, ["lnCb"])

        cur = Ctx()

        def fm_mm(col0, ncols, bk):
            for kc in range(8):
                k.mm(banks[bk][0:ncols, :], w_in[:, kc, col0:col0 + ncols], cur.hT[:, kc, :], start=(kc == 0), stop=(kc == 7),
                     r=["w_in", cur.hTk], w=["bank%d" % bk])

        def load_norm(g):
            tsl_ = slice(g * TG, (g + 1) * TG)
            k.dma(xt[:], xsrc[tsl_, :].rearrange("(j p) d -> p j d", p=128), r=[("x", g)], w=["xt"])
            T.hT = hTs[g % 2]
            T.hTk = "hT%d" % (g % 2)
            norm_mod_T(G, xt, 1 * D, 0, T)
        load_norm(0)

        for g in range(NG):
            tsl = slice(g * TG, (g + 1) * TG)
            k.dma(cs[g % 2][:], I["cosT"][:, tsl], w=["cos%d" % (g % 2)])
            k.dma(sn[g % 2][:], I["sinT"][:, tsl], w=["sin%d" % (g % 2)])
            cur.hT = hTs[g % 2]
            cur.hTk = "hT%d" % (g % 2)
            for ci in range(7):
                bk = k.rot("p1bank", 3)
                fm_mm(ci * 128, 128, bk)
                k.copy("pool", paA[:, ci, 0:1], paA[:, ci, 512:513], r=["paA%d" % ci], w=["paA%d" % ci])
                k.copy("act", paA[:, ci, 1:513], banks[bk][:], r=["bank%d" % bk], w=["paA%d" % ci])
                i = k.rot("fA", 3)
                k.tt("pool", fA[i][:], paA[:, ci, 0:512], paA[:, ci, 1:513], ALU.subtract, r=["paA%d" % ci], w=["fA%d" % i])
                o = k.rot("pmo", 2)
                k.stt("dve", pmo[o][:], fA[i][:], mu[:, ci:ci + 1], paA[:, ci, 1:513], ALU.mult, ALU.add,
                      r=["fA%d" % i, "mu", "paA%d" % ci], w=["pmo%d" % o])
                k.dma(R.pmA[ci * 128:(ci + 1) * 128, tsl], pmo[o][:], r=["pmo%d" % o], w=[("pmA", g)])
            if g + 1 < NG:
                load_norm(g + 1)
            for (mix, col0, gi, dst, rope) in (("B", 896, 2, R.qTB, True), ("B", 1152, 3, R.kTB, True),
                                               ("D", 2176, 0, R.qTD, False), ("D", 2432, 1, R.kTD, False)):
                for ci in range(2):
                    bk = k.rot("p1bank", 3)
                    fm_mm(col0 + ci * 128, 128, bk)
                    a = k.rot("sqb", 2)
                    k.act(sqb[a][:], banks[bk][:], AF.Square, r=["bank%d" % bk], w=["sqb%d" % a])
                    sbk = 3 + k.rot("p1sbank", 2)
                    k.mm(banks[sbk][:], bo32b[:] if mix == "B" else bo64b[:], sqb[a][:], r=["bo32b", "bo64b", "sqb%d" % a],
                         w=["bank%d" % sbk])
                    b = k.rot("fB", 3)
                    nd = 32.0 if mix == "B" else 64.0
                    k.act(fB[b][:], banks[sbk][:], AF.Ln, bias=EPS, scale=1.0 / nd, r=["bank%d" % sbk], w=["fB%d" % b])
                    k.act(fB[b][:], fB[b][:], AF.Exp, scale=-0.5, r=["fB%d" % b], w=["fB%d" % b])
                    o = k.rot("obf", 3)
                    if not rope:
                        k.stt("dve", obf[o][:], banks[bk][:], gcol[:, gi:gi + 1], fB[b][:], ALU.mult, ALU.mult,
                              r=["bank%d" % bk, "gcol", "fB%d" % b], w=["obf%d" % o])
                    else:
                        c_ = k.rot("fC", 3)
                        k.stt("dve", fC[c_][:], banks[bk][:], gcol[:, gi:gi + 1], fB[b][:], ALU.mult, ALU.mult,
                              r=["bank%d" % bk, "gcol", "fB%d" % b], w=["fC%d" % c_])
                        xb = k.rot("xnb", 2)
                        k.copy("act", xnb[xb][:], fC[c_][:], r=["fC%d" % c_], w=["xnb%d" % xb])
                        rbk = 3 + k.rot("p1sbank", 2)
                        k.mm(banks[rbk][:], protb[:], xnb[xb][:], r=["protb", "xnb%d" % xb], w=["bank%d" % rbk])
                        a2 = k.rot("fA", 3)
                        k.tt("pool", fA[a2][:], fC[c_][:], cs[g % 2][:], ALU.mult, r=["fC%d" % c_, "cos%d" % (g % 2)], w=["fA%d" % a2])
                        b2 = k.rot("fB", 3)
                        k.tt("dve", fB[b2][:], banks[rbk][:], sn[g % 2][:], ALU.mult, r=["bank%d" % rbk, "sin%d" % (g % 2)],
                             w=["fB%d" % b2])
                        k.tt("pool", obf[o][:], fA[a2][:], fB[b2][:], ALU.add, r=["fA%d" % a2, "fB%d" % b2], w=["obf%d" % o])
                    k.dma(dst[ci * 128:(ci + 1) * 128, tsl], obf[o][:], r=["obf%d" % o], w=[(mix + "qk", g)])
            bk = k.rot("p1bank", 3)
            fm_mm(2944, 4, bk)
            k.act(f4[:], banks[bk][0:4, :], AF.Exp, bias=negfb[:, 0:1], scale=-1.0, r=["bank%d" % bk, "negfb"], w=["f4"])
            k.act(f4[:], f4[:], AF.Ln, bias=1.0, scale=1.0, r=["f4"], w=["f4"])
            if g == 0:
                k.scan(Fg[0][:], ones4[:], f4[:], 0.0, ALU.mult, ALU.subtract, r=["ones4", "f4"], w=["Fg0"])
            else:
                k.scan(Fg[g % 2][:], ones4[:], f4[:], Fg[(g - 1) % 2][:, 511:512], ALU.mult, ALU.subtract,
                       r=["ones4", "f4", "Fg%d" % ((g - 1) % 2)], w=["Fg%d" % (g % 2)])
            k.dma(R.Ffm[:, tsl], Fg[g % 2][:], r=["Fg%d" % (g % 2)], w=[("Ffm", g)])
            for j in range(4):
                rows = slice(g * TG + j * 128, g * TG + (j + 1) * 128)
                for (mix, col0, dst) in (("B", 1408, R.vB), ("D", 2688, R.vD)):
                    bk = k.rot("p1bank", 3)
                    for kc in range(8):
                        k.mm(banks[bk][:, 0:256], cur.hT[:, kc, j * 128:(j + 1) * 128], w_in[:, kc, col0:col0 + 256],
                             start=(kc == 0), stop=(kc == 7), r=["w_in", cur.hTk], w=["bank%d" % bk])
                    vi = k.rot("vt", 4)
                    k.copy("act" if mix == "B" else "dve", vt[vi][:, :, 0:64], banks[bk][:, 0:256].rearrange("p (h e) -> p h e", h=4),
                           r=["bank%d" % bk], w=["vt%d" % vi])
                    k.dma(dst[rows, :], vt[vi][:].rearrange("p h e -> p (h e)"), r=["vt%d" % vi], w=[(mix + "v", g)])
                bk = k.rot("p1bank", 3)
                for kc in range(8):
                    k.mm(banks[bk][:], cur.hT[:, kc, j * 128:(j + 1) * 128], w_in[:, kc, 1664:2176],
                         start=(kc == 0), stop=(kc == 7), r=["w_in", cur.hTk], w=["bank%d" % bk])
                ci = k.rot("glC", 2)
                gl, st, tc, vb, yb = glC[ci], stC[ci], tmpc[ci], vnb[ci], ycb[ci]
                kk = "C%d" % ci
                k.act(gl[:], banks[bk][:], AF.Gelu, r=["bank%d" % bk], w=[kk + "gl"])
                k.red("dve", st[:, 0:1], gl[:, 256:512], r=[kk + "gl"], w=[kk + "st"])
                k.act(tc[:], gl[:, 256:512], AF.Square, r=[kk + "gl"], w=[kk + "tc"])
                k.red("dve", st[:, 1:2], tc[:], r=[kk + "tc"], w=[kk + "st"])
                k.ts("dve", st[:, 2:3], st[:, 0:1], 1.0 / 256, ALU.mult, r=[kk + "st"], w=[kk + "st"])
                k.tt("dve", st[:, 3:4], st[:, 2:3], st[:, 2:3], ALU.mult, r=[kk + "st"], w=[kk + "st"])
                k.stt("dve", st[:, 4:5], st[:, 1:2], 1.0 / 256, st[:, 3:4], ALU.mult, ALU.subtract, r=[kk + "st"], w=[kk + "st"])
                k.act(st[:, 5:6], st[:, 4:5], AF.Sqrt, bias=EPS, scale=1.0, r=[kk + "st"], w=[kk + "st"])
                k.recip(st[:, 5:6], st[:, 5:6], r=[kk + "st"], w=[kk + "st"])
                k.ts("dve", tc[:], gl[:, 256:512], st[:, 2:3], ALU.subtract, st[:, 5:6], ALU.mult, r=[kk + "gl", kk + "st"], w=[kk + "tc"])
                k.tt("pool", tc[:], tc[:], lnCg[:], ALU.mult, r=[kk + "tc", "lnCg"], w=[kk + "tc"])
                k.tt("pool", vb[:], tc[:], lnCb[:], ALU.add, r=[kk + "tc", "lnCb"], w=[kk + "vb"])
                sbk = 3 + k.rot("p1sbank", 2)
                for hg in range(4):
                    k.mm(banks[sbk][:, hg * 64:(hg + 1) * 64], WgT[:, hg, :], vb[:, hg * 64:(hg + 1) * 64],
                         r=["WgT", kk + "vb"], w=["bank%d" % sbk])
                k.tt("dve", tc[:].rearrange("p (h e) -> p h e", h=4), banks[sbk][:, 0:256].rearrange("p (h e) -> p h e", h=4),
                     sgbT[:].unsqueeze(2).to_broadcast([128, 4, 64]), ALU.add, r=["bank%d" % sbk, "sgbT", kk + "vb"], w=[kk + "tc"])
                k.tt("pool", yb[:], tc[:], gl[:, 0:256], ALU.mult, r=[kk + "tc", kk + "gl"], w=[kk + "yb"])
                if not G.y_in:
                    k.dma(R.yscr[rows, 256:512], yb[:], r=[kk + "yb"], w=[("yC", g)])
        P.barrier()


def phase_p3(G, l, xsrc, xdst):
    nc, P, k, I, R = G.nc, G.P, G.k, G.I, G.R
    banks = G.banks
    ydt = F32 if G.y_in else BF16
    with ExitStack() as s:
        sb = lambda name, shape, dt=F32: G.sb(s, name, shape, dt)
        T = Ctx()
        w_out = sb("w_out", [128, 8, D], BF16)
        xt = sb("xt3", [128, 4, D])
        T.sqj = sb("sqj3", [128, D], BF16)
        T.ss = sb("ss3", [128, 4])
        T.rstd = sb("rstd3", [128, 4])
        T.tmp = sb("tmp3", [128, D])
        T.hb = sb("hb3", [128, 4, D], BF16)
        T.hT = sb("hT3", [128, 8, 512], BF16)
        yT = T.hT
        actT = sb("actT", [128, 22, 512], BF16)
        upw = [sb("upw%d" % i, [128, 8, 512], BF16) for i in range(2)]
        dnw = sb("dnw", [128, 22, D], BF16)
        ub = [sb("ub%d" % i, [128, 514]) for i in range(3)]
        c1 = [sb("c1_%d" % i, [128, 512]) for i in range(3)]
        c2 = [sb("c2_%d" % i, [128, 512]) for i in range(3)]
        c3 = [sb("c3_%d" % i, [128, 512]) for i in range(2)]
        sgt = [sb("sgt%d" % i, [128, 512]) for i in range(2)]
        convw = sb("convw", [128, 44, 3])
        convb = sb("convb", [128, 44])
        carryF = sb("carryF", [128, 44, 2])

        xtf3 = xt[:].rearrange("p j d -> p (j d)")
        oslots = [(T.tmp[:], "tmp"), (xtf3[:, 0:D], "xt")]
        for kc in range(8):
            sap, skey = oslots[kc % 2]
            k.dma(sap, I["w_out"][l, kc * 128:(kc + 1) * 128, :], w=[skey])
            k.copy(("dve", "act")[kc % 2], w_out[:, kc, :], sap, r=[skey], w=["w_out"])
        cwn = T.tmp[0:44, 0:512].rearrange("p (t c) -> p t c", t=4)
        for t in range(3):
            k.dma(cwn[:, t, :], I["ffn_conv"][l, t].rearrange("(cc p) -> cc p", p=128), w=["tmp"])
        k.dma(cwn[:, 3, :], I["ffn_conv_b"][l].rearrange("(cc p) -> cc p", p=128), w=["tmp"])
        for t in range(4):
            k.mm(banks[0][:, t * 44:(t + 1) * 44], cwn[:, t, :], G.ident[0:44, 0:44], r=["tmp", "ident"], w=["bank0"])
        k.copy("dve", convw[:], banks[0][:, 0:132].rearrange("p (t c) -> p c t", t=3), r=["bank0"], w=["convw"])
        k.copy("dve", convb[:], banks[0][:, 132:176], r=["bank0"], w=["convb"])
        k.memset("pool", carryF[:], 0.0, w=["carryF"])

        for g in range(NG):
            tsl = slice(g * TG, (g + 1) * TG)
            k.dma(xt[:], xsrc[tsl, :].rearrange("(j p) d -> p j d", p=128), r=[("x", g)], w=["xt"])
            k.dma(dnw[:].rearrange("p a d -> p (a d)"), R.dnbf[l], r=[("ffnw", l)], w=["dnw"])
            if G.y_in:
                for j in range(4):
                    k.dma(T.tmp[:, 0:768], R.yscr[g * TG + j * 128: g * TG + (j + 1) * 128, :], w=["tmp"])
                    k.copy("pool", T.hb[:, j, 0:768], T.tmp[:, 0:768], r=["tmp"], w=["hb"])
                for kc in range(2):
                    k.dma(c1[kc][:], R.yTA[kc * 128:(kc + 1) * 128, tsl], w=["c1_%d" % kc])
                    k.copy("act", yT[:, kc, :], c1[kc][:], r=["c1_%d" % kc], w=["hT"])
            else:
                k.dma(T.hb[:, :, 0:768], R.yscr[tsl, :].rearrange("(j p) c -> p j c", p=128),
                      r=[("yC", g), ("yB", g), ("yD", g)], w=["hb"])
                k.dma(yT[:, 0:2, :], R.yTA[:, tsl].rearrange("(kc p) t -> p kc t", p=128), r=[("yA", g)], w=["hT"])
            for j in range(4):
                b = 5 + (j % 2)
                pv = banks[b][:].bitcast(BF16).rearrange("p (a t) -> p a t", a=8)
                for kc in range(6):
                    k.tr(pv[:, kc, :], T.hb[:, j, kc * 128:(kc + 1) * 128], G.identb[:], r=["hb", "identb"], w=["bank%d" % b])
                k.copy("act" if j % 2 else "dve", yT[:, 2:8, j * 128:(j + 1) * 128], pv[:, 0:6, :], r=["bank%d" % b], w=["hT"])
            for j in range(4):
                for hf in range(2):
                    bk = k.rot("p3bank", 2)
                    for kc in range(8):
                        k.mm(banks[bk][:], yT[:, kc, j * 128:(j + 1) * 128], w_out[:, kc, hf * 512:(hf + 1) * 512],
                             start=(kc == 0), stop=(kc == 7), r=["hT", "w_out"], w=["bank%d" % bk])
                    ci = k.rot("c1", 2)
                    k.tt("dve", c1[ci][:], banks[bk][:], G.modB[:, 2 * D + hf * 512: 2 * D + (hf + 1) * 512], ALU.mult,
                         r=["bank%d" % bk, "modB"], w=["c1_%d" % ci])
                    k.tt("dve", xt[:, j, hf * 512:(hf + 1) * 512], xt[:, j, hf * 512:(hf + 1) * 512], c1[ci][:], ALU.add,
                         r=["xt", "c1_%d" % ci], w=["xt"])
            if G.dbg_x1 is not None:
                k.dma(G.dbg_x1[tsl, :].rearrange("(j p) d -> p j d", p=128), xt[:], r=["xt"], w=[("dbgx1", g)])
            norm_mod_T(G, xt, 4 * D, 3 * D, T)
            for u in range(11):
                wi = u % 2
                k.dma(upw[wi][:].rearrange("p a n -> p (a n)"), R.upbf[l, u], r=[("ffnw", l)], w=["upw%d" % wi])
                for sc in range(4):
                    cc = 4 * u + sc
                    bk = 2 + k.rot("p3ubank", 3)
                    for kc in range(8):
                        k.mm(banks[bk][:], upw[wi][:, kc, sc * 128:(sc + 1) * 128], T.hT[:, kc, :], start=(kc == 0), stop=(kc == 7),
                             r=["upw%d" % wi, "hT"], w=["bank%d" % bk])
                    ui = k.rot("ub", 3)
                    k.copy("pool", ub[ui][:, 0:2], carryF[:, cc, :], r=["carryF"], w=["ubc%d" % ui])
                    k.copy("act", ub[ui][:, 2:514], banks[bk][:], r=["bank%d" % bk], w=["ub%d" % ui])
                    k.copy("pool", carryF[:, cc, :], ub[ui][:, 512:514], r=["ub%d" % ui], w=["carryF"])
                    k.act(c1[ui][:], ub[ui][:, 2:514], AF.Identity, bias=convb[:, cc:cc + 1], scale=convw[:, cc, 2:3],
                          r=["ub%d" % ui, "convw", "convb"], w=["c1_%d" % ui])
                    k.stt("dve", c2[ui][:], ub[ui][:, 1:513], convw[:, cc, 1:2], c1[ui][:], ALU.mult, ALU.add,
                          r=["ub%d" % ui, "ubc%d" % ui, "convw", "c1_%d" % ui], w=["c2_%d" % ui])
                    if cc < 22:
                        k.stt("dve", actT[:, cc, :], ub[ui][:, 0:512], convw[:, cc, 0:1], c2[ui][:], ALU.mult, ALU.add,
                              r=["ub%d" % ui, "ubc%d" % ui, "convw", "c2_%d" % ui], w=["actT%d" % cc])
                    else:
                        cu = cc - 22
                        k.stt("dve", c3[ui % 2][:], ub[ui][:, 0:512], convw[:, cc, 0:1], c2[ui][:], ALU.mult, ALU.add,
                              r=["ub%d" % ui, "ubc%d" % ui, "convw", "c2_%d" % ui], w=["c3_%d" % (ui % 2)])
                        k.act(sgt[ui % 2][:], c3[ui % 2][:], AF.Silu, r=["c3_%d" % (ui % 2)], w=["sgt%d" % (ui % 2)])
                        k.tt("pool", actT[:, cu, :], actT[:, cu, :], sgt[ui % 2][:], ALU.mult, r=["actT%d" % cu, "sgt%d" % (ui % 2)],
                             w=["actT%d" % cu])
            aks = ["actT%d" % i for i in range(22)]
            for j in range(4):
                for hf in range(2):
                    bk = k.rot("p3bank", 2)
                    for cu in range(22):
                        k.mm(banks[bk][:], actT[:, cu, j * 128:(j + 1) * 128], dnw[:, cu, hf * 512:(hf + 1) * 512],
                             start=(cu == 0), stop=(cu == 21), r=["actT%d" % cu, "dnw"], w=["bank%d" % bk])
                    ci = k.rot("c1", 2)
                    k.tt("dve", c1[ci][:], banks[bk][:], G.modB[:, 5 * D + hf * 512: 5 * D + (hf + 1) * 512], ALU.mult,
                         r=["bank%d" % bk, "modB"], w=["c1_%d" % ci])
                    k.tt("dve", xt[:, j, hf * 512:(hf + 1) * 512], xt[:, j, hf * 512:(hf + 1) * 512], c1[ci][:], ALU.add,
                         r=["xt", "c1_%d" % ci], w=["xt"])
            k.dma(xdst[tsl, :].rearrange("(j p) d -> p j d", p=128), xt[:], r=["xt"], w=[("x", g)])
        P.barrier()


def phase_A(G, l):
    nc, P, k, I, R = G.nc, G.P, G.k, G.I, G.R
    banks = G.banks
    with ExitStack() as s:
        sb = lambda name, shape, dt=F32: G.sb(s, name, shape, dt)
        prm = sb("prm", [64, 7, 4])
        wup = sb("wup", [32, 256])
        aup = sb("aup", [32, 256])
        gup = sb("gup", [64, 256])
        ones64 = sb("ones64", [64, 64])
        ones64b = sb("ones64b", [64, 64], BF16)
        wupb = sb("wupb", [32, 256], BF16)
        aupb = sb("aupb", [32, 256], BF16)
        gupb = sb("gupb", [64, 256], BF16)
        wdb = sb("wdb", [32, 512], BF16)
        adb = sb("adb", [32, 512], BF16)
        gdb = sb("gdb", [64, 512], BF16)
        t1b = sb("t1b", [64, 4, 512], BF16)
        ones512 = sb("ones512", [64, 512])
        m64 = sb("m64", [64, 3, 64])
        Tst = sb("Tst", [64, 4, 64])
        big = lambda nm: sb(nm, [64, 4, 512])
        r_, k_, v_ = big("r_"), big("k_"), big("v_")
        lwt, at, gt, kkn, k2, b_, bonus, Yblk, t1, t2 = (big("lwt"), big("at"), big("gt"), big("kkn"), big("k2"), big("b_"),
                                                         big("bonus"), big("Yblk"), big("t1"), big("t2"))
        Gblk = sb("Gblk", [64, 4, 513])
        wd = sb("wd", [32, 512])
        ad = sb("ad", [32, 512])
        gd = sb("gd", [64, 512])
        yo = sb("yo", [64, 4, 512], BF16)
        sm = lambda nm: sb(nm, [64, 4, 64])
        Pc, Pp, eGn, eGp = sm("Pc"), sm("Pp"), sm("eGn"), sm("eGp")
        smb = lambda nm: sb(nm, [64, 4, 64], BF16)
        Bt, Kt = smb("Bt"), smb("Kt")
        Pj = [smb("Pj0"), smb("Pj1")]
        U, tS = sm("U"), sm("tS")
        Ub, Tb = smb("Ub"), smb("Tb")
        DB = []
        for q in range(2):
            d = {"eG": sm("eG%d" % q)}
            for nm_ in ("At", "Rt", "Btm", "Ktm", "Vtm", "LakT", "MrbT", "MrkT"):
                d[nm_] = smb("%s%d" % (nm_, q))
            d["PT"] = [smb("PT%d_%d" % (j, q)) for j in range(6)]
            DB.append(d)
        vb = sb("vb", [64, 4, 512], BF16)

        for n, nm in enumerate(("rw_w0", "rw_a0", "rw_k_k", "rw_k_a")):
            k.dma(prm[:, n, :], I[nm][l].rearrange("(h k) -> k h", k=64), w=["prm"], slow=True)
        k.dma(prm[:, 4, :], I["rw_r_k"][l].rearrange("h k -> k h"), w=["prm"], slow=True)
        k.dma(prm[:, 5, :], I["rw_ln_g"][l].rearrange("(h k) -> k h", k=64), w=["prm"], slow=True)
        k.dma(prm[:, 6, :], I["rw_ln_b"][l].rearrange("(h k) -> k h", k=64), w=["prm"], slow=True)
        k.dma(wup[:], I["rw_w_up"][l], w=["wup"])
        k.dma(aup[:], I["rw_a_up"][l], w=["aup"])
        k.dma(gup[:], I["rw_g_up"][l], w=["gup"])
        k.dma(m64[:], I["m64"], w=["m64"])
        k.memset("dve", ones64[:], 1.0, w=["ones64"])
        k.memset("dve", ones64b[:], 1.0, w=["ones64b"])
        k.copy("dve", wupb[:], wup[:], r=["wup"], w=["wupb"])
        k.copy("dve", aupb[:], aup[:], r=["aup"], w=["aupb"])
        k.copy("dve", gupb[:], gup[:], r=["gup"], w=["gupb"])
        k.memset("dve", ones512[:], 1.0, w=["ones512"])
        k.memset("pool", Tst[:], 0.0, w=["Tst"])
        k.memset("pool", Tb[:], 0.0, w=["Tb"])
        k.memset("pool", Gblk[:], 0.0, w=["Gblk"])

        def bc(col):
            return prm[:, col, :].unsqueeze(2).to_broadcast([64, 4, 512])

        def HB(b, half):
            return banks[b][0:64, half * 256:(half + 1) * 256].rearrange("p (h t) -> p h t", h=4)

        def hk(b, half):
            return ("hb", b)

        def fb(b):
            return [("hb", b)]
        allpm = [("pmA", g) for g in range(NG)]

        for g in range(NG):
            tsl = slice(g * TG, (g + 1) * TG)
            k.dma(r_[:], R.pmA[0:256, tsl].rearrange("(h k) t -> k h t", k=64), r=allpm, w=["r_"])
            k.dma(k_[:], R.pmA[256:512, tsl].rearrange("(h k) t -> k h t", k=64), r=allpm, w=["k_"])
            k.dma(v_[:], R.pmA[512:768, tsl].rearrange("(h k) t -> k h t", k=64), r=allpm, w=["v_"])
            k.copy("pool", vb[:], v_[:], r=["v_"], w=["vb"])
            k.dma(wd[:], R.pmA[768:800, tsl], r=allpm, w=["wd"])
            k.dma(ad[:], R.pmA[800:832, tsl], r=allpm, w=["ad"])
            k.dma(gd[:], R.pmA[832:896, tsl], r=allpm, w=["gd"])
            k.act(wdb[:], wd[:], AF.Tanh, r=["wd"], w=["wdb"])
            k.act(gdb[:], gd[:], AF.Sigmoid, r=["gd"], w=["gdb"])
            k.copy("pool", adb[:], ad[:], r=["ad"], w=["adb"])
            for h in range(4):
                k.mm(banks[h][0:64, :], wupb[:, h * 64:(h + 1) * 64], wdb[:], r=["wupb", "wdb"], w=fb(h))
                k.act(lwt[:, h, :], banks[h][0:64, :], AF.Sigmoid, bias=prm[:, 0, h:h + 1], scale=1.0, r=fb(h) + ["prm"], w=["lwt"])
            for h in range(4):
                k.mm(banks[4 + h][0:64, :], aupb[:, h * 64:(h + 1) * 64], adb[:], r=["aupb", "adb"], w=fb(4 + h))
                k.act(at[:, h, :], banks[4 + h][0:64, :], AF.Sigmoid, bias=prm[:, 1, h:h + 1], scale=1.0, r=fb(4 + h) + ["prm"], w=["at"])
            for h in range(4):
                k.mm(banks[h][0:64, :], gupb[:, h * 64:(h + 1) * 64], gdb[:], r=["gupb", "gdb"], w=fb(h))
                k.copy("dve" if h % 2 else "act", gt[:, h, :], banks[h][0:64, :], r=fb(h), w=["gt"])
            k.tt("dve", kkn[:], k_[:], bc(2), ALU.mult, r=["k_", "prm"], w=["kkn"])
            k.tt("dve", t1b[:], kkn[:], kkn[:], ALU.mult, r=["kkn"], w=["t1b"])
            for h in range(4):
                k.mm(banks[4 + h][0:64, :], ones64b[:], t1b[:, h, :], r=["ones64b", "t1b"], w=fb(4 + h))
            for h in range(4):
                k.act(t2[:, h, :], banks[4 + h][0:64, :], AF.Ln, bias=1e-24, scale=1.0, r=fb(4 + h), w=["t2"])
            k.act(t2[:], t2[:], AF.Exp, scale=-0.5, r=["t2"], w=["t2"])
            k.tt("pool", kkn[:], kkn[:], t2[:], ALU.mult, r=["kkn", "t2"], w=["kkn"])
            k.stt("dve", t1[:], at[:], -1.0, bc(3), ALU.add, ALU.mult, r=["at", "prm", "t1"], w=["t1"])
            k.tt("pool", t1[:], t1[:], k_[:], ALU.mult, r=["t1", "k_"], w=["t1"])
            k.tt("dve", k2[:], t1[:], k_[:], ALU.add, r=["t1", "k_"], w=["k2"])
            k.tt("pool", b_[:], kkn[:], at[:], ALU.mult, r=["kkn", "at"], w=["b_"])
            k.tt("dve", t1[:], r_[:], k2[:], ALU.mult, r=["r_", "k2", "t1"], w=["t1"])
            k.tt("dve", t1b[:], t1[:], bc(4), ALU.mult, r=["t1", "prm"], w=["t1b"])
            for h in range(4):
                k.mm(banks[h][0:64, :], ones64b[:], t1b[:, h, :], r=["ones64b", "t1b"], w=fb(h))
                k.tt("dve", bonus[:, h, :], banks[h][0:64, :], v_[:, h, :], ALU.mult, r=fb(h) + ["v_"], w=["bonus"])
            for h in range(4):
                k.scan(Gblk[:, h, 1:513], ones512[:], lwt[:, h, :], 0.0, ALU.mult, ALU.add, r=["ones512", "lwt"], w=["Gblk"])

            if int(os.environ.get("A_STOP", "9")) <= 1:
                continue
            def prep_steps(ci):
                q = ci % 2
                D_ = DB[q]
                c0 = ci * 64
                ts_ = slice(c0, c0 + 64)
                eGq, Atq, Rtq = D_["eG"], D_["At"], D_["Rt"]
                nm = lambda x: "%s_%d" % (x, q)
                id64 = G.identb[0:64, 0:64]
                steps = []

                def s1():
                    k.tt("dve", Pc[:], Gblk[:, :, 1 + c0:1 + c0 + 64], Gblk[:, :, c0:c0 + 1].to_broadcast([64, 4, 64]), ALU.subtract,
                         r=["Gblk"], w=["Pc"])
                    k.tt("pool", Pp[:], Pc[:], lwt[:, :, ts_], ALU.subtract, r=["Pc", "lwt"], w=["Pp"])
                    k.act(eGq[:], Pc[:], AF.Exp, scale=-ALPHA, r=["Pc"], w=[nm("eG")])
                    k.act(eGn[:], Pc[:], AF.Exp, scale=ALPHA, r=["Pc"], w=["eGn"])
                    k.act(eGp[:], Pp[:], AF.Exp, scale=-ALPHA, r=["Pp"], w=["eGp"])
                steps.append(s1)

                def s2():
                    k.stt("dve", Atq[:], kkn[:, :, ts_], -1.0, eGp[:], ALU.mult, ALU.mult, r=["kkn", "eGp"], w=[nm("At")])
                    k.tt("pool", Bt[:], b_[:, :, ts_], eGn[:], ALU.mult, r=["b_", "eGn"], w=["Bt"])
                    k.tt("pool", Kt[:], k2[:, :, ts_], eGn[:], ALU.mult, r=["k2", "eGn"], w=["Kt"])
                    k.tt("dve", Rtq[:], r_[:, :, ts_], eGq[:], ALU.mult, r=["r_", nm("eG")], w=[nm("Rt")])
                steps.append(s2)

                def s3():
                    trs = ((Bt, "Bt", D_["Btm"], nm("Btm"), 2, 1, "act"), (Kt, "Kt", D_["Ktm"], nm("Ktm"), 3, 0, "act"),
                           (None, "vb", D_["Vtm"], nm("Vtm"), 3, 1, "act"))
                    for (X, xk, Xtm, xtk, bq, hf, ce) in trs:
                        for h in range(4):
                            src = vb[:, h, ts_] if X is None else X[:, h, :]
                            k.mm(HB(bq, hf)[:, h, :], src, id64, r=[xk, "identb"], w=[hk(bq, hf)])
                    for (X, xk, Xtm, xtk, bq, hf, ce) in trs:
                        k.copy(ce, Xtm[:], HB(bq, hf), r=[hk(bq, hf)], w=[xtk])
                steps.append(s3)

                def s4():
                    specs = ((Bt, "Bt", Atq, nm("At"), 0, 0, D_["PT"][0], nm("PT0"), 0), (Atq, nm("At"), Bt, "Bt", 0, 1, Pj[0], "Pj0", 2),
                             (Kt, "Kt", Atq, nm("At"), 1, 0, D_["LakT"], nm("LakT"), 0), (Bt, "Bt", Rtq, nm("Rt"), 1, 1, D_["MrbT"], nm("MrbT"), 1),
                             (Kt, "Kt", Rtq, nm("Rt"), 2, 0, D_["MrkT"], nm("MrkT"), 1))
                    for (La, lk, Ra, rk_, bq, hf, dst, dk, mi) in specs:
                        for h in range(4):
                            k.mm(HB(bq, hf)[:, h, :], La[:, h, :], Ra[:, h, :], r=[lk, rk_], w=[hk(bq, hf)])
                    for (La, lk, Ra, rk_, bq, hf, dst, dk, mi) in specs:
                        k.tt("dve", dst[:], HB(bq, hf), m64[:, mi, :].unsqueeze(1).to_broadcast([64, 4, 64]), ALU.mult,
                             r=[hk(bq, hf), "m64"], w=[dk])
                steps.append(s4)

                def mk_sq(j):
                    def sq():
                        cur, nxt = j % 2, (j + 1) % 2
                        PTc, PTn = D_["PT"][j], D_["PT"][j + 1]
                        for h in range(4):
                            k.mm(HB(5, 0)[:, h, :], PTc[:, h, :], Pj[cur][:, h, :], r=[nm("PT%d" % j), "Pj%d" % cur], w=[hk(5, 0)])
                        for h in range(4):
                            k.mm(HB(5, 1)[:, h, :], Pj[cur][:, h, :], PTc[:, h, :], r=[nm("PT%d" % j), "Pj%d" % cur], w=[hk(5, 1)])
                        k.copy("act", Pj[nxt][:], HB(5, 0), r=[hk(5, 0)], w=["Pj%d" % nxt])
                        k.copy("act", PTn[:], HB(5, 1), r=[hk(5, 1)], w=[nm("PT%d" % (j + 1))])
                    return sq
                for j in range(5):
                    steps.append(mk_sq(j))
                return steps

            def chain_steps(ci):
                q = ci % 2
                D_ = DB[q]
                c0 = ci * 64
                ts_ = slice(c0, c0 + 64)
                eGq, Atq, Rtq = D_["eG"], D_["At"], D_["Rt"]
                Btm, Ktm, Vtm, LakT, MrbT, MrkT = D_["Btm"], D_["Ktm"], D_["Vtm"], D_["LakT"], D_["MrbT"], D_["MrkT"]
                nm = lambda x: "%s_%d" % (x, q)
                steps = []

                def c1():
                    for h in range(4):
                        k.mm(HB(4, 0)[:, h, :], Atq[:, h, :], Tb[:, h, :], start=True, stop=False, r=[nm("At"), "Tb"], w=[hk(4, 0)])
                        k.mm(HB(4, 0)[:, h, :], LakT[:, h, :], Vtm[:, h, :], start=False, stop=True, r=[nm("LakT"), nm("Vtm")], w=[hk(4, 0)])
                    k.copy("dve", Ub[:], HB(4, 0), r=[hk(4, 0)], w=["Ub"])
                    k.copy("act", U[:], HB(4, 0), r=[hk(4, 0)], w=["U"])
                steps.append(c1)

                def mk_u(j):
                    def us():
                        PTc = D_["PT"][j]
                        for h in range(4):
                            k.mm(HB(4, 1)[:, h, :], PTc[:, h, :], Ub[:, h, :], r=[nm("PT%d" % j), "Ub"], w=[hk(4, 1)])
                        k.tt("dve", Ub[:], U[:], HB(4, 1), ALU.add, r=["U", hk(4, 1)], w=["Ub"])
                        if j < 5:
                            k.tt("dve", U[:], U[:], HB(4, 1), ALU.add, r=["U", hk(4, 1)], w=["U"])
                    return us
                for j in range(6):
                    steps.append(mk_u(j))

                def c8():
                    for h in range(4):
                        k.mm(HB(6, 0)[:, h, :], Tb[:, h, :], Rtq[:, h, :], start=True, stop=False, r=["Tb", nm("Rt")], w=[hk(6, 0)])
                        k.mm(HB(6, 0)[:, h, :], Ub[:, h, :], MrbT[:, h, :], start=False, stop=False, r=["Ub", nm("MrbT")], w=[hk(6, 0)])
                        k.mm(HB(6, 0)[:, h, :], Vtm[:, h, :], MrkT[:, h, :], start=False, stop=True, r=[nm("Vtm"), nm("MrkT")], w=[hk(6, 0)])
                    k.copy("act", Yblk[:, :, ts_], HB(6, 0), r=[hk(6, 0)], w=["Yblk"])
                    for h in range(4):
                        k.mm(HB(7, 0)[:, h, :], Btm[:, h, :], Ub[:, h, :], start=True, stop=False, r=[nm("Btm"), "Ub"], w=[hk(7, 0)])
                        k.mm(HB(7, 0)[:, h, :], Ktm[:, h, :], Vtm[:, h, :], start=False, stop=True, r=[nm("Ktm"), nm("Vtm")], w=[hk(7, 0)])
                    k.tt("dve", tS[:], HB(7, 0), Tst[:], ALU.add, r=[hk(7, 0), "Tst"], w=["tS"])
                    k.tt("dve", Tb[:], tS[:], eGq[:, :, 63:64].to_broadcast([64, 4, 64]), ALU.mult, r=["tS", nm("eG")], w=["Tb"])
                    k.tt("pool", Tst[:], tS[:], eGq[:, :, 63:64].to_broadcast([64, 4, 64]), ALU.mult, r=["tS", nm("eG")], w=["Tst"])
                steps.append(c8)
                return steps

            for st in prep_steps(0):
                st()
            for ci in range(8):
                cs = chain_steps(ci)
                ps = prep_steps(ci + 1) if ci < 7 else []
                n = max(len(cs), len(ps))
                for i_ in range(n):
                    if i_ < len(cs):
                        cs[i_]()
                    if i_ < len(ps):
                        ps[i_]()

            if int(os.environ.get("A_STOP", "9")) <= 3:
                continue
            for h in range(4):
                k.mm(banks[h][0:64, :], ones64[:], Yblk[:, h, :], r=["ones64", "Yblk"], w=fb(h))
                k.stt("dve", t1[:, h, :], banks[h][0:64, :], -1.0 / 64, Yblk[:, h, :], ALU.mult, ALU.add, r=fb(h) + ["Yblk"], w=["t1"])
            k.tt("dve", t1b[:], t1[:], t1[:], ALU.mult, r=["t1"], w=["t1b"])
            for h in range(4):
                k.mm(banks[4 + h][0:64, :], ones64b[:], t1b[:, h, :], r=["ones64b", "t1b"], w=fb(4 + h))
            for h in range(4):
                k.act(t2[:, h, :], banks[4 + h][0:64, :], AF.Ln, bias=64e-5, scale=1.0 / 64, r=fb(4 + h), w=["t2"])
            k.act(t2[:], t2[:], AF.Exp, scale=-0.5, r=["t2"], w=["t2"])
            k.tt("pool", t1[:], t1[:], t2[:], ALU.mult, r=["t1", "t2"], w=["t1"])
            k.tt("dve", t1[:], t1[:], bc(5), ALU.mult, r=["t1", "prm"], w=["t1"])
            k.tt("pool", t1[:], t1[:], bc(6), ALU.add, r=["t1", "prm"], w=["t1"])
            k.tt("dve", t1[:], t1[:], bonus[:], ALU.add, r=["t1", "bonus"], w=["t1"])
            k.tt("dve", yo[:], t1[:], gt[:], ALU.mult, r=["t1", "gt"], w=["yo"])
            if not G.y_in:
                k.dma(R.yTA[:, tsl].rearrange("(h v) t -> v h t", v=64), yo[:], r=["yo"], w=[("yA", g)])
        P.barrier()


def attn_common(G, s, Vsrc, mix):
    k, R, I = G.k, G.R, G.I
    sb = lambda name, shape, dt=F32: G.sb(s, name, shape, dt)
    A = Ctx()
    A.kT = [sb("kT%d" % i, [64, S], BF16) for i in range(2)]
    A.V = sb("V", [128, 32, 260], BF16)
    Vv = Vsrc.rearrange("(kt p) c -> p kt c", p=128)
    for q4 in range(4):
        k.dma(A.V[:, q4 * 8:(q4 + 1) * 8, :], Vv[:, q4 * 8:(q4 + 1) * 8, :], r=[(mix + "v", g) for g in range(NG)], w=["V"])
    A.Pm = [sb("Pm%d" % i, [128, 512], BF16) for i in range(5)]
    A.rec = [sb("rec%d" % i, [128, 8]) for i in range(2)]
    A.ob = [sb("ob%d" % i, [128, 4, 64], BF16) for i in range(2)]
    return A


def phase_D(G, l):
    nc, P, k, I, R = G.nc, G.P, G.k, G.I, G.R
    banks = G.banks
    with ExitStack() as s:
        sb = lambda name, shape, dt=F32: G.sb(s, name, shape, dt)
        A = attn_common(G, s, R.vD, "D")
        Fk = sb("Fk", [128, 4, 32])
        Frow = sb("Frow", [4, S])
        sel = sb("sel", [4, 4, 128])
        negm = sb("negm", [128, 128])
        qT = [sb("qT%d" % i, [64, 512], BF16) for i in range(2)]
        FqB = [sb("FqB%d" % i, [128, 512]) for i in range(2)]
        FqD = [sb("FqD%d" % i, [128, 512]) for i in range(2)]
        tb = [sb("tb%d" % i, [128, 512]) for i in range(5)]
        sbanks = [0, 1, 2, 6, 7]
        allF = [("Ffm", g) for g in range(NG)]
        fkn = tb[0][0:32, :].rearrange("p (h c) -> p h c", h=4)
        for h in range(4):
            k.dma(fkn[:, h, :], R.Ffm[h].rearrange("(kt p) -> kt p", p=128), r=allF, w=["tb0"])
        for h in range(4):
            k.mm(banks[5][:, h * 32:(h + 1) * 32], fkn[:, h, :], G.ident[0:32, 0:32], r=["tb0", "ident"], w=["bank5"])
        k.copy("dve", Fk[:].rearrange("p h c -> p (h c)"), banks[5][:, 0:128], r=["bank5"], w=["Fk"])
        k.dma(Frow[:], R.Ffm, r=allF, w=["Frow"])
        k.dma(sel[:], I["sel"], w=["sel"])
        k.dma(negm[:], I["negmask"], w=["negm"])
        allqk = [("Dqk", g) for g in range(NG)]
        for h in range(4):
            kt_ = A.kT[h % 2]
            kk = "kT%d" % (h % 2)
            k.dma(kt_[:], R.kTD[h * 64:(h + 1) * 64, :], r=allqk, w=[kk])
            for g in range(NG):
                i = k.rot("Dq", 2)
                k.dma(qT[i][:], R.qTD[h * 64:(h + 1) * 64, g * TG:(g + 1) * TG], r=allqk, w=["qT%d" % i])
                k.mm(banks[5][:], sel[:, h, :], Frow[:, g * TG:(g + 1) * TG], r=["sel", "Frow"], w=["bank5"])
                k.copy("act", FqB[i][:], banks[5][:], r=["bank5"], w=["FqB%d" % i])
                k.tt("pool", FqD[i][:].rearrange("p (a b) -> p a b", a=4), FqB[i][:].rearrange("p (a b) -> p a b", a=4),
                     negm[:].unsqueeze(1).to_broadcast([128, 4, 128]), ALU.add, r=["FqB%d" % i, "negm"], w=["FqD%d" % i])
                ob_ = 3 + k.rot("DO", 2)
                O = banks[ob_][:, 0:260].rearrange("p (a e) -> p a e", a=4)
                def d_stage1(kt):
                    m = kt - 4 * g
                    c0 = max(m, 0) * 128
                    N = 512 - c0
                    sbk = sbanks[k.rot("Ds", 5)]
                    k.mm(banks[sbk][:, 0:N], kt_[:, kt * 128:(kt + 1) * 128], qT[i][:, c0:512], r=[kk, "qT%d" % i], w=["bank%d" % sbk])
                    ti = k.rot("Dt", 5)
                    if m < 0:
                        k.stt("dve", tb[ti][:], banks[sbk][:], Fk[:, h, kt:kt + 1], FqB[i][:], ALU.subtract, ALU.add,
                              r=["bank%d" % sbk, "Fk", "FqB%d" % i], w=["tb%d" % ti])
                    else:
                        k.stt("dve", tb[ti][:, 0:128], banks[sbk][:, 0:128], Fk[:, h, kt:kt + 1], FqD[i][:, c0:c0 + 128],
                              ALU.subtract, ALU.add, r=["bank%d" % sbk, "Fk", "FqD%d" % i], w=["tb%d" % ti])
                        if N > 128:
                            k.stt("dve", tb[ti][:, 128:N], banks[sbk][:, 128:N], Fk[:, h, kt:kt + 1], FqB[i][:, c0 + 128:512],
                                  ALU.subtract, ALU.add, r=["bank%d" % sbk, "Fk", "FqB%d" % i], w=["tb%d" % ti])
                    k.act(A.Pm[ti][:, 0:N], tb[ti][:, 0:N], AF.Exp, r=["tb%d" % ti], w=["Pm%d" % ti])
                    return (kt, m, c0, ti)

                def d_stage2(st):
                    kt, m, c0, ti = st
                    for jq in range(max(m, 0), 4):
                        k.mm(O[:, jq, :], A.Pm[ti][:, jq * 128 - c0: jq * 128 - c0 + 128], A.V[:, kt, h * 65:(h + 1) * 65],
                             start=(kt == 0 and jq == 0), stop=(kt == 4 * g + jq), r=["Pm%d" % ti, "V"], w=["bank%d" % ob_], sgc=True)
                pend = []
                for kt in range(4 * g + 4):
                    pend.append(d_stage1(kt))
                    if len(pend) > 3:
                        d_stage2(pend.pop(0))
                while pend:
                    d_stage2(pend.pop(0))
                ri = k.rot("Drec", 2)
                k.recip(A.rec[ri][:, 0:4], O[:, :, 64], r=["bank%d" % ob_], w=["rec%d" % ri])
                k.tt("dve", A.ob[ri][:], O[:, :, 0:64], A.rec[ri][:, 0:4].unsqueeze(2).to_broadcast([128, 4, 64]), ALU.mult,
                     r=["bank%d" % ob_, "rec%d" % ri], w=["ob%d" % ri])
                k.dma(R.yscr[g * TG:(g + 1) * TG, 512 + h * 64: 512 + (h + 1) * 64].rearrange("(a p) e -> p a e", p=128),
                      A.ob[ri][:], r=["ob%d" % ri], w=[("yD", g)], slow=True)
        P.barrier()


def phase_B(G, l):
    nc, P, k, I, R = G.nc, G.P, G.k, G.I, G.R
    banks = G.banks
    lambda_init = 0.8 - 0.6 * math.exp(-0.3 * l)
    with ExitStack() as s:
        sb = lambda name, shape, dt=F32: G.sb(s, name, shape, dt)
        A = attn_common(G, s, R.vB, "B")
        cmf = sb("cmf", [128, 128])
        cm = sb("cm", [128, 128], BF16)
        qp = [[sb("qp%d_%d" % (i, mp), [64, 512], BF16) for mp in range(2)] for i in range(2)]
        lamv = sb("lamv", [128, 4, 32])
        lamw = sb("lamw", [128, 2, 32])
        lams = sb("lams", [128, 4])
        subg = sb("subg", [128, 64])
        o1 = [sb("o1_%d" % i, [128, 4, 64]) for i in range(2)]
        o2 = [sb("o2_%d" % i, [128, 4, 64]) for i in range(2)]
        k.dma(cmf[:], I["cmask"], w=["cmf"])
        k.copy("dve", cm[:], cmf[:], r=["cmf"], w=["cm"])
        for i in range(2):
            for mp in range(2):
                k.memset("pool", qp[i][mp][:], 0.0, w=["qp%d" % i])
        for n, nm in enumerate(("df_lam_q1", "df_lam_k1", "df_lam_q2", "df_lam_k2")):
            bcast_load(G, lamv[:, n, :], I[nm][l:l + 1, :], 32, ["lamv"])
        bcast_load(G, subg[:], I["df_sub_g"][l:l + 1, :], 64, ["subg"])
        k.ts("dve", subg[:], subg[:], 1.0 - lambda_init, ALU.mult, r=["subg"], w=["subg"])
        k.tt("dve", lamw[:, 0, :], lamv[:, 0, :], lamv[:, 1, :], ALU.mult, r=["lamv"], w=["lamw"])
        k.tt("dve", lamw[:, 1, :], lamv[:, 2, :], lamv[:, 3, :], ALU.mult, r=["lamv"], w=["lamw"])
        k.red("dve", lams[:, 0:2], lamw[:], r=["lamw"], w=["lams"])
        k.act(lams[:, 0:2], lams[:, 0:2], AF.Exp, r=["lams"], w=["lams"])
        k.ts("dve", lams[:, 2:3], lams[:, 1:2], -lambda_init, ALU.add, r=["lams"], w=["lams"])
        k.tt("dve", lams[:, 3:4], lams[:, 2:3], lams[:, 0:1], ALU.subtract, r=["lams"], w=["lams"])
        allqk = [("Bqk", g) for g in range(NG)]
        jobs = []
        if G.prep_in_B:
            pst = [sb("pf_st%d" % i_, [128, 8, 512]) for i_ in range(2)]
            pbf = [sb("pf_bf%d" % i_, [128, 8, 512], BF16) for i_ in range(2)]
            up = I["ffn_up"][l].rearrange("(kc p) n -> p kc n", p=128)
            dn = I["ffn_down"][l].rearrange("(fc p) d -> p fc d", p=128)
            for u in range(11):
                jobs.append((up[:, :, u * 512:(u + 1) * 512], R.upbf[l, u].rearrange("p (kc n) -> p kc n", kc=8), 8, 512))
            for u in range(11):
                jobs.append((dn[:, 2 * u:2 * u + 2, :], R.dnbf[l][:, 2 * u * 1024:(2 * u + 2) * 1024].rearrange("p (a d) -> p a d", a=2), 2, 1024))

        def pf_load(i_):
            src, dst, a_, b_ = jobs[i_]
            k.dma(pst[i_ % 2][:].rearrange("p a b -> p (a b)")[:, 0:a_ * b_].rearrange("p (a b) -> p a b", a=a_), src,
                  w=["pf_st%d" % (i_ % 2)], q="pool")

        def pf_job(i_):
            src, dst, a_, b_ = jobs[i_]
            sv = pst[i_ % 2][:].rearrange("p a b -> p (a b)")[:, 0:a_ * b_]
            bv = pbf[i_ % 2][:].rearrange("p a b -> p (a b)")[:, 0:a_ * b_]
            k.copy("dve" if i_ % 3 else "pool", bv, sv, r=["pf_st%d" % (i_ % 2)], w=["pf_bf%d" % (i_ % 2)])
            k.dma(dst, bv.rearrange("p (a b) -> p a b", a=a_), r=["pf_bf%d" % (i_ % 2)], w=[("ffnw", l)], q="pool")
            if i_ + 2 < len(jobs):
                pf_load(i_ + 2)
        if jobs:
            pf_load(0)
            pf_load(1)
        jn = [0]
        for h in range(4):
            kt_ = A.kT[h % 2]
            kk = "kT%d" % (h % 2)
            k.dma(kt_[:], R.kTB[h * 64:(h + 1) * 64, :], r=allqk, w=[kk])
            for g in range(NG):
                if jobs and jn[0] < len(jobs) and (h * NG + g) >= 2:
                    pf_job(jn[0])
                    jn[0] += 1
                i = k.rot("Bq", 2)
                k.dma(qp[i][0][0:32, :], R.qTB[h * 64:h * 64 + 32, g * TG:(g + 1) * TG], r=allqk, w=["qp%d" % i])
                k.dma(qp[i][1][32:64, :], R.qTB[h * 64 + 32:h * 64 + 64, g * TG:(g + 1) * TG], r=allqk, w=["qp%d" % i])
                oi = k.rot("BO", 2)
                Os = [banks[3 + oi][:, 0:260].rearrange("p (a e) -> p a e", a=4),
                      banks[5 + oi][:, 0:260].rearrange("p (a e) -> p a e", a=4)]
                obk = ["bank%d" % (3 + oi), "bank%d" % (5 + oi)]
                def b_stage1(kt, mp):
                    m = kt - 4 * g
                    c0 = max(m, 0) * 128
                    N = 512 - c0
                    sbk = (0, 1, 2, 7)[k.rot("Bs", 4)]
                    k.mm(banks[sbk][:, 0:N], kt_[:, kt * 128:(kt + 1) * 128], qp[i][mp][:, c0:512], r=[kk, "qp%d" % i],
                         w=["bank%d" % sbk])
                    ti = k.rot("Bt", 4)
                    k.act(A.Pm[ti][:, 0:N], banks[sbk][:, 0:N], AF.Exp, r=["bank%d" % sbk], w=["Pm%d" % ti])
                    if m >= 0:
                        k.tt("dve", A.Pm[ti][:, 0:128], A.Pm[ti][:, 0:128], cm[:], ALU.mult, r=["Pm%d" % ti, "cm"], w=["Pm%d" % ti])
                    return (kt, mp, m, c0, ti)

                def b_stage2(st):
                    kt, mp, m, c0, ti = st
                    for jq in range(max(m, 0), 4):
                        k.mm(Os[mp][:, jq, :], A.Pm[ti][:, jq * 128 - c0: jq * 128 - c0 + 128], A.V[:, kt, h * 65:(h + 1) * 65],
                             start=(kt == 0 and jq == 0), stop=(kt == 4 * g + jq), r=["Pm%d" % ti, "V"], w=[obk[mp]], sgc=True)
                pend = []
                for kt in range(4 * g + 4):
                    for mp in range(2):
                        pend.append(b_stage1(kt, mp))
                        if len(pend) > 3:
                            b_stage2(pend.pop(0))
                while pend:
                    b_stage2(pend.pop(0))
                ri = k.rot("Brec", 2)
                rc = A.rec[ri]
                rk = "rec%d" % ri
                k.recip(rc[:, 0:4], Os[0][:, :, 64], r=[obk[0]], w=[rk])
                k.recip(rc[:, 4:8], Os[1][:, :, 64], r=[obk[1]], w=[rk])
                k.ts("dve", rc[:, 4:8], rc[:, 4:8], lams[:, 3:4], ALU.mult, r=[rk, "lams"], w=[rk])
                k.tt("dve", o1[ri][:], Os[0][:, :, 0:64], rc[:, 0:4].unsqueeze(2).to_broadcast([128, 4, 64]), ALU.mult,
                     r=[obk[0], rk], w=["o1_%d" % ri])
                k.tt("dve", o2[ri][:], Os[1][:, :, 0:64], rc[:, 4:8].unsqueeze(2).to_broadcast([128, 4, 64]), ALU.mult,
                     r=[obk[1], rk], w=["o2_%d" % ri])
                k.tt("pool", o1[ri][:], o1[ri][:], o2[ri][:], ALU.add, r=["o1_%d" % ri, "o2_%d" % ri], w=["o1_%d" % ri])
                k.tt("pool", o2[ri][:], o1[ri][:], o1[ri][:], ALU.mult, r=["o1_%d" % ri], w=["o2_%d" % ri])
                k.red("dve", rc[:, 0:4], o2[ri][:], r=["o2_%d" % ri], w=[rk])
                k.act(rc[:, 0:4], rc[:, 0:4], AF.Sqrt, bias=EPS, scale=1.0 / 64, r=[rk], w=[rk])
                k.recip(rc[:, 0:4], rc[:, 0:4], r=[rk], w=[rk])
                k.tt("dve", o1[ri][:], o1[ri][:], rc[:, 0:4].unsqueeze(2).to_broadcast([128, 4, 64]), ALU.mult,
                     r=["o1_%d" % ri, rk], w=["o1_%d" % ri])
                k.tt("pool", A.ob[ri][:], o1[ri][:], subg[:].unsqueeze(1).to_broadcast([128, 4, 64]), ALU.mult,
                     r=["o1_%d" % ri, "subg"], w=["ob%d" % ri])
                k.dma(R.yscr[g * TG:(g + 1) * TG, h * 64:(h + 1) * 64].rearrange("(a p) e -> p a e", p=128),
                      A.ob[ri][:], r=["ob%d" % ri], w=[("yB", g)], slow=True)
        while jobs and jn[0] < len(jobs):
            pf_job(jn[0])
            jn[0] += 1
        P.barrier()


_CACHE = {}


def kernel(**inputs):
    if "prog" not in _CACHE:
        _CACHE["prog"] = build()
    nc, P = _CACHE["prog"]
    consts = make_consts()
    weights = {k: np.ascontiguousarray(np.asarray(inputs[k], dtype=np.float32)) for k in WEIGHT_SHAPES}
    x = np.asarray(inputs["x"], dtype=np.float32)
    c = np.asarray(inputs["c"], dtype=np.float32)
    in_maps = []
    for b in range(8):
        m = {"x": np.ascontiguousarray(x[b]), "c": np.ascontiguousarray(c[b:b + 1])}
        m.update(weights)
        m.update(consts)
        in_maps.append(m)
    res = run_bass_kernel_spmd(nc, in_maps, core_ids=list(range(8)))
    return np.stack([np.asarray(r["out"], dtype=np.float32) for r in res.results], axis=0)
```
